# Optimizing a Trainium2 kernel written in Bass

```python
import math
import jax, jax.numpy as jnp
from jax import lax
import numpy as np

D_MODEL = 1024
BATCH = 2
SEQ = 8192
DEPTH = 2
DEC_BATCH = 1
DEC_SEQ = 16384
PAST_LEN = 128

MEM_LEN = 256
D_FF = 2816
EPS = 1e-6
N_MIXERS = 2
GLA_HEADS = 4
GLA_DK = D_MODEL // 2 // GLA_HEADS
GLA_DV = D_MODEL // GLA_HEADS
GLA_RANK = 16
GLA_TAU = 16.0
GLA_CHUNK = 64
GLA_IN = 2 * GLA_HEADS * GLA_DK + 2 * GLA_HEADS * GLA_DV + 2 * GLA_RANK
DIL_GROUPS = ((128, 1), (512, 4), (2048, 16))
N_DIL = len(DIL_GROUPS)
DIL_HEADS = 16
DIL_HD = D_MODEL // DIL_HEADS
DIL_QKV = 3 * N_DIL * DIL_HEADS * DIL_HD
NUM_BUCKETS = 32
MAX_DISTANCE = 1024
X_HEADS = 4
X_HD = D_MODEL // X_HEADS
N_A = (DEPTH + 1) // 2
N_B = DEPTH // 2
NEG = -1e30

kernel_name = 'hybrid_gla_dilated_encoder'


def rmsnorm(x, g):
    xf = x.astype(jnp.float32)
    y = xf * lax.rsqrt(jnp.mean(xf * xf, axis=-1, keepdims=True) + EPS)
    return (y * g.astype(jnp.float32)).astype(x.dtype)


def swiglu(x, w_in, w_out):
    a, b = jnp.split(x @ w_in, 2, axis=-1)
    return (jax.nn.silu(a) * b) @ w_out


def t5_bucket(rel):
    half = NUM_BUCKETS // 2
    max_exact = half // 2
    ret = (rel > 0).astype(np.int32) * half
    n = np.abs(rel)
    large = max_exact + (np.log(np.maximum(n, 1) / max_exact) / np.log(MAX_DISTANCE / max_exact) * (half - max_exact)).astype(np.int32)
    large = np.minimum(large, half - 1)
    return (ret + np.where(n < max_exact, n, large)).astype(np.int32)


def gla_scan(q, k, v, g):
    B, H, S, DK = q.shape
    DV = v.shape[-1]
    n = S // GLA_CHUNK
    causal = np.tril(np.ones((GLA_CHUNK, GLA_CHUNK), dtype=bool))

    def chunks(t):
        return jnp.moveaxis(t.reshape(B, H, n, GLA_CHUNK, t.shape[-1]), 2, 0)

    def step(state, inp):
        qc, kc, vc, gc = inp
        b = jnp.cumsum(gc, axis=2)
        inter = jnp.einsum('bhck,bhkv->bhcv', qc * jnp.exp(b), state)
        diff = b[:, :, :, None, :] - b[:, :, None, :, :]
        decay = jnp.exp(jnp.where(causal[:, :, None], diff, -jnp.inf))
        att = jnp.einsum('bhijk,bhjk->bhij', qc[:, :, :, None, :] * decay, kc)
        intra = jnp.einsum('bhij,bhjv->bhiv', att, vc)
        b_last = b[:, :, -1:, :]
        new_state = jnp.exp(b_last[:, :, 0, :])[..., None] * state + jnp.einsum('bhck,bhcv->bhkv', kc * jnp.exp(b_last - b), vc)
        return new_state, inter + intra

    state0 = jnp.zeros((B, H, DK, DV), jnp.float32)
    _, out = lax.scan(step, state0, (chunks(q), chunks(k), chunks(v), chunks(g)))
    return jnp.moveaxis(out, 0, 2).reshape(B, H, S, DV)


def gla_mixer(h, w_in, wg_f, bg_f, wg_b, bg_b, g_norm, w_out):
    B, S, _ = h.shape
    dq = GLA_HEADS * GLA_DK
    dv = GLA_HEADS * GLA_DV
    proj = h @ w_in
    q, k, v, r, zf, zb = jnp.split(proj, [dq, 2 * dq, 2 * dq + dv, 2 * dq + 2 * dv, 2 * dq + 2 * dv + GLA_RANK], axis=-1)

    def heads(t, d):
        return t.reshape(B, S, GLA_HEADS, d).transpose(0, 2, 1, 3).astype(jnp.float32)

    def log_gate(z, wg, bg):
        return heads(jax.nn.log_sigmoid((z @ wg + bg).astype(jnp.float32)) / GLA_TAU, GLA_DK)

    q = heads(q, GLA_DK) * (GLA_DK ** -0.5)
    k = heads(k, GLA_DK)
    v = heads(v, GLA_DV)
    gf = log_gate(zf, wg_f, bg_f)
    gb = log_gate(zb, wg_b, bg_b)
    flip = lambda t: jnp.flip(t, axis=2)
    o = gla_scan(q, k, v, gf) + flip(gla_scan(flip(q), flip(k), flip(v), flip(gb)))
    o = o.transpose(0, 2, 1, 3)
    o = o * lax.rsqrt(jnp.mean(o * o, axis=-1, keepdims=True) + EPS) * g_norm.reshape(GLA_HEADS, GLA_DV).astype(jnp.float32)
    o = o.reshape(B, S, dv).astype(h.dtype) * jax.nn.silu(r)
    return o @ w_out


def dilated_group(q, k, v, r, n_side, bias_cols):
    B, S, H, D = q.shape
    L = S // r
    W = n_side
    nb = -(-L // W)
    Lp = nb * W
    N = B * r

    def sub(t):
        return t.reshape(B, L, r, H, D).transpose(0, 2, 1, 3, 4).reshape(N, L, H, D)

    qs, ks, vs = sub(q), sub(k), sub(v)
    qb = jnp.pad(qs, ((0, 0), (0, Lp - L), (0, 0), (0, 0))).reshape(N, nb, W, H, D)

    def band(t):
        tp = jnp.pad(t, ((0, 0), (W, Lp - L + W), (0, 0), (0, 0))).reshape(N, nb + 2, W, H, D)
        return jnp.concatenate([tp[:, :-2], tp[:, 1:-1], tp[:, 2:]], axis=2)

    kb, vb = band(ks), band(vs)
    rel = np.arange(3 * W)[None, :] - W - np.arange(W)[:, None]
    key_pos = np.arange(nb)[:, None] * W - W + np.arange(3 * W)[None, :]
    mask = (np.abs(rel) <= W)[None] & ((key_pos >= 0) & (key_pos < L))[:, None, :]
    bias = jnp.transpose(bias_cols[t5_bucket(rel * r)], (2, 0, 1)).astype(jnp.float32)
    s = jnp.einsum('nbqhd,nbkhd->nbhqk', qb, kb).astype(jnp.float32) * (D ** -0.5) + bias
    s = jnp.where(mask[None, :, None], s, NEG)
    mx = jnp.max(s, axis=-1, keepdims=True)
    p = jnp.exp(s - mx)
    l = jnp.sum(p, axis=-1, keepdims=True)
    o = jnp.einsum('nbhqk,nbkhd->nbqhd', p, vb.astype(jnp.float32))
    o = o / jnp.moveaxis(l[..., 0], 2, 3)[..., None]
    lse = jnp.moveaxis((mx + jnp.log(l))[..., 0], 2, 3)

    def unsub(t):
        rest = t.shape[3:]
        t = t.reshape((N, Lp) + rest)[:, :L]
        return jnp.swapaxes(t.reshape((B, r, L) + rest), 1, 2).reshape((B, S) + rest)

    return unsub(o), unsub(lse)


def dilated_mixer(h, w_qkv, w_out, rel_bias):
    B, S, _ = h.shape
    qkv = (h @ w_qkv).reshape(B, S, 3, N_DIL, DIL_HEADS, DIL_HD)
    outs, lses = [], []
    for g, (window, r) in enumerate(DIL_GROUPS):
        o, lse = dilated_group(qkv[:, :, 0, g], qkv[:, :, 1, g], qkv[:, :, 2, g], r, window // (2 * r),
                               rel_bias[:, g * DIL_HEADS:(g + 1) * DIL_HEADS])
        outs.append(o)
        lses.append(lse)
    wts = jax.nn.softmax(jnp.stack(lses), axis=0)
    o = jnp.sum(wts[..., None] * jnp.stack(outs), axis=0)
    return o.reshape(B, S, D_MODEL).astype(h.dtype) @ w_out


def cross_attn(h, m, w_q, w_kv, w_o):
    B, S, _ = h.shape
    M = m.shape[1]
    q = (h @ w_q).reshape(B, S, X_HEADS, X_HD)
    kv = (m @ w_kv).reshape(B, M, 2, X_HEADS, X_HD)
    s = jnp.einsum('bqhd,bkhd->bhqk', q, kv[:, :, 0]).astype(jnp.float32) * (X_HD ** -0.5)
    p = jax.nn.softmax(s, axis=-1)
    o = jnp.einsum('bhqk,bkhd->bqhd', p, kv[:, :, 1].astype(jnp.float32))
    return o.reshape(B, S, D_MODEL).astype(h.dtype) @ w_o


def trunk(x, mem, rel_bias, norm_ffn1, ffn1_in, ffn1_out, norm_mix, gla_w_in, gla_wg_f, gla_bg_f, gla_wg_b, gla_bg_b,
          gla_norm, gla_w_out, dil_w_qkv, dil_w_out, norm_cross, norm_mem, cross_w_q, cross_w_kv, cross_w_o,
          norm_ffn2, ffn2_in, ffn2_out, norm_final):
    for i in range(DEPTH):
        x = x + 0.5 * swiglu(rmsnorm(x, norm_ffn1[i]), ffn1_in[i], ffn1_out[i])
        h = rmsnorm(x, norm_mix[i])
        j = i // N_MIXERS
        if i % N_MIXERS == 0:
            x = x + gla_mixer(h, gla_w_in[j], gla_wg_f[j], gla_bg_f[j], gla_wg_b[j], gla_bg_b[j], gla_norm[j], gla_w_out[j])
        else:
            x = x + dilated_mixer(h, dil_w_qkv[j], dil_w_out[j], rel_bias)
        x = x + cross_attn(rmsnorm(x, norm_cross[i]), rmsnorm(mem, norm_mem[i]), cross_w_q[i], cross_w_kv[i], cross_w_o[i])
        x = x + 0.5 * swiglu(rmsnorm(x, norm_ffn2[i]), ffn2_in[i], ffn2_out[i])
    return rmsnorm(x, norm_final)


def setup_inputs(seed: int = 0) -> dict:
    key = jax.random.key(seed)
    ks = jax.random.split(key, 32)
    f32 = jnp.float32

    def w(k, shape, fan_in):
        return jax.random.normal(k, shape, f32) * (fan_in ** -0.5)

    def gain(k, shape):
        return 1.0 + 0.05 * jax.random.normal(k, shape, f32)

    D = D_MODEL
    return {
        'x_prompt': jax.random.normal(ks[0], (BATCH, SEQ, D), f32),
        'x_sample': jax.random.normal(ks[1], (DEC_BATCH, DEC_SEQ, D), f32),
        'mem_prompt': jax.random.normal(ks[2], (BATCH, MEM_LEN, D), f32),
        'mem_sample': jax.random.normal(ks[3], (DEC_BATCH, MEM_LEN, D), f32),
        'rel_bias': 0.1 * jax.random.normal(ks[4], (NUM_BUCKETS, N_DIL * DIL_HEADS), f32),
        'norm_ffn1': gain(ks[5], (DEPTH, D)),
        'ffn1_in': w(ks[6], (DEPTH, D, 2 * D_FF), D),
        'ffn1_out': w(ks[7], (DEPTH, D_FF, D), D_FF),
        'norm_mix': gain(ks[8], (DEPTH, D)),
        'gla_w_in': w(ks[9], (N_A, D, GLA_IN), D),
        'gla_wg_f': w(ks[10], (N_A, GLA_RANK, GLA_HEADS * GLA_DK), GLA_RANK),
        'gla_bg_f': 0.1 * jax.random.normal(ks[11], (N_A, GLA_HEADS * GLA_DK), f32),
        'gla_wg_b': w(ks[12], (N_A, GLA_RANK, GLA_HEADS * GLA_DK), GLA_RANK),
        'gla_bg_b': 0.1 * jax.random.normal(ks[13], (N_A, GLA_HEADS * GLA_DK), f32),
        'gla_norm': gain(ks[14], (N_A, GLA_HEADS * GLA_DV)),
        'gla_w_out': w(ks[15], (N_A, GLA_HEADS * GLA_DV, D), GLA_HEADS * GLA_DV),
        'dil_w_qkv': w(ks[16], (N_B, D, DIL_QKV), D),
        'dil_w_out': w(ks[17], (N_B, DIL_HEADS * DIL_HD, D), DIL_HEADS * DIL_HD),
        'norm_cross': gain(ks[18], (DEPTH, D)),
        'norm_mem': gain(ks[19], (DEPTH, D)),
        'cross_w_q': w(ks[20], (DEPTH, D, D), D),
        'cross_w_kv': w(ks[21], (DEPTH, D, 2 * D), D),
        'cross_w_o': w(ks[22], (DEPTH, D, D), D),
        'norm_ffn2': gain(ks[23], (DEPTH, D)),
        'ffn2_in': w(ks[24], (DEPTH, D, 2 * D_FF), D),
        'ffn2_out': w(ks[25], (DEPTH, D_FF, D), D_FF),
        'norm_final': gain(ks[26], (D,)),
    }


def reference(x_prompt, x_sample, mem_prompt, mem_sample, rel_bias, norm_ffn1, ffn1_in, ffn1_out, norm_mix,
              gla_w_in, gla_wg_f, gla_bg_f, gla_wg_b, gla_bg_b, gla_norm, gla_w_out, dil_w_qkv, dil_w_out,
              norm_cross, norm_mem, cross_w_q, cross_w_kv, cross_w_o, norm_ffn2, ffn2_in, ffn2_out, norm_final):
    y_prompt = trunk(x_prompt, mem_prompt, rel_bias, norm_ffn1, ffn1_in, ffn1_out, norm_mix, gla_w_in, gla_wg_f,
                     gla_bg_f, gla_wg_b, gla_bg_b, gla_norm, gla_w_out, dil_w_qkv, dil_w_out, norm_cross, norm_mem,
                     cross_w_q, cross_w_kv, cross_w_o, norm_ffn2, ffn2_in, ffn2_out, norm_final)
    y_sample = trunk(x_sample, mem_sample, rel_bias, norm_ffn1, ffn1_in, ffn1_out, norm_mix, gla_w_in, gla_wg_f,
                     gla_bg_f, gla_wg_b, gla_bg_b, gla_norm, gla_w_out, dil_w_qkv, dil_w_out, norm_cross, norm_mem,
                     cross_w_q, cross_w_kv, cross_w_o, norm_ffn2, ffn2_in, ffn2_out, norm_final)
    return (y_prompt, y_sample)
```

```python
from contextlib import ExitStack
import numpy as np
import concourse.bass as bass
import concourse.mybir as mybir

F32 = mybir.dt.float32
BF16 = mybir.dt.bfloat16
AF = mybir.ActivationFunctionType
ALU = mybir.AluOpType
AX = mybir.AxisListType

COMPUTE = ("pe", "act", "dve", "pool")
ENGS = ("pe", "act", "dve", "pool", "sp")


class Buf:
    __slots__ = ("name", "w", "rs", "excl")

    def __init__(self, name, excl=False):
        self.name = name
        self.w = None
        self.rs = []
        self.excl = excl


class Op:
    __slots__ = ("eng", "fn", "dma", "deps", "idx", "sig", "val", "semkey", "dsem", "dval", "pos", "inc")

    def __init__(self, eng, fn, dma, semkey, inc=16):
        self.inc = inc
        self.eng = eng
        self.fn = fn
        self.dma = dma
        self.deps = []
        self.sig = False
        self.val = 0
        self.semkey = semkey
        self.dsem = None
        self.dval = 0


class Prog:
    def __init__(self, nc):
        self.nc = nc
        self.ops = []
        self.es = ExitStack()
        self.last_dma = {}
        self.nbuf = 0
        self.last = {}
        self.pending = []

    def sbuf(self, name, shape, dt):
        return self.es.enter_context(self.nc.sbuf_tensor(name, list(shape), dt))

    def psum(self, name, shape, dt=F32):
        return self.es.enter_context(self.nc.psum_tensor(name, list(shape), dt))

    def dram(self, name, shape, dt, kind="Internal", addr_space="Local"):
        return self.nc.dram_tensor(name, list(shape), dt, kind=kind, addr_space=addr_space)

    def buf(self, name=None, excl=False):
        self.nbuf += 1
        return Buf(name or f"b{self.nbuf}", excl)

    def op(self, eng, fn, reads=(), writes=(), dma=False, semkey=None, inc=16):
        o = Op(eng, fn, dma, semkey, inc)
        deps = []

        def add(p, raw):
            if p is None or p is o:
                return
            if not p.dma and p.eng == eng:
                if not raw or eng == "pe":
                    return
            if p not in deps:
                deps.append(p)

        for b in reads:
            add(b.w, True)
            if b.excl:
                for r in b.rs:
                    if r.eng != eng:
                        add(r, False)
        for b in writes:
            add(b.w, False)
            for r in b.rs:
                add(r, False)
        if dma:
            assert semkey is not None
            add(self.last_dma.get(semkey), False)
            self.last_dma[semkey] = o
        for b in reads:
            if not dma:
                b.rs = [r for r in b.rs if r.dma or r.eng != eng]
            b.rs.append(o)
        for b in writes:
            b.w = o
            b.rs = []
        o.deps = deps
        o.pos = len(self.ops)
        self.ops.append(o)
        if fn is not None:
            self.last[eng] = o
        if dma:
            self.pending.append(o)
        return o

    def barrier(self):
        targets = [p for p in self.last.values()] + list(self.pending)
        self.pending = []
        for eng in ENGS:
            o = Op(eng, None, False, None)
            o.deps = [p for p in dict.fromkeys(targets)]
            o.pos = len(self.ops)
            self.ops.append(o)

    def dma(self, eng, out, in_, reads, writes, semkey, **kw):
        return self.op(eng, lambda e: e.dma_start(out=out, in_=in_, **kw), reads, writes,
                       dma=True, semkey=semkey)

    def emit(self, final_wait_ops=()):
        nc = self.nc
        es = self.es
        fin = self.op("sp", None, reads=(), writes=())
        for p in final_wait_ops:
            if p not in fin.deps:
                fin.deps.append(p)
                p.sig = True
        for o in self.ops:
            for p in o.deps:
                p.sig = True
        esem = {e: es.enter_context(nc.semaphore(f"c_{e}")) for e in COMPUTE}
        dsems = {}
        ecount = {e: 0 for e in COMPUTE}
        dcount = {}
        for o in self.ops:
            if o.dma:
                if o.semkey not in dsems:
                    dsems[o.semkey] = es.enter_context(nc.semaphore(f"d{len(dsems)}"))
                    dcount[o.semkey] = 0
                dcount[o.semkey] += o.inc
                o.dsem = dsems[o.semkey]
                o.dval = dcount[o.semkey]
            elif o.sig:
                assert o.eng in COMPUTE, ("sp non-dma op cannot signal", o.eng)
                ecount[o.eng] += 1
                o.val = ecount[o.eng]
        self.n_dsem = len(dsems)
        per = {e: [] for e in ENGS}
        for o in self.ops:
            per[o.eng].append(o)
        handles = {"pe": "tensor", "act": "scalar", "dve": "vector", "pool": "gpsimd", "sp": "sync"}

        def run(ename, eh):
            known = {}
            for o in per[ename]:
                need = {}
                for p in o.deps:
                    if p.dma:
                        k, s, v = ("d", p.semkey), p.dsem, p.dval
                    else:
                        k, s, v = ("e", p.eng), esem[p.eng], p.val
                    if known.get(k, 0) >= v:
                        continue
                    if k not in need or need[k][1] < v:
                        need[k] = (s, v)
                for k, (s, v) in need.items():
                    eh.wait_ge(s, v)
                    known[k] = v
                if o.fn is None:
                    continue
                ins = o.fn(eh)
                if o.dma:
                    ins.then_inc(o.dsem, o.inc)
                elif o.sig:
                    ins.then_inc(esem[o.eng], 1)

        with nc.Block() as block:
            for ename in ENGS:
                if not per[ename]:
                    continue
                getattr(block, handles[ename])(lambda eh, _n=ename: run(_n, eh))
        es.close()
        return nc


from concourse.bass_utils import run_bass_kernel_spmd

D = 1024
DFF = 2816
NFF = DFF // 128
UNIT = 2048
NT_U = UNIT // 128
MEM = 256
XH = 4
XHD = 256
EPS = 1e-6
NSLOT = 3
ARENA = 40960
GH, GDK, GDV = 4, 128, 256
DIL_R = (1, 4, 16)
DH = 16
DHD = 64
VW = DH * (DHD + 1)
PAD = 1024
NUM_BUCKETS = 32
MAX_DISTANCE = 1024
G_FFN1, G_MIX, G_CROSS, G_MEM, G_FFN2, G_FINAL = 0, 2, 4, 6, 8, 10
FULL_PHASES = ("ffn1:0", "mix:0", "cross:0", "ffn2:0", "ffn1:1", "mix:1", "cross:1", "ffn2:1", "final")


def t5_bucket(rel):
    half = NUM_BUCKETS // 2
    max_exact = half // 2
    ret = (rel > 0).astype(np.int32) * half
    n = np.abs(rel)
    large = max_exact + (np.log(np.maximum(n, 1) / max_exact) / np.log(MAX_DISTANCE / max_exact) * (half - max_exact)).astype(np.int32)
    large = np.minimum(large, half - 1)
    return (ret + np.where(n < max_exact, n, large)).astype(np.int32)


class K:
    def __init__(self, T, phases, dbg=None):
        self.dbg = dbg
        self.dbgt = {}
        assert T % UNIT == 0
        self.T, self.NU, self.phases = T, T // UNIT, tuple(phases)
        self.TP = T + 2 * PAD
        nc = self.nc = bass.Bass("TRN2", target_bir_lowering=False)
        P = self.P = Prog(nc)
        inp = lambda n, s: nc.dram_tensor(n, list(s), F32, kind="ExternalInput").ap()
        self.x_in = inp("x", [T, D])
        self.mem_in = inp("mem", [MEM, D])
        self.gains = inp("gains", [11, 128, D])
        self.ident = inp("ident", [128, 128])
        self.ffn_in = inp("ffn_in", [4, D, 2 * DFF])
        self.ffn_out = inp("ffn_out", [4, DFF, D])
        self.cq = inp("cross_q", [2, D, D])
        self.ckv = inp("cross_kv", [2, D, 2 * D])
        self.co = inp("cross_o", [2, D, D])
        self.dqkv = inp("dil_qkv", [D, 9 * D])
        self.dwo = inp("dil_o", [D, D])
        self.dbias = inp("dil_bias", [3, DH, 128, 256])
        self.gwin = inp("gla_w_in", [D, 3104])
        self.gwgb = inp("gla_wgb", [2, 17, 512])
        self.gnorm = inp("gla_norm", [128, D])
        self.gwout = inp("gla_w_out", [D, D])
        self.gtri = inp("gla_tri", [4, 128, 128])
        self.valid8 = inp("valid8", [T, 8])
        self.y = nc.dram_tensor("y", [T, D], F32, kind="ExternalOutput").ap()
        self.X = P.dram("Xres", [T, D], F32).ap()
        self.bX = [P.buf(f"X{u}") for u in range(self.NU)]
        self.bY = [P.buf(f"Y{u}") for u in range(self.NU)]
        self.bXg = P.buf("Xg")
        self.xh = P.sbuf("xh", [128, NT_U, D], F32); self.bx = [P.buf(f"x{t}") for t in range(NT_U)]
        self.xnt = P.sbuf("xnt", [128, 8, UNIT], BF16); self.bxn = [P.buf(f"xn{t}") for t in range(NT_U)]
        self.gt = P.sbuf("gt", [128, D], F32); self.bg = P.buf("g")
        self.idf = P.sbuf("idf", [128, 128], F32); self.bidf = P.buf("idf")
        self.idb = P.sbuf("idb", [128, 128], BF16); self.bidb = P.buf("idb")
        self.sq = P.sbuf("sq", [128, VW], F32); self.bsq = P.buf("sq")
        self.f32a = P.sbuf("f32a", [128, VW], F32); self.bf32a = P.buf("f32a")
        self.sil = [P.sbuf(f"sil{i}", [128, 512], F32) for i in range(2)]; self.bsil = [P.buf(f"sil{i}") for i in range(2)]
        self.ss = [P.sbuf(f"ss{i}", [128, 1], F32) for i in range(2)]; self.bss = [P.buf(f"ss{i}") for i in range(2)]
        self.rc = [P.sbuf(f"rc{i}", [128, 16], F32) for i in range(2)]; self.brc = [P.buf(f"rc{i}") for i in range(2)]
        self.AB = P.sbuf("AB", [128, ARENA], BF16)
        self.pA = [P.psum(f"pA{i}", [128, 512]) for i in range(2)]; self.bpA = [P.buf(f"pA{i}", excl=True) for i in range(2)]
        self.pB = [P.psum(f"pB{i}", [128, 512]) for i in range(2)]; self.bpB = [P.buf(f"pB{i}", excl=True) for i in range(2)]
        self.pO = [P.psum(f"pO{i}", [128, 512]) for i in range(3)]; self.bpO = [P.buf(f"pO{i}", excl=True) for i in range(3)]
        self.pT = P.psum("pT", [128, 8 * 128], BF16); self.bpT = P.buf("pT", excl=True)
        self.io = self.ab = self.wcnt = self.nrm = self.xk = 0
        self.outs = []
        P.dma("sp", self.idf[:], self.ident, [], [self.bidf], "id")
        P.op("act", lambda e: e.activation(out=self.idb[:], in_=self.idf[:], func=AF.Copy), [self.bidf], [self.bidb])

    def phase_begin(self):
        self.P.barrier()
        self.aoff = 0

    def carve(self, n, pat=None, **kw):
        assert self.aoff + n <= ARENA, ("arena overflow", self.aoff, n)
        v = self.AB[:, self.aoff:self.aoff + n]
        self.aoff += n
        return v.rearrange(pat, **kw) if pat else v

    def carve_common(self):
        self.xnb = [self.carve(D) for _ in range(2)]; self.bxnb = [self.P.buf() for _ in range(2)]

    def src(self, first):
        if getattr(self, "_dbg_src", None) is not None:
            return self._dbg_src.rearrange("(t p) d -> p t d", p=128)
        return (self.x_in if first else self.X).rearrange("(t p) d -> p t d", p=128)

    def load_gain(self, row):
        self.P.dma("sp", self.gt[:], self.gains[row], [], [self.bg], "g")

    def load_unit(self, u, first):
        P = self.P
        rd = [] if first else [self.bX[u]]
        for t in range(NT_U):
            P.dma("sp", self.xh[:, t, :], self.src(first)[:, u * NT_U + t, :], rd, [self.bx[t]], ("x", t % 4))

    def store_unit(self, u):
        X_t = self.X.rearrange("(t p) d -> p t d", p=128)
        self.P.dma("sp", X_t[:, u * NT_U:(u + 1) * NT_U, :], self.xh[:], list(self.bx), [self.bX[u]], ("xs", u % 2))

    def rms_rstd(self, src_ap, rdbufs):
        P = self.P
        i = self.nrm % 2
        self.nrm += 1
        P.op("dve", lambda e: e.tensor_tensor(out=self.sq[:, 0:D], in0=src_ap, in1=src_ap, op=ALU.mult), rdbufs, [self.bsq])
        P.op("dve", lambda e: e.reduce_sum(out=self.ss[i][:], in_=self.sq[:, 0:D], axis=AX.X), [self.bsq], [self.bss[i]])
        P.op("act", lambda e: e.activation(out=self.ss[i][:], in_=self.ss[i][:], func=AF.Ln, bias=EPS, scale=1.0 / D), [self.bss[i]], [self.bss[i]])
        P.op("act", lambda e: e.activation(out=self.ss[i][:], in_=self.ss[i][:], func=AF.Exp, scale=-0.5), [self.bss[i]], [self.bss[i]])
        return i

    def norm_T(self, src_ap, rdbufs, dst_ap, dstbufs):
        P = self.P
        i = self.rms_rstd(src_ap, rdbufs)
        xnb_i = self.xnb[i]
        P.op("dve", lambda e: e.scalar_tensor_tensor(out=xnb_i, in0=src_ap, scalar=self.ss[i][:], in1=self.gt[:],
                                                     op0=ALU.mult, op1=ALU.mult), rdbufs + [self.bss[i], self.bg], [self.bxnb[i]])
        self.transpose8(xnb_i, self.bxnb[i], dst_ap, dstbufs)

    def transpose8(self, src_ap, srcbuf, dst_ap, dstbufs):
        P = self.P
        for kc in range(8):
            P.op("pe", lambda e, kc=kc: e.transpose(self.pT[:, kc * 128:(kc + 1) * 128], src_ap[:, kc * 128:(kc + 1) * 128], self.idb[:]),
                 [srcbuf, self.bidb], [self.bpT])
        P.op("act", lambda e: e.activation(out=dst_ap, in_=self.pT[:].rearrange("p (k n) -> p k n", k=8), func=AF.Copy), [self.bpT], dstbufs)

    def norm_unit(self):
        for t in range(NT_U):
            self.norm_T(self.xh[:, t, :], [self.bx[t]], self.xnt[:, :, t * 128:(t + 1) * 128], [self.bxn[t]])

    def out_proj_add(self, tt, oT, boT, w, bw):
        P = self.P
        for sub in range(4):
            t = 4 * tt + sub
            for hf in range(2):
                o = self.io % 3
                self.io += 1
                for fc in range(8):
                    P.op("pe", lambda e, fc=fc, sub=sub, hf=hf, o=o: e.matmul(self.pO[o][:], lhsT=oT[:, fc, sub * 128:(sub + 1) * 128],
                                                                             rhs=w[:, fc, hf * 512:(hf + 1) * 512], start=(fc == 0), stop=(fc == 7)),
                         [boT[sub], bw], [self.bpO[o]])
                P.op("dve", lambda e, t=t, hf=hf, o=o: e.tensor_tensor(out=self.xh[:, t, hf * 512:(hf + 1) * 512], in0=self.xh[:, t, hf * 512:(hf + 1) * 512],
                                                                      in1=self.pO[o][:], op=ALU.add), [self.bpO[o], self.bx[t]], [self.bx[t]])

    def ffn_phase(self, fi, grow, first):
        P = self.P
        FG, NSL = 4, 8
        self.phase_begin()
        self.carve_common()
        wa = [self.carve(1024, "p (k n) -> p k n", k=8) for _ in range(NSL)]; bwa = [P.buf() for _ in range(NSL)]
        wb = [self.carve(1024, "p (k n) -> p k n", k=8) for _ in range(NSL)]; bwb = [P.buf() for _ in range(NSL)]
        wo = [self.carve(D) for _ in range(NSL)]; bwo = [P.buf() for _ in range(NSL)]
        act = [[self.carve(512) for _ in range(FG)] for _ in range(2)]; bact = [[P.buf() for _ in range(FG)] for _ in range(2)]
        w_in_k = self.ffn_in[fi].rearrange("(kc p) n -> p kc n", p=128)
        w_out = self.ffn_out[fi]
        groups = [list(range(c0, min(c0 + FG, NFF))) for c0 in range(0, NFF, FG)]

        def load_w(c):
            s = c % NSL
            P.dma("pool", wa[s], w_in_k[:, :, c * 128:(c + 1) * 128], [], [bwa[s]], ("wa", s))
            P.dma("pool", wb[s], w_in_k[:, :, DFF + c * 128:DFF + (c + 1) * 128], [], [bwb[s]], ("wb", s))
            P.dma("pool", wo[s], w_out[c * 128:(c + 1) * 128, :], [], [bwo[s]], ("wo", s))

        self.load_gain(grow)
        for u in range(self.NU):
            self.load_unit(u, first)
            self.norm_unit()
            for c in groups[0]:
                load_w(c)
            for gi, grp in enumerate(groups):
                if gi + 1 < len(groups):
                    for c in groups[gi + 1]:
                        load_w(c)
                for tt in range(UNIT // 512):
                    par = tt % 2
                    rd = [self.bxn[4 * tt + q] for q in range(4)]
                    for g, c in enumerate(grp):
                        s = c % NSL
                        j = self.ab % 2
                        self.ab += 1
                        a_g, ba_g = act[par][g], bact[par][g]
                        for kc in range(8):
                            P.op("pe", lambda e, s=s, kc=kc, tt=tt, j=j: e.matmul(self.pA[j][:], lhsT=wa[s][:, kc, :], rhs=self.xnt[:, kc, tt * 512:(tt + 1) * 512],
                                                                                 start=(kc == 0), stop=(kc == 7)), [bwa[s]] + rd, [self.bpA[j]])
                        for kc in range(8):
                            P.op("pe", lambda e, s=s, kc=kc, tt=tt, j=j: e.matmul(self.pB[j][:], lhsT=wb[s][:, kc, :], rhs=self.xnt[:, kc, tt * 512:(tt + 1) * 512],
                                                                                 start=(kc == 0), stop=(kc == 7)), [bwb[s]] + rd, [self.bpB[j]])
                        P.op("act", lambda e, j=j: e.activation(out=self.sil[j][:], in_=self.pA[j][:], func=AF.Silu), [self.bpA[j]], [self.bsil[j]])
                        P.op("dve", lambda e, j=j, a_g=a_g: e.tensor_tensor(out=a_g, in0=self.sil[j][:], in1=self.pB[j][:], op=ALU.mult),
                             [self.bsil[j], self.bpB[j]], [ba_g])
                    for sub in range(4):
                        t = 4 * tt + sub
                        for h in range(2):
                            o = self.io % 3
                            self.io += 1
                            for g, c in enumerate(grp):
                                s = c % NSL
                                a_g, ba_g = act[par][g], bact[par][g]
                                P.op("pe", lambda e, s=s, sub=sub, h=h, o=o, a_g=a_g, g=g, n=len(grp): e.matmul(
                                    self.pO[o][:], lhsT=a_g[:, sub * 128:(sub + 1) * 128], rhs=wo[s][:, h * 512:(h + 1) * 512], start=(g == 0), stop=(g == n - 1)),
                                     [ba_g, bwo[s]], [self.bpO[o]])
                            P.op("dve", lambda e, t=t, h=h, o=o: e.scalar_tensor_tensor(out=self.xh[:, t, h * 512:(h + 1) * 512], in0=self.pO[o][:], scalar=0.5,
                                                                                       in1=self.xh[:, t, h * 512:(h + 1) * 512], op0=ALU.mult, op1=ALU.add),
                                 [self.bpO[o], self.bx[t]], [self.bx[t]])
            self.store_unit(u)

    def cross_phase(self, li, first):
        P = self.P
        self.phase_begin()
        self.carve_common()
        wbig = [self.carve(8 * D, "p (k n) -> p k n", k=8) for _ in range(2)]; bwbig = [P.buf() for _ in range(2)]
        memT = self.carve(8 * MEM, "p (k n) -> p k n", k=8); bmemT = P.buf()
        kT = self.carve(8 * MEM, "p (k n) -> p k n", k=8); bkT = P.buf()
        vA = self.carve(2 * XH * (XHD + 1), "p (a h d) -> p a h d", a=2, h=XH); bvA = P.buf()
        qT = self.carve(8 * 512, "p (k n) -> p k n", k=8); bqT = P.buf()
        pTs = [self.carve(512) for _ in range(2)]; bpTs = [P.buf() for _ in range(2)]
        ob = self.carve(4 * D, "p (a d) -> p a d", a=4); bob = [P.buf() for _ in range(4)]
        oT = self.carve(8 * 512, "p (k n) -> p k n", k=8); boT = [P.buf() for _ in range(4)]
        memf = self.f32a[:, 0:D]; bmemf = self.bf32a

        def load_big(i, w_ap):
            P.dma("pool", wbig[i], w_ap.rearrange("(kc p) n -> p kc n", p=128), [], [bwbig[i]], ("wbig", i))

        P.op("dve", lambda e: e.memset(vA, 1.0), [], [bvA])
        self.load_gain(G_MEM + li)
        for kt in range(2):
            P.dma("sp", memf[:], self.mem_in[kt * 128:(kt + 1) * 128, :], [], [bmemf], "memf")
            self.norm_T(memf[:], [bmemf], memT[:, :, kt * 128:(kt + 1) * 128], [bmemT])
        load_big(0, self.ckv[li][:, 0:D])
        for fo in range(8):
            j = self.ab % 2
            self.ab += 1
            for kc in range(8):
                P.op("pe", lambda e, fo=fo, kc=kc, j=j: e.matmul(self.pA[j][:, 0:MEM], lhsT=wbig[0][:, kc, fo * 128:(fo + 1) * 128], rhs=memT[:, kc, :],
                                                                start=(kc == 0), stop=(kc == 7)), [bwbig[0], bmemT], [self.bpA[j]])
            P.op("act", lambda e, fo=fo, j=j: e.activation(out=kT[:, fo, :], in_=self.pA[j][:, 0:MEM], func=AF.Copy), [self.bpA[j]], [bkT])
        load_big(0, self.ckv[li][:, D:2 * D])
        for kt in range(2):
            for hf in range(2):
                j = self.ab % 2
                self.ab += 1
                for kc in range(8):
                    P.op("pe", lambda e, kt=kt, hf=hf, kc=kc, j=j: e.matmul(self.pB[j][:], lhsT=memT[:, kc, kt * 128:(kt + 1) * 128],
                                                                           rhs=wbig[0][:, kc, hf * 512:(hf + 1) * 512], start=(kc == 0), stop=(kc == 7)),
                         [bwbig[0], bmemT], [self.bpB[j]])
                P.op("act", lambda e, kt=kt, hf=hf, j=j: e.activation(out=vA[:, kt, 2 * hf:2 * hf + 2, 0:XHD],
                                                                      in_=self.pB[j][:].rearrange("p (h d) -> p h d", h=2), func=AF.Copy), [self.bpB[j]], [bvA])
        load_big(0, self.cq[li])
        load_big(1, self.co[li])
        self.load_gain(G_CROSS + li)
        for u in range(self.NU):
            self.load_unit(u, first)
            self.norm_unit()
            for tt in range(UNIT // 512):
                rd = [self.bxn[4 * tt + q] for q in range(4)]
                for fo in range(8):
                    j = self.ab % 2
                    self.ab += 1
                    for kc in range(8):
                        P.op("pe", lambda e, fo=fo, kc=kc, tt=tt, j=j: e.matmul(self.pA[j][:], lhsT=wbig[0][:, kc, fo * 128:(fo + 1) * 128],
                                                                               rhs=self.xnt[:, kc, tt * 512:(tt + 1) * 512], start=(kc == 0), stop=(kc == 7)),
                             [bwbig[0]] + rd, [self.bpA[j]])
                    P.op("act", lambda e, fo=fo, j=j: e.activation(out=qT[:, fo, :], in_=self.pA[j][:], func=AF.Copy, scale=XHD ** -0.5),
                         [self.bpA[j]], [bqT])
                for h in range(XH):
                    for kt in range(2):
                        j = self.ab % 2
                        self.ab += 1
                        for dc in range(2):
                            P.op("pe", lambda e, h=h, kt=kt, dc=dc, j=j: e.matmul(self.pB[j][:], lhsT=kT[:, 2 * h + dc, kt * 128:(kt + 1) * 128],
                                                                                 rhs=qT[:, 2 * h + dc, :], start=(dc == 0), stop=(dc == 1)),
                                 [bkT, bqT], [self.bpB[j]])
                        P.op("act", lambda e, kt=kt, j=j: e.activation(out=pTs[kt], in_=self.pB[j][:], func=AF.Exp), [self.bpB[j]], [bpTs[kt]])
                    for sub in range(4):
                        o = self.io % 3
                        self.io += 1
                        r = self.xk % 2
                        self.xk += 1
                        for kt in range(2):
                            P.op("pe", lambda e, h=h, kt=kt, sub=sub, o=o: e.matmul(self.pO[o][:, 0:XHD + 1], lhsT=pTs[kt][:, sub * 128:(sub + 1) * 128],
                                                                                   rhs=vA[:, kt, h, :], start=(kt == 0), stop=(kt == 1)),
                                 [bpTs[kt], bvA], [self.bpO[o]])
                        P.op("dve", lambda e, o=o, r=r: e.reciprocal(out=self.rc[r][:, 0:1], in_=self.pO[o][:, XHD:XHD + 1]), [self.bpO[o]], [self.brc[r]])
                        P.op("dve", lambda e, h=h, sub=sub, o=o, r=r: e.tensor_scalar(out=ob[:, sub, h * XHD:(h + 1) * XHD], in0=self.pO[o][:, 0:XHD],
                                                                                     scalar1=self.rc[r][:, 0:1], scalar2=None, op0=ALU.mult),
                             [self.bpO[o], self.brc[r]], [bob[sub]])
                for sub in range(4):
                    self.transpose8(ob[:, sub, :], bob[sub], oT[:, :, sub * 128:(sub + 1) * 128], [boT[sub]])
                self.out_proj_add(tt, oT, boT, wbig[1], bwbig[1])
            self.store_unit(u)


    def gla_phase(self, li, first):
        P = self.P
        T, NU = self.T, self.NU
        GQT = P.dram("gQT", [GH * GDK, T], BF16).ap(); bGQ = P.buf("gQT")
        GKT = P.dram("gKT", [GH * GDK, T], BF16).ap(); bGK = P.buf("gKT")
        GV = P.dram("gV", [T, D], BF16).ap(); bGV = P.buf("gV")
        GR = P.dram("gR", [T, D], F32).ap(); bGR = P.buf("gR")
        GG = [P.dram(f"gG{d}", [T, 512], F32).ap() for d in range(2)]; bGG = [P.buf(f"gG{d}") for d in range(2)]
        GO = P.dram("gO", [T, D], F32).ap(); bGO = P.buf("gO")
        wk = self.gwin.rearrange("(kc p) n -> p kc n", p=128)
        self.dbgt.update(GO=GO, GR=GR, GG0=GG[0], GG1=GG[1])
        self.phase_begin()
        self.carve_common()
        ws = [self.carve(8 * 512, "p (k n) -> p k n", k=8) for _ in range(NSLOT)]; bws = [P.buf() for _ in range(NSLOT)]
        wz = self.carve(8 * 32, "p (k n) -> p k n", k=8); bwz = P.buf()
        rowb = [self.carve(UNIT) for _ in range(2)]; browb = [P.buf() for _ in range(2)]
        vrow = [self.carve(512) for _ in range(2)]; bvrow = [P.buf() for _ in range(2)]
        zaug = [self.carve(UNIT) for _ in range(2)]; bzaug = [P.buf() for _ in range(2)]
        wgb = [self.carve(512) for _ in range(2)]; bwgb = [P.buf() for _ in range(2)]
        rrow = [self.sil[0], self.sil[1]]; brrow = self.bsil
        grow_ = [self.sq[:, 0:512], self.f32a[:, 0:512]]; bgrow = [self.bsq, self.bf32a]
        P.dma("pool", wz, wk[:, :, 3072:3104], [], [bwz], "wz")
        for d in range(2):
            P.dma("pool", wgb[d][0:17, :], self.gwgb[d], [], [bwgb[d]], ("wgb", d))
            P.op("dve", lambda e, d=d: e.memset(zaug[d][0:32, :], 1.0), [], [bzaug[d]])
        self.load_gain(G_MIX + li)
        wc = rb = vs = 0
        blocks = [("q", 0), ("k", 512), ("v", 1024), ("v", 1536), ("r", 2048), ("r", 2560)]
        for u in range(NU):
            ub = u * UNIT
            self.load_unit(u, first)
            self.norm_unit()

            def load_blk(i, s_):
                P.dma("pool", ws[s_], wk[:, :, blocks[i][1]:blocks[i][1] + 512], [], [bws[s_]], ("ws", s_))

            for i0 in range(NSLOT - 1):
                load_blk(i0, (wc + i0) % NSLOT)
            for bi, (kind, col) in enumerate(blocks):
                s_ = wc % NSLOT
                if bi + NSLOT - 1 < len(blocks):
                    load_blk(bi + NSLOT - 1, (wc + NSLOT - 1) % NSLOT)
                if kind in ("q", "k"):
                    for fl in range(4):
                        r_ = rb % 2
                        rb += 1
                        for tt in range(4):
                            j = self.ab % 2
                            self.ab += 1
                            rd = [self.bxn[4 * tt + q] for q in range(4)]
                            for kc in range(8):
                                P.op("pe", lambda e, s_=s_, kc=kc, fl=fl, tt=tt, j=j: e.matmul(self.pA[j][:], lhsT=ws[s_][:, kc, fl * 128:(fl + 1) * 128],
                                                                                             rhs=self.xnt[:, kc, tt * 512:(tt + 1) * 512], start=(kc == 0), stop=(kc == 7)),
                                     [bws[s_]] + rd, [self.bpA[j]])
                            sc = GDK ** -0.5 if kind == "q" else 1.0
                            P.op("act", lambda e, r_=r_, tt=tt, j=j, sc=sc: e.activation(out=rowb[r_][:, tt * 512:(tt + 1) * 512], in_=self.pA[j][:], func=AF.Copy, scale=sc),
                                 [self.bpA[j]], [browb[r_]])
                        dst, bdst = (GQT, bGQ) if kind == "q" else (GKT, bGK)
                        P.dma("sp", dst[fl * 128:(fl + 1) * 128, ub:ub + UNIT], rowb[r_], [browb[r_]], [bdst], ("rowb", r_))
                else:
                    hf = (col % 1024) // 512
                    for t in range(NT_U):
                        j = self.ab % 2
                        self.ab += 1
                        v_ = vs % 2
                        vs += 1
                        for kc in range(8):
                            P.op("pe", lambda e, s_=s_, kc=kc, t=t, j=j: e.matmul(self.pB[j][:], lhsT=self.xnt[:, kc, t * 128:(t + 1) * 128], rhs=ws[s_][:, kc, :],
                                                                                 start=(kc == 0), stop=(kc == 7)), [bws[s_], self.bxn[t]], [self.bpB[j]])
                        r0 = ub + t * 128
                        if kind == "v":
                            P.op("act", lambda e, v_=v_, j=j: e.activation(out=vrow[v_], in_=self.pB[j][:], func=AF.Copy), [self.bpB[j]], [bvrow[v_]])
                            P.dma("sp", GV[r0:r0 + 128, hf * 512:(hf + 1) * 512], vrow[v_], [bvrow[v_]], [bGV], ("vrow", v_))
                        else:
                            P.op("act", lambda e, v_=v_, j=j: e.activation(out=rrow[v_][:], in_=self.pB[j][:], func=AF.Silu), [self.bpB[j]], [brrow[v_]])
                            P.dma("sp", GR[r0:r0 + 128, hf * 512:(hf + 1) * 512], rrow[v_][:], [brrow[v_]], [bGR], ("rrow", v_))
                wc += 1
            for d in range(2):
                for tt in range(4):
                    j = self.ab % 2
                    self.ab += 1
                    rd = [self.bxn[4 * tt + q] for q in range(4)]
                    for kc in range(8):
                        P.op("pe", lambda e, d=d, kc=kc, tt=tt, j=j: e.matmul(self.pA[j][0:16, :], lhsT=wz[:, kc, d * 16:(d + 1) * 16],
                                                                             rhs=self.xnt[:, kc, tt * 512:(tt + 1) * 512], start=(kc == 0), stop=(kc == 7)),
                             [bwz] + rd, [self.bpA[j]])
                    P.op("act", lambda e, d=d, tt=tt, j=j: e.activation(out=zaug[d][0:16, tt * 512:(tt + 1) * 512], in_=self.pA[j][0:16, :], func=AF.Copy),
                         [self.bpA[j]], [bzaug[d]])
                for t in range(NT_U):
                    j = self.ab % 2
                    self.ab += 1
                    g_ = vs % 2
                    vs += 1
                    P.op("pe", lambda e, d=d, t=t, j=j: e.matmul(self.pB[j][:], lhsT=zaug[d][0:17, t * 128:(t + 1) * 128], rhs=wgb[d][0:17, :], start=True, stop=True),
                         [bzaug[d], bwgb[d]], [self.bpB[j]])
                    P.op("act", lambda e, g_=g_, j=j: e.activation(out=grow_[g_], in_=self.pB[j][:], func=AF.Exp, scale=-1.0), [self.bpB[j]], [bgrow[g_]])
                    P.op("act", lambda e, g_=g_: e.activation(out=grow_[g_], in_=grow_[g_], func=AF.Ln, bias=1.0), [bgrow[g_]], [bgrow[g_]])
                    r0 = ub + t * 128
                    P.dma("sp", GG[d][r0:r0 + 128, :], grow_[g_], [bgrow[g_]], [bGG[d]], ("grow", g_))
        import os as _os
        _stop = int(_os.environ.get("GLA_STOP", "9"))
        for d in range(2):
            if d + 1 >= _stop:
                break
            self.phase_begin()
            qTu = self.carve(GH * UNIT, "p (h t) -> p h t", h=GH); bqTu = P.buf()
            kTu = self.carve(GH * UNIT, "p (h t) -> p h t", h=GH); bkTu = P.buf()
            vu = self.xnt[:].rearrange("p k t -> p (k t)").rearrange("p (c f) -> p c f", c=NT_U); bvu = P.buf()
            qd = [self.carve(128) for _ in range(2)]; bqd = [P.buf() for _ in range(2)]
            kd = [self.carve(128) for _ in range(2)]; bkd = [P.buf() for _ in range(2)]
            at = [self.carve(128) for _ in range(2)]; bat = [P.buf() for _ in range(2)]
            ktok = [self.carve(128) for _ in range(2)]; bktok = [P.buf() for _ in range(2)]
            Sb = self.carve(GH * GDV, "p (h v) -> p h v", h=GH); bSb = P.buf()
            xflat = self.xh[:].rearrange("p a b -> p (a b)")
            gu = xflat[:, 0:NT_U * 512].rearrange("p (c f) -> p c f", c=NT_U); bgu = P.buf()
            ost = [xflat[:, 8192 + i * 1024:8192 + (i + 1) * 1024] for i in range(2)]; bost = [P.buf() for _ in range(2)]
            e1 = [xflat[:, 10240 + i * 128:10240 + (i + 1) * 128] for i in range(2)]; be1 = [P.buf() for _ in range(2)]
            e2 = [xflat[:, 10496 + i * 128:10496 + (i + 1) * 128] for i in range(2)]; be2 = [P.buf() for _ in range(2)]
            eL = [xflat[:, 10752 + i:10753 + i] for i in range(2)]; beL = [P.buf() for _ in range(2)]
            tri = xflat[:, 10880:11008]; msk = xflat[:, 11008:11136]; btri = P.buf()
            gof = [xflat[:, 11264 + i * 1024:11264 + (i + 1) * 1024] for i in range(2)]; bgof = [P.buf() for _ in range(2)]
            grr = [xflat[:, 13312 + i * 1024:13312 + (i + 1) * 1024] for i in range(2)]; bgrr = [P.buf() for _ in range(2)]
            xc = [xflat[:, 15360 + i * 512:15360 + (i + 1) * 512] for i in range(2)]; bxc = [P.buf() for _ in range(2)]
            S = self.f32a[:, 0:GH * GDV].rearrange("p (h v) -> p h v", h=GH); bS = self.bf32a
            gn = self.gt; bgn = self.bg
            P.dma("sp", tri, self.gtri[d], [], [btri], "tri")
            P.dma("sp", msk, self.gtri[2 + d], [], [btri], "msk")
            P.op("dve", lambda e: e.memset(S, 0.0), [], [bS])
            P.op("dve", lambda e: e.memset(Sb, 0.0), [], [bSb])
            _v = int(_os.environ.get("GLA_V", "0"))
            if d == 1 and _v == 0:
                self.carve_common()
                wo_ = self.carve(8 * D, "p (k n) -> p k n", k=8); bwo_ = P.buf()
                ob = self.carve(D); bob = P.buf()
                oT = self.carve(8 * 128, "p (k n) -> p k n", k=8); boT = P.buf()
                if _os.environ.get("GLA_NOWO", "0") != "1":
                    P.dma("pool", wo_, self.gwout.rearrange("(kc p) n -> p kc n", p=128), [], [bwo_], "gwo")
                if _os.environ.get("GLA_NOGN", "0") != "1":
                    P.dma("sp", gn[:], self.gnorm, [], [bgn], "g")
            hc = 0
            order_u = range(NU) if d == 0 else range(NU - 1, -1, -1)
            for u in order_u:
                ub = u * UNIT
                P.dma("sp", qTu, GQT.rearrange("(h p) t -> p h t", p=128)[:, :, ub:ub + UNIT], [bGQ], [bqTu], "qTu")
                P.dma("sp", kTu, GKT.rearrange("(h p) t -> p h t", p=128)[:, :, ub:ub + UNIT], [bGK], [bkTu], "kTu")
                P.dma("sp", vu, GV[ub:ub + UNIT, :].rearrange("(c p) f -> p c f", p=128), [bGV], [bvu], "vu")
                P.dma("sp", gu, GG[d][ub:ub + UNIT, :].rearrange("(c p) f -> p c f", p=128), [bGG[d]], [bgu], "gu")
                order_c = range(NT_U) if d == 0 else range(NT_U - 1, -1, -1)
                for c in order_c:
                    r0 = ub + c * 128
                    o_ = c % 2
                    if d == 1 and _v < 2:
                        P.dma("sp", gof[o_], GO[r0:r0 + 128, :], [bGO], [bgof[o_]], ("gof", o_))
                        P.dma("sp", grr[o_], GR[r0:r0 + 128, :], [bGR], [bgrr[o_]], ("grr", o_))
                    for h in range(GH):
                        i = hc % 2
                        hc += 1
                        j = self.ab % 2
                        self.ab += 1
                        P.op("pe", lambda e, c=c, h=h, j=j: e.matmul(self.pA[j][:, 0:128], lhsT=gu[:, c, h * 128:(h + 1) * 128], rhs=tri, start=True, stop=True),
                             [bgu, btri], [self.bpA[j]])
                        P.op("act", lambda e, i=i, j=j: e.activation(out=e1[i], in_=self.pA[j][:, 0:128], func=AF.Exp), [self.bpA[j]], [be1[i]])
                        P.op("act", lambda e, i=i, j=j: e.activation(out=e2[i], in_=self.pA[j][:, 0:128], func=AF.Exp, scale=-1.0), [self.bpA[j]], [be2[i]])
                        lc = 127 if d == 0 else 0
                        P.op("act", lambda e, i=i, j=j, lc=lc: e.activation(out=eL[i], in_=self.pA[j][:, lc:lc + 1], func=AF.Exp), [self.bpA[j]], [beL[i]])
                        P.op("dve", lambda e, c=c, h=h, i=i: e.tensor_tensor(out=qd[i], in0=qTu[:, h, c * 128:(c + 1) * 128], in1=e1[i], op=ALU.mult), [bqTu, be1[i]], [bqd[i]])
                        P.op("dve", lambda e, c=c, h=h, i=i: e.tensor_tensor(out=kd[i], in0=kTu[:, h, c * 128:(c + 1) * 128], in1=e2[i], op=ALU.mult), [bkTu, be2[i]], [bkd[i]])
                        P.op("pe", lambda e, i=i, j=j: e.matmul(self.pB[j][:, 0:128], lhsT=kd[i], rhs=qd[i], start=True, stop=True), [bkd[i], bqd[i]], [self.bpB[j]])
                        P.op("dve", lambda e, i=i, j=j: e.tensor_tensor(out=at[i], in0=self.pB[j][:, 0:128], in1=msk, op=ALU.mult), [self.bpB[j], btri], [bat[i]])
                        P.op("pe", lambda e, i=i: e.transpose(self.pT[:, 0:128], kd[i], self.idb[:]), [bkd[i], self.bidb], [self.bpT])
                        P.op("act", lambda e, i=i: e.activation(out=ktok[i], in_=self.pT[:, 0:128], func=AF.Copy), [self.bpT], [bktok[i]])
                        o = self.io % 3
                        self.io += 1
                        P.op("pe", lambda e, c=c, h=h, i=i, o=o: e.matmul(self.pO[o][:, 0:GDV], lhsT=at[i], rhs=vu[:, c, h * GDV:(h + 1) * GDV], start=True, stop=False),
                             [bat[i], bvu], [self.bpO[o]])
                        P.op("pe", lambda e, h=h, i=i, o=o: e.matmul(self.pO[o][:, 0:GDV], lhsT=qd[i], rhs=Sb[:, h, :], start=False, stop=True),
                             [bqd[i], bSb], [self.bpO[o]])
                        if d == 0 or _v >= 2:
                            P.op("act", lambda e, h=h, o=o, o_=o_: e.activation(out=ost[o_][:, h * GDV:(h + 1) * GDV], in_=self.pO[o][:, 0:GDV], func=AF.Copy),
                                 [self.bpO[o]], [bost[o_]])
                        else:
                            P.op("dve", lambda e, h=h, o=o, o_=o_: e.tensor_tensor(out=ost[o_][:, h * GDV:(h + 1) * GDV], in0=self.pO[o][:, 0:GDV],
                                                                                  in1=gof[o_][:, h * GDV:(h + 1) * GDV], op=ALU.add), [self.bpO[o], bgof[o_]], [bost[o_]])
                        o2 = self.io % 3
                        self.io += 1
                        P.op("pe", lambda e, c=c, h=h, i=i, o2=o2: e.matmul(self.pO[o2][:, 0:GDV], lhsT=ktok[i], rhs=vu[:, c, h * GDV:(h + 1) * GDV], start=True, stop=True),
                             [bktok[i], bvu], [self.bpO[o2]])
                        P.op("dve", lambda e, h=h, o2=o2: e.tensor_tensor(out=S[:, h, :], in0=S[:, h, :], in1=self.pO[o2][:, 0:GDV], op=ALU.add), [bS, self.bpO[o2]], [bS])
                        P.op("dve", lambda e, h=h, i=i: e.tensor_scalar(out=S[:, h, :], in0=S[:, h, :], scalar1=eL[i], scalar2=None, op0=ALU.mult), [bS, beL[i]], [bS])
                        P.op("act", lambda e, h=h: e.activation(out=Sb[:, h, :], in_=S[:, h, :], func=AF.Copy), [bS], [bSb])
                    if d == 0:
                        P.dma("sp", GO[r0:r0 + 128, :], ost[o_], [bost[o_]], [bGO], ("ost", o_))
                    elif _os.environ.get("GLA_EPI", "1") == "0":
                        pass
                    else:
                        r_ = self.xk % 2
                        self.xk += 1
                        for h in range(GH):
                            P.op("dve", lambda e, h=h, o_=o_: e.tensor_tensor(out=self.sq[:, 0:GDV], in0=ost[o_][:, h * GDV:(h + 1) * GDV], in1=ost[o_][:, h * GDV:(h + 1) * GDV], op=ALU.mult),
                                 [bost[o_]], [self.bsq])
                            P.op("dve", lambda e, h=h, r_=r_: e.reduce_sum(out=self.rc[r_][:, h:h + 1], in_=self.sq[:, 0:GDV], axis=AX.X), [self.bsq], [self.brc[r_]])
                        P.op("act", lambda e, r_=r_: e.activation(out=self.rc[r_][:, 0:GH], in_=self.rc[r_][:, 0:GH], func=AF.Ln, bias=EPS, scale=1.0 / GDV), [self.brc[r_]], [self.brc[r_]])
                        P.op("act", lambda e, r_=r_: e.activation(out=self.rc[r_][:, 0:GH], in_=self.rc[r_][:, 0:GH], func=AF.Exp, scale=-0.5), [self.brc[r_]], [self.brc[r_]])
                        for h in range(GH):
                            P.op("dve", lambda e, h=h, o_=o_, r_=r_: e.scalar_tensor_tensor(out=ost[o_][:, h * GDV:(h + 1) * GDV], in0=ost[o_][:, h * GDV:(h + 1) * GDV],
                                                                                           scalar=self.rc[r_][:, h:h + 1], in1=gn[:, h * GDV:(h + 1) * GDV], op0=ALU.mult, op1=ALU.mult),
                                 [bost[o_], self.brc[r_], bgn], [bost[o_]])
                        P.op("dve", lambda e, o_=o_: e.tensor_tensor(out=ob, in0=ost[o_], in1=grr[o_], op=ALU.mult), [bost[o_], bgrr[o_]], [bob])
                        self.transpose8(ob, bob, oT, [boT])
                        for hf in range(2):
                            o = self.io % 3
                            self.io += 1
                            src_t = (self.x_in if first else self.X)
                            P.dma("sp", xc[hf], src_t[r0:r0 + 128, hf * 512:(hf + 1) * 512], [] if first else [self.bX[u]], [bxc[hf]], ("xc", hf))
                            for fc in range(8):
                                P.op("pe", lambda e, fc=fc, hf=hf, o=o: e.matmul(self.pO[o][:], lhsT=oT[:, fc, :], rhs=wo_[:, fc, hf * 512:(hf + 1) * 512],
                                                                                start=(fc == 0), stop=(fc == 7)), [boT, bwo_], [self.bpO[o]])
                            P.op("dve", lambda e, hf=hf, o=o: e.tensor_tensor(out=xc[hf], in0=xc[hf], in1=self.pO[o][:], op=ALU.add), [self.bpO[o], bxc[hf]], [bxc[hf]])
                            P.dma("sp", self.X[r0:r0 + 128, hf * 512:(hf + 1) * 512], xc[hf], [bxc[hf]], [self.bX[u]], ("xcs", hf))

    def dilated_phase(self, li, first):
        P = self.P
        T, TP, NU = self.T, self.TP, self.NU
        QT = P.dram("dQT", [3, D, T], BF16).ap(); bQT = P.buf("dQT")
        KT = P.dram("dKT", [3, D, TP], BF16).ap(); bKT = P.buf("dKT")
        VA = P.dram("dVA", [3, TP, VW], BF16).ap(); bVA = P.buf("dVA")
        ACC = P.dram("dACC", [3, T, VW], F32).ap(); bACC = P.buf("dACC")
        self.phase_begin()
        self.carve_common()
        ws = [self.carve(8 * 512, "p (k n) -> p k n", k=8) for _ in range(NSLOT)]; bws = [P.buf() for _ in range(NSLOT)]
        rowb = [self.carve(UNIT) for _ in range(2)]; browb = [P.buf() for _ in range(2)]
        vst = [self.carve(8 * (DHD + 1), "p (h d) -> p h d", h=8) for _ in range(2)]; bvst = [P.buf() for _ in range(2)]
        zt = self.carve(VW); bzt = P.buf()
        v8 = self.f32a[:, 0:NT_U * 8].rearrange("p (t e) -> p t e", t=NT_U); bv8 = self.bf32a
        P.op("dve", lambda e: e.memset(zt, 0.0), [], [bzt])
        zk = 0
        for g in range(3):
            for side in (0, PAD + T):
                for fo in range(8):
                    P.dma("sp", KT[g, fo * 128:(fo + 1) * 128, side:side + PAD], zt[:, 0:PAD], [bzt], [bKT], ("z", zk % 4)); zk += 1
                for j in range(PAD // 128):
                    P.dma("sp", VA[g, side + j * 128:side + (j + 1) * 128, :], zt, [bzt], [bVA], ("z", zk % 4)); zk += 1
        self.load_gain(G_MIX + li)
        wq_k = self.dqkv.rearrange("(kc p) n -> p kc n", p=128)
        wc = rb = vs = 0
        for u in range(NU):
            ub = u * UNIT
            self.load_unit(u, first)
            self.norm_unit()
            P.dma("sp", v8, self.valid8[ub:ub + UNIT, :].rearrange("(t p) e -> p t e", p=128), [], [bv8], "v8")
            blocks = [(c3, g, hf) for c3 in range(3) for g in range(3) for hf in range(2)]

            def load_blk(i, s):
                c3, g, hf = blocks[i]
                col = c3 * 3 * D + g * D + hf * 512
                P.dma("pool", ws[s], wq_k[:, :, col:col + 512], [], [bws[s]], ("ws", s))

            for i0 in range(NSLOT - 1):
                load_blk(i0, (wc + i0) % NSLOT)
            for bi, (c3, g, hf) in enumerate(blocks):
                s = wc % NSLOT
                if bi + NSLOT - 1 < len(blocks):
                    load_blk(bi + NSLOT - 1, (wc + NSLOT - 1) % NSLOT)
                if c3 < 2:
                    for fl in range(4):
                        r_ = rb % 2
                        rb += 1
                        for tt in range(4):
                            j = self.ab % 2
                            self.ab += 1
                            rd = [self.bxn[4 * tt + q] for q in range(4)]
                            for kc in range(8):
                                P.op("pe", lambda e, s=s, kc=kc, fl=fl, tt=tt, j=j: e.matmul(self.pA[j][:], lhsT=ws[s][:, kc, fl * 128:(fl + 1) * 128],
                                                                                            rhs=self.xnt[:, kc, tt * 512:(tt + 1) * 512], start=(kc == 0), stop=(kc == 7)),
                                     [bws[s]] + rd, [self.bpA[j]])
                            eng = "act" if tt % 2 == 0 else "dve"
                            if eng == "act":
                                P.op("act", lambda e, r_=r_, tt=tt, j=j: e.activation(out=rowb[r_][:, tt * 512:(tt + 1) * 512], in_=self.pA[j][:], func=AF.Copy),
                                     [self.bpA[j]], [browb[r_]])
                            else:
                                P.op("dve", lambda e, r_=r_, tt=tt, j=j: e.tensor_copy(out=rowb[r_][:, tt * 512:(tt + 1) * 512], in_=self.pA[j][:]),
                                     [self.bpA[j]], [browb[r_]])
                        fr = (hf * 4 + fl) * 128
                        if c3 == 0:
                            P.dma("sp", QT[g, fr:fr + 128, ub:ub + UNIT], rowb[r_], [browb[r_]], [bQT], ("rowb", r_))
                        else:
                            P.dma("sp", KT[g, fr:fr + 128, PAD + ub:PAD + ub + UNIT], rowb[r_], [browb[r_]], [bKT], ("rowb", r_))
                else:
                    for t in range(NT_U):
                        j = self.ab % 2
                        self.ab += 1
                        v_ = vs % 2
                        vs += 1
                        for kc in range(8):
                            P.op("pe", lambda e, s=s, kc=kc, t=t, j=j: e.matmul(self.pB[j][:], lhsT=self.xnt[:, kc, t * 128:(t + 1) * 128], rhs=ws[s][:, kc, :],
                                                                               start=(kc == 0), stop=(kc == 7)), [bws[s], self.bxn[t]], [self.bpB[j]])
                        P.op("act", lambda e, v_=v_, j=j, t=t: e.activation(out=vst[v_][:, :, 0:DHD], in_=self.pB[j][:].rearrange("p (h d) -> p h d", h=8), func=AF.Copy,
                                                                         scale=v8[:, t, 0:1]), [self.bpB[j], bv8], [bvst[v_]])
                        P.op("dve", lambda e, v_=v_, t=t: e.tensor_copy(out=vst[v_][:, :, DHD], in_=v8[:, t, :]), [bv8], [bvst[v_]])
                        P.dma("sp", VA[g, PAD + ub + t * 128:PAD + ub + (t + 1) * 128, hf * 520:(hf + 1) * 520], vst[v_].rearrange("p h d -> p (h d)"),
                              [bvst[v_]], [bVA], ("vst", v_))
                wc += 1
        self.phase_begin()
        Eall = self.carve(3 * DH * 256, "p (g h c) -> p g h c", g=3, h=DH); bE = P.buf()
        qt = [self.carve(UNIT) for _ in range(3)]; bqt = [P.buf() for _ in range(3)]
        kw = [UNIT + 128 * r for r in DIL_R]
        kt = [self.carve(kw[g]) for g in range(3)]; bkt = [P.buf() for _ in range(3)]
        vsub = [self.carve(32 * 130, "p (c d) -> p c d", c=32) for _ in range(2)]; bvsub = [P.buf() for _ in range(2)]
        pexp = [self.carve(512) for _ in range(2)]; bpexp = [P.buf() for _ in range(2)]
        pmul = [self.carve(512) for _ in range(2)]; bpmul = [P.buf() for _ in range(2)]
        xflat = self.xh[:].rearrange("p a b -> p (a b)")
        stg = [xflat[:, i * 2080:(i + 1) * 2080].rearrange("p (b d) -> p b d", b=16) for i in range(2)]; bstg = [P.buf() for _ in range(2)]
        ebuf = self.f32a[:, 0:256]
        for g in range(3):
            for h in range(DH):
                P.dma("sp", ebuf, self.dbias[g, h], [], [self.bf32a], "eb")
                P.op("act", lambda e, g=g, h=h: e.activation(out=Eall[:, g, h, :], in_=ebuf, func=AF.Exp), [self.bf32a], [bE])
        vi = pe_ = si = 0
        for u in range(NU):
            ub = u * UNIT
            for hp in range(8):
                for g, r in enumerate(DIL_R):
                    P.dma("sp", qt[g], QT[g, hp * 128:(hp + 1) * 128, ub:ub + UNIT], [bQT], [bqt[g]], ("qt", g))
                    lo = PAD + ub - 64 * r
                    P.dma("sp", kt[g], KT[g, hp * 128:(hp + 1) * 128, lo:lo + kw[g]], [bKT], [bkt[g]], ("kt", g))
                    qv = qt[g].rearrange("p (n r) -> p r n", r=r)
                    kv = kt[g].rearrange("p (n r) -> p r n", r=r)
                    nb = 16 // r
                    nch = nb + 1
                    for rho in range(r):
                        v_ = vi % 2
                        vi += 1
                        s_ = si % 2
                        si += 1
                        base = PAD + ub - 64 * r
                        vsrc = VA[g, base:base + 128 * r * nch, hp * 130:(hp + 1) * 130].rearrange("(c p r) d -> p c r d", p=128, r=r)[:, :, rho, :]
                        P.dma("sp", vsub[v_][:, 0:nch, :], vsrc, [bVA], [bvsub[v_]], ("vsub", v_))
                        for qb in range(nb):
                            j = self.ab % 2
                            self.ab += 1
                            x_ = pe_ % 2
                            pe_ += 1
                            for h2 in range(2):
                                bank, bbank = (self.pA[j], self.bpA[j]) if h2 == 0 else (self.pB[j], self.bpB[j])
                                for kc in range(2):
                                    c = qb + kc
                                    P.op("pe", lambda e, h2=h2, kc=kc, c=c, qb=qb, rho=rho, kv=kv, qv=qv, bank=bank: e.matmul(
                                        bank[:, kc * 128:(kc + 1) * 128],
                                        lhsT=kv[h2 * 64:(h2 + 1) * 64, rho, c * 128:(c + 1) * 128],
                                        rhs=qv[h2 * 64:(h2 + 1) * 64, rho, qb * 128:(qb + 1) * 128], start=True, stop=True),
                                         [bkt[g], bqt[g]], [bbank])
                            for h2 in range(2):
                                bank, bbank = (self.pA[j], self.bpA[j]) if h2 == 0 else (self.pB[j], self.bpB[j])
                                P.op("act", lambda e, h2=h2, bank=bank, x_=x_: e.activation(out=pexp[x_][:, h2 * 256:(h2 + 1) * 256], in_=bank[:, 0:256], func=AF.Exp,
                                                                                           scale=DHD ** -0.5), [bbank], [bpexp[x_]])
                            P.op("dve", lambda e, g=g, hp=hp, x_=x_: e.tensor_tensor(out=pmul[x_], in0=pexp[x_], in1=Eall[:, g, 2 * hp:2 * hp + 2, :].rearrange("p h c -> p (h c)"),
                                                                                    op=ALU.mult), [bpexp[x_], bE], [bpmul[x_]])
                            o = self.io % 3
                            self.io += 1
                            for h2 in range(2):
                                for kc in range(2):
                                    c = qb + kc
                                    P.op("pe", lambda e, h2=h2, kc=kc, c=c, v_=v_, x_=x_, o=o: e.matmul(
                                        self.pO[o][:, h2 * 65:(h2 + 1) * 65], lhsT=pmul[x_][:, (h2 * 2 + kc) * 128:(h2 * 2 + kc + 1) * 128],
                                        rhs=vsub[v_][:, c, h2 * 65:(h2 + 1) * 65], start=(kc == 0), stop=(kc == 1)),
                                         [bpmul[x_], bvsub[v_]], [self.bpO[o]])
                            P.op("act", lambda e, s_=s_, qb=qb, o=o: e.activation(out=stg[s_][:, qb, :], in_=self.pO[o][:, 0:130], func=AF.Copy), [self.bpO[o]], [bstg[s_]])
                        adst = ACC[g, ub:ub + 128 * r * nb, hp * 130:(hp + 1) * 130].rearrange("(b p r) d -> p b r d", p=128, r=r)[:, :, rho, :]
                        P.dma("sp", adst, stg[s_][:, 0:nb, :], [bstg[s_]], [bACC], ("stg", s_))
        self.phase_begin()
        self.carve_common()
        wo_ = self.carve(8 * D, "p (k n) -> p k n", k=8); bwo_ = P.buf()
        ob = self.carve(4 * D, "p (a d) -> p a d", a=4); bob = [P.buf() for _ in range(4)]
        oT = self.carve(8 * 512, "p (k n) -> p k n", k=8); boT = [P.buf() for _ in range(4)]
        acc1 = self.sq[:, 0:VW].rearrange("p (h d) -> p h d", h=DH); bacc1 = self.bsq
        acc2 = self.f32a[:, 0:VW].rearrange("p (h d) -> p h d", h=DH); bacc2 = self.bf32a
        P.dma("pool", wo_, self.dwo.rearrange("(kc p) n -> p kc n", p=128), [], [bwo_], "dwo")
        for u in range(NU):
            ub = u * UNIT
            self.load_unit(u, first)
            for tt in range(4):
                for sub in range(4):
                    t = 4 * tt + sub
                    r0 = ub + t * 128
                    P.dma("sp", acc1, ACC[0, r0:r0 + 128, :].rearrange("p (h d) -> p h d", h=DH), [bACC], [bacc1], "acc1")
                    for g in (1, 2):
                        P.dma("sp", acc2, ACC[g, r0:r0 + 128, :].rearrange("p (h d) -> p h d", h=DH), [bACC], [bacc2], "acc2")
                        P.op("dve", lambda e: e.tensor_tensor(out=acc1, in0=acc1, in1=acc2, op=ALU.add), [bacc1, bacc2], [bacc1])
                    r = self.xk % 2
                    self.xk += 1
                    P.op("dve", lambda e, r=r: e.reciprocal(out=self.rc[r][:], in_=acc1[:, :, DHD]), [bacc1], [self.brc[r]])
                    for h in range(DH):
                        P.op("dve", lambda e, h=h, r=r, sub=sub: e.tensor_scalar(out=ob[:, sub, h * DHD:(h + 1) * DHD], in0=acc1[:, h, 0:DHD],
                                                                                scalar1=self.rc[r][:, h:h + 1], scalar2=None, op0=ALU.mult),
                             [bacc1, self.brc[r]], [bob[sub]])
                    self.transpose8(ob[:, sub, :], bob[sub], oT[:, :, sub * 128:(sub + 1) * 128], [boT[sub]])
                self.out_proj_add(tt, oT, boT, wo_, bwo_)
            self.store_unit(u)

    def out_phase(self, final, first):
        P = self.P
        self.phase_begin()
        y_t = self.y.rearrange("(t p) d -> p t d", p=128)
        if final:
            self.load_gain(G_FINAL)
        for u in range(self.NU):
            self.load_unit(u, first)
            if final:
                for t in range(NT_U):
                    i = self.rms_rstd(self.xh[:, t, :], [self.bx[t]])
                    P.op("dve", lambda e, t=t, i=i: e.scalar_tensor_tensor(out=self.xh[:, t, :], in0=self.xh[:, t, :], scalar=self.ss[i][:], in1=self.gt[:],
                                                                          op0=ALU.mult, op1=ALU.mult), [self.bx[t], self.bss[i], self.bg], [self.bx[t]])
            self.outs.append(P.dma("sp", y_t[:, u * NT_U:(u + 1) * NT_U, :], self.xh[:], list(self.bx), [self.bY[u]], ("y", u % 2)))

    def build(self):
        first = True
        for ph in self.phases:
            if ph == "final":
                continue
            kind, li = ph.split(":")
            li = int(li)
            if kind == "ffn1":
                self.ffn_phase(2 * li, G_FFN1 + li, first)
            elif kind == "ffn2":
                self.ffn_phase(2 * li + 1, G_FFN2 + li, first)
            elif kind == "cross":
                self.cross_phase(li, first)
            elif kind == "mix" and li == 1:
                self.dilated_phase(li, first)
            elif kind == "mix":
                self.gla_phase(li, first)
            else:
                raise ValueError(ph)
            first = False
        if self.dbg:
            self._dbg_src = self.dbgt[self.dbg]
        self.out_phase("final" in self.phases, first)
        self.P.emit(self.outs)
        return self.nc


_j = np.arange(128)[:, None]; _i = np.arange(128)[None, :]
GLA_TRI = np.stack([np.where(_j <= _i, -1.0 / 16.0, 0.0), np.where(_j >= _i, -1.0 / 16.0, 0.0),
                    np.where(_j <= _i, 1.0, 0.0), np.where(_j >= _i, 1.0, 0.0)]).astype(np.float32)


def pack_weights(inputs):
    f = lambda k: np.asarray(inputs[k], np.float32)
    gl = [f("norm_ffn1")[0], f("norm_ffn1")[1], f("norm_mix")[0], f("norm_mix")[1], f("norm_cross")[0], f("norm_cross")[1],
          f("norm_mem")[0], f("norm_mem")[1], f("norm_ffn2")[0], f("norm_ffn2")[1], f("norm_final")]
    gains = np.ascontiguousarray(np.broadcast_to(np.stack(gl)[:, None, :], (11, 128, D)))
    p = np.arange(128)[:, None]; q = np.arange(128)[None, :]
    rb = f("rel_bias")
    dbias = np.full((3, DH, 128, 256), -30000.0, np.float32)
    for g, r in enumerate(DIL_R):
        for kc in range(2):
            m = p - q - 64 + 128 * kc
            ok = np.abs(m) <= 64
            bk = t5_bucket(m * r)
            for h in range(DH):
                tile = rb[bk, g * DH + h]
                dbias[g, h, :, kc * 128:(kc + 1) * 128] = np.where(ok, tile, dbias[g, h, :, kc * 128:(kc + 1) * 128])
    return {
        "gains": gains, "ident": np.eye(128, dtype=np.float32),
        "ffn_in": np.ascontiguousarray(np.stack([f("ffn1_in")[0], f("ffn2_in")[0], f("ffn1_in")[1], f("ffn2_in")[1]])),
        "ffn_out": np.ascontiguousarray(np.stack([f("ffn1_out")[0], f("ffn2_out")[0], f("ffn1_out")[1], f("ffn2_out")[1]])),
        "cross_q": f("cross_w_q"), "cross_kv": f("cross_w_kv"), "cross_o": f("cross_w_o"),
        "gla_w_in": np.ascontiguousarray(f("gla_w_in")[0]),
        "gla_wgb": np.ascontiguousarray(np.stack([np.concatenate([f("gla_wg_f")[0], f("gla_bg_f")[0][None, :]], 0),
                                                  np.concatenate([f("gla_wg_b")[0], f("gla_bg_b")[0][None, :]], 0)])),
        "gla_norm": np.ascontiguousarray(np.broadcast_to(f("gla_norm")[0][None, :], (128, D))),
        "gla_w_out": np.ascontiguousarray(f("gla_w_out")[0]),
        "gla_tri": GLA_TRI,
        "dil_qkv": np.ascontiguousarray(f("dil_w_qkv")[0]), "dil_o": np.ascontiguousarray(f("dil_w_out")[0]), "dil_bias": dbias,
    }


def run_seqs(seqs, mems, inputs, T, phases, dbg=None):
    n = 8
    w = pack_weights(inputs)
    nc = K(T, phases, dbg).build()
    in_maps = []
    for c in range(n):
        xs = np.zeros((T, D), np.float32)
        mm = np.zeros((MEM, D), np.float32)
        v8 = np.zeros((T, 8), np.float32)
        if c < len(seqs):
            xs[:seqs[c].shape[0]] = seqs[c]
            mm[:] = mems[c]
            v8[:seqs[c].shape[0]] = 1.0
        in_maps.append(dict(w, x=xs, mem=mm, valid8=v8))
    res = run_bass_kernel_spmd(nc, in_maps, core_ids=list(range(n)))
    return [np.asarray(res.results[c]["y"][:seqs[c].shape[0]], np.float32) for c in range(len(seqs))]


def kernel(**inputs):
    xp = np.asarray(inputs["x_prompt"], np.float32)
    xs = np.asarray(inputs["x_sample"], np.float32)
    mp = np.asarray(inputs["mem_prompt"], np.float32)
    ms = np.asarray(inputs["mem_sample"], np.float32)
    seqs = [xp[b] for b in range(xp.shape[0])] + [xs[b] for b in range(xs.shape[0])]
    mems = [mp[b] for b in range(mp.shape[0])] + [ms[b] for b in range(ms.shape[0])]
    T = max(s.shape[0] for s in seqs)
    T = -(-T // UNIT) * UNIT
    outs = run_seqs(seqs, mems, inputs, T, FULL_PHASES)
    yp = np.stack(outs[:xp.shape[0]]).astype(np.float32)
    ys = np.stack(outs[xp.shape[0]:]).astype(np.float32)
    return (yp, ys)
```

```python
from contextlib import ExitStack
import numpy as np
import concourse.bass as bass
import concourse.mybir as mybir

F32 = mybir.dt.float32
BF16 = mybir.dt.bfloat16
AF = mybir.ActivationFunctionType
ALU = mybir.AluOpType
AX = mybir.AxisListType

COMPUTE = ("pe", "act", "dve", "pool")
ENGS = ("pe", "act", "dve", "pool", "sp")


class Buf:
    __slots__ = ("name", "w", "rs", "excl")

    def __init__(self, name, excl=False):
        self.name = name
        self.w = None
        self.rs = []
        self.excl = excl


class Op:
    __slots__ = ("eng", "fn", "dma", "deps", "idx", "sig", "val", "semkey", "dsem", "dval", "pos", "inc")

    def __init__(self, eng, fn, dma, semkey, inc=16):
        self.inc = inc
        self.eng = eng
        self.fn = fn
        self.dma = dma
        self.deps = []
        self.sig = False
        self.val = 0
        self.semkey = semkey
        self.dsem = None
        self.dval = 0


class Prog:
    def __init__(self, nc):
        self.nc = nc
        self.ops = []
        self.es = ExitStack()
        self.last_dma = {}
        self.nbuf = 0
        self.last = {}
        self.pending = []

    def sbuf(self, name, shape, dt):
        return self.es.enter_context(self.nc.sbuf_tensor(name, list(shape), dt))

    def psum(self, name, shape, dt=F32):
        return self.es.enter_context(self.nc.psum_tensor(name, list(shape), dt))

    def dram(self, name, shape, dt, kind="Internal", addr_space="Local"):
        return self.nc.dram_tensor(name, list(shape), dt, kind=kind, addr_space=addr_space)

    def buf(self, name=None, excl=False):
        self.nbuf += 1
        return Buf(name or f"b{self.nbuf}", excl)

    def op(self, eng, fn, reads=(), writes=(), dma=False, semkey=None, inc=16):
        o = Op(eng, fn, dma, semkey, inc)
        deps = []

        def add(p, raw):
            if p is None or p is o:
                return
            if not p.dma and p.eng == eng:
                if not raw or eng == "pe":
                    return
            if p not in deps:
                deps.append(p)

        for b in reads:
            add(b.w, True)
            if b.excl:
                for r in b.rs:
                    if r.eng != eng:
                        add(r, False)
        for b in writes:
            add(b.w, False)
            for r in b.rs:
                add(r, False)
        if dma:
            assert semkey is not None
            add(self.last_dma.get(semkey), False)
            self.last_dma[semkey] = o
        for b in reads:
            if not dma:
                b.rs = [r for r in b.rs if r.dma or r.eng != eng]
            b.rs.append(o)
        for b in writes:
            b.w = o
            b.rs = []
        o.deps = deps
        o.pos = len(self.ops)
        self.ops.append(o)
        if fn is not None:
            self.last[eng] = o
        if dma:
            self.pending.append(o)
        return o

    def barrier(self):
        targets = [p for p in self.last.values()] + list(self.pending)
        self.pending = []
        for eng in ENGS:
            o = Op(eng, None, False, None)
            o.deps = [p for p in dict.fromkeys(targets)]
            o.pos = len(self.ops)
            self.ops.append(o)

    def dma(self, eng, out, in_, reads, writes, semkey, **kw):
        return self.op(eng, lambda e: e.dma_start(out=out, in_=in_, **kw), reads, writes,
                       dma=True, semkey=semkey)

    def emit(self, final_wait_ops=()):
        nc = self.nc
        es = self.es
        fin = self.op("sp", None, reads=(), writes=())
        for p in final_wait_ops:
            if p not in fin.deps:
                fin.deps.append(p)
                p.sig = True
        for o in self.ops:
            for p in o.deps:
                p.sig = True
        esem = {e: es.enter_context(nc.semaphore(f"c_{e}")) for e in COMPUTE}
        dsems = {}
        ecount = {e: 0 for e in COMPUTE}
        dcount = {}
        for o in self.ops:
            if o.dma:
                if o.semkey not in dsems:
                    dsems[o.semkey] = es.enter_context(nc.semaphore(f"d{len(dsems)}"))
                    dcount[o.semkey] = 0
                dcount[o.semkey] += o.inc
                o.dsem = dsems[o.semkey]
                o.dval = dcount[o.semkey]
            elif o.sig:
                assert o.eng in COMPUTE, ("sp non-dma op cannot signal", o.eng)
                ecount[o.eng] += 1
                o.val = ecount[o.eng]
        self.n_dsem = len(dsems)
        per = {e: [] for e in ENGS}
        for o in self.ops:
            per[o.eng].append(o)
        handles = {"pe": "tensor", "act": "scalar", "dve": "vector", "pool": "gpsimd", "sp": "sync"}

        def run(ename, eh):
            known = {}
            for o in per[ename]:
                need = {}
                for p in o.deps:
                    if p.dma:
                        k, s, v = ("d", p.semkey), p.dsem, p.dval
                    else:
                        k, s, v = ("e", p.eng), esem[p.eng], p.val
                    if known.get(k, 0) >= v:
                        continue
                    if k not in need or need[k][1] < v:
                        need[k] = (s, v)
                for k, (s, v) in need.items():
                    eh.wait_ge(s, v)
                    known[k] = v
                if o.fn is None:
                    continue
                ins = o.fn(eh)
                if o.dma:
                    ins.then_inc(o.dsem, o.inc)
                elif o.sig:
                    ins.then_inc(esem[o.eng], 1)

        with nc.Block() as block:
            for ename in ENGS:
                if not per[ename]:
                    continue
                getattr(block, handles[ename])(lambda eh, _n=ename: run(_n, eh))
        es.close()
        return nc


from concourse.bass_utils import run_bass_kernel_spmd

D = 1024
DFF = 2816
NFF = DFF // 128
UNIT = 2048
NT_U = UNIT // 128
MEM = 256
XH = 4
XHD = 256
EPS = 1e-6
NSLOT = 3
ARENA = 40960
GH, GDK, GDV = 4, 128, 256
DIL_R = (1, 4, 16)
DH = 16
DHD = 64
VW = DH * (DHD + 1)
PAD = 1024
NUM_BUCKETS = 32
MAX_DISTANCE = 1024
G_FFN1, G_MIX, G_CROSS, G_MEM, G_FFN2, G_FINAL = 0, 2, 4, 6, 8, 10
FULL_PHASES = ("ffn1:0", "mix:0", "cross:0", "ffn2:0", "ffn1:1", "mix:1", "cross:1", "ffn2:1", "final")


def t5_bucket(rel):
    half = NUM_BUCKETS // 2
    max_exact = half // 2
    ret = (rel > 0).astype(np.int32) * half
    n = np.abs(rel)
    large = max_exact + (np.log(np.maximum(n, 1) / max_exact) / np.log(MAX_DISTANCE / max_exact) * (half - max_exact)).astype(np.int32)
    large = np.minimum(large, half - 1)
    return (ret + np.where(n < max_exact, n, large)).astype(np.int32)


class K:
    def __init__(self, T, phases, dbg=None, T1=None):
        self.T1 = T1
        self.dbg = dbg
        self.dbgt = {}
        assert T % UNIT == 0
        self.T, self.NU, self.phases = T, T // UNIT, tuple(phases)
        self.TP = T + 2 * PAD
        nc = self.nc = bass.Bass("TRN2", target_bir_lowering=False)
        P = self.P = Prog(nc)
        inp = lambda n, s: nc.dram_tensor(n, list(s), F32, kind="ExternalInput").ap()
        self.x_in = inp("x", [T, D])
        self.mem_in = inp("mem", [MEM, D])
        self.gains = inp("gains", [11, 128, D])
        self.ident = inp("ident", [128, 128])
        self.ffn_in = inp("ffn_in", [4, D, 2 * DFF])
        self.ffn_out = inp("ffn_out", [4, DFF, D])
        self.cq = inp("cross_q", [2, D, D])
        self.ckv = inp("cross_kv", [2, D, 2 * D])
        self.co = inp("cross_o", [2, D, D])
        self.dqkv = inp("dil_qkv", [D, 9 * D])
        self.dwo = inp("dil_o", [D, D])
        self.dbias = inp("dil_bias", [3, DH, 128, 256])
        self.gwin = inp("gla_w_in", [D, 3104])
        self.gwgb = inp("gla_wgb", [2, 17, 512])
        self.gnorm = inp("gla_norm", [128, D])
        self.gwout = inp("gla_w_out", [D, D])
        self.gtri = inp("gla_tri", [4, 128, 128])
        TL = T1 or T
        self.valid8 = inp("valid8", [TL, 8])
        self.y = nc.dram_tensor("y", [TL, D], F32, kind="ExternalOutput").ap()
        self.X = P.dram("Xres", [T, D], F32).ap()
        self.bX = [P.buf(f"X{u}") for u in range(self.NU)]
        self.bY = [P.buf(f"Y{u}") for u in range(TL // UNIT)]
        self._gather = False
        if T1:
            self.win_idx = nc.dram_tensor("win_idx", [128, T1 // 128], mybir.dt.int32, kind="ExternalInput").ap()
            self.idxs = P.sbuf("idxs", [128, T1 // 128], mybir.dt.int32); self.bidx = P.buf("idxs")
            P.dma("sp", self.idxs[:], self.win_idx, [], [self.bidx], "idxs")
            self.X1 = P.dram("Xres1", [T1, D], F32).ap()
        self.bXg = P.buf("Xg")
        self.xh = P.sbuf("xh", [128, NT_U, D], F32); self.bx = [P.buf(f"x{t}") for t in range(NT_U)]
        self.xnt = P.sbuf("xnt", [128, 8, UNIT], BF16); self.bxn = [P.buf(f"xn{t}") for t in range(NT_U)]
        self.gt = P.sbuf("gt", [128, D], F32); self.bg = P.buf("g")
        self.idf = P.sbuf("idf", [128, 128], F32); self.bidf = P.buf("idf")
        self.idb = P.sbuf("idb", [128, 128], BF16); self.bidb = P.buf("idb")
        self.sq = P.sbuf("sq", [128, VW], F32); self.bsq = P.buf("sq")
        self.f32a = P.sbuf("f32a", [128, VW], F32); self.bf32a = P.buf("f32a")
        self.sil = [P.sbuf(f"sil{i}", [128, 512], F32) for i in range(2)]; self.bsil = [P.buf(f"sil{i}") for i in range(2)]
        self.ss = [P.sbuf(f"ss{i}", [128, 1], F32) for i in range(2)]; self.bss = [P.buf(f"ss{i}") for i in range(2)]
        self.rc = [P.sbuf(f"rc{i}", [128, 16], F32) for i in range(2)]; self.brc = [P.buf(f"rc{i}") for i in range(2)]
        self.AB = P.sbuf("AB", [128, ARENA], BF16)
        self.pA = [P.psum(f"pA{i}", [128, 512]) for i in range(2)]; self.bpA = [P.buf(f"pA{i}", excl=True) for i in range(2)]
        self.pB = [P.psum(f"pB{i}", [128, 512]) for i in range(2)]; self.bpB = [P.buf(f"pB{i}", excl=True) for i in range(2)]
        self.pO = [P.psum(f"pO{i}", [128, 512]) for i in range(3)]; self.bpO = [P.buf(f"pO{i}", excl=True) for i in range(3)]
        self.pT = P.psum("pT", [128, 8 * 128], BF16); self.bpT = P.buf("pT", excl=True)
        self.io = self.ab = self.wcnt = self.nrm = self.xk = 0
        self.outs = []
        P.dma("sp", self.idf[:], self.ident, [], [self.bidf], "id")
        P.op("act", lambda e: e.activation(out=self.idb[:], in_=self.idf[:], func=AF.Copy), [self.bidf], [self.bidb])

    def phase_begin(self):
        self.P.barrier()
        self.aoff = 0

    def carve(self, n, pat=None, **kw):
        assert self.aoff + n <= ARENA, ("arena overflow", self.aoff, n)
        v = self.AB[:, self.aoff:self.aoff + n]
        self.aoff += n
        return v.rearrange(pat, **kw) if pat else v

    def carve_common(self):
        self.xnb = [self.carve(D) for _ in range(2)]; self.bxnb = [self.P.buf() for _ in range(2)]

    def src(self, first):
        if getattr(self, "_dbg_src", None) is not None:
            return self._dbg_src.rearrange("(t p) d -> p t d", p=128)
        return (self.x_in if first else self.X).rearrange("(t p) d -> p t d", p=128)

    def load_gain(self, row):
        self.P.dma("sp", self.gt[:], self.gains[row], [], [self.bg], "g")

    def enter_window(self):
        self.X0, self.bX0 = self.X, list(self.bX)
        self.T, self.NU, self.TP = self.T1, self.T1 // UNIT, self.T1 + 2 * PAD
        self.X = self.X1
        self.bX = [self.P.buf(f"X1_{u}") for u in range(self.NU)]
        self._gather = True

    def load_unit(self, u, first):
        P = self.P
        if self._gather:
            X0, idxs = self.X0, self.idxs
            for t in range(NT_U):
                k = u * NT_U + t
                P.op("pool", lambda e, t=t, k=k: e.indirect_dma_start(out=self.xh[:, t, :], out_offset=None, in_=X0,
                                                                     in_offset=bass.IndirectOffsetOnAxis(ap=idxs[:, k:k + 1], axis=0)),
                     list(self.bX0) + [self.bidx], [self.bx[t]], dma=True, semkey=("gx", t % 4))
            return
        rd = [] if first else [self.bX[u]]
        for t in range(NT_U):
            P.dma("sp", self.xh[:, t, :], self.src(first)[:, u * NT_U + t, :], rd, [self.bx[t]], ("x", t % 4))

    def store_unit(self, u):
        X_t = self.X.rearrange("(t p) d -> p t d", p=128)
        self.P.dma("sp", X_t[:, u * NT_U:(u + 1) * NT_U, :], self.xh[:], list(self.bx), [self.bX[u]], ("xs", u % 2))

    def rms_rstd(self, src_ap, rdbufs):
        P = self.P
        i = self.nrm % 2
        self.nrm += 1
        P.op("dve", lambda e: e.tensor_tensor(out=self.sq[:, 0:D], in0=src_ap, in1=src_ap, op=ALU.mult), rdbufs, [self.bsq])
        P.op("dve", lambda e: e.reduce_sum(out=self.ss[i][:], in_=self.sq[:, 0:D], axis=AX.X), [self.bsq], [self.bss[i]])
        P.op("act", lambda e: e.activation(out=self.ss[i][:], in_=self.ss[i][:], func=AF.Ln, bias=EPS, scale=1.0 / D), [self.bss[i]], [self.bss[i]])
        P.op("act", lambda e: e.activation(out=self.ss[i][:], in_=self.ss[i][:], func=AF.Exp, scale=-0.5), [self.bss[i]], [self.bss[i]])
        return i

    def norm_T(self, src_ap, rdbufs, dst_ap, dstbufs):
        P = self.P
        i = self.rms_rstd(src_ap, rdbufs)
        xnb_i = self.xnb[i]
        P.op("dve", lambda e: e.scalar_tensor_tensor(out=xnb_i, in0=src_ap, scalar=self.ss[i][:], in1=self.gt[:],
                                                     op0=ALU.mult, op1=ALU.mult), rdbufs + [self.bss[i], self.bg], [self.bxnb[i]])
        self.transpose8(xnb_i, self.bxnb[i], dst_ap, dstbufs)

    def transpose8(self, src_ap, srcbuf, dst_ap, dstbufs):
        P = self.P
        for kc in range(8):
            P.op("pe", lambda e, kc=kc: e.transpose(self.pT[:, kc * 128:(kc + 1) * 128], src_ap[:, kc * 128:(kc + 1) * 128], self.idb[:]),
                 [srcbuf, self.bidb], [self.bpT])
        P.op("act", lambda e: e.activation(out=dst_ap, in_=self.pT[:].rearrange("p (k n) -> p k n", k=8), func=AF.Copy), [self.bpT], dstbufs)

    def norm_unit(self):
        for t in range(NT_U):
            self.norm_T(self.xh[:, t, :], [self.bx[t]], self.xnt[:, :, t * 128:(t + 1) * 128], [self.bxn[t]])

    def out_proj_add(self, tt, oT, boT, w, bw):
        P = self.P
        for sub in range(4):
            t = 4 * tt + sub
            for hf in range(2):
                o = self.io % 3
                self.io += 1
                for fc in range(8):
                    P.op("pe", lambda e, fc=fc, sub=sub, hf=hf, o=o: e.matmul(self.pO[o][:], lhsT=oT[:, fc, sub * 128:(sub + 1) * 128],
                                                                             rhs=w[:, fc, hf * 512:(hf + 1) * 512], start=(fc == 0), stop=(fc == 7)),
                         [boT[sub], bw], [self.bpO[o]])
                P.op("dve", lambda e, t=t, hf=hf, o=o: e.tensor_tensor(out=self.xh[:, t, hf * 512:(hf + 1) * 512], in0=self.xh[:, t, hf * 512:(hf + 1) * 512],
                                                                      in1=self.pO[o][:], op=ALU.add), [self.bpO[o], self.bx[t]], [self.bx[t]])

    def ffn_phase(self, fi, grow, first):
        P = self.P
        FG, NSL = 4, 8
        self.phase_begin()
        self.carve_common()
        wa = [self.carve(1024, "p (k n) -> p k n", k=8) for _ in range(NSL)]; bwa = [P.buf() for _ in range(NSL)]
        wb = [self.carve(1024, "p (k n) -> p k n", k=8) for _ in range(NSL)]; bwb = [P.buf() for _ in range(NSL)]
        wo = [self.carve(D) for _ in range(NSL)]; bwo = [P.buf() for _ in range(NSL)]
        act = [[self.carve(512) for _ in range(FG)] for _ in range(2)]; bact = [[P.buf() for _ in range(FG)] for _ in range(2)]
        w_in_k = self.ffn_in[fi].rearrange("(kc p) n -> p kc n", p=128)
        w_out = self.ffn_out[fi]
        groups = [list(range(c0, min(c0 + FG, NFF))) for c0 in range(0, NFF, FG)]

        def load_w(c):
            s = c % NSL
            P.dma("pool", wa[s], w_in_k[:, :, c * 128:(c + 1) * 128], [], [bwa[s]], ("wa", s))
            P.dma("pool", wb[s], w_in_k[:, :, DFF + c * 128:DFF + (c + 1) * 128], [], [bwb[s]], ("wb", s))
            P.dma("pool", wo[s], w_out[c * 128:(c + 1) * 128, :], [], [bwo[s]], ("wo", s))

        self.load_gain(grow)
        for u in range(self.NU):
            self.load_unit(u, first)
            self.norm_unit()
            for c in groups[0]:
                load_w(c)
            for gi, grp in enumerate(groups):
                if gi + 1 < len(groups):
                    for c in groups[gi + 1]:
                        load_w(c)
                for tt in range(UNIT // 512):
                    par = tt % 2
                    rd = [self.bxn[4 * tt + q] for q in range(4)]
                    for g, c in enumerate(grp):
                        s = c % NSL
                        j = self.ab % 2
                        self.ab += 1
                        a_g, ba_g = act[par][g], bact[par][g]
                        for kc in range(8):
                            P.op("pe", lambda e, s=s, kc=kc, tt=tt, j=j: e.matmul(self.pA[j][:], lhsT=wa[s][:, kc, :], rhs=self.xnt[:, kc, tt * 512:(tt + 1) * 512],
                                                                                 start=(kc == 0), stop=(kc == 7)), [bwa[s]] + rd, [self.bpA[j]])
                        for kc in range(8):
                            P.op("pe", lambda e, s=s, kc=kc, tt=tt, j=j: e.matmul(self.pB[j][:], lhsT=wb[s][:, kc, :], rhs=self.xnt[:, kc, tt * 512:(tt + 1) * 512],
                                                                                 start=(kc == 0), stop=(kc == 7)), [bwb[s]] + rd, [self.bpB[j]])
                        P.op("act", lambda e, j=j: e.activation(out=self.sil[j][:], in_=self.pA[j][:], func=AF.Silu), [self.bpA[j]], [self.bsil[j]])
                        P.op("dve", lambda e, j=j, a_g=a_g: e.tensor_tensor(out=a_g, in0=self.sil[j][:], in1=self.pB[j][:], op=ALU.mult),
                             [self.bsil[j], self.bpB[j]], [ba_g])
                    for sub in range(4):
                        t = 4 * tt + sub
                        for h in range(2):
                            o = self.io % 3
                            self.io += 1
                            for g, c in enumerate(grp):
                                s = c % NSL
                                a_g, ba_g = act[par][g], bact[par][g]
                                P.op("pe", lambda e, s=s, sub=sub, h=h, o=o, a_g=a_g, g=g, n=len(grp): e.matmul(
                                    self.pO[o][:], lhsT=a_g[:, sub * 128:(sub + 1) * 128], rhs=wo[s][:, h * 512:(h + 1) * 512], start=(g == 0), stop=(g == n - 1)),
                                     [ba_g, bwo[s]], [self.bpO[o]])
                            P.op("dve", lambda e, t=t, h=h, o=o: e.scalar_tensor_tensor(out=self.xh[:, t, h * 512:(h + 1) * 512], in0=self.pO[o][:], scalar=0.5,
                                                                                       in1=self.xh[:, t, h * 512:(h + 1) * 512], op0=ALU.mult, op1=ALU.add),
                                 [self.bpO[o], self.bx[t]], [self.bx[t]])
            self.store_unit(u)

    def cross_phase(self, li, first):
        P = self.P
        self.phase_begin()
        self.carve_common()
        wbig = [self.carve(8 * D, "p (k n) -> p k n", k=8) for _ in range(2)]; bwbig = [P.buf() for _ in range(2)]
        memT = self.carve(8 * MEM, "p (k n) -> p k n", k=8); bmemT = P.buf()
        kT = self.carve(8 * MEM, "p (k n) -> p k n", k=8); bkT = P.buf()
        vA = self.carve(2 * XH * (XHD + 1), "p (a h d) -> p a h d", a=2, h=XH); bvA = P.buf()
        qT = self.carve(8 * 512, "p (k n) -> p k n", k=8); bqT = P.buf()
        pTs = [self.carve(512) for _ in range(2)]; bpTs = [P.buf() for _ in range(2)]
        ob = self.carve(4 * D, "p (a d) -> p a d", a=4); bob = [P.buf() for _ in range(4)]
        oT = self.carve(8 * 512, "p (k n) -> p k n", k=8); boT = [P.buf() for _ in range(4)]
        memf = self.f32a[:, 0:D]; bmemf = self.bf32a

        def load_big(i, w_ap):
            P.dma("pool", wbig[i], w_ap.rearrange("(kc p) n -> p kc n", p=128), [], [bwbig[i]], ("wbig", i))

        P.op("dve", lambda e: e.memset(vA, 1.0), [], [bvA])
        self.load_gain(G_MEM + li)
        for kt in range(2):
            P.dma("sp", memf[:], self.mem_in[kt * 128:(kt + 1) * 128, :], [], [bmemf], "memf")
            self.norm_T(memf[:], [bmemf], memT[:, :, kt * 128:(kt + 1) * 128], [bmemT])
        load_big(0, self.ckv[li][:, 0:D])
        for fo in range(8):
            j = self.ab % 2
            self.ab += 1
            for kc in range(8):
                P.op("pe", lambda e, fo=fo, kc=kc, j=j: e.matmul(self.pA[j][:, 0:MEM], lhsT=wbig[0][:, kc, fo * 128:(fo + 1) * 128], rhs=memT[:, kc, :],
                                                                start=(kc == 0), stop=(kc == 7)), [bwbig[0], bmemT], [self.bpA[j]])
            P.op("act", lambda e, fo=fo, j=j: e.activation(out=kT[:, fo, :], in_=self.pA[j][:, 0:MEM], func=AF.Copy), [self.bpA[j]], [bkT])
        load_big(0, self.ckv[li][:, D:2 * D])
        for kt in range(2):
            for hf in range(2):
                j = self.ab % 2
                self.ab += 1
                for kc in range(8):
                    P.op("pe", lambda e, kt=kt, hf=hf, kc=kc, j=j: e.matmul(self.pB[j][:], lhsT=memT[:, kc, kt * 128:(kt + 1) * 128],
                                                                           rhs=wbig[0][:, kc, hf * 512:(hf + 1) * 512], start=(kc == 0), stop=(kc == 7)),
                         [bwbig[0], bmemT], [self.bpB[j]])
                P.op("act", lambda e, kt=kt, hf=hf, j=j: e.activation(out=vA[:, kt, 2 * hf:2 * hf + 2, 0:XHD],
                                                                      in_=self.pB[j][:].rearrange("p (h d) -> p h d", h=2), func=AF.Copy), [self.bpB[j]], [bvA])
        load_big(0, self.cq[li])
        load_big(1, self.co[li])
        self.load_gain(G_CROSS + li)
        for u in range(self.NU):
            self.load_unit(u, first)
            self.norm_unit()
            for tt in range(UNIT // 512):
                rd = [self.bxn[4 * tt + q] for q in range(4)]
                for fo in range(8):
                    j = self.ab % 2
                    self.ab += 1
                    for kc in range(8):
                        P.op("pe", lambda e, fo=fo, kc=kc, tt=tt, j=j: e.matmul(self.pA[j][:], lhsT=wbig[0][:, kc, fo * 128:(fo + 1) * 128],
                                                                               rhs=self.xnt[:, kc, tt * 512:(tt + 1) * 512], start=(kc == 0), stop=(kc == 7)),
                             [bwbig[0]] + rd, [self.bpA[j]])
                    P.op("act", lambda e, fo=fo, j=j: e.activation(out=qT[:, fo, :], in_=self.pA[j][:], func=AF.Copy, scale=XHD ** -0.5),
                         [self.bpA[j]], [bqT])
                for h in range(XH):
                    for kt in range(2):
                        j = self.ab % 2
                        self.ab += 1
                        for dc in range(2):
                            P.op("pe", lambda e, h=h, kt=kt, dc=dc, j=j: e.matmul(self.pB[j][:], lhsT=kT[:, 2 * h + dc, kt * 128:(kt + 1) * 128],
                                                                                 rhs=qT[:, 2 * h + dc, :], start=(dc == 0), stop=(dc == 1)),
                                 [bkT, bqT], [self.bpB[j]])
                        P.op("act", lambda e, kt=kt, j=j: e.activation(out=pTs[kt], in_=self.pB[j][:], func=AF.Exp), [self.bpB[j]], [bpTs[kt]])
                    for sub in range(4):
                        o = self.io % 3
                        self.io += 1
                        r = self.xk % 2
                        self.xk += 1
                        for kt in range(2):
                            P.op("pe", lambda e, h=h, kt=kt, sub=sub, o=o: e.matmul(self.pO[o][:, 0:XHD + 1], lhsT=pTs[kt][:, sub * 128:(sub + 1) * 128],
                                                                                   rhs=vA[:, kt, h, :], start=(kt == 0), stop=(kt == 1)),
                                 [bpTs[kt], bvA], [self.bpO[o]])
                        P.op("dve", lambda e, o=o, r=r: e.reciprocal(out=self.rc[r][:, 0:1], in_=self.pO[o][:, XHD:XHD + 1]), [self.bpO[o]], [self.brc[r]])
                        P.op("dve", lambda e, h=h, sub=sub, o=o, r=r: e.tensor_scalar(out=ob[:, sub, h * XHD:(h + 1) * XHD], in0=self.pO[o][:, 0:XHD],
                                                                                     scalar1=self.rc[r][:, 0:1], scalar2=None, op0=ALU.mult),
                             [self.bpO[o], self.brc[r]], [bob[sub]])
                for sub in range(4):
                    self.transpose8(ob[:, sub, :], bob[sub], oT[:, :, sub * 128:(sub + 1) * 128], [boT[sub]])
                self.out_proj_add(tt, oT, boT, wbig[1], bwbig[1])
            self.store_unit(u)


    def gla_phase(self, li, first):
        P = self.P
        T, NU = self.T, self.NU
        GQT = P.dram("gQT", [GH * GDK, T], BF16).ap(); bGQ = P.buf("gQT")
        GKT = P.dram("gKT", [GH * GDK, T], BF16).ap(); bGK = P.buf("gKT")
        GV = P.dram("gV", [T, D], BF16).ap(); bGV = P.buf("gV")
        GR = P.dram("gR", [T, D], F32).ap(); bGR = P.buf("gR")
        GG = [P.dram(f"gG{d}", [T, 512], F32).ap() for d in range(2)]; bGG = [P.buf(f"gG{d}") for d in range(2)]
        GO = P.dram("gO", [T, D], F32).ap(); bGO = P.buf("gO")
        wk = self.gwin.rearrange("(kc p) n -> p kc n", p=128)
        self.dbgt.update(GO=GO, GR=GR, GG0=GG[0], GG1=GG[1])
        self.phase_begin()
        self.carve_common()
        ws = [self.carve(8 * 512, "p (k n) -> p k n", k=8) for _ in range(NSLOT)]; bws = [P.buf() for _ in range(NSLOT)]
        wz = self.carve(8 * 32, "p (k n) -> p k n", k=8); bwz = P.buf()
        rowb = [self.carve(UNIT) for _ in range(2)]; browb = [P.buf() for _ in range(2)]
        vrow = [self.carve(512) for _ in range(2)]; bvrow = [P.buf() for _ in range(2)]
        zaug = [self.carve(UNIT) for _ in range(2)]; bzaug = [P.buf() for _ in range(2)]
        wgb = [self.carve(512) for _ in range(2)]; bwgb = [P.buf() for _ in range(2)]
        rrow = [self.sil[0], self.sil[1]]; brrow = self.bsil
        grow_ = [self.sq[:, 0:512], self.f32a[:, 0:512]]; bgrow = [self.bsq, self.bf32a]
        P.dma("pool", wz, wk[:, :, 3072:3104], [], [bwz], "wz")
        for d in range(2):
            P.dma("pool", wgb[d][0:17, :], self.gwgb[d], [], [bwgb[d]], ("wgb", d))
            P.op("dve", lambda e, d=d: e.memset(zaug[d][0:32, :], 1.0), [], [bzaug[d]])
        self.load_gain(G_MIX + li)
        wc = rb = vs = 0
        blocks = [("q", 0), ("k", 512), ("v", 1024), ("v", 1536), ("r", 2048), ("r", 2560)]
        for u in range(NU):
            ub = u * UNIT
            self.load_unit(u, first)
            self.norm_unit()

            def load_blk(i, s_):
                P.dma("pool", ws[s_], wk[:, :, blocks[i][1]:blocks[i][1] + 512], [], [bws[s_]], ("ws", s_))

            for i0 in range(NSLOT - 1):
                load_blk(i0, (wc + i0) % NSLOT)
            for bi, (kind, col) in enumerate(blocks):
                s_ = wc % NSLOT
                if bi + NSLOT - 1 < len(blocks):
                    load_blk(bi + NSLOT - 1, (wc + NSLOT - 1) % NSLOT)
                if kind in ("q", "k"):
                    for fl in range(4):
                        r_ = rb % 2
                        rb += 1
                        for tt in range(4):
                            j = self.ab % 2
                            self.ab += 1
                            rd = [self.bxn[4 * tt + q] for q in range(4)]
                            for kc in range(8):
                                P.op("pe", lambda e, s_=s_, kc=kc, fl=fl, tt=tt, j=j: e.matmul(self.pA[j][:], lhsT=ws[s_][:, kc, fl * 128:(fl + 1) * 128],
                                                                                             rhs=self.xnt[:, kc, tt * 512:(tt + 1) * 512], start=(kc == 0), stop=(kc == 7)),
                                     [bws[s_]] + rd, [self.bpA[j]])
                            sc = GDK ** -0.5 if kind == "q" else 1.0
                            P.op("act", lambda e, r_=r_, tt=tt, j=j, sc=sc: e.activation(out=rowb[r_][:, tt * 512:(tt + 1) * 512], in_=self.pA[j][:], func=AF.Copy, scale=sc),
                                 [self.bpA[j]], [browb[r_]])
                        dst, bdst = (GQT, bGQ) if kind == "q" else (GKT, bGK)
                        P.dma("sp", dst[fl * 128:(fl + 1) * 128, ub:ub + UNIT], rowb[r_], [browb[r_]], [bdst], ("rowb", r_))
                else:
                    hf = (col % 1024) // 512
                    for t in range(NT_U):
                        j = self.ab % 2
                        self.ab += 1
                        v_ = vs % 2
                        vs += 1
                        for kc in range(8):
                            P.op("pe", lambda e, s_=s_, kc=kc, t=t, j=j: e.matmul(self.pB[j][:], lhsT=self.xnt[:, kc, t * 128:(t + 1) * 128], rhs=ws[s_][:, kc, :],
                                                                                 start=(kc == 0), stop=(kc == 7)), [bws[s_], self.bxn[t]], [self.bpB[j]])
                        r0 = ub + t * 128
                        if kind == "v":
                            P.op("act", lambda e, v_=v_, j=j: e.activation(out=vrow[v_], in_=self.pB[j][:], func=AF.Copy), [self.bpB[j]], [bvrow[v_]])
                            P.dma("sp", GV[r0:r0 + 128, hf * 512:(hf + 1) * 512], vrow[v_], [bvrow[v_]], [bGV], ("vrow", v_))
                        else:
                            P.op("act", lambda e, v_=v_, j=j: e.activation(out=rrow[v_][:], in_=self.pB[j][:], func=AF.Silu), [self.bpB[j]], [brrow[v_]])
                            P.dma("sp", GR[r0:r0 + 128, hf * 512:(hf + 1) * 512], rrow[v_][:], [brrow[v_]], [bGR], ("rrow", v_))
                wc += 1
            for d in range(2):
                for tt in range(4):
                    j = self.ab % 2
                    self.ab += 1
                    rd = [self.bxn[4 * tt + q] for q in range(4)]
                    for kc in range(8):
                        P.op("pe", lambda e, d=d, kc=kc, tt=tt, j=j: e.matmul(self.pA[j][0:16, :], lhsT=wz[:, kc, d * 16:(d + 1) * 16],
                                                                             rhs=self.xnt[:, kc, tt * 512:(tt + 1) * 512], start=(kc == 0), stop=(kc == 7)),
                             [bwz] + rd, [self.bpA[j]])
                    P.op("act", lambda e, d=d, tt=tt, j=j: e.activation(out=zaug[d][0:16, tt * 512:(tt + 1) * 512], in_=self.pA[j][0:16, :], func=AF.Copy),
                         [self.bpA[j]], [bzaug[d]])
                for t in range(NT_U):
                    j = self.ab % 2
                    self.ab += 1
                    g_ = vs % 2
                    vs += 1
                    P.op("pe", lambda e, d=d, t=t, j=j: e.matmul(self.pB[j][:], lhsT=zaug[d][0:17, t * 128:(t + 1) * 128], rhs=wgb[d][0:17, :], start=True, stop=True),
                         [bzaug[d], bwgb[d]], [self.bpB[j]])
                    P.op("act", lambda e, g_=g_, j=j: e.activation(out=grow_[g_], in_=self.pB[j][:], func=AF.Exp, scale=-1.0), [self.bpB[j]], [bgrow[g_]])
                    P.op("act", lambda e, g_=g_: e.activation(out=grow_[g_], in_=grow_[g_], func=AF.Ln, bias=1.0), [bgrow[g_]], [bgrow[g_]])
                    r0 = ub + t * 128
                    P.dma("sp", GG[d][r0:r0 + 128, :], grow_[g_], [bgrow[g_]], [bGG[d]], ("grow", g_))
        import os as _os
        _stop = int(_os.environ.get("GLA_STOP", "9"))
        for d in range(2):
            if d + 1 >= _stop:
                break
            self.phase_begin()
            qTu = self.carve(GH * UNIT, "p (h t) -> p h t", h=GH); bqTu = P.buf()
            kTu = self.carve(GH * UNIT, "p (h t) -> p h t", h=GH); bkTu = P.buf()
            vu = self.xnt[:].rearrange("p k t -> p (k t)").rearrange("p (c f) -> p c f", c=NT_U); bvu = P.buf()
            qd = [self.carve(128) for _ in range(2)]; bqd = [P.buf() for _ in range(2)]
            kd = [self.carve(128) for _ in range(2)]; bkd = [P.buf() for _ in range(2)]
            at = [self.carve(128) for _ in range(2)]; bat = [P.buf() for _ in range(2)]
            ktok = [self.carve(128) for _ in range(2)]; bktok = [P.buf() for _ in range(2)]
            Sb = self.carve(GH * GDV, "p (h v) -> p h v", h=GH); bSb = P.buf()
            xflat = self.xh[:].rearrange("p a b -> p (a b)")
            gu = xflat[:, 0:NT_U * 512].rearrange("p (c f) -> p c f", c=NT_U); bgu = P.buf()
            ost = [xflat[:, 8192 + i * 1024:8192 + (i + 1) * 1024] for i in range(2)]; bost = [P.buf() for _ in range(2)]
            e1 = [xflat[:, 10240 + i * 128:10240 + (i + 1) * 128] for i in range(2)]; be1 = [P.buf() for _ in range(2)]
            e2 = [xflat[:, 10496 + i * 128:10496 + (i + 1) * 128] for i in range(2)]; be2 = [P.buf() for _ in range(2)]
            eL = [xflat[:, 10752 + i:10753 + i] for i in range(2)]; beL = [P.buf() for _ in range(2)]
            tri = xflat[:, 10880:11008]; msk = xflat[:, 11008:11136]; btri = P.buf()
            gof = [xflat[:, 11264 + i * 1024:11264 + (i + 1) * 1024] for i in range(2)]; bgof = [P.buf() for _ in range(2)]
            grr = [xflat[:, 13312 + i * 1024:13312 + (i + 1) * 1024] for i in range(2)]; bgrr = [P.buf() for _ in range(2)]
            xc = [xflat[:, 15360 + i * 512:15360 + (i + 1) * 512] for i in range(2)]; bxc = [P.buf() for _ in range(2)]
            S = self.f32a[:, 0:GH * GDV].rearrange("p (h v) -> p h v", h=GH); bS = self.bf32a
            gn = self.gt; bgn = self.bg
            P.dma("sp", tri, self.gtri[d], [], [btri], "tri")
            P.dma("sp", msk, self.gtri[2 + d], [], [btri], "msk")
            P.op("dve", lambda e: e.memset(S, 0.0), [], [bS])
            P.op("dve", lambda e: e.memset(Sb, 0.0), [], [bSb])
            _v = int(_os.environ.get("GLA_V", "0"))
            if d == 1 and _v == 0:
                self.carve_common()
                wo_ = self.carve(8 * D, "p (k n) -> p k n", k=8); bwo_ = P.buf()
                ob = self.carve(D); bob = P.buf()
                oT = self.carve(8 * 128, "p (k n) -> p k n", k=8); boT = P.buf()
                if _os.environ.get("GLA_NOWO", "0") != "1":
                    P.dma("pool", wo_, self.gwout.rearrange("(kc p) n -> p kc n", p=128), [], [bwo_], "gwo")
                if _os.environ.get("GLA_NOGN", "0") != "1":
                    P.dma("sp", gn[:], self.gnorm, [], [bgn], "g")
            hc = 0
            order_u = range(NU) if d == 0 else range(NU - 1, -1, -1)
            for u in order_u:
                ub = u * UNIT
                P.dma("sp", qTu, GQT.rearrange("(h p) t -> p h t", p=128)[:, :, ub:ub + UNIT], [bGQ], [bqTu], "qTu")
                P.dma("sp", kTu, GKT.rearrange("(h p) t -> p h t", p=128)[:, :, ub:ub + UNIT], [bGK], [bkTu], "kTu")
                P.dma("sp", vu, GV[ub:ub + UNIT, :].rearrange("(c p) f -> p c f", p=128), [bGV], [bvu], "vu")
                P.dma("sp", gu, GG[d][ub:ub + UNIT, :].rearrange("(c p) f -> p c f", p=128), [bGG[d]], [bgu], "gu")
                order_c = range(NT_U) if d == 0 else range(NT_U - 1, -1, -1)
                for c in order_c:
                    r0 = ub + c * 128
                    o_ = c % 2
                    if d == 1 and _v < 2:
                        P.dma("sp", gof[o_], GO[r0:r0 + 128, :], [bGO], [bgof[o_]], ("gof", o_))
                        P.dma("sp", grr[o_], GR[r0:r0 + 128, :], [bGR], [bgrr[o_]], ("grr", o_))
                    for h in range(GH):
                        i = hc % 2
                        hc += 1
                        j = self.ab % 2
                        self.ab += 1
                        P.op("pe", lambda e, c=c, h=h, j=j: e.matmul(self.pA[j][:, 0:128], lhsT=gu[:, c, h * 128:(h + 1) * 128], rhs=tri, start=True, stop=True),
                             [bgu, btri], [self.bpA[j]])
                        P.op("act", lambda e, i=i, j=j: e.activation(out=e1[i], in_=self.pA[j][:, 0:128], func=AF.Exp), [self.bpA[j]], [be1[i]])
                        P.op("act", lambda e, i=i, j=j: e.activation(out=e2[i], in_=self.pA[j][:, 0:128], func=AF.Exp, scale=-1.0), [self.bpA[j]], [be2[i]])
                        lc = 127 if d == 0 else 0
                        P.op("act", lambda e, i=i, j=j, lc=lc: e.activation(out=eL[i], in_=self.pA[j][:, lc:lc + 1], func=AF.Exp), [self.bpA[j]], [beL[i]])
                        P.op("dve", lambda e, c=c, h=h, i=i: e.tensor_tensor(out=qd[i], in0=qTu[:, h, c * 128:(c + 1) * 128], in1=e1[i], op=ALU.mult), [bqTu, be1[i]], [bqd[i]])
                        P.op("dve", lambda e, c=c, h=h, i=i: e.tensor_tensor(out=kd[i], in0=kTu[:, h, c * 128:(c + 1) * 128], in1=e2[i], op=ALU.mult), [bkTu, be2[i]], [bkd[i]])
                        P.op("pe", lambda e, i=i, j=j: e.matmul(self.pB[j][:, 0:128], lhsT=kd[i], rhs=qd[i], start=True, stop=True), [bkd[i], bqd[i]], [self.bpB[j]])
                        P.op("dve", lambda e, i=i, j=j: e.tensor_tensor(out=at[i], in0=self.pB[j][:, 0:128], in1=msk, op=ALU.mult), [self.bpB[j], btri], [bat[i]])
                        P.op("pe", lambda e, i=i: e.transpose(self.pT[:, 0:128], kd[i], self.idb[:]), [bkd[i], self.bidb], [self.bpT])
                        P.op("act", lambda e, i=i: e.activation(out=ktok[i], in_=self.pT[:, 0:128], func=AF.Copy), [self.bpT], [bktok[i]])
                        o = self.io % 3
                        self.io += 1
                        P.op("pe", lambda e, c=c, h=h, i=i, o=o: e.matmul(self.pO[o][:, 0:GDV], lhsT=at[i], rhs=vu[:, c, h * GDV:(h + 1) * GDV], start=True, stop=False),
                             [bat[i], bvu], [self.bpO[o]])
                        P.op("pe", lambda e, h=h, i=i, o=o: e.matmul(self.pO[o][:, 0:GDV], lhsT=qd[i], rhs=Sb[:, h, :], start=False, stop=True),
                             [bqd[i], bSb], [self.bpO[o]])
                        if d == 0 or _v >= 2:
                            P.op("act", lambda e, h=h, o=o, o_=o_: e.activation(out=ost[o_][:, h * GDV:(h + 1) * GDV], in_=self.pO[o][:, 0:GDV], func=AF.Copy),
                                 [self.bpO[o]], [bost[o_]])
                        else:
                            P.op("dve", lambda e, h=h, o=o, o_=o_: e.tensor_tensor(out=ost[o_][:, h * GDV:(h + 1) * GDV], in0=self.pO[o][:, 0:GDV],
                                                                                  in1=gof[o_][:, h * GDV:(h + 1) * GDV], op=ALU.add), [self.bpO[o], bgof[o_]], [bost[o_]])
                        o2 = self.io % 3
                        self.io += 1
                        P.op("pe", lambda e, c=c, h=h, i=i, o2=o2: e.matmul(self.pO[o2][:, 0:GDV], lhsT=ktok[i], rhs=vu[:, c, h * GDV:(h + 1) * GDV], start=True, stop=True),
                             [bktok[i], bvu], [self.bpO[o2]])
                        P.op("dve", lambda e, h=h, o2=o2: e.tensor_tensor(out=S[:, h, :], in0=S[:, h, :], in1=self.pO[o2][:, 0:GDV], op=ALU.add), [bS, self.bpO[o2]], [bS])
                        P.op("dve", lambda e, h=h, i=i: e.tensor_scalar(out=S[:, h, :], in0=S[:, h, :], scalar1=eL[i], scalar2=None, op0=ALU.mult), [bS, beL[i]], [bS])
                        P.op("act", lambda e, h=h: e.activation(out=Sb[:, h, :], in_=S[:, h, :], func=AF.Copy), [bS], [bSb])
                    if d == 0:
                        P.dma("sp", GO[r0:r0 + 128, :], ost[o_], [bost[o_]], [bGO], ("ost", o_))
                    elif _os.environ.get("GLA_EPI", "1") == "0":
                        pass
                    else:
                        r_ = self.xk % 2
                        self.xk += 1
                        for h in range(GH):
                            P.op("dve", lambda e, h=h, o_=o_: e.tensor_tensor(out=self.sq[:, 0:GDV], in0=ost[o_][:, h * GDV:(h + 1) * GDV], in1=ost[o_][:, h * GDV:(h + 1) * GDV], op=ALU.mult),
                                 [bost[o_]], [self.bsq])
                            P.op("dve", lambda e, h=h, r_=r_: e.reduce_sum(out=self.rc[r_][:, h:h + 1], in_=self.sq[:, 0:GDV], axis=AX.X), [self.bsq], [self.brc[r_]])
                        P.op("act", lambda e, r_=r_: e.activation(out=self.rc[r_][:, 0:GH], in_=self.rc[r_][:, 0:GH], func=AF.Ln, bias=EPS, scale=1.0 / GDV), [self.brc[r_]], [self.brc[r_]])
                        P.op("act", lambda e, r_=r_: e.activation(out=self.rc[r_][:, 0:GH], in_=self.rc[r_][:, 0:GH], func=AF.Exp, scale=-0.5), [self.brc[r_]], [self.brc[r_]])
                        for h in range(GH):
                            P.op("dve", lambda e, h=h, o_=o_, r_=r_: e.scalar_tensor_tensor(out=ost[o_][:, h * GDV:(h + 1) * GDV], in0=ost[o_][:, h * GDV:(h + 1) * GDV],
                                                                                           scalar=self.rc[r_][:, h:h + 1], in1=gn[:, h * GDV:(h + 1) * GDV], op0=ALU.mult, op1=ALU.mult),
                                 [bost[o_], self.brc[r_], bgn], [bost[o_]])
                        P.op("dve", lambda e, o_=o_: e.tensor_tensor(out=ob, in0=ost[o_], in1=grr[o_], op=ALU.mult), [bost[o_], bgrr[o_]], [bob])
                        self.transpose8(ob, bob, oT, [boT])
                        for hf in range(2):
                            o = self.io % 3
                            self.io += 1
                            src_t = (self.x_in if first else self.X)
                            P.dma("sp", xc[hf], src_t[r0:r0 + 128, hf * 512:(hf + 1) * 512], [] if first else [self.bX[u]], [bxc[hf]], ("xc", hf))
                            for fc in range(8):
                                P.op("pe", lambda e, fc=fc, hf=hf, o=o: e.matmul(self.pO[o][:], lhsT=oT[:, fc, :], rhs=wo_[:, fc, hf * 512:(hf + 1) * 512],
                                                                                start=(fc == 0), stop=(fc == 7)), [boT, bwo_], [self.bpO[o]])
                            P.op("dve", lambda e, hf=hf, o=o: e.tensor_tensor(out=xc[hf], in0=xc[hf], in1=self.pO[o][:], op=ALU.add), [self.bpO[o], bxc[hf]], [bxc[hf]])
                            P.dma("sp", self.X[r0:r0 + 128, hf * 512:(hf + 1) * 512], xc[hf], [bxc[hf]], [self.bX[u]], ("xcs", hf))

    def dilated_phase(self, li, first):
        P = self.P
        T, TP, NU = self.T, self.TP, self.NU
        QT = P.dram("dQT", [3, D, T], BF16).ap(); bQT = P.buf("dQT")
        KT = P.dram("dKT", [3, D, TP], BF16).ap(); bKT = P.buf("dKT")
        VA = P.dram("dVA", [3, TP, VW], BF16).ap(); bVA = P.buf("dVA")
        ACC = P.dram("dACC", [3, T, VW], F32).ap(); bACC = P.buf("dACC")
        self.phase_begin()
        self.carve_common()
        ws = [self.carve(8 * 512, "p (k n) -> p k n", k=8) for _ in range(NSLOT)]; bws = [P.buf() for _ in range(NSLOT)]
        rowb = [self.carve(UNIT) for _ in range(2)]; browb = [P.buf() for _ in range(2)]
        vst = [self.carve(8 * (DHD + 1), "p (h d) -> p h d", h=8) for _ in range(2)]; bvst = [P.buf() for _ in range(2)]
        zt = self.carve(VW); bzt = P.buf()
        v8 = self.f32a[:, 0:NT_U * 8].rearrange("p (t e) -> p t e", t=NT_U); bv8 = self.bf32a
        P.op("dve", lambda e: e.memset(zt, 0.0), [], [bzt])
        zk = 0
        for g in range(3):
            for side in (0, PAD + T):
                for fo in range(8):
                    P.dma("sp", KT[g, fo * 128:(fo + 1) * 128, side:side + PAD], zt[:, 0:PAD], [bzt], [bKT], ("z", zk % 4)); zk += 1
                for j in range(PAD // 128):
                    P.dma("sp", VA[g, side + j * 128:side + (j + 1) * 128, :], zt, [bzt], [bVA], ("z", zk % 4)); zk += 1
        self.load_gain(G_MIX + li)
        wq_k = self.dqkv.rearrange("(kc p) n -> p kc n", p=128)
        wc = rb = vs = 0
        for u in range(NU):
            ub = u * UNIT
            self.load_unit(u, first)
            self.norm_unit()
            P.dma("sp", v8, self.valid8[ub:ub + UNIT, :].rearrange("(t p) e -> p t e", p=128), [], [bv8], "v8")
            blocks = [(c3, g, hf) for c3 in range(3) for g in range(3) for hf in range(2)]

            def load_blk(i, s):
                c3, g, hf = blocks[i]
                col = c3 * 3 * D + g * D + hf * 512
                P.dma("pool", ws[s], wq_k[:, :, col:col + 512], [], [bws[s]], ("ws", s))

            for i0 in range(NSLOT - 1):
                load_blk(i0, (wc + i0) % NSLOT)
            for bi, (c3, g, hf) in enumerate(blocks):
                s = wc % NSLOT
                if bi + NSLOT - 1 < len(blocks):
                    load_blk(bi + NSLOT - 1, (wc + NSLOT - 1) % NSLOT)
                if c3 < 2:
                    for fl in range(4):
                        r_ = rb % 2
                        rb += 1
                        for tt in range(4):
                            j = self.ab % 2
                            self.ab += 1
                            rd = [self.bxn[4 * tt + q] for q in range(4)]
                            for kc in range(8):
                                P.op("pe", lambda e, s=s, kc=kc, fl=fl, tt=tt, j=j: e.matmul(self.pA[j][:], lhsT=ws[s][:, kc, fl * 128:(fl + 1) * 128],
                                                                                            rhs=self.xnt[:, kc, tt * 512:(tt + 1) * 512], start=(kc == 0), stop=(kc == 7)),
                                     [bws[s]] + rd, [self.bpA[j]])
                            eng = "act" if tt % 2 == 0 else "dve"
                            if eng == "act":
                                P.op("act", lambda e, r_=r_, tt=tt, j=j: e.activation(out=rowb[r_][:, tt * 512:(tt + 1) * 512], in_=self.pA[j][:], func=AF.Copy),
                                     [self.bpA[j]], [browb[r_]])
                            else:
                                P.op("dve", lambda e, r_=r_, tt=tt, j=j: e.tensor_copy(out=rowb[r_][:, tt * 512:(tt + 1) * 512], in_=self.pA[j][:]),
                                     [self.bpA[j]], [browb[r_]])
                        fr = (hf * 4 + fl) * 128
                        if c3 == 0:
                            P.dma("sp", QT[g, fr:fr + 128, ub:ub + UNIT], rowb[r_], [browb[r_]], [bQT], ("rowb", r_))
                        else:
                            P.dma("sp", KT[g, fr:fr + 128, PAD + ub:PAD + ub + UNIT], rowb[r_], [browb[r_]], [bKT], ("rowb", r_))
                else:
                    for t in range(NT_U):
                        j = self.ab % 2
                        self.ab += 1
                        v_ = vs % 2
                        vs += 1
                        for kc in range(8):
                            P.op("pe", lambda e, s=s, kc=kc, t=t, j=j: e.matmul(self.pB[j][:], lhsT=self.xnt[:, kc, t * 128:(t + 1) * 128], rhs=ws[s][:, kc, :],
                                                                               start=(kc == 0), stop=(kc == 7)), [bws[s], self.bxn[t]], [self.bpB[j]])
                        P.op("act", lambda e, v_=v_, j=j, t=t: e.activation(out=vst[v_][:, :, 0:DHD], in_=self.pB[j][:].rearrange("p (h d) -> p h d", h=8), func=AF.Copy,
                                                                         scale=v8[:, t, 0:1]), [self.bpB[j], bv8], [bvst[v_]])
                        P.op("dve", lambda e, v_=v_, t=t: e.tensor_copy(out=vst[v_][:, :, DHD], in_=v8[:, t, :]), [bv8], [bvst[v_]])
                        P.dma("sp", VA[g, PAD + ub + t * 128:PAD + ub + (t + 1) * 128, hf * 520:(hf + 1) * 520], vst[v_].rearrange("p h d -> p (h d)"),
                              [bvst[v_]], [bVA], ("vst", v_))
                wc += 1
        self.phase_begin()
        Eall = self.carve(3 * DH * 256, "p (g h c) -> p g h c", g=3, h=DH); bE = P.buf()
        qt = [self.carve(UNIT) for _ in range(3)]; bqt = [P.buf() for _ in range(3)]
        kw = [UNIT + 128 * r for r in DIL_R]
        kt = [self.carve(kw[g]) for g in range(3)]; bkt = [P.buf() for _ in range(3)]
        vsub = [self.carve(32 * 130, "p (c d) -> p c d", c=32) for _ in range(2)]; bvsub = [P.buf() for _ in range(2)]
        pexp = [self.carve(512) for _ in range(2)]; bpexp = [P.buf() for _ in range(2)]
        pmul = [self.carve(512) for _ in range(2)]; bpmul = [P.buf() for _ in range(2)]
        xflat = self.xh[:].rearrange("p a b -> p (a b)")
        stg = [xflat[:, i * 2080:(i + 1) * 2080].rearrange("p (b d) -> p b d", b=16) for i in range(2)]; bstg = [P.buf() for _ in range(2)]
        ebuf = self.f32a[:, 0:256]
        for g in range(3):
            for h in range(DH):
                P.dma("sp", ebuf, self.dbias[g, h], [], [self.bf32a], "eb")
                P.op("act", lambda e, g=g, h=h: e.activation(out=Eall[:, g, h, :], in_=ebuf, func=AF.Exp), [self.bf32a], [bE])
        vi = pe_ = si = 0
        for u in range(NU):
            ub = u * UNIT
            for hp in range(8):
                for g, r in enumerate(DIL_R):
                    P.dma("sp", qt[g], QT[g, hp * 128:(hp + 1) * 128, ub:ub + UNIT], [bQT], [bqt[g]], ("qt", g))
                    lo = PAD + ub - 64 * r
                    P.dma("sp", kt[g], KT[g, hp * 128:(hp + 1) * 128, lo:lo + kw[g]], [bKT], [bkt[g]], ("kt", g))
                    qv = qt[g].rearrange("p (n r) -> p r n", r=r)
                    kv = kt[g].rearrange("p (n r) -> p r n", r=r)
                    nb = 16 // r
                    nch = nb + 1
                    for rho in range(r):
                        v_ = vi % 2
                        vi += 1
                        s_ = si % 2
                        si += 1
                        base = PAD + ub - 64 * r
                        vsrc = VA[g, base:base + 128 * r * nch, hp * 130:(hp + 1) * 130].rearrange("(c p r) d -> p c r d", p=128, r=r)[:, :, rho, :]
                        P.dma("sp", vsub[v_][:, 0:nch, :], vsrc, [bVA], [bvsub[v_]], ("vsub", v_))
                        for qb in range(nb):
                            j = self.ab % 2
                            self.ab += 1
                            x_ = pe_ % 2
                            pe_ += 1
                            for h2 in range(2):
                                bank, bbank = (self.pA[j], self.bpA[j]) if h2 == 0 else (self.pB[j], self.bpB[j])
                                for kc in range(2):
                                    c = qb + kc
                                    P.op("pe", lambda e, h2=h2, kc=kc, c=c, qb=qb, rho=rho, kv=kv, qv=qv, bank=bank: e.matmul(
                                        bank[:, kc * 128:(kc + 1) * 128],
                                        lhsT=kv[h2 * 64:(h2 + 1) * 64, rho, c * 128:(c + 1) * 128],
                                        rhs=qv[h2 * 64:(h2 + 1) * 64, rho, qb * 128:(qb + 1) * 128], start=True, stop=True),
                                         [bkt[g], bqt[g]], [bbank])
                            for h2 in range(2):
                                bank, bbank = (self.pA[j], self.bpA[j]) if h2 == 0 else (self.pB[j], self.bpB[j])
                                P.op("act", lambda e, h2=h2, bank=bank, x_=x_: e.activation(out=pexp[x_][:, h2 * 256:(h2 + 1) * 256], in_=bank[:, 0:256], func=AF.Exp,
                                                                                           scale=DHD ** -0.5), [bbank], [bpexp[x_]])
                            P.op("dve", lambda e, g=g, hp=hp, x_=x_: e.tensor_tensor(out=pmul[x_], in0=pexp[x_], in1=Eall[:, g, 2 * hp:2 * hp + 2, :].rearrange("p h c -> p (h c)"),
                                                                                    op=ALU.mult), [bpexp[x_], bE], [bpmul[x_]])
                            o = self.io % 3
                            self.io += 1
                            for h2 in range(2):
                                for kc in range(2):
                                    c = qb + kc
                                    P.op("pe", lambda e, h2=h2, kc=kc, c=c, v_=v_, x_=x_, o=o: e.matmul(
                                        self.pO[o][:, h2 * 65:(h2 + 1) * 65], lhsT=pmul[x_][:, (h2 * 2 + kc) * 128:(h2 * 2 + kc + 1) * 128],
                                        rhs=vsub[v_][:, c, h2 * 65:(h2 + 1) * 65], start=(kc == 0), stop=(kc == 1)),
                                         [bpmul[x_], bvsub[v_]], [self.bpO[o]])
                            P.op("act", lambda e, s_=s_, qb=qb, o=o: e.activation(out=stg[s_][:, qb, :], in_=self.pO[o][:, 0:130], func=AF.Copy), [self.bpO[o]], [bstg[s_]])
                        adst = ACC[g, ub:ub + 128 * r * nb, hp * 130:(hp + 1) * 130].rearrange("(b p r) d -> p b r d", p=128, r=r)[:, :, rho, :]
                        P.dma("sp", adst, stg[s_][:, 0:nb, :], [bstg[s_]], [bACC], ("stg", s_))
        self.phase_begin()
        self.carve_common()
        wo_ = self.carve(8 * D, "p (k n) -> p k n", k=8); bwo_ = P.buf()
        ob = self.carve(4 * D, "p (a d) -> p a d", a=4); bob = [P.buf() for _ in range(4)]
        oT = self.carve(8 * 512, "p (k n) -> p k n", k=8); boT = [P.buf() for _ in range(4)]
        acc1 = self.sq[:, 0:VW].rearrange("p (h d) -> p h d", h=DH); bacc1 = self.bsq
        acc2 = self.f32a[:, 0:VW].rearrange("p (h d) -> p h d", h=DH); bacc2 = self.bf32a
        P.dma("pool", wo_, self.dwo.rearrange("(kc p) n -> p kc n", p=128), [], [bwo_], "dwo")
        for u in range(NU):
            ub = u * UNIT
            self.load_unit(u, first)
            for tt in range(4):
                for sub in range(4):
                    t = 4 * tt + sub
                    r0 = ub + t * 128
                    P.dma("sp", acc1, ACC[0, r0:r0 + 128, :].rearrange("p (h d) -> p h d", h=DH), [bACC], [bacc1], "acc1")
                    for g in (1, 2):
                        P.dma("sp", acc2, ACC[g, r0:r0 + 128, :].rearrange("p (h d) -> p h d", h=DH), [bACC], [bacc2], "acc2")
                        P.op("dve", lambda e: e.tensor_tensor(out=acc1, in0=acc1, in1=acc2, op=ALU.add), [bacc1, bacc2], [bacc1])
                    r = self.xk % 2
                    self.xk += 1
                    P.op("dve", lambda e, r=r: e.reciprocal(out=self.rc[r][:], in_=acc1[:, :, DHD]), [bacc1], [self.brc[r]])
                    for h in range(DH):
                        P.op("dve", lambda e, h=h, r=r, sub=sub: e.tensor_scalar(out=ob[:, sub, h * DHD:(h + 1) * DHD], in0=acc1[:, h, 0:DHD],
                                                                                scalar1=self.rc[r][:, h:h + 1], scalar2=None, op0=ALU.mult),
                             [bacc1, self.brc[r]], [bob[sub]])
                    self.transpose8(ob[:, sub, :], bob[sub], oT[:, :, sub * 128:(sub + 1) * 128], [boT[sub]])
                self.out_proj_add(tt, oT, boT, wo_, bwo_)
            self.store_unit(u)

    def out_phase(self, final, first):
        P = self.P
        self.phase_begin()
        y_t = self.y.rearrange("(t p) d -> p t d", p=128)
        if final:
            self.load_gain(G_FINAL)
        for u in range(self.NU):
            self.load_unit(u, first)
            if final:
                for t in range(NT_U):
                    i = self.rms_rstd(self.xh[:, t, :], [self.bx[t]])
                    P.op("dve", lambda e, t=t, i=i: e.scalar_tensor_tensor(out=self.xh[:, t, :], in0=self.xh[:, t, :], scalar=self.ss[i][:], in1=self.gt[:],
                                                                          op0=ALU.mult, op1=ALU.mult), [self.bx[t], self.bss[i], self.bg], [self.bx[t]])
            self.outs.append(P.dma("sp", y_t[:, u * NT_U:(u + 1) * NT_U, :], self.xh[:], list(self.bx), [self.bY[u]], ("y", u % 2)))

    def build(self):
        first = True
        windowed = False
        for ph in self.phases:
            if ph == "final":
                continue
            kind, li = ph.split(":")
            li = int(li)
            if self.T1 and li == 1 and not windowed:
                assert not first, "windowed layer 1 needs a layer-0 phase before it"
                self.enter_window()
                windowed = True
            if kind == "ffn1":
                self.ffn_phase(2 * li, G_FFN1 + li, first)
            elif kind == "ffn2":
                self.ffn_phase(2 * li + 1, G_FFN2 + li, first)
            elif kind == "cross":
                self.cross_phase(li, first)
            elif kind == "mix" and li == 1:
                self.dilated_phase(li, first)
            elif kind == "mix":
                self.gla_phase(li, first)
            else:
                raise ValueError(ph)
            first = False
            self._gather = False
        if self.dbg:
            self._dbg_src = self.dbgt[self.dbg]
        self.out_phase("final" in self.phases, first)
        self.P.emit(self.outs)
        return self.nc


_j = np.arange(128)[:, None]; _i = np.arange(128)[None, :]
GLA_TRI = np.stack([np.where(_j <= _i, -1.0 / 16.0, 0.0), np.where(_j >= _i, -1.0 / 16.0, 0.0),
                    np.where(_j <= _i, 1.0, 0.0), np.where(_j >= _i, 1.0, 0.0)]).astype(np.float32)


def pack_weights(inputs):
    f = lambda k: np.asarray(inputs[k], np.float32)
    gl = [f("norm_ffn1")[0], f("norm_ffn1")[1], f("norm_mix")[0], f("norm_mix")[1], f("norm_cross")[0], f("norm_cross")[1],
          f("norm_mem")[0], f("norm_mem")[1], f("norm_ffn2")[0], f("norm_ffn2")[1], f("norm_final")]
    gains = np.ascontiguousarray(np.broadcast_to(np.stack(gl)[:, None, :], (11, 128, D)))
    p = np.arange(128)[:, None]; q = np.arange(128)[None, :]
    rb = f("rel_bias")
    dbias = np.full((3, DH, 128, 256), -30000.0, np.float32)
    for g, r in enumerate(DIL_R):
        for kc in range(2):
            m = p - q - 64 + 128 * kc
            ok = np.abs(m) <= 64
            bk = t5_bucket(m * r)
            for h in range(DH):
                tile = rb[bk, g * DH + h]
                dbias[g, h, :, kc * 128:(kc + 1) * 128] = np.where(ok, tile, dbias[g, h, :, kc * 128:(kc + 1) * 128])
    return {
        "gains": gains, "ident": np.eye(128, dtype=np.float32),
        "ffn_in": np.ascontiguousarray(np.stack([f("ffn1_in")[0], f("ffn2_in")[0], f("ffn1_in")[1], f("ffn2_in")[1]])),
        "ffn_out": np.ascontiguousarray(np.stack([f("ffn1_out")[0], f("ffn2_out")[0], f("ffn1_out")[1], f("ffn2_out")[1]])),
        "cross_q": f("cross_w_q"), "cross_kv": f("cross_w_kv"), "cross_o": f("cross_w_o"),
        "gla_w_in": np.ascontiguousarray(f("gla_w_in")[0]),
        "gla_wgb": np.ascontiguousarray(np.stack([np.concatenate([f("gla_wg_f")[0], f("gla_bg_f")[0][None, :]], 0),
                                                  np.concatenate([f("gla_wg_b")[0], f("gla_bg_b")[0][None, :]], 0)])),
        "gla_norm": np.ascontiguousarray(np.broadcast_to(f("gla_norm")[0][None, :], (128, D))),
        "gla_w_out": np.ascontiguousarray(f("gla_w_out")[0]),
        "gla_tri": GLA_TRI,
        "dil_qkv": np.ascontiguousarray(f("dil_w_qkv")[0]), "dil_o": np.ascontiguousarray(f("dil_w_out")[0]), "dil_bias": dbias,
    }


def run_seqs(seqs, mems, inputs, T, phases, dbg=None):
    n = 8
    w = pack_weights(inputs)
    nc = K(T, phases, dbg).build()
    in_maps = []
    for c in range(n):
        xs = np.zeros((T, D), np.float32)
        mm = np.zeros((MEM, D), np.float32)
        v8 = np.zeros((T, 8), np.float32)
        if c < len(seqs):
            xs[:seqs[c].shape[0]] = seqs[c]
            mm[:] = mems[c]
            v8[:seqs[c].shape[0]] = 1.0
        in_maps.append(dict(w, x=xs, mem=mm, valid8=v8))
    res = run_bass_kernel_spmd(nc, in_maps, core_ids=list(range(n)))
    return [np.asarray(res.results[c]["y"][:seqs[c].shape[0]], np.float32) for c in range(len(seqs))]


WIN = 4096
T1W = WIN + 2 * PAD


def kernel(**inputs):
    xp = np.asarray(inputs["x_prompt"], np.float32)
    xs = np.asarray(inputs["x_sample"], np.float32)
    mp = np.asarray(inputs["mem_prompt"], np.float32)
    ms = np.asarray(inputs["mem_sample"], np.float32)
    seqs = [xp[b] for b in range(xp.shape[0])] + [xs[b] for b in range(xs.shape[0])]
    mems = [mp[b] for b in range(mp.shape[0])] + [ms[b] for b in range(ms.shape[0])]
    jobs = [(si, w0) for si, sq in enumerate(seqs) for w0 in range(0, sq.shape[0], WIN)]
    n = 8
    assert len(jobs) <= n, len(jobs)
    T0 = -(-max(sq.shape[0] for sq in seqs) // UNIT) * UNIT
    w = pack_weights(inputs)
    nc = K(T0, FULL_PHASES, T1=T1W).build()
    in_maps = []
    for c in range(n):
        x = np.zeros((T0, D), np.float32)
        mm = np.zeros((MEM, D), np.float32)
        v8 = np.zeros((T1W, 8), np.float32)
        idx = np.zeros((T1W,), np.int32)
        if c < len(jobs):
            si, w0 = jobs[c]
            S = seqs[si].shape[0]
            x[:S] = seqs[si]
            mm[:] = mems[si]
            rows = np.arange(w0 - PAD, w0 + WIN + PAD)
            ok = (rows >= 0) & (rows < S)
            v8[ok] = 1.0
            idx[:] = np.clip(rows, 0, S - 1)
        in_maps.append(dict(w, x=x, mem=mm, valid8=v8, win_idx=np.ascontiguousarray(idx.reshape(T1W // 128, 128).T)))
    res = run_bass_kernel_spmd(nc, in_maps, core_ids=list(range(n)))
    outs = [np.zeros(sq.shape, np.float32) for sq in seqs]
    for c, (si, w0) in enumerate(jobs):
        S = seqs[si].shape[0]
        hi = min(w0 + WIN, S)
        outs[si][w0:hi] = np.asarray(res.results[c]["y"][PAD:PAD + (hi - w0)], np.float32)
    yp = np.stack(outs[:xp.shape[0]]).astype(np.float32)
    ys = np.stack(outs[xp.shape[0]:]).astype(np.float32)
    return (yp, ys)
```

```python
from contextlib import ExitStack
import numpy as np
import concourse.bass as bass
import concourse.mybir as mybir

F32 = mybir.dt.float32
BF16 = mybir.dt.bfloat16
AF = mybir.ActivationFunctionType
ALU = mybir.AluOpType
AX = mybir.AxisListType

COMPUTE = ("pe", "act", "dve", "pool")
ENGS = ("pe", "act", "dve", "pool", "sp")


class Buf:
    __slots__ = ("name", "w", "rs", "excl")

    def __init__(self, name, excl=False):
        self.name = name
        self.w = None
        self.rs = []
        self.excl = excl


class Op:
    __slots__ = ("eng", "fn", "dma", "deps", "idx", "sig", "val", "semkey", "dsem", "dval", "pos", "inc")

    def __init__(self, eng, fn, dma, semkey, inc=16):
        self.inc = inc
        self.eng = eng
        self.fn = fn
        self.dma = dma
        self.deps = []
        self.sig = False
        self.val = 0
        self.semkey = semkey
        self.dsem = None
        self.dval = 0


class Prog:
    def __init__(self, nc):
        self.nc = nc
        self.ops = []
        self.es = ExitStack()
        self.last_dma = {}
        self.nbuf = 0
        self.last = {}
        self.pending = []

    def sbuf(self, name, shape, dt):
        return self.es.enter_context(self.nc.sbuf_tensor(name, list(shape), dt))

    def psum(self, name, shape, dt=F32):
        return self.es.enter_context(self.nc.psum_tensor(name, list(shape), dt))

    def dram(self, name, shape, dt, kind="Internal", addr_space="Local"):
        return self.nc.dram_tensor(name, list(shape), dt, kind=kind, addr_space=addr_space)

    def buf(self, name=None, excl=False):
        self.nbuf += 1
        return Buf(name or f"b{self.nbuf}", excl)

    def op(self, eng, fn, reads=(), writes=(), dma=False, semkey=None, inc=16):
        o = Op(eng, fn, dma, semkey, inc)
        deps = []

        def add(p, raw):
            if p is None or p is o:
                return
            if not p.dma and p.eng == eng:
                if not raw or eng == "pe":
                    return
            if p not in deps:
                deps.append(p)

        for b in reads:
            add(b.w, True)
            if b.excl:
                for r in b.rs:
                    if r.eng != eng:
                        add(r, False)
        for b in writes:
            add(b.w, False)
            for r in b.rs:
                add(r, False)
        if dma:
            assert semkey is not None
            add(self.last_dma.get(semkey), False)
            self.last_dma[semkey] = o
        for b in reads:
            if not dma:
                b.rs = [r for r in b.rs if r.dma or r.eng != eng]
            b.rs.append(o)
        for b in writes:
            b.w = o
            b.rs = []
        o.deps = deps
        o.pos = len(self.ops)
        self.ops.append(o)
        if fn is not None:
            self.last[eng] = o
        if dma:
            self.pending.append(o)
        return o

    def barrier(self):
        targets = [p for p in self.last.values()] + list(self.pending)
        self.pending = []
        for eng in ENGS:
            o = Op(eng, None, False, None)
            o.deps = [p for p in dict.fromkeys(targets)]
            o.pos = len(self.ops)
            self.ops.append(o)

    def dma(self, eng, out, in_, reads, writes, semkey, **kw):
        return self.op(eng, lambda e: e.dma_start(out=out, in_=in_, **kw), reads, writes,
                       dma=True, semkey=semkey)

    def emit(self, final_wait_ops=()):
        nc = self.nc
        es = self.es
        fin = self.op("sp", None, reads=(), writes=())
        for p in final_wait_ops:
            if p not in fin.deps:
                fin.deps.append(p)
                p.sig = True
        for o in self.ops:
            for p in o.deps:
                p.sig = True
        esem = {e: es.enter_context(nc.semaphore(f"c_{e}")) for e in COMPUTE}
        dsems = {}
        ecount = {e: 0 for e in COMPUTE}
        dcount = {}
        for o in self.ops:
            if o.dma:
                if o.semkey not in dsems:
                    dsems[o.semkey] = es.enter_context(nc.semaphore(f"d{len(dsems)}"))
                    dcount[o.semkey] = 0
                dcount[o.semkey] += o.inc
                o.dsem = dsems[o.semkey]
                o.dval = dcount[o.semkey]
            elif o.sig:
                assert o.eng in COMPUTE, ("sp non-dma op cannot signal", o.eng)
                ecount[o.eng] += 1
                o.val = ecount[o.eng]
        self.n_dsem = len(dsems)
        per = {e: [] for e in ENGS}
        for o in self.ops:
            per[o.eng].append(o)
        handles = {"pe": "tensor", "act": "scalar", "dve": "vector", "pool": "gpsimd", "sp": "sync"}

        def run(ename, eh):
            known = {}
            for o in per[ename]:
                need = {}
                for p in o.deps:
                    if p.dma:
                        k, s, v = ("d", p.semkey), p.dsem, p.dval
                    else:
                        k, s, v = ("e", p.eng), esem[p.eng], p.val
                    if known.get(k, 0) >= v:
                        continue
                    if k not in need or need[k][1] < v:
                        need[k] = (s, v)
                for k, (s, v) in need.items():
                    eh.wait_ge(s, v)
                    known[k] = v
                if o.fn is None:
                    continue
                ins = o.fn(eh)
                if o.dma:
                    ins.then_inc(o.dsem, o.inc)
                elif o.sig:
                    ins.then_inc(esem[o.eng], 1)

        with nc.Block() as block:
            for ename in ENGS:
                if not per[ename]:
                    continue
                getattr(block, handles[ename])(lambda eh, _n=ename: run(_n, eh))
        es.close()
        return nc


from concourse.bass_utils import run_bass_kernel_spmd

D = 1024
DFF = 2816
NFF = DFF // 128
UNIT = 2048
NT_U = UNIT // 128
MEM = 256
XH = 4
XHD = 256
EPS = 1e-6
NSLOT = 3
ARENA = 40960
GH, GDK, GDV = 4, 128, 256
DIL_R = (1, 4, 16)
DH = 16
DHD = 64
VW = DH * (DHD + 1)
PAD = 1024
NUM_BUCKETS = 32
MAX_DISTANCE = 1024
G_FFN1, G_MIX, G_CROSS, G_MEM, G_FFN2, G_FINAL = 0, 2, 4, 6, 8, 10
FULL_PHASES = ("ffn1:0", "mix:0", "cross:0", "ffn2:0", "ffn1:1", "mix:1", "cross:1", "ffn2:1", "final")


def t5_bucket(rel):
    half = NUM_BUCKETS // 2
    max_exact = half // 2
    ret = (rel > 0).astype(np.int32) * half
    n = np.abs(rel)
    large = max_exact + (np.log(np.maximum(n, 1) / max_exact) / np.log(MAX_DISTANCE / max_exact) * (half - max_exact)).astype(np.int32)
    large = np.minimum(large, half - 1)
    return (ret + np.where(n < max_exact, n, large)).astype(np.int32)


class K:
    def __init__(self, T, phases, dbg=None, T1=None):
        self.T1 = T1
        self.dbg = dbg
        self.dbgt = {}
        assert T % UNIT == 0
        self.T, self.NU, self.phases = T, T // UNIT, tuple(phases)
        self.TP = T + 2 * PAD
        nc = self.nc = bass.Bass("TRN2", target_bir_lowering=False)
        P = self.P = Prog(nc)
        inp = lambda n, s: nc.dram_tensor(n, list(s), F32, kind="ExternalInput").ap()
        self.x_in = inp("x", [T, D])
        self.mem_in = inp("mem", [MEM, D])
        self.gains = inp("gains", [11, 128, D])
        self.ident = inp("ident", [128, 128])
        self.ffn_in = inp("ffn_in", [4, D, 2 * DFF])
        self.ffn_out = inp("ffn_out", [4, DFF, D])
        self.cq = inp("cross_q", [2, D, D])
        self.ckv = inp("cross_kv", [2, D, 2 * D])
        self.co = inp("cross_o", [2, D, D])
        self.dqkv = inp("dil_qkv", [D, 9 * D])
        self.dwo = inp("dil_o", [D, D])
        self.dbias = inp("dil_bias", [3, DH, 128, 256])
        self.gwin = inp("gla_w_in", [D, 3104])
        self.gwgb = inp("gla_wgb", [2, 17, 512])
        self.gnorm = inp("gla_norm", [128, D])
        self.gwout = inp("gla_w_out", [D, D])
        self.gtri = inp("gla_tri", [4, 128, 128])
        TL = T1 or T
        self.valid8 = inp("valid8", [TL, 8])
        self.y = nc.dram_tensor("y", [TL, D], F32, kind="ExternalOutput").ap()
        self.X = P.dram("Xres", [T, D], F32).ap()
        self.bX = [P.buf(f"X{u}") for u in range(self.NU)]
        self.bY = [P.buf(f"Y{u}") for u in range(TL // UNIT)]
        self._gather = False
        if T1:
            self.win_idx = nc.dram_tensor("win_idx", [128, T1 // 128], mybir.dt.int32, kind="ExternalInput").ap()
            self.idxs = P.sbuf("idxs", [128, T1 // 128], mybir.dt.int32); self.bidx = P.buf("idxs")
            P.dma("sp", self.idxs[:], self.win_idx, [], [self.bidx], "idxs")
            self.X1 = P.dram("Xres1", [T1, D], F32).ap()
        self.bXg = P.buf("Xg")
        self.xh = P.sbuf("xh", [128, NT_U, D], F32); self.bx = [P.buf(f"x{t}") for t in range(NT_U)]
        self.xnt = P.sbuf("xnt", [128, 8, UNIT], BF16); self.bxn = [P.buf(f"xn{t}") for t in range(NT_U)]
        self.gt = P.sbuf("gt", [128, D], F32); self.bg = P.buf("g")
        self.idf = P.sbuf("idf", [128, 128], F32); self.bidf = P.buf("idf")
        self.idb = P.sbuf("idb", [128, 128], BF16); self.bidb = P.buf("idb")
        self.sq = P.sbuf("sq", [128, VW], F32); self.bsq = P.buf("sq")
        self.f32a = P.sbuf("f32a", [128, VW], F32); self.bf32a = P.buf("f32a")
        self.sil = [P.sbuf(f"sil{i}", [128, 512], F32) for i in range(2)]; self.bsil = [P.buf(f"sil{i}") for i in range(2)]
        self.ss = [P.sbuf(f"ss{i}", [128, 1], F32) for i in range(2)]; self.bss = [P.buf(f"ss{i}") for i in range(2)]
        self.rc = [P.sbuf(f"rc{i}", [128, 16], F32) for i in range(2)]; self.brc = [P.buf(f"rc{i}") for i in range(2)]
        self.AB = P.sbuf("AB", [128, ARENA], BF16)
        self.pA = [P.psum(f"pA{i}", [128, 512]) for i in range(2)]; self.bpA = [P.buf(f"pA{i}", excl=True) for i in range(2)]
        self.pB = [P.psum(f"pB{i}", [128, 512]) for i in range(2)]; self.bpB = [P.buf(f"pB{i}", excl=True) for i in range(2)]
        self.pO = [P.psum(f"pO{i}", [128, 512]) for i in range(3)]; self.bpO = [P.buf(f"pO{i}", excl=True) for i in range(3)]
        self.pT = P.psum("pT", [128, 8 * 128], BF16); self.bpT = P.buf("pT", excl=True)
        self.io = self.ab = self.wcnt = self.nrm = self.xk = 0
        self.outs = []
        P.dma("sp", self.idf[:], self.ident, [], [self.bidf], "id")
        P.op("act", lambda e: e.activation(out=self.idb[:], in_=self.idf[:], func=AF.Copy), [self.bidf], [self.bidb])

    def phase_begin(self):
        self.P.barrier()
        self.aoff = 0

    def carve(self, n, pat=None, **kw):
        assert self.aoff + n <= ARENA, ("arena overflow", self.aoff, n)
        v = self.AB[:, self.aoff:self.aoff + n]
        self.aoff += n
        return v.rearrange(pat, **kw) if pat else v

    def carve_common(self):
        self.xnb = [self.carve(D) for _ in range(2)]; self.bxnb = [self.P.buf() for _ in range(2)]

    def src(self, first):
        if getattr(self, "_dbg_src", None) is not None:
            return self._dbg_src.rearrange("(t p) d -> p t d", p=128)
        return (self.x_in if first else self.X).rearrange("(t p) d -> p t d", p=128)

    def load_gain(self, row):
        self.P.dma("sp", self.gt[:], self.gains[row], [], [self.bg], "g")

    def enter_window(self):
        self.X0, self.bX0 = self.X, list(self.bX)
        self.T, self.NU, self.TP = self.T1, self.T1 // UNIT, self.T1 + 2 * PAD
        self.X = self.X1
        self.bX = [self.P.buf(f"X1_{u}") for u in range(self.NU)]
        self._gather = True

    def load_unit(self, u, first):
        P = self.P
        if self._gather:
            X0, idxs = self.X0, self.idxs
            for t in range(NT_U):
                k = u * NT_U + t
                P.op("pool", lambda e, t=t, k=k: e.indirect_dma_start(out=self.xh[:, t, :], out_offset=None, in_=X0,
                                                                     in_offset=bass.IndirectOffsetOnAxis(ap=idxs[:, k:k + 1], axis=0)),
                     list(self.bX0) + [self.bidx], [self.bx[t]], dma=True, semkey=("gx", t % 4))
            return
        rd = [] if first else [self.bX[u]]
        for t in range(NT_U):
            P.dma("sp", self.xh[:, t, :], self.src(first)[:, u * NT_U + t, :], rd, [self.bx[t]], ("x", t % 4))

    def store_unit(self, u):
        X_t = self.X.rearrange("(t p) d -> p t d", p=128)
        self.P.dma("sp", X_t[:, u * NT_U:(u + 1) * NT_U, :], self.xh[:], list(self.bx), [self.bX[u]], ("xs", u % 2))

    def rms_rstd(self, src_ap, rdbufs):
        P = self.P
        i = self.nrm % 2
        self.nrm += 1
        P.op("dve", lambda e: e.tensor_tensor(out=self.sq[:, 0:D], in0=src_ap, in1=src_ap, op=ALU.mult), rdbufs, [self.bsq])
        P.op("dve", lambda e: e.reduce_sum(out=self.ss[i][:], in_=self.sq[:, 0:D], axis=AX.X), [self.bsq], [self.bss[i]])
        P.op("act", lambda e: e.activation(out=self.ss[i][:], in_=self.ss[i][:], func=AF.Ln, bias=EPS, scale=1.0 / D), [self.bss[i]], [self.bss[i]])
        P.op("act", lambda e: e.activation(out=self.ss[i][:], in_=self.ss[i][:], func=AF.Exp, scale=-0.5), [self.bss[i]], [self.bss[i]])
        return i

    def norm_T(self, src_ap, rdbufs, dst_ap, dstbufs):
        P = self.P
        i = self.rms_rstd(src_ap, rdbufs)
        xnb_i = self.xnb[i]
        P.op("dve", lambda e: e.scalar_tensor_tensor(out=xnb_i, in0=src_ap, scalar=self.ss[i][:], in1=self.gt[:],
                                                     op0=ALU.mult, op1=ALU.mult), rdbufs + [self.bss[i], self.bg], [self.bxnb[i]])
        self.transpose8(xnb_i, self.bxnb[i], dst_ap, dstbufs)

    def transpose8(self, src_ap, srcbuf, dst_ap, dstbufs):
        P = self.P
        for kc in range(8):
            P.op("pe", lambda e, kc=kc: e.transpose(self.pT[:, kc * 128:(kc + 1) * 128], src_ap[:, kc * 128:(kc + 1) * 128], self.idb[:]),
                 [srcbuf, self.bidb], [self.bpT])
        P.op("act", lambda e: e.activation(out=dst_ap, in_=self.pT[:].rearrange("p (k n) -> p k n", k=8), func=AF.Copy), [self.bpT], dstbufs)

    def norm_unit(self):
        for t in range(NT_U):
            self.norm_T(self.xh[:, t, :], [self.bx[t]], self.xnt[:, :, t * 128:(t + 1) * 128], [self.bxn[t]])

    def out_proj_add(self, tt, oT, boT, w, bw):
        P = self.P
        for sub in range(4):
            t = 4 * tt + sub
            for hf in range(2):
                o = self.io % 3
                self.io += 1
                for fc in range(8):
                    P.op("pe", lambda e, fc=fc, sub=sub, hf=hf, o=o: e.matmul(self.pO[o][:], lhsT=oT[:, fc, sub * 128:(sub + 1) * 128],
                                                                             rhs=w[:, fc, hf * 512:(hf + 1) * 512], start=(fc == 0), stop=(fc == 7)),
                         [boT[sub], bw], [self.bpO[o]])
                P.op("dve", lambda e, t=t, hf=hf, o=o: e.tensor_tensor(out=self.xh[:, t, hf * 512:(hf + 1) * 512], in0=self.xh[:, t, hf * 512:(hf + 1) * 512],
                                                                      in1=self.pO[o][:], op=ALU.add), [self.bpO[o], self.bx[t]], [self.bx[t]])

    def ffn_phase(self, fi, grow, first):
        P = self.P
        FG, NSL = 4, 8
        self.phase_begin()
        self.carve_common()
        wa = [self.carve(1024, "p (k n) -> p k n", k=8) for _ in range(NSL)]; bwa = [P.buf() for _ in range(NSL)]
        wb = [self.carve(1024, "p (k n) -> p k n", k=8) for _ in range(NSL)]; bwb = [P.buf() for _ in range(NSL)]
        wo = [self.carve(D) for _ in range(NSL)]; bwo = [P.buf() for _ in range(NSL)]
        act = [[self.carve(512) for _ in range(FG)] for _ in range(2)]; bact = [[P.buf() for _ in range(FG)] for _ in range(2)]
        w_in_k = self.ffn_in[fi].rearrange("(kc p) n -> p kc n", p=128)
        w_out = self.ffn_out[fi]
        groups = [list(range(c0, min(c0 + FG, NFF))) for c0 in range(0, NFF, FG)]

        def load_w(c):
            s = c % NSL
            P.dma("pool", wa[s], w_in_k[:, :, c * 128:(c + 1) * 128], [], [bwa[s]], ("wa", s))
            P.dma("pool", wb[s], w_in_k[:, :, DFF + c * 128:DFF + (c + 1) * 128], [], [bwb[s]], ("wb", s))
            P.dma("pool", wo[s], w_out[c * 128:(c + 1) * 128, :], [], [bwo[s]], ("wo", s))

        self.load_gain(grow)
        for u in range(self.NU):
            self.load_unit(u, first)
            self.norm_unit()
            for c in groups[0]:
                load_w(c)
            for gi, grp in enumerate(groups):
                if gi + 1 < len(groups):
                    for c in groups[gi + 1]:
                        load_w(c)
                def in_proj(tt):
                    par = tt % 2
                    rd = [self.bxn[4 * tt + q] for q in range(4)]
                    for g, c in enumerate(grp):
                        s = c % NSL
                        j = self.ab % 2
                        self.ab += 1
                        a_g, ba_g = act[par][g], bact[par][g]
                        for kc in range(8):
                            P.op("pe", lambda e, s=s, kc=kc, tt=tt, j=j: e.matmul(self.pA[j][:], lhsT=wa[s][:, kc, :], rhs=self.xnt[:, kc, tt * 512:(tt + 1) * 512],
                                                                                 start=(kc == 0), stop=(kc == 7)), [bwa[s]] + rd, [self.bpA[j]])
                        for kc in range(8):
                            P.op("pe", lambda e, s=s, kc=kc, tt=tt, j=j: e.matmul(self.pB[j][:], lhsT=wb[s][:, kc, :], rhs=self.xnt[:, kc, tt * 512:(tt + 1) * 512],
                                                                                 start=(kc == 0), stop=(kc == 7)), [bwb[s]] + rd, [self.bpB[j]])
                        P.op("act", lambda e, j=j: e.activation(out=self.sil[j][:], in_=self.pA[j][:], func=AF.Silu), [self.bpA[j]], [self.bsil[j]])
                        P.op("dve", lambda e, j=j, a_g=a_g: e.tensor_tensor(out=a_g, in0=self.sil[j][:], in1=self.pB[j][:], op=ALU.mult),
                             [self.bsil[j], self.bpB[j]], [ba_g])

                def out_proj(tt):
                    par = tt % 2
                    for sub in range(4):
                        t = 4 * tt + sub
                        for h in range(2):
                            o = self.io % 3
                            self.io += 1
                            for g, c in enumerate(grp):
                                s = c % NSL
                                a_g, ba_g = act[par][g], bact[par][g]
                                P.op("pe", lambda e, s=s, sub=sub, h=h, o=o, a_g=a_g, g=g, n=len(grp): e.matmul(
                                    self.pO[o][:], lhsT=a_g[:, sub * 128:(sub + 1) * 128], rhs=wo[s][:, h * 512:(h + 1) * 512], start=(g == 0), stop=(g == n - 1)),
                                     [ba_g, bwo[s]], [self.bpO[o]])
                            P.op("dve", lambda e, t=t, h=h, o=o: e.scalar_tensor_tensor(out=self.xh[:, t, h * 512:(h + 1) * 512], in0=self.pO[o][:], scalar=0.5,
                                                                                       in1=self.xh[:, t, h * 512:(h + 1) * 512], op0=ALU.mult, op1=ALU.add),
                                 [self.bpO[o], self.bx[t]], [self.bx[t]])

                ntt = UNIT // 512
                in_proj(0)
                for tt in range(ntt):
                    if tt + 1 < ntt:
                        in_proj(tt + 1)
                    out_proj(tt)
            self.store_unit(u)

    def cross_phase(self, li, first):
        P = self.P
        self.phase_begin()
        self.carve_common()
        wbig = [self.carve(8 * D, "p (k n) -> p k n", k=8) for _ in range(2)]; bwbig = [P.buf() for _ in range(2)]
        memT = self.carve(8 * MEM, "p (k n) -> p k n", k=8); bmemT = P.buf()
        kT = self.carve(8 * MEM, "p (k n) -> p k n", k=8); bkT = P.buf()
        vA = self.carve(2 * XH * (XHD + 1), "p (a h d) -> p a h d", a=2, h=XH); bvA = P.buf()
        qT = self.carve(8 * 512, "p (k n) -> p k n", k=8); bqT = P.buf()
        pTs = [self.carve(512) for _ in range(2)]; bpTs = [P.buf() for _ in range(2)]
        ob = self.carve(4 * D, "p (a d) -> p a d", a=4); bob = [P.buf() for _ in range(4)]
        oT = self.carve(8 * 512, "p (k n) -> p k n", k=8); boT = [P.buf() for _ in range(4)]
        memf = self.f32a[:, 0:D]; bmemf = self.bf32a

        def load_big(i, w_ap):
            P.dma("pool", wbig[i], w_ap.rearrange("(kc p) n -> p kc n", p=128), [], [bwbig[i]], ("wbig", i))

        P.op("dve", lambda e: e.memset(vA, 1.0), [], [bvA])
        self.load_gain(G_MEM + li)
        for kt in range(2):
            P.dma("sp", memf[:], self.mem_in[kt * 128:(kt + 1) * 128, :], [], [bmemf], "memf")
            self.norm_T(memf[:], [bmemf], memT[:, :, kt * 128:(kt + 1) * 128], [bmemT])
        load_big(0, self.ckv[li][:, 0:D])
        for fo in range(8):
            j = self.ab % 2
            self.ab += 1
            for kc in range(8):
                P.op("pe", lambda e, fo=fo, kc=kc, j=j: e.matmul(self.pA[j][:, 0:MEM], lhsT=wbig[0][:, kc, fo * 128:(fo + 1) * 128], rhs=memT[:, kc, :],
                                                                start=(kc == 0), stop=(kc == 7)), [bwbig[0], bmemT], [self.bpA[j]])
            P.op("act", lambda e, fo=fo, j=j: e.activation(out=kT[:, fo, :], in_=self.pA[j][:, 0:MEM], func=AF.Copy), [self.bpA[j]], [bkT])
        load_big(0, self.ckv[li][:, D:2 * D])
        for kt in range(2):
            for hf in range(2):
                j = self.ab % 2
                self.ab += 1
                for kc in range(8):
                    P.op("pe", lambda e, kt=kt, hf=hf, kc=kc, j=j: e.matmul(self.pB[j][:], lhsT=memT[:, kc, kt * 128:(kt + 1) * 128],
                                                                           rhs=wbig[0][:, kc, hf * 512:(hf + 1) * 512], start=(kc == 0), stop=(kc == 7)),
                         [bwbig[0], bmemT], [self.bpB[j]])
                P.op("act", lambda e, kt=kt, hf=hf, j=j: e.activation(out=vA[:, kt, 2 * hf:2 * hf + 2, 0:XHD],
                                                                      in_=self.pB[j][:].rearrange("p (h d) -> p h d", h=2), func=AF.Copy), [self.bpB[j]], [bvA])
        load_big(0, self.cq[li])
        load_big(1, self.co[li])
        self.load_gain(G_CROSS + li)
        for u in range(self.NU):
            self.load_unit(u, first)
            self.norm_unit()
            for tt in range(UNIT // 512):
                rd = [self.bxn[4 * tt + q] for q in range(4)]
                for fo in range(8):
                    j = self.ab % 2
                    self.ab += 1
                    for kc in range(8):
                        P.op("pe", lambda e, fo=fo, kc=kc, tt=tt, j=j: e.matmul(self.pA[j][:], lhsT=wbig[0][:, kc, fo * 128:(fo + 1) * 128],
                                                                               rhs=self.xnt[:, kc, tt * 512:(tt + 1) * 512], start=(kc == 0), stop=(kc == 7)),
                             [bwbig[0]] + rd, [self.bpA[j]])
                    P.op("act", lambda e, fo=fo, j=j: e.activation(out=qT[:, fo, :], in_=self.pA[j][:], func=AF.Copy, scale=XHD ** -0.5),
                         [self.bpA[j]], [bqT])
                for h in range(XH):
                    for kt in range(2):
                        j = self.ab % 2
                        self.ab += 1
                        for dc in range(2):
                            P.op("pe", lambda e, h=h, kt=kt, dc=dc, j=j: e.matmul(self.pB[j][:], lhsT=kT[:, 2 * h + dc, kt * 128:(kt + 1) * 128],
                                                                                 rhs=qT[:, 2 * h + dc, :], start=(dc == 0), stop=(dc == 1)),
                                 [bkT, bqT], [self.bpB[j]])
                        P.op("act", lambda e, kt=kt, j=j: e.activation(out=pTs[kt], in_=self.pB[j][:], func=AF.Exp), [self.bpB[j]], [bpTs[kt]])
                    for sub in range(4):
                        o = self.io % 3
                        self.io += 1
                        r = self.xk % 2
                        self.xk += 1
                        for kt in range(2):
                            P.op("pe", lambda e, h=h, kt=kt, sub=sub, o=o: e.matmul(self.pO[o][:, 0:XHD + 1], lhsT=pTs[kt][:, sub * 128:(sub + 1) * 128],
                                                                                   rhs=vA[:, kt, h, :], start=(kt == 0), stop=(kt == 1)),
                                 [bpTs[kt], bvA], [self.bpO[o]])
                        P.op("dve", lambda e, o=o, r=r: e.reciprocal(out=self.rc[r][:, 0:1], in_=self.pO[o][:, XHD:XHD + 1]), [self.bpO[o]], [self.brc[r]])
                        P.op("dve", lambda e, h=h, sub=sub, o=o, r=r: e.tensor_scalar(out=ob[:, sub, h * XHD:(h + 1) * XHD], in0=self.pO[o][:, 0:XHD],
                                                                                     scalar1=self.rc[r][:, 0:1], scalar2=None, op0=ALU.mult),
                             [self.bpO[o], self.brc[r]], [bob[sub]])
                for sub in range(4):
                    self.transpose8(ob[:, sub, :], bob[sub], oT[:, :, sub * 128:(sub + 1) * 128], [boT[sub]])
                self.out_proj_add(tt, oT, boT, wbig[1], bwbig[1])
            self.store_unit(u)


    def gla_phase(self, li, first):
        P = self.P
        T, NU = self.T, self.NU
        GQT = P.dram("gQT", [GH * GDK, T], BF16).ap(); bGQ = P.buf("gQT")
        GKT = P.dram("gKT", [GH * GDK, T], BF16).ap(); bGK = P.buf("gKT")
        GV = P.dram("gV", [T, D], BF16).ap(); bGV = P.buf("gV")
        GR = P.dram("gR", [T, D], F32).ap(); bGR = P.buf("gR")
        GG = [P.dram(f"gG{d}", [T, 512], F32).ap() for d in range(2)]; bGG = [P.buf(f"gG{d}") for d in range(2)]
        GO = P.dram("gO", [T, D], F32).ap(); bGO = P.buf("gO")
        wk = self.gwin.rearrange("(kc p) n -> p kc n", p=128)
        self.dbgt.update(GO=GO, GR=GR, GG0=GG[0], GG1=GG[1])
        self.phase_begin()
        self.carve_common()
        ws = [self.carve(8 * 512, "p (k n) -> p k n", k=8) for _ in range(NSLOT)]; bws = [P.buf() for _ in range(NSLOT)]
        wz = self.carve(8 * 32, "p (k n) -> p k n", k=8); bwz = P.buf()
        rowb = [self.carve(UNIT) for _ in range(2)]; browb = [P.buf() for _ in range(2)]
        vrow = [self.carve(512) for _ in range(2)]; bvrow = [P.buf() for _ in range(2)]
        zaug = [self.carve(UNIT) for _ in range(2)]; bzaug = [P.buf() for _ in range(2)]
        wgb = [self.carve(512) for _ in range(2)]; bwgb = [P.buf() for _ in range(2)]
        rrow = [self.sil[0], self.sil[1]]; brrow = self.bsil
        grow_ = [self.sq[:, 0:512], self.f32a[:, 0:512]]; bgrow = [self.bsq, self.bf32a]
        P.dma("pool", wz, wk[:, :, 3072:3104], [], [bwz], "wz")
        for d in range(2):
            P.dma("pool", wgb[d][0:17, :], self.gwgb[d], [], [bwgb[d]], ("wgb", d))
            P.op("dve", lambda e, d=d: e.memset(zaug[d][0:32, :], 1.0), [], [bzaug[d]])
        self.load_gain(G_MIX + li)
        wc = rb = vs = 0
        blocks = [("q", 0), ("k", 512), ("v", 1024), ("v", 1536), ("r", 2048), ("r", 2560)]
        for u in range(NU):
            ub = u * UNIT
            self.load_unit(u, first)
            self.norm_unit()

            def load_blk(i, s_):
                P.dma("pool", ws[s_], wk[:, :, blocks[i][1]:blocks[i][1] + 512], [], [bws[s_]], ("ws", s_))

            for i0 in range(NSLOT - 1):
                load_blk(i0, (wc + i0) % NSLOT)
            for bi, (kind, col) in enumerate(blocks):
                s_ = wc % NSLOT
                if bi + NSLOT - 1 < len(blocks):
                    load_blk(bi + NSLOT - 1, (wc + NSLOT - 1) % NSLOT)
                if kind in ("q", "k"):
                    for fl in range(4):
                        r_ = rb % 2
                        rb += 1
                        for tt in range(4):
                            j = self.ab % 2
                            self.ab += 1
                            rd = [self.bxn[4 * tt + q] for q in range(4)]
                            for kc in range(8):
                                P.op("pe", lambda e, s_=s_, kc=kc, fl=fl, tt=tt, j=j: e.matmul(self.pA[j][:], lhsT=ws[s_][:, kc, fl * 128:(fl + 1) * 128],
                                                                                             rhs=self.xnt[:, kc, tt * 512:(tt + 1) * 512], start=(kc == 0), stop=(kc == 7)),
                                     [bws[s_]] + rd, [self.bpA[j]])
                            sc = GDK ** -0.5 if kind == "q" else 1.0
                            P.op("act", lambda e, r_=r_, tt=tt, j=j, sc=sc: e.activation(out=rowb[r_][:, tt * 512:(tt + 1) * 512], in_=self.pA[j][:], func=AF.Copy, scale=sc),
                                 [self.bpA[j]], [browb[r_]])
                        dst, bdst = (GQT, bGQ) if kind == "q" else (GKT, bGK)
                        P.dma("sp", dst[fl * 128:(fl + 1) * 128, ub:ub + UNIT], rowb[r_], [browb[r_]], [bdst], ("rowb", r_))
                else:
                    hf = (col % 1024) // 512
                    for t in range(NT_U):
                        j = self.ab % 2
                        self.ab += 1
                        v_ = vs % 2
                        vs += 1
                        for kc in range(8):
                            P.op("pe", lambda e, s_=s_, kc=kc, t=t, j=j: e.matmul(self.pB[j][:], lhsT=self.xnt[:, kc, t * 128:(t + 1) * 128], rhs=ws[s_][:, kc, :],
                                                                                 start=(kc == 0), stop=(kc == 7)), [bws[s_], self.bxn[t]], [self.bpB[j]])
                        r0 = ub + t * 128
                        if kind == "v":
                            P.op("act", lambda e, v_=v_, j=j: e.activation(out=vrow[v_], in_=self.pB[j][:], func=AF.Copy), [self.bpB[j]], [bvrow[v_]])
                            P.dma("sp", GV[r0:r0 + 128, hf * 512:(hf + 1) * 512], vrow[v_], [bvrow[v_]], [bGV], ("vrow", v_))
                        else:
                            P.op("act", lambda e, v_=v_, j=j: e.activation(out=rrow[v_][:], in_=self.pB[j][:], func=AF.Silu), [self.bpB[j]], [brrow[v_]])
                            P.dma("sp", GR[r0:r0 + 128, hf * 512:(hf + 1) * 512], rrow[v_][:], [brrow[v_]], [bGR], ("rrow", v_))
                wc += 1
            for d in range(2):
                for tt in range(4):
                    j = self.ab % 2
                    self.ab += 1
                    rd = [self.bxn[4 * tt + q] for q in range(4)]
                    for kc in range(8):
                        P.op("pe", lambda e, d=d, kc=kc, tt=tt, j=j: e.matmul(self.pA[j][0:16, :], lhsT=wz[:, kc, d * 16:(d + 1) * 16],
                                                                             rhs=self.xnt[:, kc, tt * 512:(tt + 1) * 512], start=(kc == 0), stop=(kc == 7)),
                             [bwz] + rd, [self.bpA[j]])
                    P.op("act", lambda e, d=d, tt=tt, j=j: e.activation(out=zaug[d][0:16, tt * 512:(tt + 1) * 512], in_=self.pA[j][0:16, :], func=AF.Copy),
                         [self.bpA[j]], [bzaug[d]])
                for t in range(NT_U):
                    j = self.ab % 2
                    self.ab += 1
                    g_ = vs % 2
                    vs += 1
                    P.op("pe", lambda e, d=d, t=t, j=j: e.matmul(self.pB[j][:], lhsT=zaug[d][0:17, t * 128:(t + 1) * 128], rhs=wgb[d][0:17, :], start=True, stop=True),
                         [bzaug[d], bwgb[d]], [self.bpB[j]])
                    P.op("act", lambda e, g_=g_, j=j: e.activation(out=grow_[g_], in_=self.pB[j][:], func=AF.Exp, scale=-1.0), [self.bpB[j]], [bgrow[g_]])
                    P.op("act", lambda e, g_=g_: e.activation(out=grow_[g_], in_=grow_[g_], func=AF.Ln, bias=1.0), [bgrow[g_]], [bgrow[g_]])
                    r0 = ub + t * 128
                    P.dma("sp", GG[d][r0:r0 + 128, :], grow_[g_], [bgrow[g_]], [bGG[d]], ("grow", g_))
        import os as _os
        _stop = int(_os.environ.get("GLA_STOP", "9"))
        for d in range(2):
            if d + 1 >= _stop:
                break
            self.phase_begin()
            qTu = self.carve(GH * UNIT, "p (h t) -> p h t", h=GH); bqTu = P.buf()
            kTu = self.carve(GH * UNIT, "p (h t) -> p h t", h=GH); bkTu = P.buf()
            vu = self.xnt[:].rearrange("p k t -> p (k t)").rearrange("p (c f) -> p c f", c=NT_U); bvu = P.buf()
            qd = [self.carve(128) for _ in range(2)]; bqd = [P.buf() for _ in range(2)]
            kd = [self.carve(128) for _ in range(2)]; bkd = [P.buf() for _ in range(2)]
            at = [self.carve(128) for _ in range(2)]; bat = [P.buf() for _ in range(2)]
            ktok = [self.carve(128) for _ in range(2)]; bktok = [P.buf() for _ in range(2)]
            Sb = self.carve(GH * GDV, "p (h v) -> p h v", h=GH); bSb = P.buf()
            xflat = self.xh[:].rearrange("p a b -> p (a b)")
            gu = xflat[:, 0:NT_U * 512].rearrange("p (c f) -> p c f", c=NT_U); bgu = P.buf()
            ost = [xflat[:, 8192 + i * 1024:8192 + (i + 1) * 1024] for i in range(2)]; bost = [P.buf() for _ in range(2)]
            e1 = [xflat[:, 10240 + i * 128:10240 + (i + 1) * 128] for i in range(2)]; be1 = [P.buf() for _ in range(2)]
            e2 = [xflat[:, 10496 + i * 128:10496 + (i + 1) * 128] for i in range(2)]; be2 = [P.buf() for _ in range(2)]
            eL = [xflat[:, 10752 + i:10753 + i] for i in range(2)]; beL = [P.buf() for _ in range(2)]
            tri = xflat[:, 10880:11008]; msk = xflat[:, 11008:11136]; btri = P.buf()
            gof = [xflat[:, 11264 + i * 1024:11264 + (i + 1) * 1024] for i in range(2)]; bgof = [P.buf() for _ in range(2)]
            grr = [xflat[:, 13312 + i * 1024:13312 + (i + 1) * 1024] for i in range(2)]; bgrr = [P.buf() for _ in range(2)]
            xc = [xflat[:, 15360 + i * 512:15360 + (i + 1) * 512] for i in range(2)]; bxc = [P.buf() for _ in range(2)]
            S = self.f32a[:, 0:GH * GDV].rearrange("p (h v) -> p h v", h=GH); bS = self.bf32a
            gn = self.gt; bgn = self.bg
            P.dma("sp", tri, self.gtri[d], [], [btri], "tri")
            P.dma("sp", msk, self.gtri[2 + d], [], [btri], "msk")
            P.op("dve", lambda e: e.memset(S, 0.0), [], [bS])
            P.op("dve", lambda e: e.memset(Sb, 0.0), [], [bSb])
            _v = int(_os.environ.get("GLA_V", "0"))
            if d == 1 and _v == 0:
                self.carve_common()
                wo_ = self.carve(8 * D, "p (k n) -> p k n", k=8); bwo_ = P.buf()
                ob = self.carve(D); bob = P.buf()
                oT = self.carve(8 * 128, "p (k n) -> p k n", k=8); boT = P.buf()
                if _os.environ.get("GLA_NOWO", "0") != "1":
                    P.dma("pool", wo_, self.gwout.rearrange("(kc p) n -> p kc n", p=128), [], [bwo_], "gwo")
                if _os.environ.get("GLA_NOGN", "0") != "1":
                    P.dma("sp", gn[:], self.gnorm, [], [bgn], "g")
            hc = 0
            order_u = range(NU) if d == 0 else range(NU - 1, -1, -1)
            for u in order_u:
                ub = u * UNIT
                P.dma("sp", qTu, GQT.rearrange("(h p) t -> p h t", p=128)[:, :, ub:ub + UNIT], [bGQ], [bqTu], "qTu")
                P.dma("sp", kTu, GKT.rearrange("(h p) t -> p h t", p=128)[:, :, ub:ub + UNIT], [bGK], [bkTu], "kTu")
                P.dma("sp", vu, GV[ub:ub + UNIT, :].rearrange("(c p) f -> p c f", p=128), [bGV], [bvu], "vu")
                P.dma("sp", gu, GG[d][ub:ub + UNIT, :].rearrange("(c p) f -> p c f", p=128), [bGG[d]], [bgu], "gu")
                order_c = range(NT_U) if d == 0 else range(NT_U - 1, -1, -1)
                for c in order_c:
                    r0 = ub + c * 128
                    o_ = c % 2
                    if d == 1 and _v < 2:
                        P.dma("sp", gof[o_], GO[r0:r0 + 128, :], [bGO], [bgof[o_]], ("gof", o_))
                        P.dma("sp", grr[o_], GR[r0:r0 + 128, :], [bGR], [bgrr[o_]], ("grr", o_))
                    for h in range(GH):
                        i = hc % 2
                        hc += 1
                        j = self.ab % 2
                        self.ab += 1
                        P.op("pe", lambda e, c=c, h=h, j=j: e.matmul(self.pA[j][:, 0:128], lhsT=gu[:, c, h * 128:(h + 1) * 128], rhs=tri, start=True, stop=True),
                             [bgu, btri], [self.bpA[j]])
                        P.op("act", lambda e, i=i, j=j: e.activation(out=e1[i], in_=self.pA[j][:, 0:128], func=AF.Exp), [self.bpA[j]], [be1[i]])
                        P.op("act", lambda e, i=i, j=j: e.activation(out=e2[i], in_=self.pA[j][:, 0:128], func=AF.Exp, scale=-1.0), [self.bpA[j]], [be2[i]])
                        lc = 127 if d == 0 else 0
                        P.op("act", lambda e, i=i, j=j, lc=lc: e.activation(out=eL[i], in_=self.pA[j][:, lc:lc + 1], func=AF.Exp), [self.bpA[j]], [beL[i]])
                        P.op("dve", lambda e, c=c, h=h, i=i: e.tensor_tensor(out=qd[i], in0=qTu[:, h, c * 128:(c + 1) * 128], in1=e1[i], op=ALU.mult), [bqTu, be1[i]], [bqd[i]])
                        P.op("dve", lambda e, c=c, h=h, i=i: e.tensor_tensor(out=kd[i], in0=kTu[:, h, c * 128:(c + 1) * 128], in1=e2[i], op=ALU.mult), [bkTu, be2[i]], [bkd[i]])
                        P.op("pe", lambda e, i=i, j=j: e.matmul(self.pB[j][:, 0:128], lhsT=kd[i], rhs=qd[i], start=True, stop=True), [bkd[i], bqd[i]], [self.bpB[j]])
                        P.op("dve", lambda e, i=i, j=j: e.tensor_tensor(out=at[i], in0=self.pB[j][:, 0:128], in1=msk, op=ALU.mult), [self.bpB[j], btri], [bat[i]])
                        P.op("pe", lambda e, i=i: e.transpose(self.pT[:, 0:128], kd[i], self.idb[:]), [bkd[i], self.bidb], [self.bpT])
                        P.op("act", lambda e, i=i: e.activation(out=ktok[i], in_=self.pT[:, 0:128], func=AF.Copy), [self.bpT], [bktok[i]])
                        o = self.io % 3
                        self.io += 1
                        P.op("pe", lambda e, c=c, h=h, i=i, o=o: e.matmul(self.pO[o][:, 0:GDV], lhsT=at[i], rhs=vu[:, c, h * GDV:(h + 1) * GDV], start=True, stop=False),
                             [bat[i], bvu], [self.bpO[o]])
                        P.op("pe", lambda e, h=h, i=i, o=o: e.matmul(self.pO[o][:, 0:GDV], lhsT=qd[i], rhs=Sb[:, h, :], start=False, stop=True),
                             [bqd[i], bSb], [self.bpO[o]])
                        if d == 0 or _v >= 2:
                            P.op("act", lambda e, h=h, o=o, o_=o_: e.activation(out=ost[o_][:, h * GDV:(h + 1) * GDV], in_=self.pO[o][:, 0:GDV], func=AF.Copy),
                                 [self.bpO[o]], [bost[o_]])
                        else:
                            P.op("dve", lambda e, h=h, o=o, o_=o_: e.tensor_tensor(out=ost[o_][:, h * GDV:(h + 1) * GDV], in0=self.pO[o][:, 0:GDV],
                                                                                  in1=gof[o_][:, h * GDV:(h + 1) * GDV], op=ALU.add), [self.bpO[o], bgof[o_]], [bost[o_]])
                        o2 = self.io % 3
                        self.io += 1
                        P.op("pe", lambda e, c=c, h=h, i=i, o2=o2: e.matmul(self.pO[o2][:, 0:GDV], lhsT=ktok[i], rhs=vu[:, c, h * GDV:(h + 1) * GDV], start=True, stop=True),
                             [bktok[i], bvu], [self.bpO[o2]])
                        P.op("dve", lambda e, h=h, o2=o2: e.tensor_tensor(out=S[:, h, :], in0=S[:, h, :], in1=self.pO[o2][:, 0:GDV], op=ALU.add), [bS, self.bpO[o2]], [bS])
                        P.op("dve", lambda e, h=h, i=i: e.tensor_scalar(out=S[:, h, :], in0=S[:, h, :], scalar1=eL[i], scalar2=None, op0=ALU.mult), [bS, beL[i]], [bS])
                        P.op("act", lambda e, h=h: e.activation(out=Sb[:, h, :], in_=S[:, h, :], func=AF.Copy), [bS], [bSb])
                    if d == 0:
                        P.dma("sp", GO[r0:r0 + 128, :], ost[o_], [bost[o_]], [bGO], ("ost", o_))
                    elif _os.environ.get("GLA_EPI", "1") == "0":
                        pass
                    else:
                        r_ = self.xk % 2
                        self.xk += 1
                        for h in range(GH):
                            P.op("dve", lambda e, h=h, o_=o_: e.tensor_tensor(out=self.sq[:, 0:GDV], in0=ost[o_][:, h * GDV:(h + 1) * GDV], in1=ost[o_][:, h * GDV:(h + 1) * GDV], op=ALU.mult),
                                 [bost[o_]], [self.bsq])
                            P.op("dve", lambda e, h=h, r_=r_: e.reduce_sum(out=self.rc[r_][:, h:h + 1], in_=self.sq[:, 0:GDV], axis=AX.X), [self.bsq], [self.brc[r_]])
                        P.op("act", lambda e, r_=r_: e.activation(out=self.rc[r_][:, 0:GH], in_=self.rc[r_][:, 0:GH], func=AF.Ln, bias=EPS, scale=1.0 / GDV), [self.brc[r_]], [self.brc[r_]])
                        P.op("act", lambda e, r_=r_: e.activation(out=self.rc[r_][:, 0:GH], in_=self.rc[r_][:, 0:GH], func=AF.Exp, scale=-0.5), [self.brc[r_]], [self.brc[r_]])
                        for h in range(GH):
                            P.op("dve", lambda e, h=h, o_=o_, r_=r_: e.scalar_tensor_tensor(out=ost[o_][:, h * GDV:(h + 1) * GDV], in0=ost[o_][:, h * GDV:(h + 1) * GDV],
                                                                                           scalar=self.rc[r_][:, h:h + 1], in1=gn[:, h * GDV:(h + 1) * GDV], op0=ALU.mult, op1=ALU.mult),
                                 [bost[o_], self.brc[r_], bgn], [bost[o_]])
                        P.op("dve", lambda e, o_=o_: e.tensor_tensor(out=ob, in0=ost[o_], in1=grr[o_], op=ALU.mult), [bost[o_], bgrr[o_]], [bob])
                        self.transpose8(ob, bob, oT, [boT])
                        for hf in range(2):
                            o = self.io % 3
                            self.io += 1
                            src_t = (self.x_in if first else self.X)
                            P.dma("sp", xc[hf], src_t[r0:r0 + 128, hf * 512:(hf + 1) * 512], [] if first else [self.bX[u]], [bxc[hf]], ("xc", hf))
                            for fc in range(8):
                                P.op("pe", lambda e, fc=fc, hf=hf, o=o: e.matmul(self.pO[o][:], lhsT=oT[:, fc, :], rhs=wo_[:, fc, hf * 512:(hf + 1) * 512],
                                                                                start=(fc == 0), stop=(fc == 7)), [boT, bwo_], [self.bpO[o]])
                            P.op("dve", lambda e, hf=hf, o=o: e.tensor_tensor(out=xc[hf], in0=xc[hf], in1=self.pO[o][:], op=ALU.add), [self.bpO[o], bxc[hf]], [bxc[hf]])
                            P.dma("sp", self.X[r0:r0 + 128, hf * 512:(hf + 1) * 512], xc[hf], [bxc[hf]], [self.bX[u]], ("xcs", hf))

    def dilated_phase(self, li, first):
        P = self.P
        T, TP, NU = self.T, self.TP, self.NU
        QT = P.dram("dQT", [3, D, T], BF16).ap(); bQT = P.buf("dQT")
        KT = P.dram("dKT", [3, D, TP], BF16).ap(); bKT = P.buf("dKT")
        VA = P.dram("dVA", [3, TP, VW], BF16).ap(); bVA = P.buf("dVA")
        ACC = P.dram("dACC", [3, T, VW], F32).ap(); bACC = P.buf("dACC")
        self.phase_begin()
        self.carve_common()
        ws = [self.carve(8 * 512, "p (k n) -> p k n", k=8) for _ in range(NSLOT)]; bws = [P.buf() for _ in range(NSLOT)]
        rowb = [self.carve(UNIT) for _ in range(2)]; browb = [P.buf() for _ in range(2)]
        vst = [self.carve(8 * (DHD + 1), "p (h d) -> p h d", h=8) for _ in range(2)]; bvst = [P.buf() for _ in range(2)]
        zt = self.carve(VW); bzt = P.buf()
        v8 = self.f32a[:, 0:NT_U * 8].rearrange("p (t e) -> p t e", t=NT_U); bv8 = self.bf32a
        P.op("dve", lambda e: e.memset(zt, 0.0), [], [bzt])
        zk = 0
        for g in range(3):
            for side in (0, PAD + T):
                for fo in range(8):
                    P.dma("sp", KT[g, fo * 128:(fo + 1) * 128, side:side + PAD], zt[:, 0:PAD], [bzt], [bKT], ("z", zk % 4)); zk += 1
                for j in range(PAD // 128):
                    P.dma("sp", VA[g, side + j * 128:side + (j + 1) * 128, :], zt, [bzt], [bVA], ("z", zk % 4)); zk += 1
        self.load_gain(G_MIX + li)
        wq_k = self.dqkv.rearrange("(kc p) n -> p kc n", p=128)
        wc = rb = vs = 0
        for u in range(NU):
            ub = u * UNIT
            self.load_unit(u, first)
            self.norm_unit()
            P.dma("sp", v8, self.valid8[ub:ub + UNIT, :].rearrange("(t p) e -> p t e", p=128), [], [bv8], "v8")
            blocks = [(c3, g, hf) for c3 in range(3) for g in range(3) for hf in range(2)]

            def load_blk(i, s):
                c3, g, hf = blocks[i]
                col = c3 * 3 * D + g * D + hf * 512
                P.dma("pool", ws[s], wq_k[:, :, col:col + 512], [], [bws[s]], ("ws", s))

            for i0 in range(NSLOT - 1):
                load_blk(i0, (wc + i0) % NSLOT)
            for bi, (c3, g, hf) in enumerate(blocks):
                s = wc % NSLOT
                if bi + NSLOT - 1 < len(blocks):
                    load_blk(bi + NSLOT - 1, (wc + NSLOT - 1) % NSLOT)
                if c3 < 2:
                    for fl in range(4):
                        r_ = rb % 2
                        rb += 1
                        for tt in range(4):
                            j = self.ab % 2
                            self.ab += 1
                            rd = [self.bxn[4 * tt + q] for q in range(4)]
                            for kc in range(8):
                                P.op("pe", lambda e, s=s, kc=kc, fl=fl, tt=tt, j=j: e.matmul(self.pA[j][:], lhsT=ws[s][:, kc, fl * 128:(fl + 1) * 128],
                                                                                            rhs=self.xnt[:, kc, tt * 512:(tt + 1) * 512], start=(kc == 0), stop=(kc == 7)),
                                     [bws[s]] + rd, [self.bpA[j]])
                            eng = "act" if tt % 2 == 0 else "dve"
                            if eng == "act":
                                P.op("act", lambda e, r_=r_, tt=tt, j=j: e.activation(out=rowb[r_][:, tt * 512:(tt + 1) * 512], in_=self.pA[j][:], func=AF.Copy),
                                     [self.bpA[j]], [browb[r_]])
                            else:
                                P.op("dve", lambda e, r_=r_, tt=tt, j=j: e.tensor_copy(out=rowb[r_][:, tt * 512:(tt + 1) * 512], in_=self.pA[j][:]),
                                     [self.bpA[j]], [browb[r_]])
                        fr = (hf * 4 + fl) * 128
                        if c3 == 0:
                            P.dma("sp", QT[g, fr:fr + 128, ub:ub + UNIT], rowb[r_], [browb[r_]], [bQT], ("rowb", r_))
                        else:
                            P.dma("sp", KT[g, fr:fr + 128, PAD + ub:PAD + ub + UNIT], rowb[r_], [browb[r_]], [bKT], ("rowb", r_))
                else:
                    for t in range(NT_U):
                        j = self.ab % 2
                        self.ab += 1
                        v_ = vs % 2
                        vs += 1
                        for kc in range(8):
                            P.op("pe", lambda e, s=s, kc=kc, t=t, j=j: e.matmul(self.pB[j][:], lhsT=self.xnt[:, kc, t * 128:(t + 1) * 128], rhs=ws[s][:, kc, :],
                                                                               start=(kc == 0), stop=(kc == 7)), [bws[s], self.bxn[t]], [self.bpB[j]])
                        P.op("act", lambda e, v_=v_, j=j, t=t: e.activation(out=vst[v_][:, :, 0:DHD], in_=self.pB[j][:].rearrange("p (h d) -> p h d", h=8), func=AF.Copy,
                                                                         scale=v8[:, t, 0:1]), [self.bpB[j], bv8], [bvst[v_]])
                        P.op("dve", lambda e, v_=v_, t=t: e.tensor_copy(out=vst[v_][:, :, DHD], in_=v8[:, t, :]), [bv8], [bvst[v_]])
                        P.dma("sp", VA[g, PAD + ub + t * 128:PAD + ub + (t + 1) * 128, hf * 520:(hf + 1) * 520], vst[v_].rearrange("p h d -> p (h d)"),
                              [bvst[v_]], [bVA], ("vst", v_))
                wc += 1
        self.phase_begin()
        Eall = self.carve(3 * DH * 256, "p (g h c) -> p g h c", g=3, h=DH); bE = P.buf()
        qt = [self.carve(UNIT) for _ in range(3)]; bqt = [P.buf() for _ in range(3)]
        kw = [UNIT + 128 * r for r in DIL_R]
        kt = [self.carve(kw[g]) for g in range(3)]; bkt = [P.buf() for _ in range(3)]
        vsub = [self.carve(32 * 130, "p (c d) -> p c d", c=32) for _ in range(2)]; bvsub = [P.buf() for _ in range(2)]
        pexp = [self.carve(512) for _ in range(2)]; bpexp = [P.buf() for _ in range(2)]
        pmul = [self.carve(512) for _ in range(2)]; bpmul = [P.buf() for _ in range(2)]
        xflat = self.xh[:].rearrange("p a b -> p (a b)")
        stg = [xflat[:, i * 2080:(i + 1) * 2080].rearrange("p (b d) -> p b d", b=16) for i in range(2)]; bstg = [P.buf() for _ in range(2)]
        ebuf = self.f32a[:, 0:256]
        for g in range(3):
            for h in range(DH):
                P.dma("sp", ebuf, self.dbias[g, h], [], [self.bf32a], "eb")
                P.op("act", lambda e, g=g, h=h: e.activation(out=Eall[:, g, h, :], in_=ebuf, func=AF.Exp), [self.bf32a], [bE])
        vi = pe_ = si = 0
        for u in range(NU):
            ub = u * UNIT
            for hp in range(8):
                for g, r in enumerate(DIL_R):
                    P.dma("sp", qt[g], QT[g, hp * 128:(hp + 1) * 128, ub:ub + UNIT], [bQT], [bqt[g]], ("qt", g))
                    lo = PAD + ub - 64 * r
                    P.dma("sp", kt[g], KT[g, hp * 128:(hp + 1) * 128, lo:lo + kw[g]], [bKT], [bkt[g]], ("kt", g))
                    qv = qt[g].rearrange("p (n r) -> p r n", r=r)
                    kv = kt[g].rearrange("p (n r) -> p r n", r=r)
                    nb = 16 // r
                    nch = nb + 1
                    for rho in range(r):
                        v_ = vi % 2
                        vi += 1
                        s_ = si % 2
                        si += 1
                        base = PAD + ub - 64 * r
                        vsrc = VA[g, base:base + 128 * r * nch, hp * 130:(hp + 1) * 130].rearrange("(c p r) d -> p c r d", p=128, r=r)[:, :, rho, :]
                        P.dma("sp", vsub[v_][:, 0:nch, :], vsrc, [bVA], [bvsub[v_]], ("vsub", v_))
                        for qb in range(nb):
                            j = self.ab % 2
                            self.ab += 1
                            x_ = pe_ % 2
                            pe_ += 1
                            for h2 in range(2):
                                bank, bbank = (self.pA[j], self.bpA[j]) if h2 == 0 else (self.pB[j], self.bpB[j])
                                for kc in range(2):
                                    c = qb + kc
                                    P.op("pe", lambda e, h2=h2, kc=kc, c=c, qb=qb, rho=rho, kv=kv, qv=qv, bank=bank: e.matmul(
                                        bank[:, kc * 128:(kc + 1) * 128],
                                        lhsT=kv[h2 * 64:(h2 + 1) * 64, rho, c * 128:(c + 1) * 128],
                                        rhs=qv[h2 * 64:(h2 + 1) * 64, rho, qb * 128:(qb + 1) * 128], start=True, stop=True),
                                         [bkt[g], bqt[g]], [bbank])
                            for h2 in range(2):
                                bank, bbank = (self.pA[j], self.bpA[j]) if h2 == 0 else (self.pB[j], self.bpB[j])
                                P.op("act", lambda e, h2=h2, bank=bank, x_=x_: e.activation(out=pexp[x_][:, h2 * 256:(h2 + 1) * 256], in_=bank[:, 0:256], func=AF.Exp,
                                                                                           scale=DHD ** -0.5), [bbank], [bpexp[x_]])
                            P.op("dve", lambda e, g=g, hp=hp, x_=x_: e.tensor_tensor(out=pmul[x_], in0=pexp[x_], in1=Eall[:, g, 2 * hp:2 * hp + 2, :].rearrange("p h c -> p (h c)"),
                                                                                    op=ALU.mult), [bpexp[x_], bE], [bpmul[x_]])
                            o = self.io % 3
                            self.io += 1
                            for h2 in range(2):
                                for kc in range(2):
                                    c = qb + kc
                                    P.op("pe", lambda e, h2=h2, kc=kc, c=c, v_=v_, x_=x_, o=o: e.matmul(
                                        self.pO[o][:, h2 * 65:(h2 + 1) * 65], lhsT=pmul[x_][:, (h2 * 2 + kc) * 128:(h2 * 2 + kc + 1) * 128],
                                        rhs=vsub[v_][:, c, h2 * 65:(h2 + 1) * 65], start=(kc == 0), stop=(kc == 1)),
                                         [bpmul[x_], bvsub[v_]], [self.bpO[o]])
                            P.op("act", lambda e, s_=s_, qb=qb, o=o: e.activation(out=stg[s_][:, qb, :], in_=self.pO[o][:, 0:130], func=AF.Copy), [self.bpO[o]], [bstg[s_]])
                        adst = ACC[g, ub:ub + 128 * r * nb, hp * 130:(hp + 1) * 130].rearrange("(b p r) d -> p b r d", p=128, r=r)[:, :, rho, :]
                        P.dma("sp", adst, stg[s_][:, 0:nb, :], [bstg[s_]], [bACC], ("stg", s_))
        self.phase_begin()
        self.carve_common()
        wo_ = self.carve(8 * D, "p (k n) -> p k n", k=8); bwo_ = P.buf()
        ob = self.carve(4 * D, "p (a d) -> p a d", a=4); bob = [P.buf() for _ in range(4)]
        oT = self.carve(8 * 512, "p (k n) -> p k n", k=8); boT = [P.buf() for _ in range(4)]
        acc1 = self.sq[:, 0:VW].rearrange("p (h d) -> p h d", h=DH); bacc1 = self.bsq
        acc2 = self.f32a[:, 0:VW].rearrange("p (h d) -> p h d", h=DH); bacc2 = self.bf32a
        P.dma("pool", wo_, self.dwo.rearrange("(kc p) n -> p kc n", p=128), [], [bwo_], "dwo")
        for u in range(NU):
            ub = u * UNIT
            self.load_unit(u, first)
            for tt in range(4):
                for sub in range(4):
                    t = 4 * tt + sub
                    r0 = ub + t * 128
                    P.dma("sp", acc1, ACC[0, r0:r0 + 128, :].rearrange("p (h d) -> p h d", h=DH), [bACC], [bacc1], "acc1")
                    for g in (1, 2):
                        P.dma("sp", acc2, ACC[g, r0:r0 + 128, :].rearrange("p (h d) -> p h d", h=DH), [bACC], [bacc2], "acc2")
                        P.op("dve", lambda e: e.tensor_tensor(out=acc1, in0=acc1, in1=acc2, op=ALU.add), [bacc1, bacc2], [bacc1])
                    r = self.xk % 2
                    self.xk += 1
                    P.op("dve", lambda e, r=r: e.reciprocal(out=self.rc[r][:], in_=acc1[:, :, DHD]), [bacc1], [self.brc[r]])
                    for h in range(DH):
                        P.op("dve", lambda e, h=h, r=r, sub=sub: e.tensor_scalar(out=ob[:, sub, h * DHD:(h + 1) * DHD], in0=acc1[:, h, 0:DHD],
                                                                                scalar1=self.rc[r][:, h:h + 1], scalar2=None, op0=ALU.mult),
                             [bacc1, self.brc[r]], [bob[sub]])
                    self.transpose8(ob[:, sub, :], bob[sub], oT[:, :, sub * 128:(sub + 1) * 128], [boT[sub]])
                self.out_proj_add(tt, oT, boT, wo_, bwo_)
            self.store_unit(u)

    def out_phase(self, final, first):
        P = self.P
        self.phase_begin()
        y_t = self.y.rearrange("(t p) d -> p t d", p=128)
        if final:
            self.load_gain(G_FINAL)
        for u in range(self.NU):
            self.load_unit(u, first)
            if final:
                for t in range(NT_U):
                    i = self.rms_rstd(self.xh[:, t, :], [self.bx[t]])
                    P.op("dve", lambda e, t=t, i=i: e.scalar_tensor_tensor(out=self.xh[:, t, :], in0=self.xh[:, t, :], scalar=self.ss[i][:], in1=self.gt[:],
                                                                          op0=ALU.mult, op1=ALU.mult), [self.bx[t], self.bss[i], self.bg], [self.bx[t]])
            self.outs.append(P.dma("sp", y_t[:, u * NT_U:(u + 1) * NT_U, :], self.xh[:], list(self.bx), [self.bY[u]], ("y", u % 2)))

    def build(self):
        first = True
        windowed = False
        for ph in self.phases:
            if ph == "final":
                continue
            kind, li = ph.split(":")
            li = int(li)
            if self.T1 and not windowed and (li == 1 or kind in ("cross", "ffn2")):
                assert not first, "the window needs a whole-sequence phase before it"
                self.enter_window()
                windowed = True
            if kind == "ffn1":
                self.ffn_phase(2 * li, G_FFN1 + li, first)
            elif kind == "ffn2":
                self.ffn_phase(2 * li + 1, G_FFN2 + li, first)
            elif kind == "cross":
                self.cross_phase(li, first)
            elif kind == "mix" and li == 1:
                self.dilated_phase(li, first)
            elif kind == "mix":
                self.gla_phase(li, first)
            else:
                raise ValueError(ph)
            first = False
            self._gather = False
        if self.dbg:
            self._dbg_src = self.dbgt[self.dbg]
        self.out_phase("final" in self.phases, first)
        self.P.emit(self.outs)
        return self.nc


_j = np.arange(128)[:, None]; _i = np.arange(128)[None, :]
GLA_TRI = np.stack([np.where(_j <= _i, -1.0 / 16.0, 0.0), np.where(_j >= _i, -1.0 / 16.0, 0.0),
                    np.where(_j <= _i, 1.0, 0.0), np.where(_j >= _i, 1.0, 0.0)]).astype(np.float32)


def pack_weights(inputs):
    f = lambda k: np.asarray(inputs[k], np.float32)
    gl = [f("norm_ffn1")[0], f("norm_ffn1")[1], f("norm_mix")[0], f("norm_mix")[1], f("norm_cross")[0], f("norm_cross")[1],
          f("norm_mem")[0], f("norm_mem")[1], f("norm_ffn2")[0], f("norm_ffn2")[1], f("norm_final")]
    gains = np.ascontiguousarray(np.broadcast_to(np.stack(gl)[:, None, :], (11, 128, D)))
    p = np.arange(128)[:, None]; q = np.arange(128)[None, :]
    rb = f("rel_bias")
    dbias = np.full((3, DH, 128, 256), -30000.0, np.float32)
    for g, r in enumerate(DIL_R):
        for kc in range(2):
            m = p - q - 64 + 128 * kc
            ok = np.abs(m) <= 64
            bk = t5_bucket(m * r)
            for h in range(DH):
                tile = rb[bk, g * DH + h]
                dbias[g, h, :, kc * 128:(kc + 1) * 128] = np.where(ok, tile, dbias[g, h, :, kc * 128:(kc + 1) * 128])
    return {
        "gains": gains, "ident": np.eye(128, dtype=np.float32),
        "ffn_in": np.ascontiguousarray(np.stack([f("ffn1_in")[0], f("ffn2_in")[0], f("ffn1_in")[1], f("ffn2_in")[1]])),
        "ffn_out": np.ascontiguousarray(np.stack([f("ffn1_out")[0], f("ffn2_out")[0], f("ffn1_out")[1], f("ffn2_out")[1]])),
        "cross_q": f("cross_w_q"), "cross_kv": f("cross_w_kv"), "cross_o": f("cross_w_o"),
        "gla_w_in": np.ascontiguousarray(f("gla_w_in")[0]),
        "gla_wgb": np.ascontiguousarray(np.stack([np.concatenate([f("gla_wg_f")[0], f("gla_bg_f")[0][None, :]], 0),
                                                  np.concatenate([f("gla_wg_b")[0], f("gla_bg_b")[0][None, :]], 0)])),
        "gla_norm": np.ascontiguousarray(np.broadcast_to(f("gla_norm")[0][None, :], (128, D))),
        "gla_w_out": np.ascontiguousarray(f("gla_w_out")[0]),
        "gla_tri": GLA_TRI,
        "dil_qkv": np.ascontiguousarray(f("dil_w_qkv")[0]), "dil_o": np.ascontiguousarray(f("dil_w_out")[0]), "dil_bias": dbias,
    }


def run_seqs(seqs, mems, inputs, T, phases, dbg=None):
    n = 8
    w = pack_weights(inputs)
    nc = K(T, phases, dbg).build()
    in_maps = []
    for c in range(n):
        xs = np.zeros((T, D), np.float32)
        mm = np.zeros((MEM, D), np.float32)
        v8 = np.zeros((T, 8), np.float32)
        if c < len(seqs):
            xs[:seqs[c].shape[0]] = seqs[c]
            mm[:] = mems[c]
            v8[:seqs[c].shape[0]] = 1.0
        in_maps.append(dict(w, x=xs, mem=mm, valid8=v8))
    res = run_bass_kernel_spmd(nc, in_maps, core_ids=list(range(n)))
    return [np.asarray(res.results[c]["y"][:seqs[c].shape[0]], np.float32) for c in range(len(seqs))]


WIN = 4096
T1W = WIN + 2 * PAD


def kernel(**inputs):
    xp = np.asarray(inputs["x_prompt"], np.float32)
    xs = np.asarray(inputs["x_sample"], np.float32)
    mp = np.asarray(inputs["mem_prompt"], np.float32)
    ms = np.asarray(inputs["mem_sample"], np.float32)
    seqs = [xp[b] for b in range(xp.shape[0])] + [xs[b] for b in range(xs.shape[0])]
    mems = [mp[b] for b in range(mp.shape[0])] + [ms[b] for b in range(ms.shape[0])]
    jobs = [(si, w0) for si, sq in enumerate(seqs) for w0 in range(0, sq.shape[0], WIN)]
    n = 8
    assert len(jobs) <= n, len(jobs)
    T0 = -(-max(sq.shape[0] for sq in seqs) // UNIT) * UNIT
    w = pack_weights(inputs)
    nc = K(T0, FULL_PHASES, T1=T1W).build()
    in_maps = []
    for c in range(n):
        x = np.zeros((T0, D), np.float32)
        mm = np.zeros((MEM, D), np.float32)
        v8 = np.zeros((T1W, 8), np.float32)
        idx = np.zeros((T1W,), np.int32)
        if c < len(jobs):
            si, w0 = jobs[c]
            S = seqs[si].shape[0]
            x[:S] = seqs[si]
            mm[:] = mems[si]
            rows = np.arange(w0 - PAD, w0 + WIN + PAD)
            ok = (rows >= 0) & (rows < S)
            v8[ok] = 1.0
            idx[:] = np.clip(rows, 0, S - 1)
        in_maps.append(dict(w, x=x, mem=mm, valid8=v8, win_idx=np.ascontiguousarray(idx.reshape(T1W // 128, 128).T)))
    res = run_bass_kernel_spmd(nc, in_maps, core_ids=list(range(n)))
    outs = [np.zeros(sq.shape, np.float32) for sq in seqs]
    for c, (si, w0) in enumerate(jobs):
        S = seqs[si].shape[0]
        hi = min(w0 + WIN, S)
        outs[si][w0:hi] = np.asarray(res.results[c]["y"][PAD:PAD + (hi - w0)], np.float32)
    yp = np.stack(outs[:xp.shape[0]]).astype(np.float32)
    ys = np.stack(outs[xp.shape[0]:]).astype(np.float32)
    return (yp, ys)
```

```python
from contextlib import ExitStack
import numpy as np
import concourse.bass as bass
import concourse.mybir as mybir

F32 = mybir.dt.float32
BF16 = mybir.dt.bfloat16
AF = mybir.ActivationFunctionType
ALU = mybir.AluOpType
AX = mybir.AxisListType

COMPUTE = ("pe", "act", "dve", "pool")
ENGS = ("pe", "act", "dve", "pool", "sp")


class Buf:
    __slots__ = ("name", "w", "rs", "excl")

    def __init__(self, name, excl=False):
        self.name = name
        self.w = None
        self.rs = []
        self.excl = excl


class Op:
    __slots__ = ("eng", "fn", "dma", "deps", "idx", "sig", "val", "semkey", "dsem", "dval", "pos", "inc")

    def __init__(self, eng, fn, dma, semkey, inc=16):
        self.inc = inc
        self.eng = eng
        self.fn = fn
        self.dma = dma
        self.deps = []
        self.sig = False
        self.val = 0
        self.semkey = semkey
        self.dsem = None
        self.dval = 0


class Prog:
    def __init__(self, nc):
        self.nc = nc
        self.ops = []
        self.es = ExitStack()
        self.last_dma = {}
        self.nbuf = 0
        self.last = {}
        self.pending = []

    def sbuf(self, name, shape, dt):
        return self.es.enter_context(self.nc.sbuf_tensor(name, list(shape), dt))

    def psum(self, name, shape, dt=F32):
        return self.es.enter_context(self.nc.psum_tensor(name, list(shape), dt))

    def dram(self, name, shape, dt, kind="Internal", addr_space="Local"):
        return self.nc.dram_tensor(name, list(shape), dt, kind=kind, addr_space=addr_space)

    def buf(self, name=None, excl=False):
        self.nbuf += 1
        return Buf(name or f"b{self.nbuf}", excl)

    def op(self, eng, fn, reads=(), writes=(), dma=False, semkey=None, inc=16):
        o = Op(eng, fn, dma, semkey, inc)
        deps = []

        def add(p, raw):
            if p is None or p is o:
                return
            if not p.dma and p.eng == eng:
                if not raw or eng == "pe":
                    return
            if p not in deps:
                deps.append(p)

        for b in reads:
            add(b.w, True)
            if b.excl:
                for r in b.rs:
                    if r.eng != eng:
                        add(r, False)
        for b in writes:
            add(b.w, False)
            for r in b.rs:
                add(r, False)
        if dma:
            assert semkey is not None
            add(self.last_dma.get(semkey), False)
            self.last_dma[semkey] = o
        for b in reads:
            if not dma:
                b.rs = [r for r in b.rs if r.dma or r.eng != eng]
            b.rs.append(o)
        for b in writes:
            b.w = o
            b.rs = []
        o.deps = deps
        o.pos = len(self.ops)
        self.ops.append(o)
        if fn is not None:
            self.last[eng] = o
        if dma:
            self.pending.append(o)
        return o

    def barrier(self):
        targets = [p for p in self.last.values()] + list(self.pending)
        self.pending = []
        for eng in ENGS:
            o = Op(eng, None, False, None)
            o.deps = [p for p in dict.fromkeys(targets)]
            o.pos = len(self.ops)
            self.ops.append(o)

    def dma(self, eng, out, in_, reads, writes, semkey, **kw):
        return self.op(eng, lambda e: e.dma_start(out=out, in_=in_, **kw), reads, writes,
                       dma=True, semkey=semkey)

    def emit(self, final_wait_ops=()):
        nc = self.nc
        es = self.es
        fin = self.op("sp", None, reads=(), writes=())
        for p in final_wait_ops:
            if p not in fin.deps:
                fin.deps.append(p)
                p.sig = True
        for o in self.ops:
            for p in o.deps:
                p.sig = True
        esem = {e: es.enter_context(nc.semaphore(f"c_{e}")) for e in COMPUTE}
        dsems = {}
        ecount = {e: 0 for e in COMPUTE}
        dcount = {}
        for o in self.ops:
            if o.dma:
                if o.semkey not in dsems:
                    dsems[o.semkey] = es.enter_context(nc.semaphore(f"d{len(dsems)}"))
                    dcount[o.semkey] = 0
                dcount[o.semkey] += o.inc
                o.dsem = dsems[o.semkey]
                o.dval = dcount[o.semkey]
            elif o.sig:
                assert o.eng in COMPUTE, ("sp non-dma op cannot signal", o.eng)
                ecount[o.eng] += 1
                o.val = ecount[o.eng]
        self.n_dsem = len(dsems)
        per = {e: [] for e in ENGS}
        for o in self.ops:
            per[o.eng].append(o)
        handles = {"pe": "tensor", "act": "scalar", "dve": "vector", "pool": "gpsimd", "sp": "sync"}

        def run(ename, eh):
            known = {}
            for o in per[ename]:
                need = {}
                for p in o.deps:
                    if p.dma:
                        k, s, v = ("d", p.semkey), p.dsem, p.dval
                    else:
                        k, s, v = ("e", p.eng), esem[p.eng], p.val
                    if known.get(k, 0) >= v:
                        continue
                    if k not in need or need[k][1] < v:
                        need[k] = (s, v)
                for k, (s, v) in need.items():
                    eh.wait_ge(s, v)
                    known[k] = v
                if o.fn is None:
                    continue
                ins = o.fn(eh)
                if o.dma:
                    ins.then_inc(o.dsem, o.inc)
                elif o.sig:
                    ins.then_inc(esem[o.eng], 1)

        with nc.Block() as block:
            for ename in ENGS:
                if not per[ename]:
                    continue
                getattr(block, handles[ename])(lambda eh, _n=ename: run(_n, eh))
        es.close()
        return nc


from concourse.bass_utils import run_bass_kernel_spmd

D = 1024
DFF = 2816
NFF = DFF // 128
UNIT = 2048
NT_U = UNIT // 128
MEM = 256
XH = 4
XHD = 256
EPS = 1e-6
NSLOT = 3
ARENA = 40960
GH, GDK, GDV = 4, 128, 256
DIL_R = (1, 4, 16)
DH = 16
DHD = 64
VW = DH * (DHD + 1)
PAD = 1024
NUM_BUCKETS = 32
MAX_DISTANCE = 1024
G_FFN1, G_MIX, G_CROSS, G_MEM, G_FFN2, G_FINAL = 0, 2, 4, 6, 8, 10
FULL_PHASES = ("ffn1:0", "mix:0", "cross:0", "ffn2:0", "ffn1:1", "mix:1", "cross:1", "ffn2:1", "final")


def t5_bucket(rel):
    half = NUM_BUCKETS // 2
    max_exact = half // 2
    ret = (rel > 0).astype(np.int32) * half
    n = np.abs(rel)
    large = max_exact + (np.log(np.maximum(n, 1) / max_exact) / np.log(MAX_DISTANCE / max_exact) * (half - max_exact)).astype(np.int32)
    large = np.minimum(large, half - 1)
    return (ret + np.where(n < max_exact, n, large)).astype(np.int32)


class K:
    def __init__(self, T, phases, dbg=None, T1=None):
        self.T1 = T1
        self.dbg = dbg
        self.dbgt = {}
        assert T % UNIT == 0
        self.T, self.NU, self.phases = T, T // UNIT, tuple(phases)
        self.TP = T + 2 * PAD
        nc = self.nc = bass.Bass("TRN2", target_bir_lowering=False)
        P = self.P = Prog(nc)
        inp = lambda n, s: nc.dram_tensor(n, list(s), F32, kind="ExternalInput").ap()
        self.x_in = inp("x", [T, D])
        self.mem_in = inp("mem", [MEM, D])
        self.gains = inp("gains", [11, 128, D])
        self.ident = inp("ident", [128, 128])
        self.ffn_in = inp("ffn_in", [4, D, 2 * DFF])
        self.ffn_out = inp("ffn_out", [4, DFF, D])
        self.cq = inp("cross_q", [2, D, D])
        self.ckv = inp("cross_kv", [2, D, 2 * D])
        self.co = inp("cross_o", [2, D, D])
        self.dqkv = inp("dil_qkv", [D, 9 * D])
        self.dwo = inp("dil_o", [D, D])
        self.dbias = inp("dil_bias", [3, DH, 128, 256])
        self.gwin = inp("gla_w_in", [D, 3104])
        self.gwgb = inp("gla_wgb", [2, 17, 512])
        self.gnorm = inp("gla_norm", [128, D])
        self.gwout = inp("gla_w_out", [D, D])
        self.gtri = inp("gla_tri", [4, 128, 128])
        TL = T1 or T
        self.valid8 = inp("valid8", [TL, 8])
        self.y = nc.dram_tensor("y", [TL, D], F32, kind="ExternalOutput").ap()
        self.X = P.dram("Xres", [T, D], F32).ap()
        self.bX = [P.buf(f"X{u}") for u in range(self.NU)]
        self.bY = [P.buf(f"Y{u}") for u in range(TL // UNIT)]
        self._gather = False
        if T1:
            self.win_idx = nc.dram_tensor("win_idx", [128, T1 // 128], mybir.dt.int32, kind="ExternalInput").ap()
            self.idxs = P.sbuf("idxs", [128, T1 // 128], mybir.dt.int32); self.bidx = P.buf("idxs")
            P.dma("sp", self.idxs[:], self.win_idx, [], [self.bidx], "idxs")
            self.X1 = P.dram("Xres1", [T1, D], F32).ap()
        self.bXg = P.buf("Xg")
        self.xh = P.sbuf("xh", [128, NT_U, D], F32); self.bx = [P.buf(f"x{t}") for t in range(NT_U)]
        self.xnt = P.sbuf("xnt", [128, 8, UNIT], BF16); self.bxn = [P.buf(f"xn{t}") for t in range(NT_U)]
        self.gt = P.sbuf("gt", [128, D], F32); self.bg = P.buf("g")
        self.idf = P.sbuf("idf", [128, 128], F32); self.bidf = P.buf("idf")
        self.idb = P.sbuf("idb", [128, 128], BF16); self.bidb = P.buf("idb")
        self.sq = P.sbuf("sq", [128, VW], F32); self.bsq = P.buf("sq")
        self.f32a = P.sbuf("f32a", [128, VW], F32); self.bf32a = P.buf("f32a")
        self.sil = [P.sbuf(f"sil{i}", [128, 512], F32) for i in range(2)]; self.bsil = [P.buf(f"sil{i}") for i in range(2)]
        self.ss = [P.sbuf(f"ss{i}", [128, 1], F32) for i in range(2)]; self.bss = [P.buf(f"ss{i}") for i in range(2)]
        self.rc = [P.sbuf(f"rc{i}", [128, 16], F32) for i in range(2)]; self.brc = [P.buf(f"rc{i}") for i in range(2)]
        self.AB = P.sbuf("AB", [128, ARENA], BF16)
        self.pA = [P.psum(f"pA{i}", [128, 512]) for i in range(2)]; self.bpA = [P.buf(f"pA{i}", excl=True) for i in range(2)]
        self.pB = [P.psum(f"pB{i}", [128, 512]) for i in range(2)]; self.bpB = [P.buf(f"pB{i}", excl=True) for i in range(2)]
        self.pO = [P.psum(f"pO{i}", [128, 512]) for i in range(3)]; self.bpO = [P.buf(f"pO{i}", excl=True) for i in range(3)]
        self.pT = P.psum("pT", [128, 8 * 128], BF16); self.bpT = P.buf("pT", excl=True)
        self.io = self.ab = self.wcnt = self.nrm = self.xk = 0
        self.outs = []
        P.dma("sp", self.idf[:], self.ident, [], [self.bidf], "id")
        P.op("act", lambda e: e.activation(out=self.idb[:], in_=self.idf[:], func=AF.Copy), [self.bidf], [self.bidb])

    def phase_begin(self):
        self.P.barrier()
        self.aoff = 0

    def carve(self, n, pat=None, **kw):
        assert self.aoff + n <= ARENA, ("arena overflow", self.aoff, n)
        v = self.AB[:, self.aoff:self.aoff + n]
        self.aoff += n
        return v.rearrange(pat, **kw) if pat else v

    def carve_common(self):
        self.xnb = [self.carve(D) for _ in range(2)]; self.bxnb = [self.P.buf() for _ in range(2)]

    def src(self, first):
        if getattr(self, "_dbg_src", None) is not None:
            return self._dbg_src.rearrange("(t p) d -> p t d", p=128)
        return (self.x_in if first else self.X).rearrange("(t p) d -> p t d", p=128)

    def load_gain(self, row):
        self.P.dma("sp", self.gt[:], self.gains[row], [], [self.bg], "g")

    def enter_window(self):
        self.X0, self.bX0 = self.X, list(self.bX)
        self.T, self.NU, self.TP = self.T1, self.T1 // UNIT, self.T1 + 2 * PAD
        self.X = self.X1
        self.bX = [self.P.buf(f"X1_{u}") for u in range(self.NU)]
        self._gather = True

    def load_unit(self, u, first):
        P = self.P
        if self._gather:
            X0, idxs = self.X0, self.idxs
            for t in range(NT_U):
                k = u * NT_U + t
                P.op("pool", lambda e, t=t, k=k: e.indirect_dma_start(out=self.xh[:, t, :], out_offset=None, in_=X0,
                                                                     in_offset=bass.IndirectOffsetOnAxis(ap=idxs[:, k:k + 1], axis=0)),
                     list(self.bX0) + [self.bidx], [self.bx[t]], dma=True, semkey=("gx", t % 4))
            return
        rd = [] if first else [self.bX[u]]
        for t in range(NT_U):
            P.dma("sp", self.xh[:, t, :], self.src(first)[:, u * NT_U + t, :], rd, [self.bx[t]], ("x", t % 4))

    def store_unit(self, u):
        X_t = self.X.rearrange("(t p) d -> p t d", p=128)
        self.P.dma("sp", X_t[:, u * NT_U:(u + 1) * NT_U, :], self.xh[:], list(self.bx), [self.bX[u]], ("xs", u % 2))

    def rms_rstd(self, src_ap, rdbufs):
        P = self.P
        i = self.nrm % 2
        self.nrm += 1
        P.op("dve", lambda e: e.tensor_tensor(out=self.sq[:, 0:D], in0=src_ap, in1=src_ap, op=ALU.mult), rdbufs, [self.bsq])
        P.op("dve", lambda e: e.reduce_sum(out=self.ss[i][:], in_=self.sq[:, 0:D], axis=AX.X), [self.bsq], [self.bss[i]])
        P.op("act", lambda e: e.activation(out=self.ss[i][:], in_=self.ss[i][:], func=AF.Ln, bias=EPS, scale=1.0 / D), [self.bss[i]], [self.bss[i]])
        P.op("act", lambda e: e.activation(out=self.ss[i][:], in_=self.ss[i][:], func=AF.Exp, scale=-0.5), [self.bss[i]], [self.bss[i]])
        return i

    def norm_T(self, src_ap, rdbufs, dst_ap, dstbufs):
        P = self.P
        i = self.rms_rstd(src_ap, rdbufs)
        xnb_i = self.xnb[i]
        P.op("dve", lambda e: e.scalar_tensor_tensor(out=xnb_i, in0=src_ap, scalar=self.ss[i][:], in1=self.gt[:],
                                                     op0=ALU.mult, op1=ALU.mult), rdbufs + [self.bss[i], self.bg], [self.bxnb[i]])
        self.transpose8(xnb_i, self.bxnb[i], dst_ap, dstbufs)

    def transpose8(self, src_ap, srcbuf, dst_ap, dstbufs):
        P = self.P
        for kc in range(8):
            P.op("pe", lambda e, kc=kc: e.transpose(self.pT[:, kc * 128:(kc + 1) * 128], src_ap[:, kc * 128:(kc + 1) * 128], self.idb[:]),
                 [srcbuf, self.bidb], [self.bpT])
        P.op("act", lambda e: e.activation(out=dst_ap, in_=self.pT[:].rearrange("p (k n) -> p k n", k=8), func=AF.Copy), [self.bpT], dstbufs)

    def norm_unit(self):
        for t in range(NT_U):
            self.norm_T(self.xh[:, t, :], [self.bx[t]], self.xnt[:, :, t * 128:(t + 1) * 128], [self.bxn[t]])

    def out_proj_add(self, tt, oT, boT, w, bw):
        P = self.P
        for sub in range(4):
            t = 4 * tt + sub
            for hf in range(2):
                o = self.io % 3
                self.io += 1
                for fc in range(8):
                    P.op("pe", lambda e, fc=fc, sub=sub, hf=hf, o=o: e.matmul(self.pO[o][:], lhsT=oT[:, fc, sub * 128:(sub + 1) * 128],
                                                                             rhs=w[:, fc, hf * 512:(hf + 1) * 512], start=(fc == 0), stop=(fc == 7)),
                         [boT[sub], bw], [self.bpO[o]])
                P.op("dve", lambda e, t=t, hf=hf, o=o: e.tensor_tensor(out=self.xh[:, t, hf * 512:(hf + 1) * 512], in0=self.xh[:, t, hf * 512:(hf + 1) * 512],
                                                                      in1=self.pO[o][:], op=ALU.add), [self.bpO[o], self.bx[t]], [self.bx[t]])

    def ffn_phase(self, fi, grow, first):
        P = self.P
        FG, NSL = 4, 8
        self.phase_begin()
        self.carve_common()
        wa = [self.carve(1024, "p (k n) -> p k n", k=8) for _ in range(NSL)]; bwa = [P.buf() for _ in range(NSL)]
        wb = [self.carve(1024, "p (k n) -> p k n", k=8) for _ in range(NSL)]; bwb = [P.buf() for _ in range(NSL)]
        wo = [self.carve(D) for _ in range(NSL)]; bwo = [P.buf() for _ in range(NSL)]
        act = [[self.carve(512) for _ in range(FG)] for _ in range(2)]; bact = [[P.buf() for _ in range(FG)] for _ in range(2)]
        w_in_k = self.ffn_in[fi].rearrange("(kc p) n -> p kc n", p=128)
        w_out = self.ffn_out[fi]
        groups = [list(range(c0, min(c0 + FG, NFF))) for c0 in range(0, NFF, FG)]

        def load_w(c):
            s = c % NSL
            P.dma("pool", wa[s], w_in_k[:, :, c * 128:(c + 1) * 128], [], [bwa[s]], ("wa", s))
            P.dma("pool", wb[s], w_in_k[:, :, DFF + c * 128:DFF + (c + 1) * 128], [], [bwb[s]], ("wb", s))
            P.dma("pool", wo[s], w_out[c * 128:(c + 1) * 128, :], [], [bwo[s]], ("wo", s))

        self.load_gain(grow)
        for u in range(self.NU):
            self.load_unit(u, first)
            self.norm_unit()
            for c in groups[0]:
                load_w(c)
            for gi, grp in enumerate(groups):
                if gi + 1 < len(groups):
                    for c in groups[gi + 1]:
                        load_w(c)
                def in_proj(tt):
                    par = tt % 2
                    rd = [self.bxn[4 * tt + q] for q in range(4)]
                    for g, c in enumerate(grp):
                        s = c % NSL
                        j = self.ab % 2
                        self.ab += 1
                        a_g, ba_g = act[par][g], bact[par][g]
                        for kc in range(8):
                            P.op("pe", lambda e, s=s, kc=kc, tt=tt, j=j: e.matmul(self.pA[j][:], lhsT=wa[s][:, kc, :], rhs=self.xnt[:, kc, tt * 512:(tt + 1) * 512],
                                                                                 start=(kc == 0), stop=(kc == 7)), [bwa[s]] + rd, [self.bpA[j]])
                        for kc in range(8):
                            P.op("pe", lambda e, s=s, kc=kc, tt=tt, j=j: e.matmul(self.pB[j][:], lhsT=wb[s][:, kc, :], rhs=self.xnt[:, kc, tt * 512:(tt + 1) * 512],
                                                                                 start=(kc == 0), stop=(kc == 7)), [bwb[s]] + rd, [self.bpB[j]])
                        P.op("act", lambda e, j=j: e.activation(out=self.sil[j][:], in_=self.pA[j][:], func=AF.Silu), [self.bpA[j]], [self.bsil[j]])
                        P.op("dve", lambda e, j=j, a_g=a_g: e.tensor_tensor(out=a_g, in0=self.sil[j][:], in1=self.pB[j][:], op=ALU.mult),
                             [self.bsil[j], self.bpB[j]], [ba_g])

                def out_proj(tt):
                    par = tt % 2
                    for sub in range(4):
                        t = 4 * tt + sub
                        for h in range(2):
                            o = self.io % 3
                            self.io += 1
                            for g, c in enumerate(grp):
                                s = c % NSL
                                a_g, ba_g = act[par][g], bact[par][g]
                                P.op("pe", lambda e, s=s, sub=sub, h=h, o=o, a_g=a_g, g=g, n=len(grp): e.matmul(
                                    self.pO[o][:], lhsT=a_g[:, sub * 128:(sub + 1) * 128], rhs=wo[s][:, h * 512:(h + 1) * 512], start=(g == 0), stop=(g == n - 1)),
                                     [ba_g, bwo[s]], [self.bpO[o]])
                            P.op("dve", lambda e, t=t, h=h, o=o: e.scalar_tensor_tensor(out=self.xh[:, t, h * 512:(h + 1) * 512], in0=self.pO[o][:], scalar=0.5,
                                                                                       in1=self.xh[:, t, h * 512:(h + 1) * 512], op0=ALU.mult, op1=ALU.add),
                                 [self.bpO[o], self.bx[t]], [self.bx[t]])

                ntt = UNIT // 512
                in_proj(0)
                for tt in range(ntt):
                    if tt + 1 < ntt:
                        in_proj(tt + 1)
                    out_proj(tt)
            self.store_unit(u)

    def cross_phase(self, li, first):
        P = self.P
        self.phase_begin()
        self.carve_common()
        wbig = [self.carve(8 * D, "p (k n) -> p k n", k=8) for _ in range(2)]; bwbig = [P.buf() for _ in range(2)]
        memT = self.carve(8 * MEM, "p (k n) -> p k n", k=8); bmemT = P.buf()
        kT = self.carve(8 * MEM, "p (k n) -> p k n", k=8); bkT = P.buf()
        vA = self.carve(2 * XH * (XHD + 1), "p (a h d) -> p a h d", a=2, h=XH); bvA = P.buf()
        qT = self.carve(8 * 512, "p (k n) -> p k n", k=8); bqT = P.buf()
        pTs = [self.carve(512) for _ in range(2)]; bpTs = [P.buf() for _ in range(2)]
        ob = self.carve(4 * D, "p (a d) -> p a d", a=4); bob = [P.buf() for _ in range(4)]
        oT = self.carve(8 * 512, "p (k n) -> p k n", k=8); boT = [P.buf() for _ in range(4)]
        memf = self.f32a[:, 0:D]; bmemf = self.bf32a

        def load_big(i, w_ap):
            P.dma("pool", wbig[i], w_ap.rearrange("(kc p) n -> p kc n", p=128), [], [bwbig[i]], ("wbig", i))

        P.op("dve", lambda e: e.memset(vA, 1.0), [], [bvA])
        self.load_gain(G_MEM + li)
        for kt in range(2):
            P.dma("sp", memf[:], self.mem_in[kt * 128:(kt + 1) * 128, :], [], [bmemf], "memf")
            self.norm_T(memf[:], [bmemf], memT[:, :, kt * 128:(kt + 1) * 128], [bmemT])
        load_big(0, self.ckv[li][:, 0:D])
        for fo in range(8):
            j = self.ab % 2
            self.ab += 1
            for kc in range(8):
                P.op("pe", lambda e, fo=fo, kc=kc, j=j: e.matmul(self.pA[j][:, 0:MEM], lhsT=wbig[0][:, kc, fo * 128:(fo + 1) * 128], rhs=memT[:, kc, :],
                                                                start=(kc == 0), stop=(kc == 7)), [bwbig[0], bmemT], [self.bpA[j]])
            P.op("act", lambda e, fo=fo, j=j: e.activation(out=kT[:, fo, :], in_=self.pA[j][:, 0:MEM], func=AF.Copy), [self.bpA[j]], [bkT])
        load_big(0, self.ckv[li][:, D:2 * D])
        for kt in range(2):
            for hf in range(2):
                j = self.ab % 2
                self.ab += 1
                for kc in range(8):
                    P.op("pe", lambda e, kt=kt, hf=hf, kc=kc, j=j: e.matmul(self.pB[j][:], lhsT=memT[:, kc, kt * 128:(kt + 1) * 128],
                                                                           rhs=wbig[0][:, kc, hf * 512:(hf + 1) * 512], start=(kc == 0), stop=(kc == 7)),
                         [bwbig[0], bmemT], [self.bpB[j]])
                P.op("act", lambda e, kt=kt, hf=hf, j=j: e.activation(out=vA[:, kt, 2 * hf:2 * hf + 2, 0:XHD],
                                                                      in_=self.pB[j][:].rearrange("p (h d) -> p h d", h=2), func=AF.Copy), [self.bpB[j]], [bvA])
        load_big(0, self.cq[li])
        load_big(1, self.co[li])
        self.load_gain(G_CROSS + li)
        for u in range(self.NU):
            self.load_unit(u, first)
            self.norm_unit()
            for tt in range(UNIT // 512):
                rd = [self.bxn[4 * tt + q] for q in range(4)]
                for fo in range(8):
                    j = self.ab % 2
                    self.ab += 1
                    for kc in range(8):
                        P.op("pe", lambda e, fo=fo, kc=kc, tt=tt, j=j: e.matmul(self.pA[j][:], lhsT=wbig[0][:, kc, fo * 128:(fo + 1) * 128],
                                                                               rhs=self.xnt[:, kc, tt * 512:(tt + 1) * 512], start=(kc == 0), stop=(kc == 7)),
                             [bwbig[0]] + rd, [self.bpA[j]])
                    P.op("act", lambda e, fo=fo, j=j: e.activation(out=qT[:, fo, :], in_=self.pA[j][:], func=AF.Copy, scale=XHD ** -0.5),
                         [self.bpA[j]], [bqT])
                for h in range(XH):
                    for kt in range(2):
                        j = self.ab % 2
                        self.ab += 1
                        for dc in range(2):
                            P.op("pe", lambda e, h=h, kt=kt, dc=dc, j=j: e.matmul(self.pB[j][:], lhsT=kT[:, 2 * h + dc, kt * 128:(kt + 1) * 128],
                                                                                 rhs=qT[:, 2 * h + dc, :], start=(dc == 0), stop=(dc == 1)),
                                 [bkT, bqT], [self.bpB[j]])
                        P.op("act", lambda e, kt=kt, j=j: e.activation(out=pTs[kt], in_=self.pB[j][:], func=AF.Exp), [self.bpB[j]], [bpTs[kt]])
                    for sub in range(4):
                        o = self.io % 3
                        self.io += 1
                        r = self.xk % 2
                        self.xk += 1
                        for kt in range(2):
                            P.op("pe", lambda e, h=h, kt=kt, sub=sub, o=o: e.matmul(self.pO[o][:, 0:XHD + 1], lhsT=pTs[kt][:, sub * 128:(sub + 1) * 128],
                                                                                   rhs=vA[:, kt, h, :], start=(kt == 0), stop=(kt == 1)),
                                 [bpTs[kt], bvA], [self.bpO[o]])
                        P.op("dve", lambda e, o=o, r=r: e.reciprocal(out=self.rc[r][:, 0:1], in_=self.pO[o][:, XHD:XHD + 1]), [self.bpO[o]], [self.brc[r]])
                        P.op("dve", lambda e, h=h, sub=sub, o=o, r=r: e.tensor_scalar(out=ob[:, sub, h * XHD:(h + 1) * XHD], in0=self.pO[o][:, 0:XHD],
                                                                                     scalar1=self.rc[r][:, 0:1], scalar2=None, op0=ALU.mult),
                             [self.bpO[o], self.brc[r]], [bob[sub]])
                for sub in range(4):
                    self.transpose8(ob[:, sub, :], bob[sub], oT[:, :, sub * 128:(sub + 1) * 128], [boT[sub]])
                self.out_proj_add(tt, oT, boT, wbig[1], bwbig[1])
            self.store_unit(u)


    def gla_phase(self, li, first):
        P = self.P
        T, NU = self.T, self.NU
        GQT = P.dram("gQT", [GH * GDK, T], BF16).ap(); bGQ = P.buf("gQT")
        GKT = P.dram("gKT", [GH * GDK, T], BF16).ap(); bGK = P.buf("gKT")
        GV = P.dram("gV", [T, D], BF16).ap(); bGV = P.buf("gV")
        GR = P.dram("gR", [T, D], F32).ap(); bGR = P.buf("gR")
        GG = [P.dram(f"gG{d}", [T, 512], F32).ap() for d in range(2)]; bGG = [P.buf(f"gG{d}") for d in range(2)]
        GO = P.dram("gO", [T, D], F32).ap(); bGO = P.buf("gO")
        wk = self.gwin.rearrange("(kc p) n -> p kc n", p=128)
        self.dbgt.update(GO=GO, GR=GR, GG0=GG[0], GG1=GG[1])
        self.phase_begin()
        self.carve_common()
        ws = [self.carve(8 * 512, "p (k n) -> p k n", k=8) for _ in range(NSLOT)]; bws = [P.buf() for _ in range(NSLOT)]
        wz = self.carve(8 * 32, "p (k n) -> p k n", k=8); bwz = P.buf()
        rowb = [self.carve(UNIT) for _ in range(2)]; browb = [P.buf() for _ in range(2)]
        vrow = [self.carve(512) for _ in range(2)]; bvrow = [P.buf() for _ in range(2)]
        zaug = [self.carve(UNIT) for _ in range(2)]; bzaug = [P.buf() for _ in range(2)]
        wgb = [self.carve(512) for _ in range(2)]; bwgb = [P.buf() for _ in range(2)]
        rrow = [self.sil[0], self.sil[1]]; brrow = self.bsil
        grow_ = [self.sq[:, 0:512], self.f32a[:, 0:512]]; bgrow = [self.bsq, self.bf32a]
        P.dma("pool", wz, wk[:, :, 3072:3104], [], [bwz], "wz")
        for d in range(2):
            P.dma("pool", wgb[d][0:17, :], self.gwgb[d], [], [bwgb[d]], ("wgb", d))
            P.op("dve", lambda e, d=d: e.memset(zaug[d][0:32, :], 1.0), [], [bzaug[d]])
        self.load_gain(G_MIX + li)
        wc = rb = vs = 0
        blocks = [("q", 0), ("k", 512), ("v", 1024), ("v", 1536), ("r", 2048), ("r", 2560)]
        for u in range(NU):
            ub = u * UNIT
            self.load_unit(u, first)
            self.norm_unit()

            def load_blk(i, s_):
                P.dma("pool", ws[s_], wk[:, :, blocks[i][1]:blocks[i][1] + 512], [], [bws[s_]], ("ws", s_))

            for i0 in range(NSLOT - 1):
                load_blk(i0, (wc + i0) % NSLOT)
            for bi, (kind, col) in enumerate(blocks):
                s_ = wc % NSLOT
                if bi + NSLOT - 1 < len(blocks):
                    load_blk(bi + NSLOT - 1, (wc + NSLOT - 1) % NSLOT)
                if kind in ("q", "k"):
                    for fl in range(4):
                        r_ = rb % 2
                        rb += 1
                        for tt in range(4):
                            j = self.ab % 2
                            self.ab += 1
                            rd = [self.bxn[4 * tt + q] for q in range(4)]
                            for kc in range(8):
                                P.op("pe", lambda e, s_=s_, kc=kc, fl=fl, tt=tt, j=j: e.matmul(self.pA[j][:], lhsT=ws[s_][:, kc, fl * 128:(fl + 1) * 128],
                                                                                             rhs=self.xnt[:, kc, tt * 512:(tt + 1) * 512], start=(kc == 0), stop=(kc == 7)),
                                     [bws[s_]] + rd, [self.bpA[j]])
                            sc = GDK ** -0.5 if kind == "q" else 1.0
                            P.op("act", lambda e, r_=r_, tt=tt, j=j, sc=sc: e.activation(out=rowb[r_][:, tt * 512:(tt + 1) * 512], in_=self.pA[j][:], func=AF.Copy, scale=sc),
                                 [self.bpA[j]], [browb[r_]])
                        dst, bdst = (GQT, bGQ) if kind == "q" else (GKT, bGK)
                        P.dma("sp", dst[fl * 128:(fl + 1) * 128, ub:ub + UNIT], rowb[r_], [browb[r_]], [bdst], ("rowb", r_))
                else:
                    hf = (col % 1024) // 512
                    for t in range(NT_U):
                        j = self.ab % 2
                        self.ab += 1
                        v_ = vs % 2
                        vs += 1
                        for kc in range(8):
                            P.op("pe", lambda e, s_=s_, kc=kc, t=t, j=j: e.matmul(self.pB[j][:], lhsT=self.xnt[:, kc, t * 128:(t + 1) * 128], rhs=ws[s_][:, kc, :],
                                                                                 start=(kc == 0), stop=(kc == 7)), [bws[s_], self.bxn[t]], [self.bpB[j]])
                        r0 = ub + t * 128
                        if kind == "v":
                            P.op("act", lambda e, v_=v_, j=j: e.activation(out=vrow[v_], in_=self.pB[j][:], func=AF.Copy), [self.bpB[j]], [bvrow[v_]])
                            P.dma("sp", GV[r0:r0 + 128, hf * 512:(hf + 1) * 512], vrow[v_], [bvrow[v_]], [bGV], ("vrow", v_))
                        else:
                            P.op("act", lambda e, v_=v_, j=j: e.activation(out=rrow[v_][:], in_=self.pB[j][:], func=AF.Silu), [self.bpB[j]], [brrow[v_]])
                            P.dma("sp", GR[r0:r0 + 128, hf * 512:(hf + 1) * 512], rrow[v_][:], [brrow[v_]], [bGR], ("rrow", v_))
                wc += 1
            for d in range(2):
                for tt in range(4):
                    j = self.ab % 2
                    self.ab += 1
                    rd = [self.bxn[4 * tt + q] for q in range(4)]
                    for kc in range(8):
                        P.op("pe", lambda e, d=d, kc=kc, tt=tt, j=j: e.matmul(self.pA[j][0:16, :], lhsT=wz[:, kc, d * 16:(d + 1) * 16],
                                                                             rhs=self.xnt[:, kc, tt * 512:(tt + 1) * 512], start=(kc == 0), stop=(kc == 7)),
                             [bwz] + rd, [self.bpA[j]])
                    P.op("act", lambda e, d=d, tt=tt, j=j: e.activation(out=zaug[d][0:16, tt * 512:(tt + 1) * 512], in_=self.pA[j][0:16, :], func=AF.Copy),
                         [self.bpA[j]], [bzaug[d]])
                for t in range(NT_U):
                    j = self.ab % 2
                    self.ab += 1
                    g_ = vs % 2
                    vs += 1
                    P.op("pe", lambda e, d=d, t=t, j=j: e.matmul(self.pB[j][:], lhsT=zaug[d][0:17, t * 128:(t + 1) * 128], rhs=wgb[d][0:17, :], start=True, stop=True),
                         [bzaug[d], bwgb[d]], [self.bpB[j]])
                    P.op("act", lambda e, g_=g_, j=j: e.activation(out=grow_[g_], in_=self.pB[j][:], func=AF.Exp, scale=-1.0), [self.bpB[j]], [bgrow[g_]])
                    P.op("act", lambda e, g_=g_: e.activation(out=grow_[g_], in_=grow_[g_], func=AF.Ln, bias=1.0), [bgrow[g_]], [bgrow[g_]])
                    r0 = ub + t * 128
                    P.dma("sp", GG[d][r0:r0 + 128, :], grow_[g_], [bgrow[g_]], [bGG[d]], ("grow", g_))
        HB = GH * 128
        for d in range(2):
            self.phase_begin()
            qTu = self.carve(GH * UNIT, "p (h t) -> p h t", h=GH); bqTu = P.buf()
            kTu = self.carve(GH * UNIT, "p (h t) -> p h t", h=GH); bkTu = P.buf()
            vu = self.xnt[:].rearrange("p k t -> p (k t)").rearrange("p (c f) -> p c f", c=NT_U); bvu = P.buf()
            qd = [self.carve(HB) for _ in range(2)]; bqd = [P.buf() for _ in range(2)]
            kd = [self.carve(HB) for _ in range(2)]; bkd = [P.buf() for _ in range(2)]
            at = [self.carve(HB) for _ in range(2)]; bat = [P.buf() for _ in range(2)]
            ktok = [self.carve(HB) for _ in range(2)]; bktok = [P.buf() for _ in range(2)]
            Sb = self.carve(GH * GDV, "p (h v) -> p h v", h=GH); bSb = P.buf()
            xflat = self.xh[:].rearrange("p a b -> p (a b)")
            gu = xflat[:, 0:NT_U * 512].rearrange("p (c f) -> p c f", c=NT_U); bgu = P.buf()
            ost = [xflat[:, 8192 + i * 1024:8192 + (i + 1) * 1024] for i in range(2)]; bost = [P.buf() for _ in range(2)]
            e1 = [xflat[:, 10240 + i * HB:10240 + (i + 1) * HB] for i in range(2)]; be1 = [P.buf() for _ in range(2)]
            e2 = [xflat[:, 11264 + i * HB:11264 + (i + 1) * HB] for i in range(2)]; be2 = [P.buf() for _ in range(2)]
            gof = [xflat[:, 12288 + i * 1024:12288 + (i + 1) * 1024] for i in range(2)]; bgof = [P.buf() for _ in range(2)]
            grr = [xflat[:, 14336 + i * 1024:14336 + (i + 1) * 1024] for i in range(2)]; bgrr = [P.buf() for _ in range(2)]
            msk4 = self.sil[0][:, 0:HB]; tri = self.sil[1][:, 0:128]; btri = P.buf()
            eL = [self.sil[1][:, 128 + 4 * i:132 + 4 * i] for i in range(2)]; beL = [P.buf() for _ in range(2)]
            S = self.f32a[:, 0:GH * GDV].rearrange("p (h v) -> p h v", h=GH); bS = self.bf32a
            Sf = self.f32a[:, 0:GH * GDV]
            gn = self.gt; bgn = self.bg
            P.dma("sp", tri, self.gtri[d], [], [btri], "tri")
            for h in range(GH):
                P.dma("sp", msk4[:, h * 128:(h + 1) * 128], self.gtri[2 + d], [], [btri], "msk")
            P.op("dve", lambda e: e.memset(S, 0.0), [], [bS])
            P.op("dve", lambda e: e.memset(Sb, 0.0), [], [bSb])
            if d == 1:
                self.carve_common()
                wo_ = self.carve(8 * D, "p (k n) -> p k n", k=8); bwo_ = P.buf()
                ob = self.carve(D); bob = P.buf()
                oT = self.carve(8 * 128, "p (k n) -> p k n", k=8); boT = P.buf()
                P.dma("pool", wo_, self.gwout.rearrange("(kc p) n -> p kc n", p=128), [], [bwo_], "gwo")
                P.dma("sp", gn[:], self.gnorm, [], [bgn], "g")
            cc = 0
            lc = 127 if d == 0 else 0
            order_u = range(NU) if d == 0 else range(NU - 1, -1, -1)
            for u in order_u:
                ub = u * UNIT
                P.dma("sp", qTu, GQT.rearrange("(h p) t -> p h t", p=128)[:, :, ub:ub + UNIT], [bGQ], [bqTu], "qTu")
                P.dma("sp", kTu, GKT.rearrange("(h p) t -> p h t", p=128)[:, :, ub:ub + UNIT], [bGK], [bkTu], "kTu")
                P.dma("sp", vu, GV[ub:ub + UNIT, :].rearrange("(c p) f -> p c f", p=128), [bGV], [bvu], "vu")
                P.dma("sp", gu, GG[d][ub:ub + UNIT, :].rearrange("(c p) f -> p c f", p=128), [bGG[d]], [bgu], "gu")
                order_c = range(NT_U) if d == 0 else range(NT_U - 1, -1, -1)
                for c in order_c:
                    r0 = ub + c * 128
                    i = cc % 2
                    cc += 1
                    j = i
                    o_ = i
                    pA, bpA_, pB, bpB_ = self.pA[j], self.bpA[j], self.pB[j], self.bpB[j]
                    e1i, e2i, eLi, qdi, kdi, ati, kti, osti = e1[i], e2[i], eL[i], qd[i], kd[i], at[i], ktok[i], ost[o_]
                    if d == 1:
                        P.dma("sp", gof[o_], GO[r0:r0 + 128, :], [bGO], [bgof[o_]], ("gof", o_))
                        P.dma("sp", grr[o_], GR[r0:r0 + 128, :], [bGR], [bgrr[o_]], ("grr", o_))
                    for h in range(GH):
                        P.op("pe", lambda e, c=c, h=h, pA=pA: e.matmul(pA[:, h * 128:(h + 1) * 128], lhsT=gu[:, c, h * 128:(h + 1) * 128], rhs=tri, start=True, stop=True),
                             [bgu, btri], [bpA_])
                    P.op("act", lambda e, pA=pA, e1i=e1i: e.activation(out=e1i, in_=pA[:, 0:HB], func=AF.Exp), [bpA_], [be1[i]])
                    P.op("act", lambda e, pA=pA, e2i=e2i: e.activation(out=e2i, in_=pA[:, 0:HB], func=AF.Exp, scale=-1.0), [bpA_], [be2[i]])
                    P.op("act", lambda e, pA=pA, eLi=eLi, lc=lc: e.activation(out=eLi, in_=pA[:, 0:HB].rearrange("p (h n) -> p h n", h=GH)[:, :, lc], func=AF.Exp), [bpA_], [beL[i]])
                    P.op("dve", lambda e, c=c, qdi=qdi, e1i=e1i: e.tensor_tensor(out=qdi.rearrange("p (h n) -> p h n", h=GH), in0=qTu[:, :, c * 128:(c + 1) * 128],
                                                                                in1=e1i.rearrange("p (h n) -> p h n", h=GH), op=ALU.mult), [bqTu, be1[i]], [bqd[i]])
                    P.op("dve", lambda e, c=c, kdi=kdi, e2i=e2i: e.tensor_tensor(out=kdi.rearrange("p (h n) -> p h n", h=GH), in0=kTu[:, :, c * 128:(c + 1) * 128],
                                                                                in1=e2i.rearrange("p (h n) -> p h n", h=GH), op=ALU.mult), [bkTu, be2[i]], [bkd[i]])
                    for h in range(GH):
                        P.op("pe", lambda e, h=h, pB=pB, kdi=kdi, qdi=qdi: e.matmul(pB[:, h * 128:(h + 1) * 128], lhsT=kdi[:, h * 128:(h + 1) * 128], rhs=qdi[:, h * 128:(h + 1) * 128],
                                                                                   start=True, stop=True), [bkd[i], bqd[i]], [bpB_])
                    for h in range(GH):
                        P.op("pe", lambda e, h=h, kdi=kdi: e.transpose(self.pT[:, h * 128:(h + 1) * 128], kdi[:, h * 128:(h + 1) * 128], self.idb[:]), [bkd[i], self.bidb], [self.bpT])
                    P.op("dve", lambda e, pB=pB, ati=ati: e.tensor_tensor(out=ati, in0=pB[:, 0:HB], in1=msk4, op=ALU.mult), [bpB_, btri], [bat[i]])
                    P.op("act", lambda e, kti=kti: e.activation(out=kti, in_=self.pT[:, 0:HB], func=AF.Copy), [self.bpT], [bktok[i]])
                    for h in range(GH):
                        po, bpo = self.pO[h // 2], self.bpO[h // 2]
                        cs = (h % 2) * GDV
                        P.op("pe", lambda e, c=c, h=h, po=po, cs=cs, ati=ati: e.matmul(po[:, cs:cs + GDV], lhsT=ati[:, h * 128:(h + 1) * 128], rhs=vu[:, c, h * GDV:(h + 1) * GDV],
                                                                                      start=True, stop=False), [bat[i], bvu], [bpo])
                        P.op("pe", lambda e, h=h, po=po, cs=cs, qdi=qdi: e.matmul(po[:, cs:cs + GDV], lhsT=qdi[:, h * 128:(h + 1) * 128], rhs=Sb[:, h, :], start=False, stop=True),
                             [bqd[i], bSb], [bpo])
                    for h in range(GH):
                        pu, bpu = (self.pO[2], self.bpO[2]) if h < 2 else (pB, bpB_)
                        cs = (h % 2) * GDV
                        P.op("pe", lambda e, c=c, h=h, pu=pu, cs=cs, kti=kti: e.matmul(pu[:, cs:cs + GDV], lhsT=kti[:, h * 128:(h + 1) * 128], rhs=vu[:, c, h * GDV:(h + 1) * GDV],
                                                                                      start=True, stop=True), [bktok[i], bvu], [bpu])
                    for hh in range(2):
                        po, bpo = self.pO[hh], self.bpO[hh]
                        if d == 0:
                            P.op("act", lambda e, hh=hh, po=po, osti=osti: e.activation(out=osti[:, hh * 512:(hh + 1) * 512], in_=po[:, 0:512], func=AF.Copy), [bpo], [bost[o_]])
                        else:
                            gofi = gof[o_]
                            P.op("dve", lambda e, hh=hh, po=po, osti=osti, gofi=gofi: e.tensor_tensor(out=osti[:, hh * 512:(hh + 1) * 512], in0=po[:, 0:512],
                                                                                                     in1=gofi[:, hh * 512:(hh + 1) * 512], op=ALU.add), [bpo, bgof[o_]], [bost[o_]])
                    P.op("dve", lambda e: e.tensor_tensor(out=Sf[:, 0:512], in0=Sf[:, 0:512], in1=self.pO[2][:, 0:512], op=ALU.add), [bS, self.bpO[2]], [bS])
                    P.op("dve", lambda e, pB=pB: e.tensor_tensor(out=Sf[:, 512:1024], in0=Sf[:, 512:1024], in1=pB[:, 0:512], op=ALU.add), [bS, bpB_], [bS])
                    for h in range(GH):
                        P.op("dve", lambda e, h=h, eLi=eLi: e.tensor_scalar(out=S[:, h, :], in0=S[:, h, :], scalar1=eLi[:, h:h + 1], scalar2=None, op0=ALU.mult), [bS, beL[i]], [bS])
                    P.op("act", lambda e: e.activation(out=Sb.rearrange("p h v -> p (h v)"), in_=Sf, func=AF.Copy), [bS], [bSb])
                    if d == 0:
                        P.dma("sp", GO[r0:r0 + 128, :], osti, [bost[o_]], [bGO], ("ost", o_))
                    else:
                        r_ = self.xk % 2
                        self.xk += 1
                        grri = grr[o_]
                        xcs = [e1i[:, 0:512], e2i[:, 0:512]]; bxcs = [be1[i], be2[i]]
                        for h in range(GH):
                            P.op("dve", lambda e, h=h, osti=osti: e.tensor_tensor(out=self.sq[:, 0:GDV], in0=osti[:, h * GDV:(h + 1) * GDV], in1=osti[:, h * GDV:(h + 1) * GDV], op=ALU.mult),
                                 [bost[o_]], [self.bsq])
                            P.op("dve", lambda e, h=h, r_=r_: e.reduce_sum(out=self.rc[r_][:, h:h + 1], in_=self.sq[:, 0:GDV], axis=AX.X), [self.bsq], [self.brc[r_]])
                        P.op("act", lambda e, r_=r_: e.activation(out=self.rc[r_][:, 0:GH], in_=self.rc[r_][:, 0:GH], func=AF.Ln, bias=EPS, scale=1.0 / GDV), [self.brc[r_]], [self.brc[r_]])
                        P.op("act", lambda e, r_=r_: e.activation(out=self.rc[r_][:, 0:GH], in_=self.rc[r_][:, 0:GH], func=AF.Exp, scale=-0.5), [self.brc[r_]], [self.brc[r_]])
                        for h in range(GH):
                            P.op("dve", lambda e, h=h, osti=osti, r_=r_: e.scalar_tensor_tensor(out=osti[:, h * GDV:(h + 1) * GDV], in0=osti[:, h * GDV:(h + 1) * GDV],
                                                                                               scalar=self.rc[r_][:, h:h + 1], in1=gn[:, h * GDV:(h + 1) * GDV], op0=ALU.mult, op1=ALU.mult),
                                 [bost[o_], self.brc[r_], bgn], [bost[o_]])
                        P.op("dve", lambda e, osti=osti, grri=grri: e.tensor_tensor(out=ob, in0=osti, in1=grri, op=ALU.mult), [bost[o_], bgrr[o_]], [bob])
                        self.transpose8(ob, bob, oT, [boT])
                        for hf in range(2):
                            o = 2 - hf
                            xci, bxci = xcs[hf], bxcs[hf]
                            src_t = (self.x_in if first else self.X)
                            P.dma("sp", xci, src_t[r0:r0 + 128, hf * 512:(hf + 1) * 512], [] if first else [self.bX[u]], [bxci], ("xc", hf))
                            for fc in range(8):
                                P.op("pe", lambda e, fc=fc, hf=hf, o=o: e.matmul(self.pO[o][:], lhsT=oT[:, fc, :], rhs=wo_[:, fc, hf * 512:(hf + 1) * 512],
                                                                                start=(fc == 0), stop=(fc == 7)), [boT, bwo_], [self.bpO[o]])
                            P.op("dve", lambda e, o=o, xci=xci: e.tensor_tensor(out=xci, in0=xci, in1=self.pO[o][:], op=ALU.add), [self.bpO[o], bxci], [bxci])
                            P.dma("sp", self.X[r0:r0 + 128, hf * 512:(hf + 1) * 512], xci, [bxci], [self.bX[u]], ("xcs", hf))

    def dilated_phase(self, li, first):
        P = self.P
        T, TP, NU = self.T, self.TP, self.NU
        QT = P.dram("dQT", [3, D, T], BF16).ap(); bQT = P.buf("dQT")
        KT = P.dram("dKT", [3, D, TP], BF16).ap(); bKT = P.buf("dKT")
        VA = P.dram("dVA", [3, TP, VW], BF16).ap(); bVA = P.buf("dVA")
        ACC = P.dram("dACC", [3, T, VW], F32).ap(); bACC = P.buf("dACC")
        self.phase_begin()
        self.carve_common()
        ws = [self.carve(8 * 512, "p (k n) -> p k n", k=8) for _ in range(NSLOT)]; bws = [P.buf() for _ in range(NSLOT)]
        rowb = [self.carve(UNIT) for _ in range(2)]; browb = [P.buf() for _ in range(2)]
        vst = [self.carve(8 * (DHD + 1), "p (h d) -> p h d", h=8) for _ in range(2)]; bvst = [P.buf() for _ in range(2)]
        zt = self.carve(VW); bzt = P.buf()
        v8 = self.f32a[:, 0:NT_U * 8].rearrange("p (t e) -> p t e", t=NT_U); bv8 = self.bf32a
        P.op("dve", lambda e: e.memset(zt, 0.0), [], [bzt])
        zk = 0
        for g in range(3):
            for side in (0, PAD + T):
                for fo in range(8):
                    P.dma("sp", KT[g, fo * 128:(fo + 1) * 128, side:side + PAD], zt[:, 0:PAD], [bzt], [bKT], ("z", zk % 4)); zk += 1
                for j in range(PAD // 128):
                    P.dma("sp", VA[g, side + j * 128:side + (j + 1) * 128, :], zt, [bzt], [bVA], ("z", zk % 4)); zk += 1
        self.load_gain(G_MIX + li)
        wq_k = self.dqkv.rearrange("(kc p) n -> p kc n", p=128)
        wc = rb = vs = 0
        for u in range(NU):
            ub = u * UNIT
            self.load_unit(u, first)
            self.norm_unit()
            P.dma("sp", v8, self.valid8[ub:ub + UNIT, :].rearrange("(t p) e -> p t e", p=128), [], [bv8], "v8")
            blocks = [(c3, g, hf) for c3 in range(3) for g in range(3) for hf in range(2)]

            def load_blk(i, s):
                c3, g, hf = blocks[i]
                col = c3 * 3 * D + g * D + hf * 512
                P.dma("pool", ws[s], wq_k[:, :, col:col + 512], [], [bws[s]], ("ws", s))

            for i0 in range(NSLOT - 1):
                load_blk(i0, (wc + i0) % NSLOT)
            for bi, (c3, g, hf) in enumerate(blocks):
                s = wc % NSLOT
                if bi + NSLOT - 1 < len(blocks):
                    load_blk(bi + NSLOT - 1, (wc + NSLOT - 1) % NSLOT)
                if c3 < 2:
                    for fl in range(4):
                        r_ = rb % 2
                        rb += 1
                        for tt in range(4):
                            j = self.ab % 2
                            self.ab += 1
                            rd = [self.bxn[4 * tt + q] for q in range(4)]
                            for kc in range(8):
                                P.op("pe", lambda e, s=s, kc=kc, fl=fl, tt=tt, j=j: e.matmul(self.pA[j][:], lhsT=ws[s][:, kc, fl * 128:(fl + 1) * 128],
                                                                                            rhs=self.xnt[:, kc, tt * 512:(tt + 1) * 512], start=(kc == 0), stop=(kc == 7)),
                                     [bws[s]] + rd, [self.bpA[j]])
                            eng = "act" if tt % 2 == 0 else "dve"
                            if eng == "act":
                                P.op("act", lambda e, r_=r_, tt=tt, j=j: e.activation(out=rowb[r_][:, tt * 512:(tt + 1) * 512], in_=self.pA[j][:], func=AF.Copy),
                                     [self.bpA[j]], [browb[r_]])
                            else:
                                P.op("dve", lambda e, r_=r_, tt=tt, j=j: e.tensor_copy(out=rowb[r_][:, tt * 512:(tt + 1) * 512], in_=self.pA[j][:]),
                                     [self.bpA[j]], [browb[r_]])
                        fr = (hf * 4 + fl) * 128
                        if c3 == 0:
                            P.dma("sp", QT[g, fr:fr + 128, ub:ub + UNIT], rowb[r_], [browb[r_]], [bQT], ("rowb", r_))
                        else:
                            P.dma("sp", KT[g, fr:fr + 128, PAD + ub:PAD + ub + UNIT], rowb[r_], [browb[r_]], [bKT], ("rowb", r_))
                else:
                    for t in range(NT_U):
                        j = self.ab % 2
                        self.ab += 1
                        v_ = vs % 2
                        vs += 1
                        for kc in range(8):
                            P.op("pe", lambda e, s=s, kc=kc, t=t, j=j: e.matmul(self.pB[j][:], lhsT=self.xnt[:, kc, t * 128:(t + 1) * 128], rhs=ws[s][:, kc, :],
                                                                               start=(kc == 0), stop=(kc == 7)), [bws[s], self.bxn[t]], [self.bpB[j]])
                        P.op("act", lambda e, v_=v_, j=j, t=t: e.activation(out=vst[v_][:, :, 0:DHD], in_=self.pB[j][:].rearrange("p (h d) -> p h d", h=8), func=AF.Copy,
                                                                         scale=v8[:, t, 0:1]), [self.bpB[j], bv8], [bvst[v_]])
                        P.op("dve", lambda e, v_=v_, t=t: e.tensor_copy(out=vst[v_][:, :, DHD], in_=v8[:, t, :]), [bv8], [bvst[v_]])
                        P.dma("sp", VA[g, PAD + ub + t * 128:PAD + ub + (t + 1) * 128, hf * 520:(hf + 1) * 520], vst[v_].rearrange("p h d -> p (h d)"),
                              [bvst[v_]], [bVA], ("vst", v_))
                wc += 1
        self.phase_begin()
        Eall = self.carve(3 * DH * 256, "p (g h c) -> p g h c", g=3, h=DH); bE = P.buf()
        qt = [self.carve(UNIT) for _ in range(3)]; bqt = [P.buf() for _ in range(3)]
        kw = [UNIT + 128 * r for r in DIL_R]
        kt = [self.carve(kw[g]) for g in range(3)]; bkt = [P.buf() for _ in range(3)]
        vsub = [self.carve(32 * 130, "p (c d) -> p c d", c=32) for _ in range(2)]; bvsub = [P.buf() for _ in range(2)]
        pexp = [self.carve(512) for _ in range(2)]; bpexp = [P.buf() for _ in range(2)]
        pmul = [self.carve(512) for _ in range(2)]; bpmul = [P.buf() for _ in range(2)]
        xflat = self.xh[:].rearrange("p a b -> p (a b)")
        stg = [xflat[:, i * 2080:(i + 1) * 2080].rearrange("p (b d) -> p b d", b=16) for i in range(2)]; bstg = [P.buf() for _ in range(2)]
        ebuf = self.f32a[:, 0:256]
        for g in range(3):
            for h in range(DH):
                P.dma("sp", ebuf, self.dbias[g, h], [], [self.bf32a], "eb")
                P.op("act", lambda e, g=g, h=h: e.activation(out=Eall[:, g, h, :], in_=ebuf, func=AF.Exp), [self.bf32a], [bE])
        vi = pe_ = si = 0
        for u in range(NU):
            ub = u * UNIT
            for hp in range(8):
                for g, r in enumerate(DIL_R):
                    P.dma("sp", qt[g], QT[g, hp * 128:(hp + 1) * 128, ub:ub + UNIT], [bQT], [bqt[g]], ("qt", g))
                    lo = PAD + ub - 64 * r
                    P.dma("sp", kt[g], KT[g, hp * 128:(hp + 1) * 128, lo:lo + kw[g]], [bKT], [bkt[g]], ("kt", g))
                    qv = qt[g].rearrange("p (n r) -> p r n", r=r)
                    kv = kt[g].rearrange("p (n r) -> p r n", r=r)
                    nb = 16 // r
                    nch = nb + 1
                    for rho in range(r):
                        v_ = vi % 2
                        vi += 1
                        s_ = si % 2
                        si += 1
                        base = PAD + ub - 64 * r
                        vsrc = VA[g, base:base + 128 * r * nch, hp * 130:(hp + 1) * 130].rearrange("(c p r) d -> p c r d", p=128, r=r)[:, :, rho, :]
                        P.dma("sp", vsub[v_][:, 0:nch, :], vsrc, [bVA], [bvsub[v_]], ("vsub", v_))
                        for qb in range(nb):
                            j = self.ab % 2
                            self.ab += 1
                            x_ = pe_ % 2
                            pe_ += 1
                            for h2 in range(2):
                                bank, bbank = (self.pA[j], self.bpA[j]) if h2 == 0 else (self.pB[j], self.bpB[j])
                                for kc in range(2):
                                    c = qb + kc
                                    P.op("pe", lambda e, h2=h2, kc=kc, c=c, qb=qb, rho=rho, kv=kv, qv=qv, bank=bank: e.matmul(
                                        bank[:, kc * 128:(kc + 1) * 128],
                                        lhsT=kv[h2 * 64:(h2 + 1) * 64, rho, c * 128:(c + 1) * 128],
                                        rhs=qv[h2 * 64:(h2 + 1) * 64, rho, qb * 128:(qb + 1) * 128], start=True, stop=True),
                                         [bkt[g], bqt[g]], [bbank])
                            for h2 in range(2):
                                bank, bbank = (self.pA[j], self.bpA[j]) if h2 == 0 else (self.pB[j], self.bpB[j])
                                P.op("act", lambda e, h2=h2, bank=bank, x_=x_: e.activation(out=pexp[x_][:, h2 * 256:(h2 + 1) * 256], in_=bank[:, 0:256], func=AF.Exp,
                                                                                           scale=DHD ** -0.5), [bbank], [bpexp[x_]])
                            P.op("dve", lambda e, g=g, hp=hp, x_=x_: e.tensor_tensor(out=pmul[x_], in0=pexp[x_], in1=Eall[:, g, 2 * hp:2 * hp + 2, :].rearrange("p h c -> p (h c)"),
                                                                                    op=ALU.mult), [bpexp[x_], bE], [bpmul[x_]])
                            o = self.io % 3
                            self.io += 1
                            for h2 in range(2):
                                for kc in range(2):
                                    c = qb + kc
                                    P.op("pe", lambda e, h2=h2, kc=kc, c=c, v_=v_, x_=x_, o=o: e.matmul(
                                        self.pO[o][:, h2 * 65:(h2 + 1) * 65], lhsT=pmul[x_][:, (h2 * 2 + kc) * 128:(h2 * 2 + kc + 1) * 128],
                                        rhs=vsub[v_][:, c, h2 * 65:(h2 + 1) * 65], start=(kc == 0), stop=(kc == 1)),
                                         [bpmul[x_], bvsub[v_]], [self.bpO[o]])
                            P.op("act", lambda e, s_=s_, qb=qb, o=o: e.activation(out=stg[s_][:, qb, :], in_=self.pO[o][:, 0:130], func=AF.Copy), [self.bpO[o]], [bstg[s_]])
                        adst = ACC[g, ub:ub + 128 * r * nb, hp * 130:(hp + 1) * 130].rearrange("(b p r) d -> p b r d", p=128, r=r)[:, :, rho, :]
                        P.dma("sp", adst, stg[s_][:, 0:nb, :], [bstg[s_]], [bACC], ("stg", s_))
        self.phase_begin()
        self.carve_common()
        wo_ = self.carve(8 * D, "p (k n) -> p k n", k=8); bwo_ = P.buf()
        ob = self.carve(4 * D, "p (a d) -> p a d", a=4); bob = [P.buf() for _ in range(4)]
        oT = self.carve(8 * 512, "p (k n) -> p k n", k=8); boT = [P.buf() for _ in range(4)]
        acc1 = self.sq[:, 0:VW].rearrange("p (h d) -> p h d", h=DH); bacc1 = self.bsq
        acc2 = self.f32a[:, 0:VW].rearrange("p (h d) -> p h d", h=DH); bacc2 = self.bf32a
        P.dma("pool", wo_, self.dwo.rearrange("(kc p) n -> p kc n", p=128), [], [bwo_], "dwo")
        for u in range(NU):
            ub = u * UNIT
            self.load_unit(u, first)
            for tt in range(4):
                for sub in range(4):
                    t = 4 * tt + sub
                    r0 = ub + t * 128
                    P.dma("sp", acc1, ACC[0, r0:r0 + 128, :].rearrange("p (h d) -> p h d", h=DH), [bACC], [bacc1], "acc1")
                    for g in (1, 2):
                        P.dma("sp", acc2, ACC[g, r0:r0 + 128, :].rearrange("p (h d) -> p h d", h=DH), [bACC], [bacc2], "acc2")
                        P.op("dve", lambda e: e.tensor_tensor(out=acc1, in0=acc1, in1=acc2, op=ALU.add), [bacc1, bacc2], [bacc1])
                    r = self.xk % 2
                    self.xk += 1
                    P.op("dve", lambda e, r=r: e.reciprocal(out=self.rc[r][:], in_=acc1[:, :, DHD]), [bacc1], [self.brc[r]])
                    for h in range(DH):
                        P.op("dve", lambda e, h=h, r=r, sub=sub: e.tensor_scalar(out=ob[:, sub, h * DHD:(h + 1) * DHD], in0=acc1[:, h, 0:DHD],
                                                                                scalar1=self.rc[r][:, h:h + 1], scalar2=None, op0=ALU.mult),
                             [bacc1, self.brc[r]], [bob[sub]])
                    self.transpose8(ob[:, sub, :], bob[sub], oT[:, :, sub * 128:(sub + 1) * 128], [boT[sub]])
                self.out_proj_add(tt, oT, boT, wo_, bwo_)
            self.store_unit(u)

    def out_phase(self, final, first):
        P = self.P
        self.phase_begin()
        y_t = self.y.rearrange("(t p) d -> p t d", p=128)
        if final:
            self.load_gain(G_FINAL)
        for u in range(self.NU):
            self.load_unit(u, first)
            if final:
                for t in range(NT_U):
                    i = self.rms_rstd(self.xh[:, t, :], [self.bx[t]])
                    P.op("dve", lambda e, t=t, i=i: e.scalar_tensor_tensor(out=self.xh[:, t, :], in0=self.xh[:, t, :], scalar=self.ss[i][:], in1=self.gt[:],
                                                                          op0=ALU.mult, op1=ALU.mult), [self.bx[t], self.bss[i], self.bg], [self.bx[t]])
            self.outs.append(P.dma("sp", y_t[:, u * NT_U:(u + 1) * NT_U, :], self.xh[:], list(self.bx), [self.bY[u]], ("y", u % 2)))

    def build(self):
        first = True
        windowed = False
        for ph in self.phases:
            if ph == "final":
                continue
            kind, li = ph.split(":")
            li = int(li)
            if self.T1 and not windowed and (li == 1 or kind in ("cross", "ffn2")):
                assert not first, "the window needs a whole-sequence phase before it"
                self.enter_window()
                windowed = True
            if kind == "ffn1":
                self.ffn_phase(2 * li, G_FFN1 + li, first)
            elif kind == "ffn2":
                self.ffn_phase(2 * li + 1, G_FFN2 + li, first)
            elif kind == "cross":
                self.cross_phase(li, first)
            elif kind == "mix" and li == 1:
                self.dilated_phase(li, first)
            elif kind == "mix":
                self.gla_phase(li, first)
            else:
                raise ValueError(ph)
            first = False
            self._gather = False
        if self.dbg:
            self._dbg_src = self.dbgt[self.dbg]
        self.out_phase("final" in self.phases, first)
        self.P.emit(self.outs)
        return self.nc


_j = np.arange(128)[:, None]; _i = np.arange(128)[None, :]
GLA_TRI = np.stack([np.where(_j <= _i, -1.0 / 16.0, 0.0), np.where(_j >= _i, -1.0 / 16.0, 0.0),
                    np.where(_j <= _i, 1.0, 0.0), np.where(_j >= _i, 1.0, 0.0)]).astype(np.float32)


def pack_weights(inputs):
    f = lambda k: np.asarray(inputs[k], np.float32)
    gl = [f("norm_ffn1")[0], f("norm_ffn1")[1], f("norm_mix")[0], f("norm_mix")[1], f("norm_cross")[0], f("norm_cross")[1],
          f("norm_mem")[0], f("norm_mem")[1], f("norm_ffn2")[0], f("norm_ffn2")[1], f("norm_final")]
    gains = np.ascontiguousarray(np.broadcast_to(np.stack(gl)[:, None, :], (11, 128, D)))
    p = np.arange(128)[:, None]; q = np.arange(128)[None, :]
    rb = f("rel_bias")
    dbias = np.full((3, DH, 128, 256), -30000.0, np.float32)
    for g, r in enumerate(DIL_R):
        for kc in range(2):
            m = p - q - 64 + 128 * kc
            ok = np.abs(m) <= 64
            bk = t5_bucket(m * r)
            for h in range(DH):
                tile = rb[bk, g * DH + h]
                dbias[g, h, :, kc * 128:(kc + 1) * 128] = np.where(ok, tile, dbias[g, h, :, kc * 128:(kc + 1) * 128])
    return {
        "gains": gains, "ident": np.eye(128, dtype=np.float32),
        "ffn_in": np.ascontiguousarray(np.stack([f("ffn1_in")[0], f("ffn2_in")[0], f("ffn1_in")[1], f("ffn2_in")[1]])),
        "ffn_out": np.ascontiguousarray(np.stack([f("ffn1_out")[0], f("ffn2_out")[0], f("ffn1_out")[1], f("ffn2_out")[1]])),
        "cross_q": f("cross_w_q"), "cross_kv": f("cross_w_kv"), "cross_o": f("cross_w_o"),
        "gla_w_in": np.ascontiguousarray(f("gla_w_in")[0]),
        "gla_wgb": np.ascontiguousarray(np.stack([np.concatenate([f("gla_wg_f")[0], f("gla_bg_f")[0][None, :]], 0),
                                                  np.concatenate([f("gla_wg_b")[0], f("gla_bg_b")[0][None, :]], 0)])),
        "gla_norm": np.ascontiguousarray(np.broadcast_to(f("gla_norm")[0][None, :], (128, D))),
        "gla_w_out": np.ascontiguousarray(f("gla_w_out")[0]),
        "gla_tri": GLA_TRI,
        "dil_qkv": np.ascontiguousarray(f("dil_w_qkv")[0]), "dil_o": np.ascontiguousarray(f("dil_w_out")[0]), "dil_bias": dbias,
    }


def run_seqs(seqs, mems, inputs, T, phases, dbg=None):
    n = 8
    w = pack_weights(inputs)
    nc = K(T, phases, dbg).build()
    in_maps = []
    for c in range(n):
        xs = np.zeros((T, D), np.float32)
        mm = np.zeros((MEM, D), np.float32)
        v8 = np.zeros((T, 8), np.float32)
        if c < len(seqs):
            xs[:seqs[c].shape[0]] = seqs[c]
            mm[:] = mems[c]
            v8[:seqs[c].shape[0]] = 1.0
        in_maps.append(dict(w, x=xs, mem=mm, valid8=v8))
    res = run_bass_kernel_spmd(nc, in_maps, core_ids=list(range(n)))
    return [np.asarray(res.results[c]["y"][:seqs[c].shape[0]], np.float32) for c in range(len(seqs))]


WIN = 4096
T1W = WIN + 2 * PAD


def kernel(**inputs):
    xp = np.asarray(inputs["x_prompt"], np.float32)
    xs = np.asarray(inputs["x_sample"], np.float32)
    mp = np.asarray(inputs["mem_prompt"], np.float32)
    ms = np.asarray(inputs["mem_sample"], np.float32)
    seqs = [xp[b] for b in range(xp.shape[0])] + [xs[b] for b in range(xs.shape[0])]
    mems = [mp[b] for b in range(mp.shape[0])] + [ms[b] for b in range(ms.shape[0])]
    jobs = [(si, w0) for si, sq in enumerate(seqs) for w0 in range(0, sq.shape[0], WIN)]
    n = 8
    assert len(jobs) <= n, len(jobs)
    T0 = -(-max(sq.shape[0] for sq in seqs) // UNIT) * UNIT
    w = pack_weights(inputs)
    nc = K(T0, FULL_PHASES, T1=T1W).build()
    in_maps = []
    for c in range(n):
        x = np.zeros((T0, D), np.float32)
        mm = np.zeros((MEM, D), np.float32)
        v8 = np.zeros((T1W, 8), np.float32)
        idx = np.zeros((T1W,), np.int32)
        if c < len(jobs):
            si, w0 = jobs[c]
            S = seqs[si].shape[0]
            x[:S] = seqs[si]
            mm[:] = mems[si]
            rows = np.arange(w0 - PAD, w0 + WIN + PAD)
            ok = (rows >= 0) & (rows < S)
            v8[ok] = 1.0
            idx[:] = np.clip(rows, 0, S - 1)
        in_maps.append(dict(w, x=x, mem=mm, valid8=v8, win_idx=np.ascontiguousarray(idx.reshape(T1W // 128, 128).T)))
    res = run_bass_kernel_spmd(nc, in_maps, core_ids=list(range(n)))
    outs = [np.zeros(sq.shape, np.float32) for sq in seqs]
    for c, (si, w0) in enumerate(jobs):
        S = seqs[si].shape[0]
        hi = min(w0 + WIN, S)
        outs[si][w0:hi] = np.asarray(res.results[c]["y"][PAD:PAD + (hi - w0)], np.float32)
    yp = np.stack(outs[:xp.shape[0]]).astype(np.float32)
    ys = np.stack(outs[xp.shape[0]:]).astype(np.float32)
    return (yp, ys)
```

```python
from contextlib import ExitStack
import numpy as np
import concourse.bass as bass
import concourse.mybir as mybir

F32 = mybir.dt.float32
BF16 = mybir.dt.bfloat16
AF = mybir.ActivationFunctionType
ALU = mybir.AluOpType
AX = mybir.AxisListType

COMPUTE = ("pe", "act", "dve", "pool")
ENGS = ("pe", "act", "dve", "pool", "sp")


class Buf:
    __slots__ = ("name", "w", "rs", "excl")

    def __init__(self, name, excl=False):
        self.name = name
        self.w = None
        self.rs = []
        self.excl = excl


class Op:
    __slots__ = ("eng", "fn", "dma", "deps", "idx", "sig", "val", "semkey", "dsem", "dval", "pos", "inc")

    def __init__(self, eng, fn, dma, semkey, inc=16):
        self.inc = inc
        self.eng = eng
        self.fn = fn
        self.dma = dma
        self.deps = []
        self.sig = False
        self.val = 0
        self.semkey = semkey
        self.dsem = None
        self.dval = 0


class Prog:
    def __init__(self, nc):
        self.nc = nc
        self.ops = []
        self.es = ExitStack()
        self.last_dma = {}
        self.nbuf = 0
        self.last = {}
        self.pending = []

    def sbuf(self, name, shape, dt):
        return self.es.enter_context(self.nc.sbuf_tensor(name, list(shape), dt))

    def psum(self, name, shape, dt=F32):
        return self.es.enter_context(self.nc.psum_tensor(name, list(shape), dt))

    def dram(self, name, shape, dt, kind="Internal", addr_space="Local"):
        return self.nc.dram_tensor(name, list(shape), dt, kind=kind, addr_space=addr_space)

    def buf(self, name=None, excl=False):
        self.nbuf += 1
        return Buf(name or f"b{self.nbuf}", excl)

    def op(self, eng, fn, reads=(), writes=(), dma=False, semkey=None, inc=16):
        o = Op(eng, fn, dma, semkey, inc)
        deps = []

        def add(p, raw):
            if p is None or p is o:
                return
            if not p.dma and p.eng == eng:
                if not raw or eng == "pe":
                    return
            if p not in deps:
                deps.append(p)

        for b in reads:
            add(b.w, True)
            if b.excl:
                for r in b.rs:
                    if r.eng != eng:
                        add(r, False)
        for b in writes:
            add(b.w, False)
            for r in b.rs:
                add(r, False)
        if dma:
            assert semkey is not None
            add(self.last_dma.get(semkey), False)
            self.last_dma[semkey] = o
        for b in reads:
            if not dma:
                b.rs = [r for r in b.rs if r.dma or r.eng != eng]
            b.rs.append(o)
        for b in writes:
            b.w = o
            b.rs = []
        o.deps = deps
        o.pos = len(self.ops)
        self.ops.append(o)
        if fn is not None:
            self.last[eng] = o
        if dma:
            self.pending.append(o)
        return o

    def barrier(self):
        targets = [p for p in self.last.values()] + list(self.pending)
        self.pending = []
        for eng in ENGS:
            o = Op(eng, None, False, None)
            o.deps = [p for p in dict.fromkeys(targets)]
            o.pos = len(self.ops)
            self.ops.append(o)

    def dma(self, eng, out, in_, reads, writes, semkey, **kw):
        return self.op(eng, lambda e: e.dma_start(out=out, in_=in_, **kw), reads, writes,
                       dma=True, semkey=semkey)

    def emit(self, final_wait_ops=()):
        nc = self.nc
        es = self.es
        fin = self.op("sp", None, reads=(), writes=())
        for p in final_wait_ops:
            if p not in fin.deps:
                fin.deps.append(p)
                p.sig = True
        for o in self.ops:
            for p in o.deps:
                p.sig = True
        esem = {e: es.enter_context(nc.semaphore(f"c_{e}")) for e in COMPUTE}
        dsems = {}
        ecount = {e: 0 for e in COMPUTE}
        dcount = {}
        for o in self.ops:
            if o.dma:
                if o.semkey not in dsems:
                    dsems[o.semkey] = es.enter_context(nc.semaphore(f"d{len(dsems)}"))
                    dcount[o.semkey] = 0
                dcount[o.semkey] += o.inc
                o.dsem = dsems[o.semkey]
                o.dval = dcount[o.semkey]
            elif o.sig:
                assert o.eng in COMPUTE, ("sp non-dma op cannot signal", o.eng)
                ecount[o.eng] += 1
                o.val = ecount[o.eng]
        self.n_dsem = len(dsems)
        per = {e: [] for e in ENGS}
        for o in self.ops:
            per[o.eng].append(o)
        handles = {"pe": "tensor", "act": "scalar", "dve": "vector", "pool": "gpsimd", "sp": "sync"}

        def run(ename, eh):
            known = {}
            for o in per[ename]:
                need = {}
                for p in o.deps:
                    if p.dma:
                        k, s, v = ("d", p.semkey), p.dsem, p.dval
                    else:
                        k, s, v = ("e", p.eng), esem[p.eng], p.val
                    if known.get(k, 0) >= v:
                        continue
                    if k not in need or need[k][1] < v:
                        need[k] = (s, v)
                for k, (s, v) in need.items():
                    eh.wait_ge(s, v)
                    known[k] = v
                if o.fn is None:
                    continue
                ins = o.fn(eh)
                if o.dma:
                    ins.then_inc(o.dsem, o.inc)
                elif o.sig:
                    ins.then_inc(esem[o.eng], 1)

        with nc.Block() as block:
            for ename in ENGS:
                if not per[ename]:
                    continue
                getattr(block, handles[ename])(lambda eh, _n=ename: run(_n, eh))
        es.close()
        return nc


from concourse.bass_utils import run_bass_kernel_spmd

D = 1024
DFF = 2816
NFF = DFF // 128
UNIT = 2048
NT_U = UNIT // 128
MEM = 256
XH = 4
XHD = 256
EPS = 1e-6
NSLOT = 3
ARENA = 40960
GH, GDK, GDV = 4, 128, 256
DIL_R = (1, 4, 16)
DH = 16
DHD = 64
VW = DH * (DHD + 1)
PAD = 1024
NUM_BUCKETS = 32
MAX_DISTANCE = 1024
G_FFN1, G_MIX, G_CROSS, G_MEM, G_FFN2, G_FINAL = 0, 2, 4, 6, 8, 10
FULL_PHASES = ("ffn1:0", "mix:0", "cross:0", "ffn2:0", "ffn1:1", "mix:1", "cross:1", "ffn2:1", "final")


def t5_bucket(rel):
    half = NUM_BUCKETS // 2
    max_exact = half // 2
    ret = (rel > 0).astype(np.int32) * half
    n = np.abs(rel)
    large = max_exact + (np.log(np.maximum(n, 1) / max_exact) / np.log(MAX_DISTANCE / max_exact) * (half - max_exact)).astype(np.int32)
    large = np.minimum(large, half - 1)
    return (ret + np.where(n < max_exact, n, large)).astype(np.int32)


class K:
    def __init__(self, T, phases, dbg=None, T1=None):
        self.T1 = T1
        self.dbg = dbg
        self.dbgt = {}
        assert T % UNIT == 0
        self.T, self.NU, self.phases = T, T // UNIT, tuple(phases)
        self.TP = T + 2 * PAD
        nc = self.nc = bass.Bass("TRN2", target_bir_lowering=False)
        P = self.P = Prog(nc)
        inp = lambda n, s: nc.dram_tensor(n, list(s), F32, kind="ExternalInput").ap()
        self.x_in = inp("x", [T, D])
        self.mem_in = inp("mem", [MEM, D])
        self.gains = inp("gains", [11, 128, D])
        self.ident = inp("ident", [128, 128])
        self.ffn_in = inp("ffn_in", [4, D, 2 * DFF])
        self.ffn_out = inp("ffn_out", [4, DFF, D])
        self.cq = inp("cross_q", [2, D, D])
        self.ckv = inp("cross_kv", [2, D, 2 * D])
        self.co = inp("cross_o", [2, D, D])
        self.dqkv = inp("dil_qkv", [D, 9 * D])
        self.dwo = inp("dil_o", [D, D])
        self.dbias = inp("dil_bias", [3, DH, 128, 256])
        self.gwin = inp("gla_w_in", [D, 3104])
        self.gwgb = inp("gla_wgb", [2, 17, 512])
        self.gnorm = inp("gla_norm", [128, D])
        self.gwout = inp("gla_w_out", [D, D])
        self.gtri = inp("gla_tri", [4, 128, 128])
        TL = T1 or T
        self.valid8 = inp("valid8", [TL, 8])
        self.y = nc.dram_tensor("y", [TL, D], F32, kind="ExternalOutput").ap()
        self.X = P.dram("Xres", [T, D], F32).ap()
        self.bX = [P.buf(f"X{u}") for u in range(self.NU)]
        self.bY = [P.buf(f"Y{u}") for u in range(TL // UNIT)]
        self._gather = False
        if T1:
            self.win_idx = nc.dram_tensor("win_idx", [128, T1 // 128], mybir.dt.int32, kind="ExternalInput").ap()
            self.idxs = P.sbuf("idxs", [128, T1 // 128], mybir.dt.int32); self.bidx = P.buf("idxs")
            P.dma("sp", self.idxs[:], self.win_idx, [], [self.bidx], "idxs")
            self.X1 = P.dram("Xres1", [T1, D], F32).ap()
        self.bXg = P.buf("Xg")
        self.xh = P.sbuf("xh", [128, NT_U, D], F32); self.bx = [P.buf(f"x{t}") for t in range(NT_U)]
        self.xnt = P.sbuf("xnt", [128, 8, UNIT], BF16); self.bxn = [P.buf(f"xn{t}") for t in range(NT_U)]
        self.gt = P.sbuf("gt", [128, D], F32); self.bg = P.buf("g")
        self.idf = P.sbuf("idf", [128, 128], F32); self.bidf = P.buf("idf")
        self.idb = P.sbuf("idb", [128, 128], BF16); self.bidb = P.buf("idb")
        self.sq = P.sbuf("sq", [128, VW], F32); self.bsq = P.buf("sq")
        self.f32a = P.sbuf("f32a", [128, VW], F32); self.bf32a = P.buf("f32a")
        self.sil = [P.sbuf(f"sil{i}", [128, 512], F32) for i in range(2)]; self.bsil = [P.buf(f"sil{i}") for i in range(2)]
        self.ss = [P.sbuf(f"ss{i}", [128, 1], F32) for i in range(2)]; self.bss = [P.buf(f"ss{i}") for i in range(2)]
        self.rc = [P.sbuf(f"rc{i}", [128, 16], F32) for i in range(2)]; self.brc = [P.buf(f"rc{i}") for i in range(2)]
        self.AB = P.sbuf("AB", [128, ARENA], BF16)
        self.pA = [P.psum(f"pA{i}", [128, 512]) for i in range(2)]; self.bpA = [P.buf(f"pA{i}", excl=True) for i in range(2)]
        self.pB = [P.psum(f"pB{i}", [128, 512]) for i in range(2)]; self.bpB = [P.buf(f"pB{i}", excl=True) for i in range(2)]
        self.pO = [P.psum(f"pO{i}", [128, 512]) for i in range(3)]; self.bpO = [P.buf(f"pO{i}", excl=True) for i in range(3)]
        self.pT = P.psum("pT", [128, 8 * 128], BF16); self.bpT = P.buf("pT", excl=True)
        self.io = self.ab = self.wcnt = self.nrm = self.xk = 0
        self.outs = []
        P.dma("sp", self.idf[:], self.ident, [], [self.bidf], "id")
        P.op("act", lambda e: e.activation(out=self.idb[:], in_=self.idf[:], func=AF.Copy), [self.bidf], [self.bidb])

    def phase_begin(self):
        self.P.barrier()
        self.aoff = 0

    def carve(self, n, pat=None, **kw):
        assert self.aoff + n <= ARENA, ("arena overflow", self.aoff, n)
        v = self.AB[:, self.aoff:self.aoff + n]
        self.aoff += n
        return v.rearrange(pat, **kw) if pat else v

    def carve_common(self):
        self.xnb = [self.carve(D) for _ in range(2)]; self.bxnb = [self.P.buf() for _ in range(2)]

    def src(self, first):
        if getattr(self, "_dbg_src", None) is not None:
            return self._dbg_src.rearrange("(t p) d -> p t d", p=128)
        return (self.x_in if first else self.X).rearrange("(t p) d -> p t d", p=128)

    def load_gain(self, row):
        self.P.dma("sp", self.gt[:], self.gains[row], [], [self.bg], "g")

    def enter_window(self):
        self.X0, self.bX0 = self.X, list(self.bX)
        self.T, self.NU, self.TP = self.T1, self.T1 // UNIT, self.T1 + 2 * PAD
        self.X = self.X1
        self.bX = [self.P.buf(f"X1_{u}") for u in range(self.NU)]
        self._gather = True

    def load_unit(self, u, first):
        P = self.P
        if self._gather:
            X0, idxs = self.X0, self.idxs
            for t in range(NT_U):
                k = u * NT_U + t
                P.op("pool", lambda e, t=t, k=k: e.indirect_dma_start(out=self.xh[:, t, :], out_offset=None, in_=X0,
                                                                     in_offset=bass.IndirectOffsetOnAxis(ap=idxs[:, k:k + 1], axis=0)),
                     list(self.bX0) + [self.bidx], [self.bx[t]], dma=True, semkey=("gx", t % 4))
            return
        rd = [] if first else [self.bX[u]]
        for t in range(NT_U):
            P.dma("sp", self.xh[:, t, :], self.src(first)[:, u * NT_U + t, :], rd, [self.bx[t]], ("x", t % 4))

    def store_unit(self, u):
        X_t = self.X.rearrange("(t p) d -> p t d", p=128)
        self.P.dma("sp", X_t[:, u * NT_U:(u + 1) * NT_U, :], self.xh[:], list(self.bx), [self.bX[u]], ("xs", u % 2))

    def rms_rstd(self, src_ap, rdbufs):
        P = self.P
        i = self.nrm % 2
        self.nrm += 1
        P.op("dve", lambda e: e.tensor_tensor(out=self.sq[:, 0:D], in0=src_ap, in1=src_ap, op=ALU.mult), rdbufs, [self.bsq])
        P.op("dve", lambda e: e.reduce_sum(out=self.ss[i][:], in_=self.sq[:, 0:D], axis=AX.X), [self.bsq], [self.bss[i]])
        P.op("act", lambda e: e.activation(out=self.ss[i][:], in_=self.ss[i][:], func=AF.Ln, bias=EPS, scale=1.0 / D), [self.bss[i]], [self.bss[i]])
        P.op("act", lambda e: e.activation(out=self.ss[i][:], in_=self.ss[i][:], func=AF.Exp, scale=-0.5), [self.bss[i]], [self.bss[i]])
        return i

    def norm_T(self, src_ap, rdbufs, dst_ap, dstbufs):
        P = self.P
        i = self.rms_rstd(src_ap, rdbufs)
        xnb_i = self.xnb[i]
        P.op("dve", lambda e: e.scalar_tensor_tensor(out=xnb_i, in0=src_ap, scalar=self.ss[i][:], in1=self.gt[:],
                                                     op0=ALU.mult, op1=ALU.mult), rdbufs + [self.bss[i], self.bg], [self.bxnb[i]])
        self.transpose8(xnb_i, self.bxnb[i], dst_ap, dstbufs)

    def transpose8(self, src_ap, srcbuf, dst_ap, dstbufs):
        P = self.P
        for kc in range(8):
            P.op("pe", lambda e, kc=kc: e.transpose(self.pT[:, kc * 128:(kc + 1) * 128], src_ap[:, kc * 128:(kc + 1) * 128], self.idb[:]),
                 [srcbuf, self.bidb], [self.bpT])
        P.op("act", lambda e: e.activation(out=dst_ap, in_=self.pT[:].rearrange("p (k n) -> p k n", k=8), func=AF.Copy), [self.bpT], dstbufs)

    def norm_unit(self):
        for t in range(NT_U):
            self.norm_T(self.xh[:, t, :], [self.bx[t]], self.xnt[:, :, t * 128:(t + 1) * 128], [self.bxn[t]])

    def out_proj_add(self, tt, oT, boT, w, bw):
        P = self.P
        for sub in range(4):
            t = 4 * tt + sub
            for hf in range(2):
                o = self.io % 3
                self.io += 1
                for fc in range(8):
                    P.op("pe", lambda e, fc=fc, sub=sub, hf=hf, o=o: e.matmul(self.pO[o][:], lhsT=oT[:, fc, sub * 128:(sub + 1) * 128],
                                                                             rhs=w[:, fc, hf * 512:(hf + 1) * 512], start=(fc == 0), stop=(fc == 7)),
                         [boT[sub], bw], [self.bpO[o]])
                P.op("dve", lambda e, t=t, hf=hf, o=o: e.tensor_tensor(out=self.xh[:, t, hf * 512:(hf + 1) * 512], in0=self.xh[:, t, hf * 512:(hf + 1) * 512],
                                                                      in1=self.pO[o][:], op=ALU.add), [self.bpO[o], self.bx[t]], [self.bx[t]])

    def ffn_phase(self, fi, grow, first):
        P = self.P
        FG, NSL = 4, 8
        self.phase_begin()
        self.carve_common()
        wa = [self.carve(1024, "p (k n) -> p k n", k=8) for _ in range(NSL)]; bwa = [P.buf() for _ in range(NSL)]
        wb = [self.carve(1024, "p (k n) -> p k n", k=8) for _ in range(NSL)]; bwb = [P.buf() for _ in range(NSL)]
        wo = [self.carve(D) for _ in range(NSL)]; bwo = [P.buf() for _ in range(NSL)]
        act = [[self.carve(512) for _ in range(FG)] for _ in range(2)]; bact = [[P.buf() for _ in range(FG)] for _ in range(2)]
        w_in_k = self.ffn_in[fi].rearrange("(kc p) n -> p kc n", p=128)
        w_out = self.ffn_out[fi]
        groups = [list(range(c0, min(c0 + FG, NFF))) for c0 in range(0, NFF, FG)]

        def load_w(c):
            s = c % NSL
            P.dma("pool", wa[s], w_in_k[:, :, c * 128:(c + 1) * 128], [], [bwa[s]], ("wa", s))
            P.dma("pool", wb[s], w_in_k[:, :, DFF + c * 128:DFF + (c + 1) * 128], [], [bwb[s]], ("wb", s))
            P.dma("pool", wo[s], w_out[c * 128:(c + 1) * 128, :], [], [bwo[s]], ("wo", s))

        self.load_gain(grow)
        for u in range(self.NU):
            self.load_unit(u, first)
            self.norm_unit()
            for c in groups[0]:
                load_w(c)
            for gi, grp in enumerate(groups):
                if gi + 1 < len(groups):
                    for c in groups[gi + 1]:
                        load_w(c)
                def in_proj(tt):
                    par = tt % 2
                    rd = [self.bxn[4 * tt + q] for q in range(4)]
                    for g, c in enumerate(grp):
                        s = c % NSL
                        j = self.ab % 2
                        self.ab += 1
                        a_g, ba_g = act[par][g], bact[par][g]
                        for kc in range(8):
                            P.op("pe", lambda e, s=s, kc=kc, tt=tt, j=j: e.matmul(self.pA[j][:], lhsT=wa[s][:, kc, :], rhs=self.xnt[:, kc, tt * 512:(tt + 1) * 512],
                                                                                 start=(kc == 0), stop=(kc == 7)), [bwa[s]] + rd, [self.bpA[j]])
                        for kc in range(8):
                            P.op("pe", lambda e, s=s, kc=kc, tt=tt, j=j: e.matmul(self.pB[j][:], lhsT=wb[s][:, kc, :], rhs=self.xnt[:, kc, tt * 512:(tt + 1) * 512],
                                                                                 start=(kc == 0), stop=(kc == 7)), [bwb[s]] + rd, [self.bpB[j]])
                        P.op("act", lambda e, j=j: e.activation(out=self.sil[j][:], in_=self.pA[j][:], func=AF.Silu), [self.bpA[j]], [self.bsil[j]])
                        P.op("dve", lambda e, j=j, a_g=a_g: e.tensor_tensor(out=a_g, in0=self.sil[j][:], in1=self.pB[j][:], op=ALU.mult),
                             [self.bsil[j], self.bpB[j]], [ba_g])

                def out_proj(tt):
                    par = tt % 2
                    for sub in range(4):
                        t = 4 * tt + sub
                        for h in range(2):
                            o = self.io % 3
                            self.io += 1
                            for g, c in enumerate(grp):
                                s = c % NSL
                                a_g, ba_g = act[par][g], bact[par][g]
                                P.op("pe", lambda e, s=s, sub=sub, h=h, o=o, a_g=a_g, g=g, n=len(grp): e.matmul(
                                    self.pO[o][:], lhsT=a_g[:, sub * 128:(sub + 1) * 128], rhs=wo[s][:, h * 512:(h + 1) * 512], start=(g == 0), stop=(g == n - 1)),
                                     [ba_g, bwo[s]], [self.bpO[o]])
                            P.op("dve", lambda e, t=t, h=h, o=o: e.scalar_tensor_tensor(out=self.xh[:, t, h * 512:(h + 1) * 512], in0=self.pO[o][:], scalar=0.5,
                                                                                       in1=self.xh[:, t, h * 512:(h + 1) * 512], op0=ALU.mult, op1=ALU.add),
                                 [self.bpO[o], self.bx[t]], [self.bx[t]])

                ntt = UNIT // 512
                in_proj(0)
                for tt in range(ntt):
                    if tt + 1 < ntt:
                        in_proj(tt + 1)
                    out_proj(tt)
            self.store_unit(u)

    def cross_phase(self, li, first):
        P = self.P
        self.phase_begin()
        self.carve_common()
        wbig = [self.carve(8 * D, "p (k n) -> p k n", k=8) for _ in range(2)]; bwbig = [P.buf() for _ in range(2)]
        memT = self.carve(8 * MEM, "p (k n) -> p k n", k=8); bmemT = P.buf()
        kT = self.carve(8 * MEM, "p (k n) -> p k n", k=8); bkT = P.buf()
        vA = self.carve(2 * XH * (XHD + 1), "p (a h d) -> p a h d", a=2, h=XH); bvA = P.buf()
        qT = self.carve(8 * 512, "p (k n) -> p k n", k=8); bqT = P.buf()
        pTs = [self.carve(512) for _ in range(2)]; bpTs = [P.buf() for _ in range(2)]
        ob = self.carve(4 * D, "p (a d) -> p a d", a=4); bob = [P.buf() for _ in range(4)]
        oT = self.carve(8 * 512, "p (k n) -> p k n", k=8); boT = [P.buf() for _ in range(4)]
        memf = self.f32a[:, 0:D]; bmemf = self.bf32a

        def load_big(i, w_ap):
            P.dma("pool", wbig[i], w_ap.rearrange("(kc p) n -> p kc n", p=128), [], [bwbig[i]], ("wbig", i))

        P.op("dve", lambda e: e.memset(vA, 1.0), [], [bvA])
        self.load_gain(G_MEM + li)
        for kt in range(2):
            P.dma("sp", memf[:], self.mem_in[kt * 128:(kt + 1) * 128, :], [], [bmemf], "memf")
            self.norm_T(memf[:], [bmemf], memT[:, :, kt * 128:(kt + 1) * 128], [bmemT])
        load_big(0, self.ckv[li][:, 0:D])
        for fo in range(8):
            j = self.ab % 2
            self.ab += 1
            for kc in range(8):
                P.op("pe", lambda e, fo=fo, kc=kc, j=j: e.matmul(self.pA[j][:, 0:MEM], lhsT=wbig[0][:, kc, fo * 128:(fo + 1) * 128], rhs=memT[:, kc, :],
                                                                start=(kc == 0), stop=(kc == 7)), [bwbig[0], bmemT], [self.bpA[j]])
            P.op("act", lambda e, fo=fo, j=j: e.activation(out=kT[:, fo, :], in_=self.pA[j][:, 0:MEM], func=AF.Copy), [self.bpA[j]], [bkT])
        load_big(0, self.ckv[li][:, D:2 * D])
        for kt in range(2):
            for hf in range(2):
                j = self.ab % 2
                self.ab += 1
                for kc in range(8):
                    P.op("pe", lambda e, kt=kt, hf=hf, kc=kc, j=j: e.matmul(self.pB[j][:], lhsT=memT[:, kc, kt * 128:(kt + 1) * 128],
                                                                           rhs=wbig[0][:, kc, hf * 512:(hf + 1) * 512], start=(kc == 0), stop=(kc == 7)),
                         [bwbig[0], bmemT], [self.bpB[j]])
                P.op("act", lambda e, kt=kt, hf=hf, j=j: e.activation(out=vA[:, kt, 2 * hf:2 * hf + 2, 0:XHD],
                                                                      in_=self.pB[j][:].rearrange("p (h d) -> p h d", h=2), func=AF.Copy), [self.bpB[j]], [bvA])
        load_big(0, self.cq[li])
        load_big(1, self.co[li])
        self.load_gain(G_CROSS + li)
        for u in range(self.NU):
            self.load_unit(u, first)
            self.norm_unit()
            for tt in range(UNIT // 512):
                rd = [self.bxn[4 * tt + q] for q in range(4)]
                for fo in range(8):
                    j = self.ab % 2
                    self.ab += 1
                    for kc in range(8):
                        P.op("pe", lambda e, fo=fo, kc=kc, tt=tt, j=j: e.matmul(self.pA[j][:], lhsT=wbig[0][:, kc, fo * 128:(fo + 1) * 128],
                                                                               rhs=self.xnt[:, kc, tt * 512:(tt + 1) * 512], start=(kc == 0), stop=(kc == 7)),
                             [bwbig[0]] + rd, [self.bpA[j]])
                    P.op("act", lambda e, fo=fo, j=j: e.activation(out=qT[:, fo, :], in_=self.pA[j][:], func=AF.Copy, scale=XHD ** -0.5),
                         [self.bpA[j]], [bqT])
                for h in range(XH):
                    for kt in range(2):
                        j = self.ab % 2
                        self.ab += 1
                        for dc in range(2):
                            P.op("pe", lambda e, h=h, kt=kt, dc=dc, j=j: e.matmul(self.pB[j][:], lhsT=kT[:, 2 * h + dc, kt * 128:(kt + 1) * 128],
                                                                                 rhs=qT[:, 2 * h + dc, :], start=(dc == 0), stop=(dc == 1)),
                                 [bkT, bqT], [self.bpB[j]])
                        P.op("act", lambda e, kt=kt, j=j: e.activation(out=pTs[kt], in_=self.pB[j][:], func=AF.Exp), [self.bpB[j]], [bpTs[kt]])
                    for sub in range(4):
                        o = self.io % 3
                        self.io += 1
                        r = self.xk % 2
                        self.xk += 1
                        for kt in range(2):
                            P.op("pe", lambda e, h=h, kt=kt, sub=sub, o=o: e.matmul(self.pO[o][:, 0:XHD + 1], lhsT=pTs[kt][:, sub * 128:(sub + 1) * 128],
                                                                                   rhs=vA[:, kt, h, :], start=(kt == 0), stop=(kt == 1)),
                                 [bpTs[kt], bvA], [self.bpO[o]])
                        P.op("dve", lambda e, o=o, r=r: e.reciprocal(out=self.rc[r][:, 0:1], in_=self.pO[o][:, XHD:XHD + 1]), [self.bpO[o]], [self.brc[r]])
                        P.op("dve", lambda e, h=h, sub=sub, o=o, r=r: e.tensor_scalar(out=ob[:, sub, h * XHD:(h + 1) * XHD], in0=self.pO[o][:, 0:XHD],
                                                                                     scalar1=self.rc[r][:, 0:1], scalar2=None, op0=ALU.mult),
                             [self.bpO[o], self.brc[r]], [bob[sub]])
                for sub in range(4):
                    self.transpose8(ob[:, sub, :], bob[sub], oT[:, :, sub * 128:(sub + 1) * 128], [boT[sub]])
                self.out_proj_add(tt, oT, boT, wbig[1], bwbig[1])
            self.store_unit(u)


    def gla_phase(self, li, first):
        P = self.P
        T, NU = self.T, self.NU
        GQT = P.dram("gQT", [GH * GDK, T], BF16).ap(); bGQ = P.buf("gQT")
        GKT = P.dram("gKT", [GH * GDK, T], BF16).ap(); bGK = P.buf("gKT")
        GV = P.dram("gV", [T, D], BF16).ap(); bGV = P.buf("gV")
        GR = P.dram("gR", [T, D], F32).ap(); bGR = P.buf("gR")
        GG = [P.dram(f"gG{d}", [T, 512], F32).ap() for d in range(2)]; bGG = [P.buf(f"gG{d}") for d in range(2)]
        GO = P.dram("gO", [T, D], F32).ap(); bGO = P.buf("gO")
        wk = self.gwin.rearrange("(kc p) n -> p kc n", p=128)
        self.dbgt.update(GO=GO, GR=GR, GG0=GG[0], GG1=GG[1])
        self.phase_begin()
        self.carve_common()
        ws = [self.carve(8 * 512, "p (k n) -> p k n", k=8) for _ in range(NSLOT)]; bws = [P.buf() for _ in range(NSLOT)]
        wz = self.carve(8 * 32, "p (k n) -> p k n", k=8); bwz = P.buf()
        rowb = [self.carve(UNIT) for _ in range(2)]; browb = [P.buf() for _ in range(2)]
        vrow = [self.carve(512) for _ in range(2)]; bvrow = [P.buf() for _ in range(2)]
        zaug = [self.carve(UNIT) for _ in range(2)]; bzaug = [P.buf() for _ in range(2)]
        wgb = [self.carve(512) for _ in range(2)]; bwgb = [P.buf() for _ in range(2)]
        rrow = [self.sil[0], self.sil[1]]; brrow = self.bsil
        grow_ = [self.sq[:, 0:512], self.f32a[:, 0:512]]; bgrow = [self.bsq, self.bf32a]
        P.dma("pool", wz, wk[:, :, 3072:3104], [], [bwz], "wz")
        for d in range(2):
            P.dma("pool", wgb[d][0:17, :], self.gwgb[d], [], [bwgb[d]], ("wgb", d))
            P.op("dve", lambda e, d=d: e.memset(zaug[d][0:32, :], 1.0), [], [bzaug[d]])
        self.load_gain(G_MIX + li)
        wc = rb = vs = 0
        blocks = [("q", 0), ("k", 512), ("v", 1024), ("v", 1536), ("r", 2048), ("r", 2560)]
        for u in range(NU):
            ub = u * UNIT
            self.load_unit(u, first)
            self.norm_unit()

            def load_blk(i, s_):
                P.dma("pool", ws[s_], wk[:, :, blocks[i][1]:blocks[i][1] + 512], [], [bws[s_]], ("ws", s_))

            for i0 in range(NSLOT - 1):
                load_blk(i0, (wc + i0) % NSLOT)
            for bi, (kind, col) in enumerate(blocks):
                s_ = wc % NSLOT
                if bi + NSLOT - 1 < len(blocks):
                    load_blk(bi + NSLOT - 1, (wc + NSLOT - 1) % NSLOT)
                if kind in ("q", "k"):
                    for fl in range(4):
                        r_ = rb % 2
                        rb += 1
                        for tt in range(4):
                            j = self.ab % 2
                            self.ab += 1
                            rd = [self.bxn[4 * tt + q] for q in range(4)]
                            for kc in range(8):
                                P.op("pe", lambda e, s_=s_, kc=kc, fl=fl, tt=tt, j=j: e.matmul(self.pA[j][:], lhsT=ws[s_][:, kc, fl * 128:(fl + 1) * 128],
                                                                                             rhs=self.xnt[:, kc, tt * 512:(tt + 1) * 512], start=(kc == 0), stop=(kc == 7)),
                                     [bws[s_]] + rd, [self.bpA[j]])
                            sc = GDK ** -0.5 if kind == "q" else 1.0
                            P.op("act", lambda e, r_=r_, tt=tt, j=j, sc=sc: e.activation(out=rowb[r_][:, tt * 512:(tt + 1) * 512], in_=self.pA[j][:], func=AF.Copy, scale=sc),
                                 [self.bpA[j]], [browb[r_]])
                        dst, bdst = (GQT, bGQ) if kind == "q" else (GKT, bGK)
                        P.dma("sp", dst[fl * 128:(fl + 1) * 128, ub:ub + UNIT], rowb[r_], [browb[r_]], [bdst], ("rowb", r_))
                else:
                    hf = (col % 1024) // 512
                    for t in range(NT_U):
                        j = self.ab % 2
                        self.ab += 1
                        v_ = vs % 2
                        vs += 1
                        for kc in range(8):
                            P.op("pe", lambda e, s_=s_, kc=kc, t=t, j=j: e.matmul(self.pB[j][:], lhsT=self.xnt[:, kc, t * 128:(t + 1) * 128], rhs=ws[s_][:, kc, :],
                                                                                 start=(kc == 0), stop=(kc == 7)), [bws[s_], self.bxn[t]], [self.bpB[j]])
                        r0 = ub + t * 128
                        if kind == "v":
                            P.op("act", lambda e, v_=v_, j=j: e.activation(out=vrow[v_], in_=self.pB[j][:], func=AF.Copy), [self.bpB[j]], [bvrow[v_]])
                            P.dma("sp", GV[r0:r0 + 128, hf * 512:(hf + 1) * 512], vrow[v_], [bvrow[v_]], [bGV], ("vrow", v_))
                        else:
                            P.op("act", lambda e, v_=v_, j=j: e.activation(out=rrow[v_][:], in_=self.pB[j][:], func=AF.Silu), [self.bpB[j]], [brrow[v_]])
                            P.dma("sp", GR[r0:r0 + 128, hf * 512:(hf + 1) * 512], rrow[v_][:], [brrow[v_]], [bGR], ("rrow", v_))
                wc += 1
            for d in range(2):
                for tt in range(4):
                    j = self.ab % 2
                    self.ab += 1
                    rd = [self.bxn[4 * tt + q] for q in range(4)]
                    for kc in range(8):
                        P.op("pe", lambda e, d=d, kc=kc, tt=tt, j=j: e.matmul(self.pA[j][0:16, :], lhsT=wz[:, kc, d * 16:(d + 1) * 16],
                                                                             rhs=self.xnt[:, kc, tt * 512:(tt + 1) * 512], start=(kc == 0), stop=(kc == 7)),
                             [bwz] + rd, [self.bpA[j]])
                    P.op("act", lambda e, d=d, tt=tt, j=j: e.activation(out=zaug[d][0:16, tt * 512:(tt + 1) * 512], in_=self.pA[j][0:16, :], func=AF.Copy),
                         [self.bpA[j]], [bzaug[d]])
                for t in range(NT_U):
                    j = self.ab % 2
                    self.ab += 1
                    g_ = vs % 2
                    vs += 1
                    P.op("pe", lambda e, d=d, t=t, j=j: e.matmul(self.pB[j][:], lhsT=zaug[d][0:17, t * 128:(t + 1) * 128], rhs=wgb[d][0:17, :], start=True, stop=True),
                         [bzaug[d], bwgb[d]], [self.bpB[j]])
                    P.op("act", lambda e, g_=g_, j=j: e.activation(out=grow_[g_], in_=self.pB[j][:], func=AF.Exp, scale=-1.0), [self.bpB[j]], [bgrow[g_]])
                    P.op("act", lambda e, g_=g_: e.activation(out=grow_[g_], in_=grow_[g_], func=AF.Ln, bias=1.0), [bgrow[g_]], [bgrow[g_]])
                    r0 = ub + t * 128
                    P.dma("sp", GG[d][r0:r0 + 128, :], grow_[g_], [bgrow[g_]], [bGG[d]], ("grow", g_))
        HB = GH * 128
        for d in range(2):
            self.phase_begin()
            HALF, NCH = UNIT // 2, NT_U // 2
            qT2 = [self.carve(GH * HALF, "p (h t) -> p h t", h=GH) for _ in range(2)]; bqT2 = [P.buf() for _ in range(2)]
            kT2 = [self.carve(GH * HALF, "p (h t) -> p h t", h=GH) for _ in range(2)]; bkT2 = [P.buf() for _ in range(2)]
            xntf = self.xnt[:].rearrange("p k t -> p (k t)")
            v2 = [xntf[:, b_ * NCH * D:(b_ + 1) * NCH * D].rearrange("p (c f) -> p c f", c=NCH) for b_ in range(2)]; bv2 = [P.buf() for _ in range(2)]
            qd = [self.carve(HB) for _ in range(2)]; bqd = [P.buf() for _ in range(2)]
            kd = [self.carve(HB) for _ in range(2)]; bkd = [P.buf() for _ in range(2)]
            at = [self.carve(HB) for _ in range(2)]; bat = [P.buf() for _ in range(2)]
            ktok = [self.carve(HB) for _ in range(2)]; bktok = [P.buf() for _ in range(2)]
            Sb = self.carve(GH * GDV, "p (h v) -> p h v", h=GH); bSb = P.buf()
            xflat = self.xh[:].rearrange("p a b -> p (a b)")
            g2 = [xflat[:, b_ * NCH * 512:(b_ + 1) * NCH * 512].rearrange("p (c f) -> p c f", c=NCH) for b_ in range(2)]; bg2 = [P.buf() for _ in range(2)]
            ost = [xflat[:, 8192 + i * 1024:8192 + (i + 1) * 1024] for i in range(2)]; bost = [P.buf() for _ in range(2)]
            e1 = [xflat[:, 10240 + i * HB:10240 + (i + 1) * HB] for i in range(2)]; be1 = [P.buf() for _ in range(2)]
            e2 = [xflat[:, 11264 + i * HB:11264 + (i + 1) * HB] for i in range(2)]; be2 = [P.buf() for _ in range(2)]
            gof = [xflat[:, 12288 + i * 1024:12288 + (i + 1) * 1024] for i in range(2)]; bgof = [P.buf() for _ in range(2)]
            msk4 = self.sil[0][:, 0:HB]; tri = self.sil[1][:, 0:128]; btri = P.buf()
            eL = [self.sil[1][:, 128 + 4 * i:132 + 4 * i] for i in range(2)]; beL = [P.buf() for _ in range(2)]
            S = self.f32a[:, 0:GH * GDV].rearrange("p (h v) -> p h v", h=GH); bS = self.bf32a
            Sf = self.f32a[:, 0:GH * GDV]
            gn = self.gt; bgn = self.bg
            P.dma("sp", tri, self.gtri[d], [], [btri], "tri")
            for h in range(GH):
                P.dma("sp", msk4[:, h * 128:(h + 1) * 128], self.gtri[2 + d], [], [btri], "msk")
            P.op("dve", lambda e: e.memset(S, 0.0), [], [bS])
            P.op("dve", lambda e: e.memset(Sb, 0.0), [], [bSb])
            if d == 1:
                self.carve_common()
                wo_ = self.carve(8 * D, "p (k n) -> p k n", k=8); bwo_ = P.buf()
                ob = self.carve(D); bob = P.buf()
                oT = self.carve(8 * 128, "p (k n) -> p k n", k=8); boT = P.buf()
                P.dma("pool", wo_, self.gwout.rearrange("(kc p) n -> p kc n", p=128), [], [bwo_], "gwo")
                P.dma("sp", gn[:], self.gnorm, [], [bgn], "g")
            lc = 127 if d == 0 else 0
            grr1 = xflat[:, 14336:15360]; bgrr1 = P.buf()
            xcs = [xflat[:, 15360 + hf * 512:15360 + (hf + 1) * 512] for hf in range(2)]; bxcs = [P.buf() for _ in range(2)]
            NH = T // HALF
            order_h = list(range(NH)) if d == 0 else list(range(NH - 1, -1, -1))
            order_c = list(range(NCH)) if d == 0 else list(range(NCH - 1, -1, -1))
            nsteps = NH * NCH

            def loads_half(pos):
                hb_ = pos % 2
                h0 = order_h[pos] * HALF
                P.dma("sp", qT2[hb_], GQT.rearrange("(h p) t -> p h t", p=128)[:, :, h0:h0 + HALF], [bGQ], [bqT2[hb_]], ("qT2", hb_))
                P.dma("sp", kT2[hb_], GKT.rearrange("(h p) t -> p h t", p=128)[:, :, h0:h0 + HALF], [bGK], [bkT2[hb_]], ("kT2", hb_))
                P.dma("sp", g2[hb_], GG[d][h0:h0 + HALF, :].rearrange("(c p) f -> p c f", p=128), [bGG[d]], [bg2[hb_]], ("g2", hb_))
                P.dma("sp", v2[hb_], GV[h0:h0 + HALF, :].rearrange("(c p) f -> p c f", p=128), [bGV], [bv2[hb_]], ("v2", hb_))

            def where(k):
                pos = k // NCH
                hu = order_h[pos]
                c = order_c[k % NCH]
                return pos % 2, c, hu * HALF + c * 128, hu // 2

            def stage_A(k):
                hb, c, r0, u = where(k)
                qTu, kTu, gu, bqTu, bkTu, bgu = qT2[hb], kT2[hb], g2[hb], bqT2[hb], bkT2[hb], bg2[hb]
                i = k % 2
                pA, bpA_, pB, bpB_ = self.pA[i], self.bpA[i], self.pB[i], self.bpB[i]
                e1i, e2i, eLi, qdi, kdi, ati, kti = e1[i], e2[i], eL[i], qd[i], kd[i], at[i], ktok[i]
                if d == 1:
                    P.dma("sp", gof[i], GO[r0:r0 + 128, :], [bGO], [bgof[i]], ("gof", i))
                for h in range(GH):
                    P.op("pe", lambda e, c=c, h=h, pA=pA, gu=gu: e.matmul(pA[:, h * 128:(h + 1) * 128], lhsT=gu[:, c, h * 128:(h + 1) * 128], rhs=tri, start=True, stop=True),
                         [bgu, btri], [bpA_])
                P.op("act", lambda e, pA=pA, e1i=e1i: e.activation(out=e1i, in_=pA[:, 0:HB], func=AF.Exp), [bpA_], [be1[i]])
                P.op("act", lambda e, pA=pA, e2i=e2i: e.activation(out=e2i, in_=pA[:, 0:HB], func=AF.Exp, scale=-1.0), [bpA_], [be2[i]])
                P.op("act", lambda e, pA=pA, eLi=eLi, lc=lc: e.activation(out=eLi, in_=pA[:, 0:HB].rearrange("p (h n) -> p h n", h=GH)[:, :, lc], func=AF.Exp), [bpA_], [beL[i]])
                P.op("dve", lambda e, c=c, qdi=qdi, e1i=e1i, qTu=qTu: e.tensor_tensor(out=qdi.rearrange("p (h n) -> p h n", h=GH), in0=qTu[:, :, c * 128:(c + 1) * 128],
                                                                            in1=e1i.rearrange("p (h n) -> p h n", h=GH), op=ALU.mult), [bqTu, be1[i]], [bqd[i]])
                P.op("dve", lambda e, c=c, kdi=kdi, e2i=e2i, kTu=kTu: e.tensor_tensor(out=kdi.rearrange("p (h n) -> p h n", h=GH), in0=kTu[:, :, c * 128:(c + 1) * 128],
                                                                            in1=e2i.rearrange("p (h n) -> p h n", h=GH), op=ALU.mult), [bkTu, be2[i]], [bkd[i]])
                for h in range(GH):
                    P.op("pe", lambda e, h=h, pB=pB, kdi=kdi, qdi=qdi: e.matmul(pB[:, h * 128:(h + 1) * 128], lhsT=kdi[:, h * 128:(h + 1) * 128], rhs=qdi[:, h * 128:(h + 1) * 128],
                                                                               start=True, stop=True), [bkd[i], bqd[i]], [bpB_])
                for h in range(GH):
                    P.op("pe", lambda e, h=h, kdi=kdi: e.transpose(self.pT[:, h * 128:(h + 1) * 128], kdi[:, h * 128:(h + 1) * 128], self.idb[:]), [bkd[i], self.bidb], [self.bpT])
                P.op("dve", lambda e, pB=pB, ati=ati: e.tensor_tensor(out=ati, in0=pB[:, 0:HB], in1=msk4, op=ALU.mult), [bpB_, btri], [bat[i]])
                P.op("act", lambda e, kti=kti: e.activation(out=kti, in_=self.pT[:, 0:HB], func=AF.Copy), [self.bpT], [bktok[i]])

            def stage_B(k):
                hb, c, r0, u = where(k)
                vu, bvu = v2[hb], bv2[hb]
                i = k % 2
                pB, bpB_ = self.pB[i], self.bpB[i]
                eLi, qdi, ati, kti, osti = eL[i], qd[i], at[i], ktok[i], ost[i]
                if d == 1:
                    P.dma("sp", grr1, GR[r0:r0 + 128, :], [bGR], [bgrr1], "grr")
                for h in range(GH):
                    po, bpo = self.pO[h // 2], self.bpO[h // 2]
                    cs = (h % 2) * GDV
                    P.op("pe", lambda e, c=c, h=h, po=po, cs=cs, ati=ati, vu=vu: e.matmul(po[:, cs:cs + GDV], lhsT=ati[:, h * 128:(h + 1) * 128], rhs=vu[:, c, h * GDV:(h + 1) * GDV],
                                                                                  start=True, stop=False), [bat[i], bvu], [bpo])
                    P.op("pe", lambda e, h=h, po=po, cs=cs, qdi=qdi: e.matmul(po[:, cs:cs + GDV], lhsT=qdi[:, h * 128:(h + 1) * 128], rhs=Sb[:, h, :], start=False, stop=True),
                         [bqd[i], bSb], [bpo])
                for h in range(GH):
                    pu, bpu = (self.pO[2], self.bpO[2]) if h < 2 else (pB, bpB_)
                    cs = (h % 2) * GDV
                    P.op("pe", lambda e, c=c, h=h, pu=pu, cs=cs, kti=kti, vu=vu: e.matmul(pu[:, cs:cs + GDV], lhsT=kti[:, h * 128:(h + 1) * 128], rhs=vu[:, c, h * GDV:(h + 1) * GDV],
                                                                                  start=True, stop=True), [bktok[i], bvu], [bpu])
                for hh in range(2):
                    po, bpo = self.pO[hh], self.bpO[hh]
                    if d == 0:
                        P.op("act", lambda e, hh=hh, po=po, osti=osti: e.activation(out=osti[:, hh * 512:(hh + 1) * 512], in_=po[:, 0:512], func=AF.Copy), [bpo], [bost[i]])
                    else:
                        gofi = gof[i]
                        P.op("dve", lambda e, hh=hh, po=po, osti=osti, gofi=gofi: e.tensor_tensor(out=osti[:, hh * 512:(hh + 1) * 512], in0=po[:, 0:512],
                                                                                                 in1=gofi[:, hh * 512:(hh + 1) * 512], op=ALU.add), [bpo, bgof[i]], [bost[i]])
                P.op("dve", lambda e: e.tensor_tensor(out=Sf[:, 0:512], in0=Sf[:, 0:512], in1=self.pO[2][:, 0:512], op=ALU.add), [bS, self.bpO[2]], [bS])
                P.op("dve", lambda e, pB=pB: e.tensor_tensor(out=Sf[:, 512:1024], in0=Sf[:, 512:1024], in1=pB[:, 0:512], op=ALU.add), [bS, bpB_], [bS])
                for h in range(GH):
                    P.op("dve", lambda e, h=h, eLi=eLi: e.tensor_scalar(out=S[:, h, :], in0=S[:, h, :], scalar1=eLi[:, h:h + 1], scalar2=None, op0=ALU.mult), [bS, beL[i]], [bS])
                P.op("act", lambda e: e.activation(out=Sb.rearrange("p h v -> p (h v)"), in_=Sf, func=AF.Copy), [bS], [bSb])
                if d == 0:
                    P.dma("sp", GO[r0:r0 + 128, :], osti, [bost[i]], [bGO], ("ost", i))
                    return
                r_ = self.xk % 2
                self.xk += 1
                for h in range(GH):
                    P.op("dve", lambda e, h=h, osti=osti: e.tensor_tensor(out=self.sq[:, 0:GDV], in0=osti[:, h * GDV:(h + 1) * GDV], in1=osti[:, h * GDV:(h + 1) * GDV], op=ALU.mult),
                         [bost[i]], [self.bsq])
                    P.op("dve", lambda e, h=h, r_=r_: e.reduce_sum(out=self.rc[r_][:, h:h + 1], in_=self.sq[:, 0:GDV], axis=AX.X), [self.bsq], [self.brc[r_]])
                P.op("act", lambda e, r_=r_: e.activation(out=self.rc[r_][:, 0:GH], in_=self.rc[r_][:, 0:GH], func=AF.Ln, bias=EPS, scale=1.0 / GDV), [self.brc[r_]], [self.brc[r_]])
                P.op("act", lambda e, r_=r_: e.activation(out=self.rc[r_][:, 0:GH], in_=self.rc[r_][:, 0:GH], func=AF.Exp, scale=-0.5), [self.brc[r_]], [self.brc[r_]])
                for h in range(GH):
                    P.op("dve", lambda e, h=h, osti=osti, r_=r_: e.scalar_tensor_tensor(out=osti[:, h * GDV:(h + 1) * GDV], in0=osti[:, h * GDV:(h + 1) * GDV],
                                                                                       scalar=self.rc[r_][:, h:h + 1], in1=gn[:, h * GDV:(h + 1) * GDV], op0=ALU.mult, op1=ALU.mult),
                         [bost[i], self.brc[r_], bgn], [bost[i]])
                P.op("dve", lambda e, osti=osti: e.tensor_tensor(out=ob, in0=osti, in1=grr1, op=ALU.mult), [bost[i], bgrr1], [bob])
                self.transpose8(ob, bob, oT, [boT])
                for hf in range(2):
                    o = 2 - hf
                    xci, bxci = xcs[hf], bxcs[hf]
                    src_t = (self.x_in if first else self.X)
                    P.dma("sp", xci, src_t[r0:r0 + 128, hf * 512:(hf + 1) * 512], [] if first else [self.bX[u]], [bxci], ("xc", hf))
                    for fc in range(8):
                        P.op("pe", lambda e, fc=fc, hf=hf, o=o: e.matmul(self.pO[o][:], lhsT=oT[:, fc, :], rhs=wo_[:, fc, hf * 512:(hf + 1) * 512],
                                                                        start=(fc == 0), stop=(fc == 7)), [boT, bwo_], [self.bpO[o]])
                    P.op("dve", lambda e, o=o, xci=xci: e.tensor_tensor(out=xci, in0=xci, in1=self.pO[o][:], op=ALU.add), [self.bpO[o], bxci], [bxci])
                    P.dma("sp", self.X[r0:r0 + 128, hf * 512:(hf + 1) * 512], xci, [bxci], [self.bX[u]], ("xcs", hf))

            loads_half(0)
            stage_A(0)
            for k in range(nsteps):
                if k % NCH == 0 and k // NCH + 1 < NH:
                    loads_half(k // NCH + 1)
                if k + 1 < nsteps:
                    stage_A(k + 1)
                stage_B(k)

    def dilated_phase(self, li, first):
        P = self.P
        T, TP, NU = self.T, self.TP, self.NU
        QT = P.dram("dQT", [3, D, T], BF16).ap(); bQT = P.buf("dQT")
        KT = P.dram("dKT", [3, D, TP], BF16).ap(); bKT = P.buf("dKT")
        VA = P.dram("dVA", [3, TP, VW], BF16).ap(); bVA = P.buf("dVA")
        ACC = P.dram("dACC", [3, T, VW], F32).ap(); bACC = P.buf("dACC")
        self.phase_begin()
        self.carve_common()
        ws = [self.carve(8 * 512, "p (k n) -> p k n", k=8) for _ in range(NSLOT)]; bws = [P.buf() for _ in range(NSLOT)]
        rowb = [self.carve(UNIT) for _ in range(2)]; browb = [P.buf() for _ in range(2)]
        vst = [self.carve(8 * (DHD + 1), "p (h d) -> p h d", h=8) for _ in range(2)]; bvst = [P.buf() for _ in range(2)]
        zt = self.carve(VW); bzt = P.buf()
        v8 = self.f32a[:, 0:NT_U * 8].rearrange("p (t e) -> p t e", t=NT_U); bv8 = self.bf32a
        P.op("dve", lambda e: e.memset(zt, 0.0), [], [bzt])
        zk = 0
        for g in range(3):
            for side in (0, PAD + T):
                for fo in range(8):
                    P.dma("sp", KT[g, fo * 128:(fo + 1) * 128, side:side + PAD], zt[:, 0:PAD], [bzt], [bKT], ("z", zk % 4)); zk += 1
                for j in range(PAD // 128):
                    P.dma("sp", VA[g, side + j * 128:side + (j + 1) * 128, :], zt, [bzt], [bVA], ("z", zk % 4)); zk += 1
        self.load_gain(G_MIX + li)
        wq_k = self.dqkv.rearrange("(kc p) n -> p kc n", p=128)
        wc = rb = vs = 0
        for u in range(NU):
            ub = u * UNIT
            self.load_unit(u, first)
            self.norm_unit()
            P.dma("sp", v8, self.valid8[ub:ub + UNIT, :].rearrange("(t p) e -> p t e", p=128), [], [bv8], "v8")
            blocks = [(c3, g, hf) for c3 in range(3) for g in range(3) for hf in range(2)]

            def load_blk(i, s):
                c3, g, hf = blocks[i]
                col = c3 * 3 * D + g * D + hf * 512
                P.dma("pool", ws[s], wq_k[:, :, col:col + 512], [], [bws[s]], ("ws", s))

            for i0 in range(NSLOT - 1):
                load_blk(i0, (wc + i0) % NSLOT)
            for bi, (c3, g, hf) in enumerate(blocks):
                s = wc % NSLOT
                if bi + NSLOT - 1 < len(blocks):
                    load_blk(bi + NSLOT - 1, (wc + NSLOT - 1) % NSLOT)
                if c3 < 2:
                    for fl in range(4):
                        r_ = rb % 2
                        rb += 1
                        for tt in range(4):
                            j = self.ab % 2
                            self.ab += 1
                            rd = [self.bxn[4 * tt + q] for q in range(4)]
                            for kc in range(8):
                                P.op("pe", lambda e, s=s, kc=kc, fl=fl, tt=tt, j=j: e.matmul(self.pA[j][:], lhsT=ws[s][:, kc, fl * 128:(fl + 1) * 128],
                                                                                            rhs=self.xnt[:, kc, tt * 512:(tt + 1) * 512], start=(kc == 0), stop=(kc == 7)),
                                     [bws[s]] + rd, [self.bpA[j]])
                            eng = "act" if tt % 2 == 0 else "dve"
                            if eng == "act":
                                P.op("act", lambda e, r_=r_, tt=tt, j=j: e.activation(out=rowb[r_][:, tt * 512:(tt + 1) * 512], in_=self.pA[j][:], func=AF.Copy),
                                     [self.bpA[j]], [browb[r_]])
                            else:
                                P.op("dve", lambda e, r_=r_, tt=tt, j=j: e.tensor_copy(out=rowb[r_][:, tt * 512:(tt + 1) * 512], in_=self.pA[j][:]),
                                     [self.bpA[j]], [browb[r_]])
                        fr = (hf * 4 + fl) * 128
                        if c3 == 0:
                            P.dma("sp", QT[g, fr:fr + 128, ub:ub + UNIT], rowb[r_], [browb[r_]], [bQT], ("rowb", r_))
                        else:
                            P.dma("sp", KT[g, fr:fr + 128, PAD + ub:PAD + ub + UNIT], rowb[r_], [browb[r_]], [bKT], ("rowb", r_))
                else:
                    for t in range(NT_U):
                        j = self.ab % 2
                        self.ab += 1
                        v_ = vs % 2
                        vs += 1
                        for kc in range(8):
                            P.op("pe", lambda e, s=s, kc=kc, t=t, j=j: e.matmul(self.pB[j][:], lhsT=self.xnt[:, kc, t * 128:(t + 1) * 128], rhs=ws[s][:, kc, :],
                                                                               start=(kc == 0), stop=(kc == 7)), [bws[s], self.bxn[t]], [self.bpB[j]])
                        P.op("act", lambda e, v_=v_, j=j, t=t: e.activation(out=vst[v_][:, :, 0:DHD], in_=self.pB[j][:].rearrange("p (h d) -> p h d", h=8), func=AF.Copy,
                                                                         scale=v8[:, t, 0:1]), [self.bpB[j], bv8], [bvst[v_]])
                        P.op("dve", lambda e, v_=v_, t=t: e.tensor_copy(out=vst[v_][:, :, DHD], in_=v8[:, t, :]), [bv8], [bvst[v_]])
                        P.dma("sp", VA[g, PAD + ub + t * 128:PAD + ub + (t + 1) * 128, hf * 520:(hf + 1) * 520], vst[v_].rearrange("p h d -> p (h d)"),
                              [bvst[v_]], [bVA], ("vst", v_))
                wc += 1
        self.phase_begin()
        Eall = self.carve(3 * DH * 256, "p (g h c) -> p g h c", g=3, h=DH); bE = P.buf()
        qt = [self.carve(UNIT) for _ in range(3)]; bqt = [P.buf() for _ in range(3)]
        kw = [UNIT + 128 * r for r in DIL_R]
        kt = [self.carve(kw[g]) for g in range(3)]; bkt = [P.buf() for _ in range(3)]
        vsub = [self.carve(32 * 130, "p (c d) -> p c d", c=32) for _ in range(2)]; bvsub = [P.buf() for _ in range(2)]
        pexp = [self.carve(512) for _ in range(2)]; bpexp = [P.buf() for _ in range(2)]
        pmul = [self.carve(512) for _ in range(2)]; bpmul = [P.buf() for _ in range(2)]
        xflat = self.xh[:].rearrange("p a b -> p (a b)")
        stg = [xflat[:, i * 2080:(i + 1) * 2080].rearrange("p (b d) -> p b d", b=16) for i in range(2)]; bstg = [P.buf() for _ in range(2)]
        ebuf = self.f32a[:, 0:256]
        for g in range(3):
            for h in range(DH):
                P.dma("sp", ebuf, self.dbias[g, h], [], [self.bf32a], "eb")
                P.op("act", lambda e, g=g, h=h: e.activation(out=Eall[:, g, h, :], in_=ebuf, func=AF.Exp), [self.bf32a], [bE])
        vi = pe_ = si = 0
        for u in range(NU):
            ub = u * UNIT
            for hp in range(8):
                for g, r in enumerate(DIL_R):
                    P.dma("sp", qt[g], QT[g, hp * 128:(hp + 1) * 128, ub:ub + UNIT], [bQT], [bqt[g]], ("qt", g))
                    lo = PAD + ub - 64 * r
                    P.dma("sp", kt[g], KT[g, hp * 128:(hp + 1) * 128, lo:lo + kw[g]], [bKT], [bkt[g]], ("kt", g))
                    qv = qt[g].rearrange("p (n r) -> p r n", r=r)
                    kv = kt[g].rearrange("p (n r) -> p r n", r=r)
                    nb = 16 // r
                    nch = nb + 1
                    for rho in range(r):
                        v_ = vi % 2
                        vi += 1
                        s_ = si % 2
                        si += 1
                        base = PAD + ub - 64 * r
                        vsrc = VA[g, base:base + 128 * r * nch, hp * 130:(hp + 1) * 130].rearrange("(c p r) d -> p c r d", p=128, r=r)[:, :, rho, :]
                        P.dma("sp", vsub[v_][:, 0:nch, :], vsrc, [bVA], [bvsub[v_]], ("vsub", v_))
                        for qb in range(nb):
                            j = self.ab % 2
                            self.ab += 1
                            x_ = pe_ % 2
                            pe_ += 1
                            for h2 in range(2):
                                bank, bbank = (self.pA[j], self.bpA[j]) if h2 == 0 else (self.pB[j], self.bpB[j])
                                for kc in range(2):
                                    c = qb + kc
                                    P.op("pe", lambda e, h2=h2, kc=kc, c=c, qb=qb, rho=rho, kv=kv, qv=qv, bank=bank: e.matmul(
                                        bank[:, kc * 128:(kc + 1) * 128],
                                        lhsT=kv[h2 * 64:(h2 + 1) * 64, rho, c * 128:(c + 1) * 128],
                                        rhs=qv[h2 * 64:(h2 + 1) * 64, rho, qb * 128:(qb + 1) * 128], start=True, stop=True),
                                         [bkt[g], bqt[g]], [bbank])
                            for h2 in range(2):
                                bank, bbank = (self.pA[j], self.bpA[j]) if h2 == 0 else (self.pB[j], self.bpB[j])
                                P.op("act", lambda e, h2=h2, bank=bank, x_=x_: e.activation(out=pexp[x_][:, h2 * 256:(h2 + 1) * 256], in_=bank[:, 0:256], func=AF.Exp,
                                                                                           scale=DHD ** -0.5), [bbank], [bpexp[x_]])
                            P.op("dve", lambda e, g=g, hp=hp, x_=x_: e.tensor_tensor(out=pmul[x_], in0=pexp[x_], in1=Eall[:, g, 2 * hp:2 * hp + 2, :].rearrange("p h c -> p (h c)"),
                                                                                    op=ALU.mult), [bpexp[x_], bE], [bpmul[x_]])
                            o = self.io % 3
                            self.io += 1
                            for h2 in range(2):
                                for kc in range(2):
                                    c = qb + kc
                                    P.op("pe", lambda e, h2=h2, kc=kc, c=c, v_=v_, x_=x_, o=o: e.matmul(
                                        self.pO[o][:, h2 * 65:(h2 + 1) * 65], lhsT=pmul[x_][:, (h2 * 2 + kc) * 128:(h2 * 2 + kc + 1) * 128],
                                        rhs=vsub[v_][:, c, h2 * 65:(h2 + 1) * 65], start=(kc == 0), stop=(kc == 1)),
                                         [bpmul[x_], bvsub[v_]], [self.bpO[o]])
                            P.op("act", lambda e, s_=s_, qb=qb, o=o: e.activation(out=stg[s_][:, qb, :], in_=self.pO[o][:, 0:130], func=AF.Copy), [self.bpO[o]], [bstg[s_]])
                        adst = ACC[g, ub:ub + 128 * r * nb, hp * 130:(hp + 1) * 130].rearrange("(b p r) d -> p b r d", p=128, r=r)[:, :, rho, :]
                        P.dma("sp", adst, stg[s_][:, 0:nb, :], [bstg[s_]], [bACC], ("stg", s_))
        self.phase_begin()
        self.carve_common()
        wo_ = self.carve(8 * D, "p (k n) -> p k n", k=8); bwo_ = P.buf()
        ob = self.carve(4 * D, "p (a d) -> p a d", a=4); bob = [P.buf() for _ in range(4)]
        oT = self.carve(8 * 512, "p (k n) -> p k n", k=8); boT = [P.buf() for _ in range(4)]
        acc1 = self.sq[:, 0:VW].rearrange("p (h d) -> p h d", h=DH); bacc1 = self.bsq
        acc2 = self.f32a[:, 0:VW].rearrange("p (h d) -> p h d", h=DH); bacc2 = self.bf32a
        P.dma("pool", wo_, self.dwo.rearrange("(kc p) n -> p kc n", p=128), [], [bwo_], "dwo")
        for u in range(NU):
            ub = u * UNIT
            self.load_unit(u, first)
            for tt in range(4):
                for sub in range(4):
                    t = 4 * tt + sub
                    r0 = ub + t * 128
                    P.dma("sp", acc1, ACC[0, r0:r0 + 128, :].rearrange("p (h d) -> p h d", h=DH), [bACC], [bacc1], "acc1")
                    for g in (1, 2):
                        P.dma("sp", acc2, ACC[g, r0:r0 + 128, :].rearrange("p (h d) -> p h d", h=DH), [bACC], [bacc2], "acc2")
                        P.op("dve", lambda e: e.tensor_tensor(out=acc1, in0=acc1, in1=acc2, op=ALU.add), [bacc1, bacc2], [bacc1])
                    r = self.xk % 2
                    self.xk += 1
                    P.op("dve", lambda e, r=r: e.reciprocal(out=self.rc[r][:], in_=acc1[:, :, DHD]), [bacc1], [self.brc[r]])
                    for h in range(DH):
                        P.op("dve", lambda e, h=h, r=r, sub=sub: e.tensor_scalar(out=ob[:, sub, h * DHD:(h + 1) * DHD], in0=acc1[:, h, 0:DHD],
                                                                                scalar1=self.rc[r][:, h:h + 1], scalar2=None, op0=ALU.mult),
                             [bacc1, self.brc[r]], [bob[sub]])
                    self.transpose8(ob[:, sub, :], bob[sub], oT[:, :, sub * 128:(sub + 1) * 128], [boT[sub]])
                self.out_proj_add(tt, oT, boT, wo_, bwo_)
            self.store_unit(u)

    def out_phase(self, final, first):
        P = self.P
        self.phase_begin()
        y_t = self.y.rearrange("(t p) d -> p t d", p=128)
        if final:
            self.load_gain(G_FINAL)
        for u in range(self.NU):
            self.load_unit(u, first)
            if final:
                for t in range(NT_U):
                    i = self.rms_rstd(self.xh[:, t, :], [self.bx[t]])
                    P.op("dve", lambda e, t=t, i=i: e.scalar_tensor_tensor(out=self.xh[:, t, :], in0=self.xh[:, t, :], scalar=self.ss[i][:], in1=self.gt[:],
                                                                          op0=ALU.mult, op1=ALU.mult), [self.bx[t], self.bss[i], self.bg], [self.bx[t]])
            self.outs.append(P.dma("sp", y_t[:, u * NT_U:(u + 1) * NT_U, :], self.xh[:], list(self.bx), [self.bY[u]], ("y", u % 2)))

    def build(self):
        first = True
        windowed = False
        for ph in self.phases:
            if ph == "final":
                continue
            kind, li = ph.split(":")
            li = int(li)
            if self.T1 and not windowed and (li == 1 or kind in ("cross", "ffn2")):
                assert not first, "the window needs a whole-sequence phase before it"
                self.enter_window()
                windowed = True
            if kind == "ffn1":
                self.ffn_phase(2 * li, G_FFN1 + li, first)
            elif kind == "ffn2":
                self.ffn_phase(2 * li + 1, G_FFN2 + li, first)
            elif kind == "cross":
                self.cross_phase(li, first)
            elif kind == "mix" and li == 1:
                self.dilated_phase(li, first)
            elif kind == "mix":
                self.gla_phase(li, first)
            else:
                raise ValueError(ph)
            first = False
            self._gather = False
        if self.dbg:
            self._dbg_src = self.dbgt[self.dbg]
        self.out_phase("final" in self.phases, first)
        self.P.emit(self.outs)
        return self.nc


_j = np.arange(128)[:, None]; _i = np.arange(128)[None, :]
GLA_TRI = np.stack([np.where(_j <= _i, -1.0 / 16.0, 0.0), np.where(_j >= _i, -1.0 / 16.0, 0.0),
                    np.where(_j <= _i, 1.0, 0.0), np.where(_j >= _i, 1.0, 0.0)]).astype(np.float32)


def pack_weights(inputs):
    f = lambda k: np.asarray(inputs[k], np.float32)
    gl = [f("norm_ffn1")[0], f("norm_ffn1")[1], f("norm_mix")[0], f("norm_mix")[1], f("norm_cross")[0], f("norm_cross")[1],
          f("norm_mem")[0], f("norm_mem")[1], f("norm_ffn2")[0], f("norm_ffn2")[1], f("norm_final")]
    gains = np.ascontiguousarray(np.broadcast_to(np.stack(gl)[:, None, :], (11, 128, D)))
    p = np.arange(128)[:, None]; q = np.arange(128)[None, :]
    rb = f("rel_bias")
    dbias = np.full((3, DH, 128, 256), -30000.0, np.float32)
    for g, r in enumerate(DIL_R):
        for kc in range(2):
            m = p - q - 64 + 128 * kc
            ok = np.abs(m) <= 64
            bk = t5_bucket(m * r)
            for h in range(DH):
                tile = rb[bk, g * DH + h]
                dbias[g, h, :, kc * 128:(kc + 1) * 128] = np.where(ok, tile, dbias[g, h, :, kc * 128:(kc + 1) * 128])
    return {
        "gains": gains, "ident": np.eye(128, dtype=np.float32),
        "ffn_in": np.ascontiguousarray(np.stack([f("ffn1_in")[0], f("ffn2_in")[0], f("ffn1_in")[1], f("ffn2_in")[1]])),
        "ffn_out": np.ascontiguousarray(np.stack([f("ffn1_out")[0], f("ffn2_out")[0], f("ffn1_out")[1], f("ffn2_out")[1]])),
        "cross_q": f("cross_w_q"), "cross_kv": f("cross_w_kv"), "cross_o": f("cross_w_o"),
        "gla_w_in": np.ascontiguousarray(f("gla_w_in")[0]),
        "gla_wgb": np.ascontiguousarray(np.stack([np.concatenate([f("gla_wg_f")[0], f("gla_bg_f")[0][None, :]], 0),
                                                  np.concatenate([f("gla_wg_b")[0], f("gla_bg_b")[0][None, :]], 0)])),
        "gla_norm": np.ascontiguousarray(np.broadcast_to(f("gla_norm")[0][None, :], (128, D))),
        "gla_w_out": np.ascontiguousarray(f("gla_w_out")[0]),
        "gla_tri": GLA_TRI,
        "dil_qkv": np.ascontiguousarray(f("dil_w_qkv")[0]), "dil_o": np.ascontiguousarray(f("dil_w_out")[0]), "dil_bias": dbias,
    }


def run_seqs(seqs, mems, inputs, T, phases, dbg=None):
    n = 8
    w = pack_weights(inputs)
    nc = K(T, phases, dbg).build()
    in_maps = []
    for c in range(n):
        xs = np.zeros((T, D), np.float32)
        mm = np.zeros((MEM, D), np.float32)
        v8 = np.zeros((T, 8), np.float32)
        if c < len(seqs):
            xs[:seqs[c].shape[0]] = seqs[c]
            mm[:] = mems[c]
            v8[:seqs[c].shape[0]] = 1.0
        in_maps.append(dict(w, x=xs, mem=mm, valid8=v8))
    res = run_bass_kernel_spmd(nc, in_maps, core_ids=list(range(n)))
    return [np.asarray(res.results[c]["y"][:seqs[c].shape[0]], np.float32) for c in range(len(seqs))]


WIN = 4096
T1W = WIN + 2 * PAD


def kernel(**inputs):
    xp = np.asarray(inputs["x_prompt"], np.float32)
    xs = np.asarray(inputs["x_sample"], np.float32)
    mp = np.asarray(inputs["mem_prompt"], np.float32)
    ms = np.asarray(inputs["mem_sample"], np.float32)
    seqs = [xp[b] for b in range(xp.shape[0])] + [xs[b] for b in range(xs.shape[0])]
    mems = [mp[b] for b in range(mp.shape[0])] + [ms[b] for b in range(ms.shape[0])]
    jobs = [(si, w0) for si, sq in enumerate(seqs) for w0 in range(0, sq.shape[0], WIN)]
    n = 8
    assert len(jobs) <= n, len(jobs)
    T0 = -(-max(sq.shape[0] for sq in seqs) // UNIT) * UNIT
    w = pack_weights(inputs)
    nc = K(T0, FULL_PHASES, T1=T1W).build()
    in_maps = []
    for c in range(n):
        x = np.zeros((T0, D), np.float32)
        mm = np.zeros((MEM, D), np.float32)
        v8 = np.zeros((T1W, 8), np.float32)
        idx = np.zeros((T1W,), np.int32)
        if c < len(jobs):
            si, w0 = jobs[c]
            S = seqs[si].shape[0]
            x[:S] = seqs[si]
            mm[:] = mems[si]
            rows = np.arange(w0 - PAD, w0 + WIN + PAD)
            ok = (rows >= 0) & (rows < S)
            v8[ok] = 1.0
            idx[:] = np.clip(rows, 0, S - 1)
        in_maps.append(dict(w, x=x, mem=mm, valid8=v8, win_idx=np.ascontiguousarray(idx.reshape(T1W // 128, 128).T)))
    res = run_bass_kernel_spmd(nc, in_maps, core_ids=list(range(n)))
    outs = [np.zeros(sq.shape, np.float32) for sq in seqs]
    for c, (si, w0) in enumerate(jobs):
        S = seqs[si].shape[0]
        hi = min(w0 + WIN, S)
        outs[si][w0:hi] = np.asarray(res.results[c]["y"][PAD:PAD + (hi - w0)], np.float32)
    yp = np.stack(outs[:xp.shape[0]]).astype(np.float32)
    ys = np.stack(outs[xp.shape[0]:]).astype(np.float32)
    return (yp, ys)
```

```python
from contextlib import ExitStack
import numpy as np
import concourse.bass as bass
import concourse.mybir as mybir

F32 = mybir.dt.float32
BF16 = mybir.dt.bfloat16
AF = mybir.ActivationFunctionType
ALU = mybir.AluOpType
AX = mybir.AxisListType

COMPUTE = ("pe", "act", "dve", "pool")
ENGS = ("pe", "act", "dve", "pool", "sp")


class Buf:
    __slots__ = ("name", "w", "rs", "excl")

    def __init__(self, name, excl=False):
        self.name = name
        self.w = None
        self.rs = []
        self.excl = excl


class Op:
    __slots__ = ("eng", "fn", "dma", "deps", "idx", "sig", "val", "semkey", "dsem", "dval", "pos", "inc")

    def __init__(self, eng, fn, dma, semkey, inc=16):
        self.inc = inc
        self.eng = eng
        self.fn = fn
        self.dma = dma
        self.deps = []
        self.sig = False
        self.val = 0
        self.semkey = semkey
        self.dsem = None
        self.dval = 0


class Prog:
    def __init__(self, nc):
        self.nc = nc
        self.ops = []
        self.es = ExitStack()
        self.last_dma = {}
        self.nbuf = 0
        self.last = {}
        self.pending = []

    def sbuf(self, name, shape, dt):
        return self.es.enter_context(self.nc.sbuf_tensor(name, list(shape), dt))

    def psum(self, name, shape, dt=F32):
        return self.es.enter_context(self.nc.psum_tensor(name, list(shape), dt))

    def dram(self, name, shape, dt, kind="Internal", addr_space="Local"):
        return self.nc.dram_tensor(name, list(shape), dt, kind=kind, addr_space=addr_space)

    def buf(self, name=None, excl=False):
        self.nbuf += 1
        return Buf(name or f"b{self.nbuf}", excl)

    def op(self, eng, fn, reads=(), writes=(), dma=False, semkey=None, inc=16):
        o = Op(eng, fn, dma, semkey, inc)
        deps = []

        def add(p, raw):
            if p is None or p is o:
                return
            if not p.dma and p.eng == eng:
                if not raw or eng == "pe":
                    return
            if p not in deps:
                deps.append(p)

        for b in reads:
            add(b.w, True)
            if b.excl:
                for r in b.rs:
                    if r.eng != eng:
                        add(r, False)
        for b in writes:
            add(b.w, False)
            for r in b.rs:
                add(r, False)
        if dma:
            assert semkey is not None
            add(self.last_dma.get(semkey), False)
            self.last_dma[semkey] = o
        for b in reads:
            if not dma:
                b.rs = [r for r in b.rs if r.dma or r.eng != eng]
            b.rs.append(o)
        for b in writes:
            b.w = o
            b.rs = []
        o.deps = deps
        o.pos = len(self.ops)
        self.ops.append(o)
        if fn is not None:
            self.last[eng] = o
        if dma:
            self.pending.append(o)
        return o

    def barrier(self):
        targets = [p for p in self.last.values()] + list(self.pending)
        self.pending = []
        for eng in ENGS:
            o = Op(eng, None, False, None)
            o.deps = [p for p in dict.fromkeys(targets)]
            o.pos = len(self.ops)
            self.ops.append(o)

    def dma(self, eng, out, in_, reads, writes, semkey, **kw):
        return self.op(eng, lambda e: e.dma_start(out=out, in_=in_, **kw), reads, writes,
                       dma=True, semkey=semkey)

    def emit(self, final_wait_ops=()):
        nc = self.nc
        es = self.es
        fin = self.op("sp", None, reads=(), writes=())
        for p in final_wait_ops:
            if p not in fin.deps:
                fin.deps.append(p)
                p.sig = True
        for o in self.ops:
            for p in o.deps:
                p.sig = True
        esem = {e: es.enter_context(nc.semaphore(f"c_{e}")) for e in COMPUTE}
        dsems = {}
        ecount = {e: 0 for e in COMPUTE}
        dcount = {}
        for o in self.ops:
            if o.dma:
                if o.semkey not in dsems:
                    dsems[o.semkey] = es.enter_context(nc.semaphore(f"d{len(dsems)}"))
                    dcount[o.semkey] = 0
                dcount[o.semkey] += o.inc
                o.dsem = dsems[o.semkey]
                o.dval = dcount[o.semkey]
            elif o.sig:
                assert o.eng in COMPUTE, ("sp non-dma op cannot signal", o.eng)
                ecount[o.eng] += 1
                o.val = ecount[o.eng]
        self.n_dsem = len(dsems)
        per = {e: [] for e in ENGS}
        for o in self.ops:
            per[o.eng].append(o)
        handles = {"pe": "tensor", "act": "scalar", "dve": "vector", "pool": "gpsimd", "sp": "sync"}

        def run(ename, eh):
            known = {}
            for o in per[ename]:
                need = {}
                for p in o.deps:
                    if p.dma:
                        k, s, v = ("d", p.semkey), p.dsem, p.dval
                    else:
                        k, s, v = ("e", p.eng), esem[p.eng], p.val
                    if known.get(k, 0) >= v:
                        continue
                    if k not in need or need[k][1] < v:
                        need[k] = (s, v)
                for k, (s, v) in need.items():
                    eh.wait_ge(s, v)
                    known[k] = v
                if o.fn is None:
                    continue
                ins = o.fn(eh)
                if o.dma:
                    ins.then_inc(o.dsem, o.inc)
                elif o.sig:
                    ins.then_inc(esem[o.eng], 1)

        with nc.Block() as block:
            for ename in ENGS:
                if not per[ename]:
                    continue
                getattr(block, handles[ename])(lambda eh, _n=ename: run(_n, eh))
        es.close()
        return nc


from concourse.bass_utils import run_bass_kernel_spmd

D = 1024
DFF = 2816
NFF = DFF // 128
UNIT = 2048
NT_U = UNIT // 128
MEM = 256
XH = 4
XHD = 256
EPS = 1e-6
NSLOT = 3
ARENA = 40960
GH, GDK, GDV = 4, 128, 256
DIL_R = (1, 4, 16)
DH = 16
DHD = 64
VW = DH * (DHD + 1)
PAD = 1024
NUM_BUCKETS = 32
MAX_DISTANCE = 1024
G_FFN1, G_MIX, G_CROSS, G_MEM, G_FFN2, G_FINAL = 0, 2, 4, 6, 8, 10
FULL_PHASES = ("ffn1:0", "mix:0", "cross:0", "ffn2:0", "ffn1:1", "mix:1", "cross:1", "ffn2:1", "final")


def t5_bucket(rel):
    half = NUM_BUCKETS // 2
    max_exact = half // 2
    ret = (rel > 0).astype(np.int32) * half
    n = np.abs(rel)
    large = max_exact + (np.log(np.maximum(n, 1) / max_exact) / np.log(MAX_DISTANCE / max_exact) * (half - max_exact)).astype(np.int32)
    large = np.minimum(large, half - 1)
    return (ret + np.where(n < max_exact, n, large)).astype(np.int32)


class K:
    def __init__(self, T, phases, dbg=None, T1=None):
        self.T1 = T1
        self.dbg = dbg
        self.dbgt = {}
        assert T % UNIT == 0
        self.T, self.NU, self.phases = T, T // UNIT, tuple(phases)
        self.TP = T + 2 * PAD
        nc = self.nc = bass.Bass("TRN2", target_bir_lowering=False)
        P = self.P = Prog(nc)
        inp = lambda n, s: nc.dram_tensor(n, list(s), F32, kind="ExternalInput").ap()
        self.x_in = inp("x", [T, D])
        self.mem_in = inp("mem", [MEM, D])
        self.gains = inp("gains", [11, 128, D])
        self.ident = inp("ident", [128, 128])
        self.ffn_in = inp("ffn_in", [4, D, 2 * DFF])
        self.ffn_out = inp("ffn_out", [4, DFF, D])
        self.cq = inp("cross_q", [2, D, D])
        self.ckv = inp("cross_kv", [2, D, 2 * D])
        self.co = inp("cross_o", [2, D, D])
        self.dqkv = inp("dil_qkv", [D, 9 * D])
        self.dwo = inp("dil_o", [D, D])
        self.dbias = inp("dil_bias", [3, DH, 128, 256])
        self.gwin = inp("gla_w_in", [D, 3104])
        self.gwgb = inp("gla_wgb", [2, 17, 512])
        self.gnorm = inp("gla_norm", [128, D])
        self.gwout = inp("gla_w_out", [D, D])
        self.gtri = inp("gla_tri", [4, 128, 128])
        TL = T1 or T
        self.valid8 = inp("valid8", [TL, 8])
        self.y = nc.dram_tensor("y", [TL, D], F32, kind="ExternalOutput").ap()
        self.X = P.dram("Xres", [T, D], F32).ap()
        self.bX = [P.buf(f"X{u}") for u in range(self.NU)]
        self.bY = [P.buf(f"Y{u}") for u in range(TL // UNIT)]
        self._gather = False
        if T1:
            self.win_idx = nc.dram_tensor("win_idx", [128, T1 // 128], mybir.dt.int32, kind="ExternalInput").ap()
            self.idxs = P.sbuf("idxs", [128, T1 // 128], mybir.dt.int32); self.bidx = P.buf("idxs")
            P.dma("sp", self.idxs[:], self.win_idx, [], [self.bidx], "idxs")
            self.X1 = P.dram("Xres1", [T1, D], F32).ap()
        self.bXg = P.buf("Xg")
        self.xh = P.sbuf("xh", [128, NT_U, D], F32); self.bx = [P.buf(f"x{t}") for t in range(NT_U)]
        self.xnt = P.sbuf("xnt", [128, 8, UNIT], BF16); self.bxn = [P.buf(f"xn{t}") for t in range(NT_U)]
        self.gt = P.sbuf("gt", [128, D], F32); self.bg = P.buf("g")
        self.idf = P.sbuf("idf", [128, 128], F32); self.bidf = P.buf("idf")
        self.idb = P.sbuf("idb", [128, 128], BF16); self.bidb = P.buf("idb")
        self.sq = P.sbuf("sq", [128, VW], F32); self.bsq = P.buf("sq")
        self.f32a = P.sbuf("f32a", [128, VW], F32); self.bf32a = P.buf("f32a")
        self.sil = [P.sbuf(f"sil{i}", [128, 512], F32) for i in range(2)]; self.bsil = [P.buf(f"sil{i}") for i in range(2)]
        self.ss = [P.sbuf(f"ss{i}", [128, 1], F32) for i in range(2)]; self.bss = [P.buf(f"ss{i}") for i in range(2)]
        self.rc = [P.sbuf(f"rc{i}", [128, 16], F32) for i in range(2)]; self.brc = [P.buf(f"rc{i}") for i in range(2)]
        self.AB = P.sbuf("AB", [128, ARENA], BF16)
        self.pA = [P.psum(f"pA{i}", [128, 512]) for i in range(2)]; self.bpA = [P.buf(f"pA{i}", excl=True) for i in range(2)]
        self.pB = [P.psum(f"pB{i}", [128, 512]) for i in range(2)]; self.bpB = [P.buf(f"pB{i}", excl=True) for i in range(2)]
        self.pO = [P.psum(f"pO{i}", [128, 512]) for i in range(3)]; self.bpO = [P.buf(f"pO{i}", excl=True) for i in range(3)]
        self.pT = P.psum("pT", [128, 8 * 128], BF16); self.bpT = P.buf("pT", excl=True)
        self.io = self.ab = self.wcnt = self.nrm = self.xk = 0
        self.outs = []
        P.dma("sp", self.idf[:], self.ident, [], [self.bidf], "id")
        P.op("act", lambda e: e.activation(out=self.idb[:], in_=self.idf[:], func=AF.Copy), [self.bidf], [self.bidb])

    def phase_begin(self):
        self.P.barrier()
        self.aoff = 0

    def carve(self, n, pat=None, **kw):
        assert self.aoff + n <= ARENA, ("arena overflow", self.aoff, n)
        v = self.AB[:, self.aoff:self.aoff + n]
        self.aoff += n
        return v.rearrange(pat, **kw) if pat else v

    def carve_common(self):
        self.xnb = [self.carve(D) for _ in range(2)]; self.bxnb = [self.P.buf() for _ in range(2)]

    def src(self, first):
        if getattr(self, "_dbg_src", None) is not None:
            return self._dbg_src.rearrange("(t p) d -> p t d", p=128)
        return (self.x_in if first else self.X).rearrange("(t p) d -> p t d", p=128)

    def load_gain(self, row):
        self.P.dma("sp", self.gt[:], self.gains[row], [], [self.bg], "g")

    def enter_window(self):
        self.X0, self.bX0 = self.X, list(self.bX)
        self.T, self.NU, self.TP = self.T1, self.T1 // UNIT, self.T1 + 2 * PAD
        self.X = self.X1
        self.bX = [self.P.buf(f"X1_{u}") for u in range(self.NU)]
        self._gather = True

    def load_unit(self, u, first):
        P = self.P
        if self._gather:
            X0, idxs = self.X0, self.idxs
            for t in range(NT_U):
                k = u * NT_U + t
                P.op("pool", lambda e, t=t, k=k: e.indirect_dma_start(out=self.xh[:, t, :], out_offset=None, in_=X0,
                                                                     in_offset=bass.IndirectOffsetOnAxis(ap=idxs[:, k:k + 1], axis=0)),
                     list(self.bX0) + [self.bidx], [self.bx[t]], dma=True, semkey=("gx", t % 4))
            return
        rd = [] if first else [self.bX[u]]
        for t in range(NT_U):
            P.dma("sp", self.xh[:, t, :], self.src(first)[:, u * NT_U + t, :], rd, [self.bx[t]], ("x", t % 4))

    def store_tiles(self, u, tt):
        X_t = self.X.rearrange("(t p) d -> p t d", p=128)
        self.P.dma("sp", X_t[:, u * NT_U + 4 * tt:u * NT_U + 4 * tt + 4, :], self.xh[:, 4 * tt:4 * tt + 4, :], [self.bx[4 * tt + q] for q in range(4)],
                   [self.bX[u]], ("xs", tt % 2))

    def load_tiles(self, u, tt, first):
        rd = [] if first else [self.bX[u]]
        for t in range(4 * tt, 4 * tt + 4):
            self.P.dma("sp", self.xh[:, t, :], self.src(first)[:, u * NT_U + t, :], rd, [self.bx[t]], ("x", t % 4))

    def store_unit(self, u):
        X_t = self.X.rearrange("(t p) d -> p t d", p=128)
        self.P.dma("sp", X_t[:, u * NT_U:(u + 1) * NT_U, :], self.xh[:], list(self.bx), [self.bX[u]], ("xs", u % 2))

    def rms_rstd(self, src_ap, rdbufs):
        P = self.P
        i = self.nrm % 2
        self.nrm += 1
        P.op("dve", lambda e: e.tensor_tensor(out=self.sq[:, 0:D], in0=src_ap, in1=src_ap, op=ALU.mult), rdbufs, [self.bsq])
        P.op("dve", lambda e: e.reduce_sum(out=self.ss[i][:], in_=self.sq[:, 0:D], axis=AX.X), [self.bsq], [self.bss[i]])
        P.op("act", lambda e: e.activation(out=self.ss[i][:], in_=self.ss[i][:], func=AF.Ln, bias=EPS, scale=1.0 / D), [self.bss[i]], [self.bss[i]])
        P.op("act", lambda e: e.activation(out=self.ss[i][:], in_=self.ss[i][:], func=AF.Exp, scale=-0.5), [self.bss[i]], [self.bss[i]])
        return i

    def norm_T(self, src_ap, rdbufs, dst_ap, dstbufs):
        P = self.P
        i = self.rms_rstd(src_ap, rdbufs)
        xnb_i = self.xnb[i]
        P.op("dve", lambda e: e.scalar_tensor_tensor(out=xnb_i, in0=src_ap, scalar=self.ss[i][:], in1=self.gt[:],
                                                     op0=ALU.mult, op1=ALU.mult), rdbufs + [self.bss[i], self.bg], [self.bxnb[i]])
        self.transpose8(xnb_i, self.bxnb[i], dst_ap, dstbufs)

    def transpose8(self, src_ap, srcbuf, dst_ap, dstbufs):
        P = self.P
        for kc in range(8):
            P.op("pe", lambda e, kc=kc: e.transpose(self.pT[:, kc * 128:(kc + 1) * 128], src_ap[:, kc * 128:(kc + 1) * 128], self.idb[:]),
                 [srcbuf, self.bidb], [self.bpT])
        P.op("act", lambda e: e.activation(out=dst_ap, in_=self.pT[:].rearrange("p (k n) -> p k n", k=8), func=AF.Copy), [self.bpT], dstbufs)

    def norm_unit(self):
        for t in range(NT_U):
            self.norm_T(self.xh[:, t, :], [self.bx[t]], self.xnt[:, :, t * 128:(t + 1) * 128], [self.bxn[t]])

    def out_proj_add(self, tt, oT, boT, w, bw):
        P = self.P
        for sub in range(4):
            t = 4 * tt + sub
            for hf in range(2):
                o = self.io % 3
                self.io += 1
                for fc in range(8):
                    P.op("pe", lambda e, fc=fc, sub=sub, hf=hf, o=o: e.matmul(self.pO[o][:], lhsT=oT[:, fc, sub * 128:(sub + 1) * 128],
                                                                             rhs=w[:, fc, hf * 512:(hf + 1) * 512], start=(fc == 0), stop=(fc == 7)),
                         [boT[sub], bw], [self.bpO[o]])
                P.op("dve", lambda e, t=t, hf=hf, o=o: e.tensor_tensor(out=self.xh[:, t, hf * 512:(hf + 1) * 512], in0=self.xh[:, t, hf * 512:(hf + 1) * 512],
                                                                      in1=self.pO[o][:], op=ALU.add), [self.bpO[o], self.bx[t]], [self.bx[t]])

    def ffn_phase(self, fi, grow, first):
        P = self.P
        FG, NSL = 4, 8
        self.phase_begin()
        self.carve_common()
        wa = [self.carve(1024, "p (k n) -> p k n", k=8) for _ in range(NSL)]; bwa = [P.buf() for _ in range(NSL)]
        wb = [self.carve(1024, "p (k n) -> p k n", k=8) for _ in range(NSL)]; bwb = [P.buf() for _ in range(NSL)]
        wo = [self.carve(D) for _ in range(NSL)]; bwo = [P.buf() for _ in range(NSL)]
        act = [[self.carve(512) for _ in range(FG)] for _ in range(2)]; bact = [[P.buf() for _ in range(FG)] for _ in range(2)]
        w_in_k = self.ffn_in[fi].rearrange("(kc p) n -> p kc n", p=128)
        w_out = self.ffn_out[fi]
        groups = [list(range(c0, min(c0 + FG, NFF))) for c0 in range(0, NFF, FG)]

        def load_w(c):
            s = c % NSL
            P.dma("pool", wa[s], w_in_k[:, :, c * 128:(c + 1) * 128], [], [bwa[s]], ("wa", s))
            P.dma("pool", wb[s], w_in_k[:, :, DFF + c * 128:DFF + (c + 1) * 128], [], [bwb[s]], ("wb", s))
            P.dma("pool", wo[s], w_out[c * 128:(c + 1) * 128, :], [], [bwo[s]], ("wo", s))

        self.load_gain(grow)
        pio = not self._gather
        for u in range(self.NU):
            if u == 0 or not pio:
                self.load_unit(u, first)
            self.norm_unit()
            for c in groups[0]:
                load_w(c)
            for gi, grp in enumerate(groups):
                if gi + 1 < len(groups):
                    for c in groups[gi + 1]:
                        load_w(c)
                def in_proj(tt):
                    par = tt % 2
                    rd = [self.bxn[4 * tt + q] for q in range(4)]
                    for g, c in enumerate(grp):
                        s = c % NSL
                        j = self.ab % 2
                        self.ab += 1
                        a_g, ba_g = act[par][g], bact[par][g]
                        for kc in range(8):
                            P.op("pe", lambda e, s=s, kc=kc, tt=tt, j=j: e.matmul(self.pA[j][:], lhsT=wa[s][:, kc, :], rhs=self.xnt[:, kc, tt * 512:(tt + 1) * 512],
                                                                                 start=(kc == 0), stop=(kc == 7)), [bwa[s]] + rd, [self.bpA[j]])
                        for kc in range(8):
                            P.op("pe", lambda e, s=s, kc=kc, tt=tt, j=j: e.matmul(self.pB[j][:], lhsT=wb[s][:, kc, :], rhs=self.xnt[:, kc, tt * 512:(tt + 1) * 512],
                                                                                 start=(kc == 0), stop=(kc == 7)), [bwb[s]] + rd, [self.bpB[j]])
                        P.op("act", lambda e, j=j: e.activation(out=self.sil[j][:], in_=self.pA[j][:], func=AF.Silu), [self.bpA[j]], [self.bsil[j]])
                        P.op("dve", lambda e, j=j, a_g=a_g: e.tensor_tensor(out=a_g, in0=self.sil[j][:], in1=self.pB[j][:], op=ALU.mult),
                             [self.bsil[j], self.bpB[j]], [ba_g])

                def out_proj(tt):
                    par = tt % 2
                    for sub in range(4):
                        t = 4 * tt + sub
                        for h in range(2):
                            o = self.io % 3
                            self.io += 1
                            for g, c in enumerate(grp):
                                s = c % NSL
                                a_g, ba_g = act[par][g], bact[par][g]
                                P.op("pe", lambda e, s=s, sub=sub, h=h, o=o, a_g=a_g, g=g, n=len(grp): e.matmul(
                                    self.pO[o][:], lhsT=a_g[:, sub * 128:(sub + 1) * 128], rhs=wo[s][:, h * 512:(h + 1) * 512], start=(g == 0), stop=(g == n - 1)),
                                     [ba_g, bwo[s]], [self.bpO[o]])
                            P.op("dve", lambda e, t=t, h=h, o=o: e.scalar_tensor_tensor(out=self.xh[:, t, h * 512:(h + 1) * 512], in0=self.pO[o][:], scalar=0.5,
                                                                                       in1=self.xh[:, t, h * 512:(h + 1) * 512], op0=ALU.mult, op1=ALU.add),
                                 [self.bpO[o], self.bx[t]], [self.bx[t]])

                ntt = UNIT // 512
                in_proj(0)
                for tt in range(ntt):
                    if tt + 1 < ntt:
                        in_proj(tt + 1)
                    out_proj(tt)
                    if pio and gi == len(groups) - 1:
                        self.store_tiles(u, tt)
                        if u + 1 < self.NU:
                            self.load_tiles(u + 1, tt, first)
            if not pio:
                self.store_unit(u)

    def cross_phase(self, li, first):
        P = self.P
        self.phase_begin()
        self.carve_common()
        wbig = [self.carve(8 * D, "p (k n) -> p k n", k=8) for _ in range(2)]; bwbig = [P.buf() for _ in range(2)]
        memT = self.carve(8 * MEM, "p (k n) -> p k n", k=8); bmemT = P.buf()
        kT = self.carve(8 * MEM, "p (k n) -> p k n", k=8); bkT = P.buf()
        vA = self.carve(2 * XH * (XHD + 1), "p (a h d) -> p a h d", a=2, h=XH); bvA = P.buf()
        qT = self.carve(8 * 512, "p (k n) -> p k n", k=8); bqT = P.buf()
        pTs = [self.carve(512) for _ in range(2)]; bpTs = [P.buf() for _ in range(2)]
        ob = self.carve(4 * D, "p (a d) -> p a d", a=4); bob = [P.buf() for _ in range(4)]
        oT = self.carve(8 * 512, "p (k n) -> p k n", k=8); boT = [P.buf() for _ in range(4)]
        memf = self.f32a[:, 0:D]; bmemf = self.bf32a

        def load_big(i, w_ap):
            P.dma("pool", wbig[i], w_ap.rearrange("(kc p) n -> p kc n", p=128), [], [bwbig[i]], ("wbig", i))

        P.op("dve", lambda e: e.memset(vA, 1.0), [], [bvA])
        self.load_gain(G_MEM + li)
        for kt in range(2):
            P.dma("sp", memf[:], self.mem_in[kt * 128:(kt + 1) * 128, :], [], [bmemf], "memf")
            self.norm_T(memf[:], [bmemf], memT[:, :, kt * 128:(kt + 1) * 128], [bmemT])
        load_big(0, self.ckv[li][:, 0:D])
        for fo in range(8):
            j = self.ab % 2
            self.ab += 1
            for kc in range(8):
                P.op("pe", lambda e, fo=fo, kc=kc, j=j: e.matmul(self.pA[j][:, 0:MEM], lhsT=wbig[0][:, kc, fo * 128:(fo + 1) * 128], rhs=memT[:, kc, :],
                                                                start=(kc == 0), stop=(kc == 7)), [bwbig[0], bmemT], [self.bpA[j]])
            P.op("act", lambda e, fo=fo, j=j: e.activation(out=kT[:, fo, :], in_=self.pA[j][:, 0:MEM], func=AF.Copy), [self.bpA[j]], [bkT])
        load_big(0, self.ckv[li][:, D:2 * D])
        for kt in range(2):
            for hf in range(2):
                j = self.ab % 2
                self.ab += 1
                for kc in range(8):
                    P.op("pe", lambda e, kt=kt, hf=hf, kc=kc, j=j: e.matmul(self.pB[j][:], lhsT=memT[:, kc, kt * 128:(kt + 1) * 128],
                                                                           rhs=wbig[0][:, kc, hf * 512:(hf + 1) * 512], start=(kc == 0), stop=(kc == 7)),
                         [bwbig[0], bmemT], [self.bpB[j]])
                P.op("act", lambda e, kt=kt, hf=hf, j=j: e.activation(out=vA[:, kt, 2 * hf:2 * hf + 2, 0:XHD],
                                                                      in_=self.pB[j][:].rearrange("p (h d) -> p h d", h=2), func=AF.Copy), [self.bpB[j]], [bvA])
        load_big(0, self.cq[li])
        load_big(1, self.co[li])
        self.load_gain(G_CROSS + li)
        for u in range(self.NU):
            self.load_unit(u, first)
            self.norm_unit()
            for tt in range(UNIT // 512):
                rd = [self.bxn[4 * tt + q] for q in range(4)]
                for fo in range(8):
                    j = self.ab % 2
                    self.ab += 1
                    for kc in range(8):
                        P.op("pe", lambda e, fo=fo, kc=kc, tt=tt, j=j: e.matmul(self.pA[j][:], lhsT=wbig[0][:, kc, fo * 128:(fo + 1) * 128],
                                                                               rhs=self.xnt[:, kc, tt * 512:(tt + 1) * 512], start=(kc == 0), stop=(kc == 7)),
                             [bwbig[0]] + rd, [self.bpA[j]])
                    P.op("act", lambda e, fo=fo, j=j: e.activation(out=qT[:, fo, :], in_=self.pA[j][:], func=AF.Copy, scale=XHD ** -0.5),
                         [self.bpA[j]], [bqT])
                for h in range(XH):
                    for kt in range(2):
                        j = self.ab % 2
                        self.ab += 1
                        for dc in range(2):
                            P.op("pe", lambda e, h=h, kt=kt, dc=dc, j=j: e.matmul(self.pB[j][:], lhsT=kT[:, 2 * h + dc, kt * 128:(kt + 1) * 128],
                                                                                 rhs=qT[:, 2 * h + dc, :], start=(dc == 0), stop=(dc == 1)),
                                 [bkT, bqT], [self.bpB[j]])
                        P.op("act", lambda e, kt=kt, j=j: e.activation(out=pTs[kt], in_=self.pB[j][:], func=AF.Exp), [self.bpB[j]], [bpTs[kt]])
                    for sub in range(4):
                        o = self.io % 3
                        self.io += 1
                        r = self.xk % 2
                        self.xk += 1
                        for kt in range(2):
                            P.op("pe", lambda e, h=h, kt=kt, sub=sub, o=o: e.matmul(self.pO[o][:, 0:XHD + 1], lhsT=pTs[kt][:, sub * 128:(sub + 1) * 128],
                                                                                   rhs=vA[:, kt, h, :], start=(kt == 0), stop=(kt == 1)),
                                 [bpTs[kt], bvA], [self.bpO[o]])
                        P.op("dve", lambda e, o=o, r=r: e.reciprocal(out=self.rc[r][:, 0:1], in_=self.pO[o][:, XHD:XHD + 1]), [self.bpO[o]], [self.brc[r]])
                        P.op("dve", lambda e, h=h, sub=sub, o=o, r=r: e.tensor_scalar(out=ob[:, sub, h * XHD:(h + 1) * XHD], in0=self.pO[o][:, 0:XHD],
                                                                                     scalar1=self.rc[r][:, 0:1], scalar2=None, op0=ALU.mult),
                             [self.bpO[o], self.brc[r]], [bob[sub]])
                for sub in range(4):
                    self.transpose8(ob[:, sub, :], bob[sub], oT[:, :, sub * 128:(sub + 1) * 128], [boT[sub]])
                self.out_proj_add(tt, oT, boT, wbig[1], bwbig[1])
            self.store_unit(u)


    def gla_phase(self, li, first):
        P = self.P
        T, NU = self.T, self.NU
        GQT = P.dram("gQT", [GH * GDK, T], BF16).ap(); bGQ = P.buf("gQT")
        GKT = P.dram("gKT", [GH * GDK, T], BF16).ap(); bGK = P.buf("gKT")
        GV = P.dram("gV", [T, D], BF16).ap(); bGV = P.buf("gV")
        GR = P.dram("gR", [T, D], F32).ap(); bGR = P.buf("gR")
        GG = [P.dram(f"gG{d}", [T, 512], F32).ap() for d in range(2)]; bGG = [P.buf(f"gG{d}") for d in range(2)]
        GO = P.dram("gO", [T, D], F32).ap(); bGO = P.buf("gO")
        wk = self.gwin.rearrange("(kc p) n -> p kc n", p=128)
        self.dbgt.update(GO=GO, GR=GR, GG0=GG[0], GG1=GG[1])
        self.phase_begin()
        self.carve_common()
        ws = [self.carve(8 * 512, "p (k n) -> p k n", k=8) for _ in range(NSLOT)]; bws = [P.buf() for _ in range(NSLOT)]
        wz = self.carve(8 * 32, "p (k n) -> p k n", k=8); bwz = P.buf()
        rowb = [self.carve(UNIT) for _ in range(2)]; browb = [P.buf() for _ in range(2)]
        vrow = [self.carve(512) for _ in range(2)]; bvrow = [P.buf() for _ in range(2)]
        zaug = [self.carve(UNIT) for _ in range(2)]; bzaug = [P.buf() for _ in range(2)]
        wgb = [self.carve(512) for _ in range(2)]; bwgb = [P.buf() for _ in range(2)]
        rrow = [self.sil[0], self.sil[1]]; brrow = self.bsil
        grow_ = [self.sq[:, 0:512], self.f32a[:, 0:512]]; bgrow = [self.bsq, self.bf32a]
        P.dma("pool", wz, wk[:, :, 3072:3104], [], [bwz], "wz")
        for d in range(2):
            P.dma("pool", wgb[d][0:17, :], self.gwgb[d], [], [bwgb[d]], ("wgb", d))
            P.op("dve", lambda e, d=d: e.memset(zaug[d][0:32, :], 1.0), [], [bzaug[d]])
        self.load_gain(G_MIX + li)
        wc = rb = vs = 0
        blocks = [("q", 0), ("k", 512), ("v", 1024), ("v", 1536), ("r", 2048), ("r", 2560)]
        for u in range(NU):
            ub = u * UNIT
            if u == 0:
                self.load_unit(u, first)
            self.norm_unit()
            if u + 1 < NU:
                self.load_unit(u + 1, first)

            def load_blk(i, s_):
                P.dma("pool", ws[s_], wk[:, :, blocks[i][1]:blocks[i][1] + 512], [], [bws[s_]], ("ws", s_))

            for i0 in range(NSLOT - 1):
                load_blk(i0, (wc + i0) % NSLOT)
            for bi, (kind, col) in enumerate(blocks):
                s_ = wc % NSLOT
                if bi + NSLOT - 1 < len(blocks):
                    load_blk(bi + NSLOT - 1, (wc + NSLOT - 1) % NSLOT)
                if kind in ("q", "k"):
                    for fl in range(4):
                        r_ = rb % 2
                        rb += 1
                        for tt in range(4):
                            j = self.ab % 2
                            self.ab += 1
                            rd = [self.bxn[4 * tt + q] for q in range(4)]
                            for kc in range(8):
                                P.op("pe", lambda e, s_=s_, kc=kc, fl=fl, tt=tt, j=j: e.matmul(self.pA[j][:], lhsT=ws[s_][:, kc, fl * 128:(fl + 1) * 128],
                                                                                             rhs=self.xnt[:, kc, tt * 512:(tt + 1) * 512], start=(kc == 0), stop=(kc == 7)),
                                     [bws[s_]] + rd, [self.bpA[j]])
                            sc = GDK ** -0.5 if kind == "q" else 1.0
                            P.op("act", lambda e, r_=r_, tt=tt, j=j, sc=sc: e.activation(out=rowb[r_][:, tt * 512:(tt + 1) * 512], in_=self.pA[j][:], func=AF.Copy, scale=sc),
                                 [self.bpA[j]], [browb[r_]])
                        dst, bdst = (GQT, bGQ) if kind == "q" else (GKT, bGK)
                        P.dma("sp", dst[fl * 128:(fl + 1) * 128, ub:ub + UNIT], rowb[r_], [browb[r_]], [bdst], ("rowb", r_))
                else:
                    hf = (col % 1024) // 512
                    for t in range(NT_U):
                        j = self.ab % 2
                        self.ab += 1
                        v_ = vs % 2
                        vs += 1
                        for kc in range(8):
                            P.op("pe", lambda e, s_=s_, kc=kc, t=t, j=j: e.matmul(self.pB[j][:], lhsT=self.xnt[:, kc, t * 128:(t + 1) * 128], rhs=ws[s_][:, kc, :],
                                                                                 start=(kc == 0), stop=(kc == 7)), [bws[s_], self.bxn[t]], [self.bpB[j]])
                        r0 = ub + t * 128
                        if kind == "v":
                            P.op("act", lambda e, v_=v_, j=j: e.activation(out=vrow[v_], in_=self.pB[j][:], func=AF.Copy), [self.bpB[j]], [bvrow[v_]])
                            P.dma("sp", GV[r0:r0 + 128, hf * 512:(hf + 1) * 512], vrow[v_], [bvrow[v_]], [bGV], ("vrow", v_))
                        else:
                            P.op("act", lambda e, v_=v_, j=j: e.activation(out=rrow[v_][:], in_=self.pB[j][:], func=AF.Silu), [self.bpB[j]], [brrow[v_]])
                            P.dma("sp", GR[r0:r0 + 128, hf * 512:(hf + 1) * 512], rrow[v_][:], [brrow[v_]], [bGR], ("rrow", v_))
                wc += 1
            for d in range(2):
                for tt in range(4):
                    j = self.ab % 2
                    self.ab += 1
                    rd = [self.bxn[4 * tt + q] for q in range(4)]
                    for kc in range(8):
                        P.op("pe", lambda e, d=d, kc=kc, tt=tt, j=j: e.matmul(self.pA[j][0:16, :], lhsT=wz[:, kc, d * 16:(d + 1) * 16],
                                                                             rhs=self.xnt[:, kc, tt * 512:(tt + 1) * 512], start=(kc == 0), stop=(kc == 7)),
                             [bwz] + rd, [self.bpA[j]])
                    P.op("act", lambda e, d=d, tt=tt, j=j: e.activation(out=zaug[d][0:16, tt * 512:(tt + 1) * 512], in_=self.pA[j][0:16, :], func=AF.Copy),
                         [self.bpA[j]], [bzaug[d]])
                for t in range(NT_U):
                    j = self.ab % 2
                    self.ab += 1
                    g_ = vs % 2
                    vs += 1
                    P.op("pe", lambda e, d=d, t=t, j=j: e.matmul(self.pB[j][:], lhsT=zaug[d][0:17, t * 128:(t + 1) * 128], rhs=wgb[d][0:17, :], start=True, stop=True),
                         [bzaug[d], bwgb[d]], [self.bpB[j]])
                    P.op("act", lambda e, g_=g_, j=j: e.activation(out=grow_[g_], in_=self.pB[j][:], func=AF.Exp, scale=-1.0), [self.bpB[j]], [bgrow[g_]])
                    P.op("act", lambda e, g_=g_: e.activation(out=grow_[g_], in_=grow_[g_], func=AF.Ln, bias=1.0), [bgrow[g_]], [bgrow[g_]])
                    r0 = ub + t * 128
                    P.dma("sp", GG[d][r0:r0 + 128, :], grow_[g_], [bgrow[g_]], [bGG[d]], ("grow", g_))
        HB = GH * 128
        for d in range(2):
            self.phase_begin()
            HALF, NCH = UNIT // 2, NT_U // 2
            qT2 = [self.carve(GH * HALF, "p (h t) -> p h t", h=GH) for _ in range(2)]; bqT2 = [P.buf() for _ in range(2)]
            kT2 = [self.carve(GH * HALF, "p (h t) -> p h t", h=GH) for _ in range(2)]; bkT2 = [P.buf() for _ in range(2)]
            xntf = self.xnt[:].rearrange("p k t -> p (k t)")
            v2 = [xntf[:, b_ * NCH * D:(b_ + 1) * NCH * D].rearrange("p (c f) -> p c f", c=NCH) for b_ in range(2)]; bv2 = [P.buf() for _ in range(2)]
            qd = [self.carve(HB) for _ in range(2)]; bqd = [P.buf() for _ in range(2)]
            kd = [self.carve(HB) for _ in range(2)]; bkd = [P.buf() for _ in range(2)]
            at = [self.carve(HB) for _ in range(2)]; bat = [P.buf() for _ in range(2)]
            ktok = [self.carve(HB) for _ in range(2)]; bktok = [P.buf() for _ in range(2)]
            Sb = self.carve(GH * GDV, "p (h v) -> p h v", h=GH); bSb = P.buf()
            xflat = self.xh[:].rearrange("p a b -> p (a b)")
            g2 = [xflat[:, b_ * NCH * 512:(b_ + 1) * NCH * 512].rearrange("p (c f) -> p c f", c=NCH) for b_ in range(2)]; bg2 = [P.buf() for _ in range(2)]
            ost = [xflat[:, 8192 + i * 1024:8192 + (i + 1) * 1024] for i in range(2)]; bost = [P.buf() for _ in range(2)]
            e1 = [xflat[:, 10240 + i * HB:10240 + (i + 1) * HB] for i in range(2)]; be1 = [P.buf() for _ in range(2)]
            e2 = [xflat[:, 11264 + i * HB:11264 + (i + 1) * HB] for i in range(2)]; be2 = [P.buf() for _ in range(2)]
            gof = [xflat[:, 12288 + i * 1024:12288 + (i + 1) * 1024] for i in range(2)]; bgof = [P.buf() for _ in range(2)]
            msk4 = self.sil[0][:, 0:HB]; tri = self.sil[1][:, 0:128]; btri = P.buf()
            eL = [self.sil[1][:, 128 + 4 * i:132 + 4 * i] for i in range(2)]; beL = [P.buf() for _ in range(2)]
            S = self.f32a[:, 0:GH * GDV].rearrange("p (h v) -> p h v", h=GH); bS = self.bf32a
            Sf = self.f32a[:, 0:GH * GDV]
            gn = self.gt; bgn = self.bg
            P.dma("sp", tri, self.gtri[d], [], [btri], "tri")
            for h in range(GH):
                P.dma("sp", msk4[:, h * 128:(h + 1) * 128], self.gtri[2 + d], [], [btri], "msk")
            P.op("dve", lambda e: e.memset(S, 0.0), [], [bS])
            P.op("dve", lambda e: e.memset(Sb, 0.0), [], [bSb])
            if d == 1:
                self.carve_common()
                wo_ = self.carve(8 * D, "p (k n) -> p k n", k=8); bwo_ = P.buf()
                ob = self.carve(D); bob = P.buf()
                oT = self.carve(8 * 128, "p (k n) -> p k n", k=8); boT = P.buf()
                P.dma("pool", wo_, self.gwout.rearrange("(kc p) n -> p kc n", p=128), [], [bwo_], "gwo")
                P.dma("sp", gn[:], self.gnorm, [], [bgn], "g")
            lc = 127 if d == 0 else 0
            grr1 = xflat[:, 14336:15360]; bgrr1 = P.buf()
            xcs = [xflat[:, 15360 + hf * 512:15360 + (hf + 1) * 512] for hf in range(2)]; bxcs = [P.buf() for _ in range(2)]
            NH = T // HALF
            order_h = list(range(NH)) if d == 0 else list(range(NH - 1, -1, -1))
            order_c = list(range(NCH)) if d == 0 else list(range(NCH - 1, -1, -1))
            nsteps = NH * NCH

            def loads_half(pos):
                hb_ = pos % 2
                h0 = order_h[pos] * HALF
                P.dma("sp", qT2[hb_], GQT.rearrange("(h p) t -> p h t", p=128)[:, :, h0:h0 + HALF], [bGQ], [bqT2[hb_]], ("qT2", hb_))
                P.dma("sp", kT2[hb_], GKT.rearrange("(h p) t -> p h t", p=128)[:, :, h0:h0 + HALF], [bGK], [bkT2[hb_]], ("kT2", hb_))
                P.dma("sp", g2[hb_], GG[d][h0:h0 + HALF, :].rearrange("(c p) f -> p c f", p=128), [bGG[d]], [bg2[hb_]], ("g2", hb_))
                P.dma("sp", v2[hb_], GV[h0:h0 + HALF, :].rearrange("(c p) f -> p c f", p=128), [bGV], [bv2[hb_]], ("v2", hb_))

            def where(k):
                pos = k // NCH
                hu = order_h[pos]
                c = order_c[k % NCH]
                return pos % 2, c, hu * HALF + c * 128, hu // 2

            def stage_A(k):
                hb, c, r0, u = where(k)
                qTu, kTu, gu, bqTu, bkTu, bgu = qT2[hb], kT2[hb], g2[hb], bqT2[hb], bkT2[hb], bg2[hb]
                i = k % 2
                pA, bpA_, pB, bpB_ = self.pA[i], self.bpA[i], self.pB[i], self.bpB[i]
                e1i, e2i, eLi, qdi, kdi, ati, kti = e1[i], e2[i], eL[i], qd[i], kd[i], at[i], ktok[i]
                if d == 1:
                    P.dma("sp", gof[i], GO[r0:r0 + 128, :], [bGO], [bgof[i]], ("gof", i))
                for h in range(GH):
                    P.op("pe", lambda e, c=c, h=h, pA=pA, gu=gu: e.matmul(pA[:, h * 128:(h + 1) * 128], lhsT=gu[:, c, h * 128:(h + 1) * 128], rhs=tri, start=True, stop=True),
                         [bgu, btri], [bpA_])
                P.op("act", lambda e, pA=pA, e1i=e1i: e.activation(out=e1i, in_=pA[:, 0:HB], func=AF.Exp), [bpA_], [be1[i]])
                P.op("act", lambda e, pA=pA, e2i=e2i: e.activation(out=e2i, in_=pA[:, 0:HB], func=AF.Exp, scale=-1.0), [bpA_], [be2[i]])
                P.op("act", lambda e, pA=pA, eLi=eLi, lc=lc: e.activation(out=eLi, in_=pA[:, 0:HB].rearrange("p (h n) -> p h n", h=GH)[:, :, lc], func=AF.Exp), [bpA_], [beL[i]])
                P.op("dve", lambda e, c=c, qdi=qdi, e1i=e1i, qTu=qTu: e.tensor_tensor(out=qdi.rearrange("p (h n) -> p h n", h=GH), in0=qTu[:, :, c * 128:(c + 1) * 128],
                                                                            in1=e1i.rearrange("p (h n) -> p h n", h=GH), op=ALU.mult), [bqTu, be1[i]], [bqd[i]])
                P.op("dve", lambda e, c=c, kdi=kdi, e2i=e2i, kTu=kTu: e.tensor_tensor(out=kdi.rearrange("p (h n) -> p h n", h=GH), in0=kTu[:, :, c * 128:(c + 1) * 128],
                                                                            in1=e2i.rearrange("p (h n) -> p h n", h=GH), op=ALU.mult), [bkTu, be2[i]], [bkd[i]])
                for h in range(GH):
                    P.op("pe", lambda e, h=h, pB=pB, kdi=kdi, qdi=qdi: e.matmul(pB[:, h * 128:(h + 1) * 128], lhsT=kdi[:, h * 128:(h + 1) * 128], rhs=qdi[:, h * 128:(h + 1) * 128],
                                                                               start=True, stop=True), [bkd[i], bqd[i]], [bpB_])
                for h in range(GH):
                    P.op("pe", lambda e, h=h, kdi=kdi: e.transpose(self.pT[:, h * 128:(h + 1) * 128], kdi[:, h * 128:(h + 1) * 128], self.idb[:]), [bkd[i], self.bidb], [self.bpT])
                P.op("dve", lambda e, pB=pB, ati=ati: e.tensor_tensor(out=ati, in0=pB[:, 0:HB], in1=msk4, op=ALU.mult), [bpB_, btri], [bat[i]])
                P.op("act", lambda e, kti=kti: e.activation(out=kti, in_=self.pT[:, 0:HB], func=AF.Copy), [self.bpT], [bktok[i]])

            def stage_B(k):
                hb, c, r0, u = where(k)
                vu, bvu = v2[hb], bv2[hb]
                i = k % 2
                pB, bpB_ = self.pB[i], self.bpB[i]
                eLi, qdi, ati, kti, osti = eL[i], qd[i], at[i], ktok[i], ost[i]
                if d == 1:
                    P.dma("sp", grr1, GR[r0:r0 + 128, :], [bGR], [bgrr1], "grr")
                for h in range(GH):
                    po, bpo = self.pO[h // 2], self.bpO[h // 2]
                    cs = (h % 2) * GDV
                    P.op("pe", lambda e, c=c, h=h, po=po, cs=cs, ati=ati, vu=vu: e.matmul(po[:, cs:cs + GDV], lhsT=ati[:, h * 128:(h + 1) * 128], rhs=vu[:, c, h * GDV:(h + 1) * GDV],
                                                                                  start=True, stop=False), [bat[i], bvu], [bpo])
                    P.op("pe", lambda e, h=h, po=po, cs=cs, qdi=qdi: e.matmul(po[:, cs:cs + GDV], lhsT=qdi[:, h * 128:(h + 1) * 128], rhs=Sb[:, h, :], start=False, stop=True),
                         [bqd[i], bSb], [bpo])
                for h in range(GH):
                    pu, bpu = (self.pO[2], self.bpO[2]) if h < 2 else (pB, bpB_)
                    cs = (h % 2) * GDV
                    P.op("pe", lambda e, c=c, h=h, pu=pu, cs=cs, kti=kti, vu=vu: e.matmul(pu[:, cs:cs + GDV], lhsT=kti[:, h * 128:(h + 1) * 128], rhs=vu[:, c, h * GDV:(h + 1) * GDV],
                                                                                  start=True, stop=True), [bktok[i], bvu], [bpu])
                for hh in range(2):
                    po, bpo = self.pO[hh], self.bpO[hh]
                    if d == 0:
                        P.op("act", lambda e, hh=hh, po=po, osti=osti: e.activation(out=osti[:, hh * 512:(hh + 1) * 512], in_=po[:, 0:512], func=AF.Copy), [bpo], [bost[i]])
                    else:
                        gofi = gof[i]
                        P.op("dve", lambda e, hh=hh, po=po, osti=osti, gofi=gofi: e.tensor_tensor(out=osti[:, hh * 512:(hh + 1) * 512], in0=po[:, 0:512],
                                                                                                 in1=gofi[:, hh * 512:(hh + 1) * 512], op=ALU.add), [bpo, bgof[i]], [bost[i]])
                P.op("dve", lambda e: e.tensor_tensor(out=Sf[:, 0:512], in0=Sf[:, 0:512], in1=self.pO[2][:, 0:512], op=ALU.add), [bS, self.bpO[2]], [bS])
                P.op("dve", lambda e, pB=pB: e.tensor_tensor(out=Sf[:, 512:1024], in0=Sf[:, 512:1024], in1=pB[:, 0:512], op=ALU.add), [bS, bpB_], [bS])
                for h in range(GH):
                    P.op("dve", lambda e, h=h, eLi=eLi: e.tensor_scalar(out=S[:, h, :], in0=S[:, h, :], scalar1=eLi[:, h:h + 1], scalar2=None, op0=ALU.mult), [bS, beL[i]], [bS])
                P.op("act", lambda e: e.activation(out=Sb.rearrange("p h v -> p (h v)"), in_=Sf, func=AF.Copy), [bS], [bSb])
                if d == 0:
                    P.dma("sp", GO[r0:r0 + 128, :], osti, [bost[i]], [bGO], ("ost", i))
                    return
                r_ = self.xk % 2
                self.xk += 1
                for h in range(GH):
                    P.op("dve", lambda e, h=h, osti=osti: e.tensor_tensor(out=self.sq[:, 0:GDV], in0=osti[:, h * GDV:(h + 1) * GDV], in1=osti[:, h * GDV:(h + 1) * GDV], op=ALU.mult),
                         [bost[i]], [self.bsq])
                    P.op("dve", lambda e, h=h, r_=r_: e.reduce_sum(out=self.rc[r_][:, h:h + 1], in_=self.sq[:, 0:GDV], axis=AX.X), [self.bsq], [self.brc[r_]])
                P.op("act", lambda e, r_=r_: e.activation(out=self.rc[r_][:, 0:GH], in_=self.rc[r_][:, 0:GH], func=AF.Ln, bias=EPS, scale=1.0 / GDV), [self.brc[r_]], [self.brc[r_]])
                P.op("act", lambda e, r_=r_: e.activation(out=self.rc[r_][:, 0:GH], in_=self.rc[r_][:, 0:GH], func=AF.Exp, scale=-0.5), [self.brc[r_]], [self.brc[r_]])
                for h in range(GH):
                    P.op("dve", lambda e, h=h, osti=osti, r_=r_: e.scalar_tensor_tensor(out=osti[:, h * GDV:(h + 1) * GDV], in0=osti[:, h * GDV:(h + 1) * GDV],
                                                                                       scalar=self.rc[r_][:, h:h + 1], in1=gn[:, h * GDV:(h + 1) * GDV], op0=ALU.mult, op1=ALU.mult),
                         [bost[i], self.brc[r_], bgn], [bost[i]])
                P.op("dve", lambda e, osti=osti: e.tensor_tensor(out=ob, in0=osti, in1=grr1, op=ALU.mult), [bost[i], bgrr1], [bob])
                self.transpose8(ob, bob, oT, [boT])
                for hf in range(2):
                    o = 2 - hf
                    xci, bxci = xcs[hf], bxcs[hf]
                    src_t = (self.x_in if first else self.X)
                    P.dma("sp", xci, src_t[r0:r0 + 128, hf * 512:(hf + 1) * 512], [] if first else [self.bX[u]], [bxci], ("xc", hf))
                    for fc in range(8):
                        P.op("pe", lambda e, fc=fc, hf=hf, o=o: e.matmul(self.pO[o][:], lhsT=oT[:, fc, :], rhs=wo_[:, fc, hf * 512:(hf + 1) * 512],
                                                                        start=(fc == 0), stop=(fc == 7)), [boT, bwo_], [self.bpO[o]])
                    P.op("dve", lambda e, o=o, xci=xci: e.tensor_tensor(out=xci, in0=xci, in1=self.pO[o][:], op=ALU.add), [self.bpO[o], bxci], [bxci])
                    P.dma("sp", self.X[r0:r0 + 128, hf * 512:(hf + 1) * 512], xci, [bxci], [self.bX[u]], ("xcs", hf))

            loads_half(0)
            stage_A(0)
            for k in range(nsteps):
                if k % NCH == 0 and k // NCH + 1 < NH:
                    loads_half(k // NCH + 1)
                if k + 1 < nsteps:
                    stage_A(k + 1)
                stage_B(k)

    def dilated_phase(self, li, first):
        P = self.P
        T, TP, NU = self.T, self.TP, self.NU
        QT = P.dram("dQT", [3, D, T], BF16).ap(); bQT = P.buf("dQT")
        KT = P.dram("dKT", [3, D, TP], BF16).ap(); bKT = P.buf("dKT")
        VA = P.dram("dVA", [3, TP, VW], BF16).ap(); bVA = P.buf("dVA")
        ACC = P.dram("dACC", [3, T, VW], F32).ap(); bACC = P.buf("dACC")
        self.phase_begin()
        self.carve_common()
        ws = [self.carve(8 * 512, "p (k n) -> p k n", k=8) for _ in range(NSLOT)]; bws = [P.buf() for _ in range(NSLOT)]
        rowb = [self.carve(UNIT) for _ in range(2)]; browb = [P.buf() for _ in range(2)]
        vst = [self.carve(8 * (DHD + 1), "p (h d) -> p h d", h=8) for _ in range(2)]; bvst = [P.buf() for _ in range(2)]
        zt = self.carve(VW); bzt = P.buf()
        v8 = self.f32a[:, 0:NT_U * 8].rearrange("p (t e) -> p t e", t=NT_U); bv8 = self.bf32a
        P.op("dve", lambda e: e.memset(zt, 0.0), [], [bzt])
        zk = 0
        for g in range(3):
            for side in (0, PAD + T):
                for fo in range(8):
                    P.dma("sp", KT[g, fo * 128:(fo + 1) * 128, side:side + PAD], zt[:, 0:PAD], [bzt], [bKT], ("z", zk % 4)); zk += 1
                for j in range(PAD // 128):
                    P.dma("sp", VA[g, side + j * 128:side + (j + 1) * 128, :], zt, [bzt], [bVA], ("z", zk % 4)); zk += 1
        self.load_gain(G_MIX + li)
        wq_k = self.dqkv.rearrange("(kc p) n -> p kc n", p=128)
        wc = rb = vs = 0
        for u in range(NU):
            ub = u * UNIT
            if u == 0:
                self.load_unit(u, first)
            self.norm_unit()
            if u + 1 < NU:
                self.load_unit(u + 1, first)
            P.dma("sp", v8, self.valid8[ub:ub + UNIT, :].rearrange("(t p) e -> p t e", p=128), [], [bv8], "v8")
            blocks = [(c3, g, hf) for c3 in range(3) for g in range(3) for hf in range(2)]

            def load_blk(i, s):
                c3, g, hf = blocks[i]
                col = c3 * 3 * D + g * D + hf * 512
                P.dma("pool", ws[s], wq_k[:, :, col:col + 512], [], [bws[s]], ("ws", s))

            for i0 in range(NSLOT - 1):
                load_blk(i0, (wc + i0) % NSLOT)
            for bi, (c3, g, hf) in enumerate(blocks):
                s = wc % NSLOT
                if bi + NSLOT - 1 < len(blocks):
                    load_blk(bi + NSLOT - 1, (wc + NSLOT - 1) % NSLOT)
                if c3 < 2:
                    for fl in range(4):
                        r_ = rb % 2
                        rb += 1
                        for tt in range(4):
                            j = self.ab % 2
                            self.ab += 1
                            rd = [self.bxn[4 * tt + q] for q in range(4)]
                            for kc in range(8):
                                P.op("pe", lambda e, s=s, kc=kc, fl=fl, tt=tt, j=j: e.matmul(self.pA[j][:], lhsT=ws[s][:, kc, fl * 128:(fl + 1) * 128],
                                                                                            rhs=self.xnt[:, kc, tt * 512:(tt + 1) * 512], start=(kc == 0), stop=(kc == 7)),
                                     [bws[s]] + rd, [self.bpA[j]])
                            eng = "act" if tt % 2 == 0 else "dve"
                            if eng == "act":
                                P.op("act", lambda e, r_=r_, tt=tt, j=j: e.activation(out=rowb[r_][:, tt * 512:(tt + 1) * 512], in_=self.pA[j][:], func=AF.Copy),
                                     [self.bpA[j]], [browb[r_]])
                            else:
                                P.op("dve", lambda e, r_=r_, tt=tt, j=j: e.tensor_copy(out=rowb[r_][:, tt * 512:(tt + 1) * 512], in_=self.pA[j][:]),
                                     [self.bpA[j]], [browb[r_]])
                        fr = (hf * 4 + fl) * 128
                        if c3 == 0:
                            P.dma("sp", QT[g, fr:fr + 128, ub:ub + UNIT], rowb[r_], [browb[r_]], [bQT], ("rowb", r_))
                        else:
                            P.dma("sp", KT[g, fr:fr + 128, PAD + ub:PAD + ub + UNIT], rowb[r_], [browb[r_]], [bKT], ("rowb", r_))
                else:
                    for t in range(NT_U):
                        j = self.ab % 2
                        self.ab += 1
                        v_ = vs % 2
                        vs += 1
                        for kc in range(8):
                            P.op("pe", lambda e, s=s, kc=kc, t=t, j=j: e.matmul(self.pB[j][:], lhsT=self.xnt[:, kc, t * 128:(t + 1) * 128], rhs=ws[s][:, kc, :],
                                                                               start=(kc == 0), stop=(kc == 7)), [bws[s], self.bxn[t]], [self.bpB[j]])
                        P.op("act", lambda e, v_=v_, j=j, t=t: e.activation(out=vst[v_][:, :, 0:DHD], in_=self.pB[j][:].rearrange("p (h d) -> p h d", h=8), func=AF.Copy,
                                                                         scale=v8[:, t, 0:1]), [self.bpB[j], bv8], [bvst[v_]])
                        P.op("dve", lambda e, v_=v_, t=t: e.tensor_copy(out=vst[v_][:, :, DHD], in_=v8[:, t, :]), [bv8], [bvst[v_]])
                        P.dma("sp", VA[g, PAD + ub + t * 128:PAD + ub + (t + 1) * 128, hf * 520:(hf + 1) * 520], vst[v_].rearrange("p h d -> p (h d)"),
                              [bvst[v_]], [bVA], ("vst", v_))
                wc += 1
        self.phase_begin()
        Eall = self.carve(3 * DH * 256, "p (g h c) -> p g h c", g=3, h=DH); bE = P.buf()
        qt = [self.carve(UNIT) for _ in range(3)]; bqt = [P.buf() for _ in range(3)]
        kw = [UNIT + 128 * r for r in DIL_R]
        kt = [self.carve(kw[g]) for g in range(3)]; bkt = [P.buf() for _ in range(3)]
        vsub = [self.carve(32 * 130, "p (c d) -> p c d", c=32) for _ in range(2)]; bvsub = [P.buf() for _ in range(2)]
        pexp = [self.carve(512) for _ in range(2)]; bpexp = [P.buf() for _ in range(2)]
        pmul = [self.carve(512) for _ in range(2)]; bpmul = [P.buf() for _ in range(2)]
        xflat = self.xh[:].rearrange("p a b -> p (a b)")
        stg = [xflat[:, i * 2080:(i + 1) * 2080].rearrange("p (b d) -> p b d", b=16) for i in range(2)]; bstg = [P.buf() for _ in range(2)]
        ebuf = self.f32a[:, 0:256]
        for g in range(3):
            for h in range(DH):
                P.dma("sp", ebuf, self.dbias[g, h], [], [self.bf32a], "eb")
                P.op("act", lambda e, g=g, h=h: e.activation(out=Eall[:, g, h, :], in_=ebuf, func=AF.Exp), [self.bf32a], [bE])
        vi = pe_ = si = 0
        for u in range(NU):
            ub = u * UNIT
            for hp in range(8):
                for g, r in enumerate(DIL_R):
                    P.dma("sp", qt[g], QT[g, hp * 128:(hp + 1) * 128, ub:ub + UNIT], [bQT], [bqt[g]], ("qt", g))
                    lo = PAD + ub - 64 * r
                    P.dma("sp", kt[g], KT[g, hp * 128:(hp + 1) * 128, lo:lo + kw[g]], [bKT], [bkt[g]], ("kt", g))
                    qv = qt[g].rearrange("p (n r) -> p r n", r=r)
                    kv = kt[g].rearrange("p (n r) -> p r n", r=r)
                    nb = 16 // r
                    nch = nb + 1
                    for rho in range(r):
                        v_ = vi % 2
                        vi += 1
                        s_ = si % 2
                        si += 1
                        base = PAD + ub - 64 * r
                        vsrc = VA[g, base:base + 128 * r * nch, hp * 130:(hp + 1) * 130].rearrange("(c p r) d -> p c r d", p=128, r=r)[:, :, rho, :]
                        P.dma("sp", vsub[v_][:, 0:nch, :], vsrc, [bVA], [bvsub[v_]], ("vsub", v_))
                        for qb in range(nb):
                            j = self.ab % 2
                            self.ab += 1
                            x_ = pe_ % 2
                            pe_ += 1
                            for h2 in range(2):
                                bank, bbank = (self.pA[j], self.bpA[j]) if h2 == 0 else (self.pB[j], self.bpB[j])
                                for kc in range(2):
                                    c = qb + kc
                                    P.op("pe", lambda e, h2=h2, kc=kc, c=c, qb=qb, rho=rho, kv=kv, qv=qv, bank=bank: e.matmul(
                                        bank[:, kc * 128:(kc + 1) * 128],
                                        lhsT=kv[h2 * 64:(h2 + 1) * 64, rho, c * 128:(c + 1) * 128],
                                        rhs=qv[h2 * 64:(h2 + 1) * 64, rho, qb * 128:(qb + 1) * 128], start=True, stop=True),
                                         [bkt[g], bqt[g]], [bbank])
                            for h2 in range(2):
                                bank, bbank = (self.pA[j], self.bpA[j]) if h2 == 0 else (self.pB[j], self.bpB[j])
                                P.op("act", lambda e, h2=h2, bank=bank, x_=x_: e.activation(out=pexp[x_][:, h2 * 256:(h2 + 1) * 256], in_=bank[:, 0:256], func=AF.Exp,
                                                                                           scale=DHD ** -0.5), [bbank], [bpexp[x_]])
                            P.op("dve", lambda e, g=g, hp=hp, x_=x_: e.tensor_tensor(out=pmul[x_], in0=pexp[x_], in1=Eall[:, g, 2 * hp:2 * hp + 2, :].rearrange("p h c -> p (h c)"),
                                                                                    op=ALU.mult), [bpexp[x_], bE], [bpmul[x_]])
                            o = self.io % 3
                            self.io += 1
                            for h2 in range(2):
                                for kc in range(2):
                                    c = qb + kc
                                    P.op("pe", lambda e, h2=h2, kc=kc, c=c, v_=v_, x_=x_, o=o: e.matmul(
                                        self.pO[o][:, h2 * 65:(h2 + 1) * 65], lhsT=pmul[x_][:, (h2 * 2 + kc) * 128:(h2 * 2 + kc + 1) * 128],
                                        rhs=vsub[v_][:, c, h2 * 65:(h2 + 1) * 65], start=(kc == 0), stop=(kc == 1)),
                                         [bpmul[x_], bvsub[v_]], [self.bpO[o]])
                            P.op("act", lambda e, s_=s_, qb=qb, o=o: e.activation(out=stg[s_][:, qb, :], in_=self.pO[o][:, 0:130], func=AF.Copy), [self.bpO[o]], [bstg[s_]])
                        adst = ACC[g, ub:ub + 128 * r * nb, hp * 130:(hp + 1) * 130].rearrange("(b p r) d -> p b r d", p=128, r=r)[:, :, rho, :]
                        P.dma("sp", adst, stg[s_][:, 0:nb, :], [bstg[s_]], [bACC], ("stg", s_))
        self.phase_begin()
        self.carve_common()
        wo_ = self.carve(8 * D, "p (k n) -> p k n", k=8); bwo_ = P.buf()
        ob = self.carve(4 * D, "p (a d) -> p a d", a=4); bob = [P.buf() for _ in range(4)]
        oT = self.carve(8 * 512, "p (k n) -> p k n", k=8); boT = [P.buf() for _ in range(4)]
        acc1 = self.sq[:, 0:VW].rearrange("p (h d) -> p h d", h=DH); bacc1 = self.bsq
        acc2 = self.f32a[:, 0:VW].rearrange("p (h d) -> p h d", h=DH); bacc2 = self.bf32a
        P.dma("pool", wo_, self.dwo.rearrange("(kc p) n -> p kc n", p=128), [], [bwo_], "dwo")
        for u in range(NU):
            ub = u * UNIT
            self.load_unit(u, first)
            for tt in range(4):
                for sub in range(4):
                    t = 4 * tt + sub
                    r0 = ub + t * 128
                    P.dma("sp", acc1, ACC[0, r0:r0 + 128, :].rearrange("p (h d) -> p h d", h=DH), [bACC], [bacc1], "acc1")
                    for g in (1, 2):
                        P.dma("sp", acc2, ACC[g, r0:r0 + 128, :].rearrange("p (h d) -> p h d", h=DH), [bACC], [bacc2], "acc2")
                        P.op("dve", lambda e: e.tensor_tensor(out=acc1, in0=acc1, in1=acc2, op=ALU.add), [bacc1, bacc2], [bacc1])
                    r = self.xk % 2
                    self.xk += 1
                    P.op("dve", lambda e, r=r: e.reciprocal(out=self.rc[r][:], in_=acc1[:, :, DHD]), [bacc1], [self.brc[r]])
                    for h in range(DH):
                        P.op("dve", lambda e, h=h, r=r, sub=sub: e.tensor_scalar(out=ob[:, sub, h * DHD:(h + 1) * DHD], in0=acc1[:, h, 0:DHD],
                                                                                scalar1=self.rc[r][:, h:h + 1], scalar2=None, op0=ALU.mult),
                             [bacc1, self.brc[r]], [bob[sub]])
                    self.transpose8(ob[:, sub, :], bob[sub], oT[:, :, sub * 128:(sub + 1) * 128], [boT[sub]])
                self.out_proj_add(tt, oT, boT, wo_, bwo_)
            self.store_unit(u)

    def out_phase(self, final, first):
        P = self.P
        self.phase_begin()
        y_t = self.y.rearrange("(t p) d -> p t d", p=128)
        if final:
            self.load_gain(G_FINAL)
        for u in range(self.NU):
            self.load_unit(u, first)
            if final:
                for t in range(NT_U):
                    i = self.rms_rstd(self.xh[:, t, :], [self.bx[t]])
                    P.op("dve", lambda e, t=t, i=i: e.scalar_tensor_tensor(out=self.xh[:, t, :], in0=self.xh[:, t, :], scalar=self.ss[i][:], in1=self.gt[:],
                                                                          op0=ALU.mult, op1=ALU.mult), [self.bx[t], self.bss[i], self.bg], [self.bx[t]])
            self.outs.append(P.dma("sp", y_t[:, u * NT_U:(u + 1) * NT_U, :], self.xh[:], list(self.bx), [self.bY[u]], ("y", u % 2)))

    def build(self):
        first = True
        windowed = False
        for ph in self.phases:
            if ph == "final":
                continue
            kind, li = ph.split(":")
            li = int(li)
            if self.T1 and not windowed and (li == 1 or kind in ("cross", "ffn2")):
                assert not first, "the window needs a whole-sequence phase before it"
                self.enter_window()
                windowed = True
            if kind == "ffn1":
                self.ffn_phase(2 * li, G_FFN1 + li, first)
            elif kind == "ffn2":
                self.ffn_phase(2 * li + 1, G_FFN2 + li, first)
            elif kind == "cross":
                self.cross_phase(li, first)
            elif kind == "mix" and li == 1:
                self.dilated_phase(li, first)
            elif kind == "mix":
                self.gla_phase(li, first)
            else:
                raise ValueError(ph)
            first = False
            self._gather = False
        if self.dbg:
            self._dbg_src = self.dbgt[self.dbg]
        self.out_phase("final" in self.phases, first)
        self.P.emit(self.outs)
        return self.nc


_j = np.arange(128)[:, None]; _i = np.arange(128)[None, :]
GLA_TRI = np.stack([np.where(_j <= _i, -1.0 / 16.0, 0.0), np.where(_j >= _i, -1.0 / 16.0, 0.0),
                    np.where(_j <= _i, 1.0, 0.0), np.where(_j >= _i, 1.0, 0.0)]).astype(np.float32)


def pack_weights(inputs):
    f = lambda k: np.asarray(inputs[k], np.float32)
    gl = [f("norm_ffn1")[0], f("norm_ffn1")[1], f("norm_mix")[0], f("norm_mix")[1], f("norm_cross")[0], f("norm_cross")[1],
          f("norm_mem")[0], f("norm_mem")[1], f("norm_ffn2")[0], f("norm_ffn2")[1], f("norm_final")]
    gains = np.ascontiguousarray(np.broadcast_to(np.stack(gl)[:, None, :], (11, 128, D)))
    p = np.arange(128)[:, None]; q = np.arange(128)[None, :]
    rb = f("rel_bias")
    dbias = np.full((3, DH, 128, 256), -30000.0, np.float32)
    for g, r in enumerate(DIL_R):
        for kc in range(2):
            m = p - q - 64 + 128 * kc
            ok = np.abs(m) <= 64
            bk = t5_bucket(m * r)
            for h in range(DH):
                tile = rb[bk, g * DH + h]
                dbias[g, h, :, kc * 128:(kc + 1) * 128] = np.where(ok, tile, dbias[g, h, :, kc * 128:(kc + 1) * 128])
    return {
        "gains": gains, "ident": np.eye(128, dtype=np.float32),
        "ffn_in": np.ascontiguousarray(np.stack([f("ffn1_in")[0], f("ffn2_in")[0], f("ffn1_in")[1], f("ffn2_in")[1]])),
        "ffn_out": np.ascontiguousarray(np.stack([f("ffn1_out")[0], f("ffn2_out")[0], f("ffn1_out")[1], f("ffn2_out")[1]])),
        "cross_q": f("cross_w_q"), "cross_kv": f("cross_w_kv"), "cross_o": f("cross_w_o"),
        "gla_w_in": np.ascontiguousarray(f("gla_w_in")[0]),
        "gla_wgb": np.ascontiguousarray(np.stack([np.concatenate([f("gla_wg_f")[0], f("gla_bg_f")[0][None, :]], 0),
                                                  np.concatenate([f("gla_wg_b")[0], f("gla_bg_b")[0][None, :]], 0)])),
        "gla_norm": np.ascontiguousarray(np.broadcast_to(f("gla_norm")[0][None, :], (128, D))),
        "gla_w_out": np.ascontiguousarray(f("gla_w_out")[0]),
        "gla_tri": GLA_TRI,
        "dil_qkv": np.ascontiguousarray(f("dil_w_qkv")[0]), "dil_o": np.ascontiguousarray(f("dil_w_out")[0]), "dil_bias": dbias,
    }


def run_seqs(seqs, mems, inputs, T, phases, dbg=None):
    n = 8
    w = pack_weights(inputs)
    nc = K(T, phases, dbg).build()
    in_maps = []
    for c in range(n):
        xs = np.zeros((T, D), np.float32)
        mm = np.zeros((MEM, D), np.float32)
        v8 = np.zeros((T, 8), np.float32)
        if c < len(seqs):
            xs[:seqs[c].shape[0]] = seqs[c]
            mm[:] = mems[c]
            v8[:seqs[c].shape[0]] = 1.0
        in_maps.append(dict(w, x=xs, mem=mm, valid8=v8))
    res = run_bass_kernel_spmd(nc, in_maps, core_ids=list(range(n)))
    return [np.asarray(res.results[c]["y"][:seqs[c].shape[0]], np.float32) for c in range(len(seqs))]


WIN = 4096
T1W = WIN + 2 * PAD


def kernel(**inputs):
    xp = np.asarray(inputs["x_prompt"], np.float32)
    xs = np.asarray(inputs["x_sample"], np.float32)
    mp = np.asarray(inputs["mem_prompt"], np.float32)
    ms = np.asarray(inputs["mem_sample"], np.float32)
    seqs = [xp[b] for b in range(xp.shape[0])] + [xs[b] for b in range(xs.shape[0])]
    mems = [mp[b] for b in range(mp.shape[0])] + [ms[b] for b in range(ms.shape[0])]
    jobs = [(si, w0) for si, sq in enumerate(seqs) for w0 in range(0, sq.shape[0], WIN)]
    n = 8
    assert len(jobs) <= n, len(jobs)
    T0 = -(-max(sq.shape[0] for sq in seqs) // UNIT) * UNIT
    w = pack_weights(inputs)
    nc = K(T0, FULL_PHASES, T1=T1W).build()
    in_maps = []
    for c in range(n):
        x = np.zeros((T0, D), np.float32)
        mm = np.zeros((MEM, D), np.float32)
        v8 = np.zeros((T1W, 8), np.float32)
        idx = np.zeros((T1W,), np.int32)
        if c < len(jobs):
            si, w0 = jobs[c]
            S = seqs[si].shape[0]
            x[:S] = seqs[si]
            mm[:] = mems[si]
            rows = np.arange(w0 - PAD, w0 + WIN + PAD)
            ok = (rows >= 0) & (rows < S)
            v8[ok] = 1.0
            idx[:] = np.clip(rows, 0, S - 1)
        in_maps.append(dict(w, x=x, mem=mm, valid8=v8, win_idx=np.ascontiguousarray(idx.reshape(T1W // 128, 128).T)))
    res = run_bass_kernel_spmd(nc, in_maps, core_ids=list(range(n)))
    outs = [np.zeros(sq.shape, np.float32) for sq in seqs]
    for c, (si, w0) in enumerate(jobs):
        S = seqs[si].shape[0]
        hi = min(w0 + WIN, S)
        outs[si][w0:hi] = np.asarray(res.results[c]["y"][PAD:PAD + (hi - w0)], np.float32)
    yp = np.stack(outs[:xp.shape[0]]).astype(np.float32)
    ys = np.stack(outs[xp.shape[0]:]).astype(np.float32)
    return (yp, ys)
```
